# Optimizing a Trainium2 kernel written in Bass

```python
import math
import functools
import jax
import jax.numpy as jnp
from jax import lax
import numpy as np

D_MODEL = 1024
BATCH = 2
SEQ = 8192
DEPTH = 2
DEC_BATCH = 32
DEC_SEQ = 1
PAST_LEN = 8192
PAGE_SIZE = 128

BRANCH_W = D_MODEL // 2
N_BRANCH = 4
D_CONV = BRANCH_W
CONV_A_W = 31
N_HEADS = 8
HEAD_DIM = BRANCH_W // N_HEADS
N_KV_HEADS = 2
N_IDX_HEADS = 8
D_IDX = 64
TOP_K = 256
Q_BLOCK = 128
IDX_SCALE = (D_IDX ** -0.5) * (N_IDX_HEADS ** -0.5)
SSM_D_INNER = BRANCH_W
SSM_HEADS = 8
SSM_HEAD_DIM = SSM_D_INNER // SSM_HEADS
SSM_GROUPS = 2
D_STATE = 64
SSM_CONV_W = 4
SSM_CHUNK = 128
SSM_CONV_DIM = SSM_D_INNER + 2 * SSM_GROUPS * D_STATE
N_MEM = 256
MEM_HEADS = 4
MEM_HEAD_DIM = BRANCH_W // MEM_HEADS
D_FF = 2816
FFN_CONV_W = 3
ROPE_THETA = 500000.0
EPS = 1e-6

IN_SIZES = (
    2 * D_CONV,
    N_HEADS * HEAD_DIM,
    N_KV_HEADS * HEAD_DIM,
    N_KV_HEADS * HEAD_DIM,
    N_IDX_HEADS * D_IDX,
    D_IDX,
    N_IDX_HEADS,
    SSM_D_INNER,
    SSM_CONV_DIM,
    SSM_HEADS,
    MEM_HEADS * MEM_HEAD_DIM,
    N_BRANCH * D_MODEL,
)
IN_COLS = sum(IN_SIZES)

kernel_name = 'hybrid_conformer_dsa_ssd_memory_step'


def split_cols(proj):
    outs = []
    off = 0
    for n in IN_SIZES:
        outs.append(proj[..., off:off + n])
        off += n
    return outs


def rmsnorm(x, g):
    xf = x.astype(jnp.float32)
    y = xf * lax.rsqrt(jnp.mean(xf * xf, axis=-1, keepdims=True) + EPS)
    return (y * g.astype(jnp.float32)).astype(x.dtype)


def layernorm(x, g, b):
    xf = x.astype(jnp.float32)
    xc = xf - jnp.mean(xf, axis=-1, keepdims=True)
    var = jnp.mean(xc * xc, axis=-1, keepdims=True)
    return (xc * lax.rsqrt(var + EPS) * g.astype(jnp.float32) + b.astype(jnp.float32)).astype(x.dtype)


def rope_partial(x, pos):
    rot = x.shape[-1] // 4
    half = rot // 2
    inv_freq = ROPE_THETA ** (-jnp.arange(half, dtype=jnp.float32) * (2.0 / rot))
    ang = pos.astype(jnp.float32)[:, None] * inv_freq[None, :]
    cos = jnp.cos(ang)[:, None, :]
    sin = jnp.sin(ang)[:, None, :]
    xr = x[..., :rot].astype(jnp.float32)
    x1, x2 = xr[..., :half], xr[..., half:]
    out = jnp.concatenate([x1 * cos - x2 * sin, x2 * cos + x1 * sin], axis=-1)
    return jnp.concatenate([out.astype(x.dtype), x[..., rot:]], axis=-1)


def causal_dwconv(x, prefix, w, b):
    xp = jnp.concatenate([prefix.astype(x.dtype), x], axis=1)
    y = lax.conv_general_dilated(xp, w[:, None, :].astype(x.dtype), window_strides=(1,), padding='VALID',
                                 dimension_numbers=('NWC', 'WIO', 'NWC'), feature_group_count=x.shape[-1])
    return y + b.astype(x.dtype), xp[:, -(w.shape[0] - 1):, :]


def gather_rows(t, idx):
    return jax.vmap(lambda tb, ib: tb[ib])(t, idx)


def indexer_scores(qi, ki, wi, qpos):
    s = jnp.einsum('bqhd,bsd->bqhs', qi.astype(jnp.float32), ki.astype(jnp.float32))
    sc = jnp.einsum('bqhs,bqh->bqs', jax.nn.relu(s), wi.astype(jnp.float32)) * IDX_SCALE
    kpos = jnp.arange(ki.shape[1])
    return jnp.where(kpos[None, None, :] <= qpos[None, :, None], sc, -jnp.inf)


def sparse_attend(q, ks, vs, valid):
    bsz, nq = q.shape[:2]
    qg = q.reshape(bsz, nq, N_KV_HEADS, N_HEADS // N_KV_HEADS, HEAD_DIM).astype(jnp.float32)
    s = jnp.einsum('bqvgd,bqkvd->bqvgk', qg, ks.astype(jnp.float32)) * (HEAD_DIM ** -0.5)
    s = jnp.where(valid[:, :, None, None, :], s, -jnp.inf)
    p = jax.nn.softmax(s, axis=-1)
    o = jnp.einsum('bqvgk,bqkvd->bqvgd', p, vs.astype(jnp.float32))
    return o.reshape(bsz, nq, N_HEADS * HEAD_DIM).astype(q.dtype)


def dsa_prompt(q, k, v, qi, ki, wi):
    bsz, seq = q.shape[:2]
    nb = seq // Q_BLOCK
    ksel = min(TOP_K, seq // 4)

    def to_blocks(t):
        return jnp.moveaxis(t.reshape((bsz, nb, Q_BLOCK) + t.shape[2:]), 1, 0)

    def block(inp):
        i, qb, qib, wib = inp
        qpos = i * Q_BLOCK + jnp.arange(Q_BLOCK)
        _, idx = lax.top_k(indexer_scores(qib, ki, wib, qpos), ksel)
        valid = idx <= qpos[None, :, None]
        return sparse_attend(qb, gather_rows(k, idx), gather_rows(v, idx), valid)

    out = lax.map(block, (jnp.arange(nb), to_blocks(q), to_blocks(qi), to_blocks(wi)))
    return jnp.moveaxis(out, 0, 1).reshape(bsz, seq, N_HEADS * HEAD_DIM)


def dsa_sample(q, k, v, qi, ki, wi, ck, cv, cki, page_table):
    dbsz, nsq = q.shape[:2]
    n_pages = PAST_LEN // PAGE_SIZE
    ki_past = cki[page_table].reshape(dbsz, n_pages * PAGE_SIZE, D_IDX)
    ki_all = jnp.concatenate([ki_past.astype(ki.dtype), ki], axis=1)
    qpos = PAST_LEN + jnp.arange(nsq)
    ksel = min(TOP_K, (PAST_LEN + nsq) // 4)
    _, idx = lax.top_k(indexer_scores(qi, ki_all, wi, qpos), ksel)
    ip = jnp.minimum(idx, PAST_LEN - 1)
    phys = jax.vmap(lambda pt, pg: pt[pg])(page_table, ip // PAGE_SIZE)
    off = ip % PAGE_SIZE
    inew = jnp.clip(idx - PAST_LEN, 0, nsq - 1)
    is_past = (idx < PAST_LEN)[..., None, None]
    ks = jnp.where(is_past, ck[phys, off].astype(k.dtype), gather_rows(k, inew))
    vs = jnp.where(is_past, cv[phys, off].astype(v.dtype), gather_rows(v, inew))
    valid = idx <= qpos[None, :, None]
    return sparse_attend(q, ks, vs, valid)


def memory_kv(mem, prm):
    bsz, nm, _ = mem.shape
    m = rmsnorm(mem, prm['mem_norm_g']) @ prm['w_mem_kv']
    mk, mv = jnp.split(m, 2, axis=-1)
    mk = rmsnorm(mk.reshape(bsz, nm, MEM_HEADS, MEM_HEAD_DIM), prm['mk_norm_g'])
    return mk, mv.reshape(bsz, nm, MEM_HEADS, MEM_HEAD_DIM)


def mem_attend(q, mk, mv):
    bsz, seq = q.shape[:2]
    s = jnp.einsum('blhd,bmhd->blhm', q.astype(jnp.float32), mk.astype(jnp.float32)) * (MEM_HEAD_DIM ** -0.5)
    p = jax.nn.softmax(s, axis=-1)
    o = jnp.einsum('blhm,bmhd->blhd', p, mv.astype(jnp.float32))
    return o.reshape(bsz, seq, MEM_HEADS * MEM_HEAD_DIM).astype(q.dtype)


def ssd_scan(x, dt, a, bmat, cmat, h0):
    bsz, seq = x.shape[:2]
    q = min(SSM_CHUNK, seq)
    pad = (-seq) % q
    nc = (seq + pad) // q
    rep = SSM_HEADS // SSM_GROUPS

    def pad_t(t):
        return jnp.pad(t, [(0, 0), (0, pad)] + [(0, 0)] * (t.ndim - 2))

    xdt = pad_t(x.astype(jnp.float32) * dt[..., None])
    adt = pad_t(dt * a)
    bh = pad_t(jnp.repeat(bmat.astype(jnp.float32), rep, axis=2))
    ch = pad_t(jnp.repeat(cmat.astype(jnp.float32), rep, axis=2))
    xc = xdt.reshape(bsz, nc, q, SSM_HEADS, SSM_HEAD_DIM)
    bc = bh.reshape(bsz, nc, q, SSM_HEADS, D_STATE)
    cc = ch.reshape(bsz, nc, q, SSM_HEADS, D_STATE)
    a_cs = jnp.cumsum(adt.reshape(bsz, nc, q, SSM_HEADS).transpose(0, 3, 1, 2), axis=-1)
    seg = a_cs[..., :, None] - a_cs[..., None, :]
    causal = jnp.tril(jnp.ones((q, q), dtype=bool))
    lmat = jnp.exp(jnp.where(causal, seg, -jnp.inf))
    scores = jnp.einsum('bclhn,bcshn->bhcls', cc, bc) * lmat
    y_diag = jnp.einsum('bhcls,bcshp->bclhp', scores, xc)
    decay = jnp.exp(a_cs[..., -1:] - a_cs)
    chunk_states = jnp.einsum('bclhn,bhcl,bclhp->bchpn', bc, decay, xc)
    chunk_decay = jnp.exp(a_cs[..., -1])

    def step(h, inp):
        st, dec = inp
        return h * dec[:, :, None, None] + st, h

    h_final, h_prev = lax.scan(step, h0.astype(jnp.float32),
                               (jnp.moveaxis(chunk_states, 1, 0), jnp.moveaxis(chunk_decay, 2, 0)))
    h_prev = jnp.moveaxis(h_prev, 0, 1)
    y_off = jnp.einsum('bclhn,bchpn,bhcl->bclhp', cc, h_prev, jnp.exp(a_cs))
    y = (y_diag + y_off).reshape(bsz, nc * q, SSM_HEADS, SSM_HEAD_DIM)[:, :seq]
    return y, h_final


def ssd_branch(z, xbc, dt_raw, conv_buf, h0, prm):
    bsz, seq, _ = z.shape
    xbc, conv_new = causal_dwconv(xbc, conv_buf, prm['ssm_conv_w'], prm['ssm_conv_b'])
    xbc = jax.nn.silu(xbc)
    nb = SSM_GROUPS * D_STATE
    xs = xbc[..., :SSM_D_INNER].reshape(bsz, seq, SSM_HEADS, SSM_HEAD_DIM)
    bm = xbc[..., SSM_D_INNER:SSM_D_INNER + nb].reshape(bsz, seq, SSM_GROUPS, D_STATE)
    cm = xbc[..., SSM_D_INNER + nb:].reshape(bsz, seq, SSM_GROUPS, D_STATE)
    dt = jax.nn.softplus(dt_raw.astype(jnp.float32) + prm['dt_bias'].astype(jnp.float32))
    a = -jnp.exp(prm['a_log'].astype(jnp.float32))
    y, h = ssd_scan(xs, dt, a, bm, cm, h0)
    y = y + prm['d_skip'].astype(jnp.float32)[:, None] * xs.astype(jnp.float32)
    y = y.reshape(bsz, seq, SSM_D_INNER).astype(z.dtype)
    return rmsnorm(y * jax.nn.silu(z), prm['ssm_norm_g']), conv_new, h.astype(h0.dtype)


def layer_forward(x, pos, prm, attn_fn, mem_k, mem_v, conf_buf, sconv_buf, ssm_h, ffn_buf):
    bsz, seq, _ = x.shape
    h = rmsnorm(x, prm['norm_mix_g'])
    glu, q, k, v, qi, ki, wi, z, xbc, dt_raw, mq, gates = split_cols(h @ prm['w_in'])
    a_val, a_gate = jnp.split(glu, 2, axis=-1)
    u, conf_new = causal_dwconv(a_val * jax.nn.sigmoid(a_gate), conf_buf, prm['conv_a_w'], prm['conv_a_b'])
    br_a = jax.nn.silu(layernorm(u, prm['ln_a_g'], prm['ln_a_b']))
    q = rope_partial(rmsnorm(q.reshape(bsz, seq, N_HEADS, HEAD_DIM), prm['q_norm_g']), pos)
    k = rope_partial(rmsnorm(k.reshape(bsz, seq, N_KV_HEADS, HEAD_DIM), prm['k_norm_g']), pos)
    v = v.reshape(bsz, seq, N_KV_HEADS, HEAD_DIM)
    qi = rope_partial(qi.reshape(bsz, seq, N_IDX_HEADS, D_IDX), pos)
    ki = rope_partial(ki[:, :, None, :], pos)[:, :, 0, :]
    br_b = attn_fn(q, k, v, qi, ki, wi)
    br_c, sconv_new, ssm_new = ssd_branch(z, xbc, dt_raw, sconv_buf, ssm_h, prm)
    mq = rmsnorm(mq.reshape(bsz, seq, MEM_HEADS, MEM_HEAD_DIM), prm['mq_norm_g'])
    br_m = mem_attend(mq, mem_k, mem_v)
    br = jnp.stack([br_a, br_b, br_c, br_m], axis=2)
    proj_br = jnp.einsum('blnc,ncd->blnd', br, prm['w_branch'])
    gate = jax.nn.sigmoid(gates.reshape(bsz, seq, N_BRANCH, D_MODEL))
    x = x + jnp.sum(gate * proj_br, axis=2) @ prm['w_out']
    u2, ffn_new = causal_dwconv(rmsnorm(x, prm['norm_ffn_g']) @ prm['w_ffn_up'], ffn_buf,
                                prm['ffn_conv_w'], prm['ffn_conv_b'])
    f_gate, f_up = jnp.split(u2, 2, axis=-1)
    x = x + (jax.nn.silu(f_gate) * f_up) @ prm['w_ffn_down']
    return x, (k, v, ki, conf_new, sconv_new, ssm_new, ffn_new)


def setup_inputs(seed: int = 0) -> dict:
    key = jax.random.key(seed)
    ks = iter(jax.random.split(key, 64))

    def nrm(shape, scale=1.0):
        return jax.random.normal(next(ks), shape, jnp.float32) * scale

    def gain(shape):
        return 1.0 + nrm(shape, 0.02)

    n_pages = PAST_LEN // PAGE_SIZE
    n_used = DEC_BATCH * n_pages
    n_phys = n_used + max(1, n_used // 4)
    page_table = jax.random.permutation(next(ks), n_phys)[:n_used].reshape(DEC_BATCH, n_pages).astype(jnp.int32)
    dt0 = jnp.exp(jax.random.uniform(next(ks), (DEPTH, SSM_HEADS), jnp.float32, math.log(1e-3), math.log(1e-1)))
    dt_bias = dt0 + jnp.log(-jnp.expm1(-dt0))
    a_log = jnp.log(jax.random.uniform(next(ks), (DEPTH, SSM_HEADS), jnp.float32, 1.0, 16.0))
    return {
        'x_prompt': nrm((BATCH, SEQ, D_MODEL)),
        'x_sample': nrm((DEC_BATCH, DEC_SEQ, D_MODEL)),
        'cache_k': nrm((DEPTH, n_phys, PAGE_SIZE, N_KV_HEADS, HEAD_DIM)),
        'cache_v': nrm((DEPTH, n_phys, PAGE_SIZE, N_KV_HEADS, HEAD_DIM)),
        'cache_kidx': nrm((DEPTH, n_phys, PAGE_SIZE, D_IDX)),
        'cache_mem_k': nrm((DEPTH, DEC_BATCH, N_MEM, MEM_HEADS, MEM_HEAD_DIM)),
        'cache_mem_v': nrm((DEPTH, DEC_BATCH, N_MEM, MEM_HEADS, MEM_HEAD_DIM)),
        'state_conformer': nrm((DEPTH, DEC_BATCH, CONV_A_W - 1, D_CONV), 0.5),
        'state_ssm_conv': nrm((DEPTH, DEC_BATCH, SSM_CONV_W - 1, SSM_CONV_DIM)),
        'state_ssm': nrm((DEPTH, DEC_BATCH, SSM_HEADS, SSM_HEAD_DIM, D_STATE), 0.3),
        'state_ffn_conv': nrm((DEPTH, DEC_BATCH, FFN_CONV_W - 1, 2 * D_FF)),
        'page_table': page_table,
        'mem_prompt': nrm((BATCH, N_MEM, D_MODEL)),
        'norm_mix_g': gain((DEPTH, D_MODEL)),
        'w_in': nrm((DEPTH, D_MODEL, IN_COLS), D_MODEL ** -0.5),
        'conv_a_w': nrm((DEPTH, CONV_A_W, D_CONV), CONV_A_W ** -0.5),
        'conv_a_b': nrm((DEPTH, D_CONV), 0.01),
        'ln_a_g': gain((DEPTH, D_CONV)),
        'ln_a_b': nrm((DEPTH, D_CONV), 0.01),
        'q_norm_g': gain((DEPTH, HEAD_DIM)),
        'k_norm_g': gain((DEPTH, HEAD_DIM)),
        'ssm_conv_w': nrm((DEPTH, SSM_CONV_W, SSM_CONV_DIM), SSM_CONV_W ** -0.5),
        'ssm_conv_b': nrm((DEPTH, SSM_CONV_DIM), 0.01),
        'dt_bias': dt_bias,
        'a_log': a_log,
        'd_skip': gain((DEPTH, SSM_HEADS)),
        'ssm_norm_g': gain((DEPTH, SSM_D_INNER)),
        'mem_norm_g': gain((DEPTH, D_MODEL)),
        'w_mem_kv': nrm((DEPTH, D_MODEL, 2 * MEM_HEADS * MEM_HEAD_DIM), D_MODEL ** -0.5),
        'mq_norm_g': gain((DEPTH, MEM_HEAD_DIM)),
        'mk_norm_g': gain((DEPTH, MEM_HEAD_DIM)),
        'w_branch': nrm((DEPTH, N_BRANCH, BRANCH_W, D_MODEL), BRANCH_W ** -0.5),
        'w_out': nrm((DEPTH, D_MODEL, D_MODEL), D_MODEL ** -0.5),
        'norm_ffn_g': gain((DEPTH, D_MODEL)),
        'w_ffn_up': nrm((DEPTH, D_MODEL, 2 * D_FF), D_MODEL ** -0.5),
        'ffn_conv_w': nrm((DEPTH, FFN_CONV_W, 2 * D_FF), FFN_CONV_W ** -0.5),
        'ffn_conv_b': nrm((DEPTH, 2 * D_FF), 0.01),
        'w_ffn_down': nrm((DEPTH, D_FF, D_MODEL), D_FF ** -0.5),
    }


def reference(x_prompt, x_sample, cache_k, cache_v, cache_kidx, cache_mem_k, cache_mem_v,
              state_conformer, state_ssm_conv, state_ssm, state_ffn_conv, page_table, mem_prompt,
              norm_mix_g, w_in, conv_a_w, conv_a_b, ln_a_g, ln_a_b, q_norm_g, k_norm_g,
              ssm_conv_w, ssm_conv_b, dt_bias, a_log, d_skip, ssm_norm_g,
              mem_norm_g, w_mem_kv, mq_norm_g, mk_norm_g, w_branch, w_out,
              norm_ffn_g, w_ffn_up, ffn_conv_w, ffn_conv_b, w_ffn_down):
    bsz, seq, _ = x_prompt.shape
    dseq = x_sample.shape[1]
    dt = x_prompt.dtype
    pos_p = jnp.arange(seq, dtype=jnp.int32)
    pos_s = PAST_LEN + jnp.arange(dseq, dtype=jnp.int32)
    xp, xs = x_prompt, x_sample
    p_k, p_v, p_ki, p_mk, p_mv, p_conf, p_sc, p_ssm, p_ffn = [], [], [], [], [], [], [], [], []
    s_k, s_v, s_ki, s_conf, s_sc, s_ssm, s_ffn = [], [], [], [], [], [], []
    for l in range(DEPTH):
        prm = dict(norm_mix_g=norm_mix_g[l], w_in=w_in[l], conv_a_w=conv_a_w[l], conv_a_b=conv_a_b[l],
                   ln_a_g=ln_a_g[l], ln_a_b=ln_a_b[l], q_norm_g=q_norm_g[l], k_norm_g=k_norm_g[l],
                   ssm_conv_w=ssm_conv_w[l], ssm_conv_b=ssm_conv_b[l], dt_bias=dt_bias[l], a_log=a_log[l],
                   d_skip=d_skip[l], ssm_norm_g=ssm_norm_g[l], mem_norm_g=mem_norm_g[l], w_mem_kv=w_mem_kv[l],
                   mq_norm_g=mq_norm_g[l], mk_norm_g=mk_norm_g[l], w_branch=w_branch[l], w_out=w_out[l],
                   norm_ffn_g=norm_ffn_g[l], w_ffn_up=w_ffn_up[l], ffn_conv_w=ffn_conv_w[l],
                   ffn_conv_b=ffn_conv_b[l], w_ffn_down=w_ffn_down[l])
        mk, mv = memory_kv(mem_prompt, prm)
        xp, st = layer_forward(
            xp, pos_p, prm, dsa_prompt, mk, mv,
            jnp.zeros((bsz, CONV_A_W - 1, D_CONV), dt),
            jnp.zeros((bsz, SSM_CONV_W - 1, SSM_CONV_DIM), dt),
            jnp.zeros((bsz, SSM_HEADS, SSM_HEAD_DIM, D_STATE), dt),
            jnp.zeros((bsz, FFN_CONV_W - 1, 2 * D_FF), dt))
        p_k.append(st[0]); p_v.append(st[1]); p_ki.append(st[2]); p_mk.append(mk); p_mv.append(mv)
        p_conf.append(st[3]); p_sc.append(st[4]); p_ssm.append(st[5]); p_ffn.append(st[6])
        attn_s = functools.partial(dsa_sample, ck=cache_k[l], cv=cache_v[l], cki=cache_kidx[l],
                                   page_table=page_table)
        xs, st = layer_forward(xs, pos_s, prm, attn_s, cache_mem_k[l], cache_mem_v[l],
                               state_conformer[l], state_ssm_conv[l], state_ssm[l], state_ffn_conv[l])
        s_k.append(st[0]); s_v.append(st[1]); s_ki.append(st[2])
        s_conf.append(st[3]); s_sc.append(st[4]); s_ssm.append(st[5]); s_ffn.append(st[6])
    return (xp, xs,
            jnp.stack(p_k), jnp.stack(p_v), jnp.stack(p_ki), jnp.stack(p_mk), jnp.stack(p_mv),
            jnp.stack(p_conf), jnp.stack(p_sc), jnp.stack(p_ssm), jnp.stack(p_ffn),
            jnp.stack(s_k), jnp.stack(s_v), jnp.stack(s_ki),
            jnp.stack(s_conf), jnp.stack(s_sc), jnp.stack(s_ssm), jnp.stack(s_ffn))
```

```python
import numpy as np
from contextlib import ExitStack
import concourse.bass as bass
import concourse.mybir as mybir
from concourse.bass_utils import run_bass_kernel_spmd

F32 = mybir.dt.float32
BF16 = mybir.dt.bfloat16
I32 = mybir.dt.int32
U32 = mybir.dt.uint32
ALU = mybir.AluOpType
AF = mybir.ActivationFunctionType
AX = mybir.AxisListType

ENGS = ("pe", "act", "dve", "pool", "sp")
NDMA = 8
SAME_ENG_SYNC = True


class Buf:
    def __init__(self, name, t, root=None):
        self.name = name
        self.t = t
        self.root = root if root is not None else self
        if root is None:
            self._lastw = None
            self._readers = []

    lastw = property(lambda s: s.root._lastw, lambda s, v: setattr(s.root, "_lastw", v))
    readers = property(lambda s: s.root._readers, lambda s, v: setattr(s.root, "_readers", v))

    def __getitem__(self, idx):
        return self.t[idx]

    def alias(self, name, ap):
        return Buf(name, ap, root=self.root)


class Prog:
    def __init__(self, nc, es):
        self.nc = nc
        self.es = es
        self.ops = {e: [] for e in ENGS}
        self.cnt = {e: 0 for e in ENGS}
        self.dma_i = {e: 0 for e in ENGS}
        self.seen = {e: {} for e in ENGS}
        self.sems = {}
        for e in ("pe", "act", "dve", "pool"):
            self.sems[("c", e)] = es.enter_context(nc.semaphore("s_" + e))
        for e in ("sp", "act", "pool"):
            for i in range(NDMA):
                self.sems[("d", e, i)] = es.enter_context(nc.semaphore("d_%s%d" % (e, i)))
        self.nbuf = 0

    def sb(self, name, shape, dt=F32):
        t = self.es.enter_context(self.nc.sbuf_tensor(name, list(shape), dt))
        return Buf(name, t)

    def ps(self, name, shape, dt=F32):
        t = self.es.enter_context(self.nc.psum_tensor(name, list(shape), dt))
        return Buf(name, t)

    def view(self, name, t):
        return Buf(name, t)

    def _need(self, eng, tok, waits):
        if tok is None:
            return
        key, val, teng = tok
        if key[0] == "c" and teng == eng and not (SAME_ENG_SYNC and eng != "pe"):
            return
        if self.seen[eng].get(key, 0) >= val:
            return
        self.seen[eng][key] = val
        waits.append((key, val))

    def _deps(self, eng, reads, writes):
        waits = []
        for b in reads:
            self._need(eng, b.lastw, waits)
        for b in writes:
            self._need(eng, b.lastw, waits)
            for r in b.readers:
                self._need(eng, r, waits)
        return waits

    def _commit(self, tok, reads, writes):
        for b in reads:
            b.readers.append(tok)
        for b in writes:
            b.lastw = tok
            b.readers = []

    def op(self, eng, fn, reads=(), writes=()):
        waits = self._deps(eng, reads, writes)
        self.cnt[eng] += 1
        key = ("c", eng)
        tok = (key, self.cnt[eng], eng)
        self.ops[eng].append((waits, fn, (key, 1)))
        self._commit(tok, reads, writes)
        return tok

    def dma(self, fn, reads=(), writes=(), q="sp"):
        i = self.dma_i[q]
        self.dma_i[q] += 1
        slot = i % NDMA
        key = ("d", q, slot)
        val = 16 * (i // NDMA + 1)
        waits = self._deps(q, reads, writes)
        if i >= NDMA:
            self._need(q, (key, val - 16, "dma"), waits)
        tok = (key, val, "dma")
        self.ops[q].append((waits, fn, (key, 16)))
        self._commit(tok, reads, writes)
        return tok

    def finish(self, toks):
        waits = []
        for t in toks:
            self._need("sp", t, waits)
        self.ops["sp"].append((waits, None, None))

    def emit(self):
        nc = self.nc
        P = self

        def replay(name, e):
            for waits, fn, inc in P.ops[name]:
                for key, val in waits:
                    e.wait_ge(P.sems[key], val)
                if fn is None:
                    continue
                ins = fn(e)
                ins.then_inc(P.sems[inc[0]], inc[1])

        with nc.Block() as block:
            @block.sync
            def _(e):
                replay("sp", e)

            @block.tensor
            def _(e):
                replay("pe", e)

            @block.scalar
            def _(e):
                replay("act", e)

            @block.vector
            def _(e):
                replay("dve", e)

            @block.gpsimd
            def _(e):
                replay("pool", e)


D = 1024
NEG = -1.0e30
IDX_SCALE = (64 ** -0.5) * (8 ** -0.5)
EPS = 1e-6
IN_COLS = 8272
C_GLU, C_Q, C_K, C_V, C_QI, C_KI, C_WI, C_Z, C_XBC, C_DT, C_MQ, C_G = (
    0, 1024, 1536, 1664, 1792, 2304, 2368, 2376, 2888, 3656, 3664, 4176)
NITER = 22


class CFG:
    def __init__(self, SEQ=8192, PAST=8192, NPHYS=2560, DEPTH=2, NS=4, T=128):
        self.SEQ, self.PAST, self.NPHYS, self.DEPTH, self.NS, self.T = SEQ, PAST, NPHYS, DEPTH, NS, T
        self.SMAX = max(SEQ, PAST + 128)
        self.KP = min(256, SEQ // 4)
        self.KS = min(256, (PAST + 1) // 4)
        self.NPG = PAST // 128


def build(cfg):
    SEQ, PAST, NPHYS, DEPTH, NS, T = cfg.SEQ, cfg.PAST, cfg.NPHYS, cfg.DEPTH, cfg.NS, cfg.T
    NSUB = T // 128
    nc = bass.Bass("TRN2", target_bir_lowering=False)
    es = ExitStack()
    P = Prog(nc, es)

    def din(name, shape, dt=F32):
        return Buf(name, nc.dram_tensor(name, list(shape), dt, kind="ExternalInput").ap())

    def dout(name, shape, dt=F32):
        return Buf(name, nc.dram_tensor(name, list(shape), dt, kind="ExternalOutput").ap())

    def dscr(name, shape, dt=BF16):
        return Buf(name, nc.dram_tensor(name, list(shape), dt, kind="Internal").ap())

    I = {}
    I["xp"] = din("xp", [SEQ, D]); I["xs"] = din("xs", [NS, D]); I["memp"] = din("memp", [256, D])
    I["ck"] = din("ck", [DEPTH, NPHYS * 128, 128]); I["cv"] = din("cv", [DEPTH, NPHYS * 128, 128])
    I["cki"] = din("cki", [DEPTH, NPHYS * 128, 64])
    I["cmk"] = din("cmk", [DEPTH, NS, 256, 512]); I["cmv"] = din("cmv", [DEPTH, NS, 256, 512])
    I["sconf"] = din("sconf", [DEPTH, NS, 30, 512]); I["ssc"] = din("ssc", [DEPTH, NS, 3, 768])
    I["sssm"] = din("sssm", [DEPTH, NS, 8, 64, 64]); I["sffn"] = din("sffn", [DEPTH, NS, 2, 5632])
    I["pt"] = din("pt", [NS, cfg.NPG], I32)
    I["ropep"] = din("ropep", [SEQ, 16]); I["ropes"] = din("ropes", [1, 16])
    for nm, shp in [("norm_mix_g", [DEPTH, D]), ("w_in", [DEPTH, D, IN_COLS]), ("conv_a_w", [DEPTH, 31, 512]),
                    ("conv_a_b", [DEPTH, 512]), ("ln_a_g", [DEPTH, 512]), ("ln_a_b", [DEPTH, 512]),
                    ("q_norm_g", [DEPTH, 64]), ("k_norm_g", [DEPTH, 64]), ("ssm_conv_w", [DEPTH, 4, 768]),
                    ("ssm_conv_b", [DEPTH, 768]), ("dt_bias", [DEPTH, 8]), ("a_log", [DEPTH, 8]),
                    ("d_skip", [DEPTH, 8]), ("ssm_norm_g", [DEPTH, 512]), ("mem_norm_g", [DEPTH, D]),
                    ("w_mem_kv", [DEPTH, D, 1024]), ("mq_norm_g", [DEPTH, 128]), ("mk_norm_g", [DEPTH, 128]),
                    ("w_branch", [DEPTH, 2048, D]), ("w_out", [DEPTH, D, D]), ("norm_ffn_g", [DEPTH, D]),
                    ("w_ffn_up", [DEPTH, D, 5632]), ("ffn_conv_w", [DEPTH, 3, 5632]),
                    ("ffn_conv_b", [DEPTH, 5632]), ("w_ffn_down", [DEPTH, 2816, D])]:
        I[nm] = din(nm, shp)
    O = {}
    O["y_p"] = dout("y_p", [SEQ, D]); O["y_s"] = dout("y_s", [NS, D])
    O["k_p"] = dout("k_p", [DEPTH, SEQ, 128]); O["v_p"] = dout("v_p", [DEPTH, SEQ, 128])
    O["ki_p"] = dout("ki_p", [DEPTH, SEQ, 64])
    O["mk_p"] = dout("mk_p", [DEPTH, 256, 512]); O["mv_p"] = dout("mv_p", [DEPTH, 256, 512])
    O["conf_p"] = dout("conf_p", [DEPTH, 30, 512]); O["sc_p"] = dout("sc_p", [DEPTH, 3, 768])
    O["ssm_p"] = dout("ssm_p", [DEPTH, 8, 64, 64]); O["ffn_p"] = dout("ffn_p", [DEPTH, 2, 5632])
    O["k_s"] = dout("k_s", [DEPTH, NS, 128]); O["v_s"] = dout("v_s", [DEPTH, NS, 128])
    O["ki_s"] = dout("ki_s", [DEPTH, NS, 64])
    O["conf_s"] = dout("conf_s", [DEPTH, NS, 30, 512]); O["sc_s"] = dout("sc_s", [DEPTH, NS, 3, 768])
    O["ssm_s"] = dout("ssm_s", [DEPTH, NS, 8, 64, 64]); O["ffn_s"] = dout("ffn_s", [DEPTH, NS, 2, 5632])
    outtoks = []
    DBG = False
    if DBG:
        O["dbg"] = dout("dbg", [4, 512, SEQ], BF16)
        O["dbgx"] = dout("dbgx", [SEQ, D])
    WB = {nm: dscr(nm + "_b", list(I[nm].t.shape)) for nm in
          ("w_in", "w_mem_kv", "w_branch", "w_out", "w_ffn_up", "w_ffn_down")}
    xres = dscr("xres", [SEQ, D], F32)
    xsres = dscr("xsres", [NS, D], F32)
    h_kiT = dscr("h_kiT", [64, cfg.SMAX]); h_kT = dscr("h_kT", [128, cfg.SMAX]); h_v = dscr("h_v", [cfg.SMAX, 256])

    sb, ps = P.sb, P.ps
    identf = sb("identf", [128, 128]); identb = sb("identb", [128, 128], BF16)
    onesf = sb("onesf", [128, 128]); onesb = sb("onesb", [128, 128], BF16)
    trif = sb("trif", [128, 128]); elast = sb("elast", [128, 128])
    cadd = sb("cadd", [128, 128]); sadd = sb("sadd", [128, 128])
    epsT = sb("epsT", [128, 1]); oneT = sb("oneT", [128, 1])
    tokm1 = sb("tokm1", [128, 1]); tokms = sb("tokms", [128, 1])
    gmix = sb("gmix", [128, D]); gffn = sb("gffn", [128, D])
    lnag = sb("lnag", [128, 512]); lnab = sb("lnab", [128, 512]); ssmg = sb("ssmg", [128, 512])
    qg = sb("qg", [128, 64]); kg = sb("kg", [128, 64]); mqg = sb("mqg", [128, 128]); mkg = sb("mkg", [128, 128])
    dtb = sb("dtb", [128, 8]); aneg = sb("aneg", [128, 8]); dsk = sb("dsk", [128, 8])
    cwa = sb("cwa", [128, 4, 31]); cba = sb("cba", [128, 4]); cws = sb("cws", [128, 6, 4]); cbs = sb("cbs", [128, 6])
    cwf = sb("cwf", [128, 44, 3]); cbf = sb("cbf", [128, 44])
    x_mt = sb("x_mt", [128, NSUB, D]); hT = sb("hT", [128, 8, T], BF16); hb = sb("hb", [128, D], BF16)
    wsl = [sb("wsl%d" % i, [128, 8, 512], BF16) for i in range(3)]
    projA = [sb("projA%d" % s, [128, 1864]) for s in range(NSUB)]
    projB = [sb("projB%d" % s, [128, 520]) for s in range(NSUB)]
    glu_u = sb("glu_u", [128, 4, 30 + T]); glu_c = sb("glu_c", [128, 4, T]); sgt = sb("sgt", [128, 512])
    xbc_u = sb("xbc_u", [128, 6, 3 + T]); xbc_c = sb("xbc_c", [128, 6, T])
    fst_g = sb("fst_g", [128, 2 + T]); fst_u = sb("fst_u", [128, 2 + T]); fcar = sb("fcar", [128, 44, 2])
    fso = sb("fso", [128, 44, 2]); fcg = sb("fcg", [128, T]); fcu = sb("fcu", [128, T])
    gT = sb("gT", [128, 22, T], BF16)
    brT = [sb("brT%d" % n, [128, 4, T], BF16) for n in (0, 2, 3)]
    brTb = sb("brTb", [64, 8, T], BF16)
    mixed = [projA[s].alias("mixed%d" % s, projA[s][:, 0:D]) for s in range(NSUB)]
    gmem = projA[0].alias("gmem", projA[0][:, 0:D])
    tm1 = sb("tm1", [128, 512]); tm2 = sb("tm2", [128, 512]); tmb = sb("tmb", [128, 512], BF16)
    sm = [sb("sm%d" % i, [128, 16]) for i in range(8)]
    xs_tm = sb("xs_tm", [128, 512]); B_tm = sb("B_tm", [128, 128], BF16)
    BT = sb("BT", [128, 128], BF16); CT = sb("CT", [128, 128], BF16)
    CTm = [sb("CTm0", [128, 128], BF16), sb("CTm1", [128, 128], BF16)]
    qTm = [sb("qTm0", [128, 4, 128], BF16), sb("qTm1", [128, 4, 128], BF16)]
    xdt = sb("xdt", [128, 512], BF16); xdtd = sb("xdtd", [128, 512], BF16)
    GT = sb("GT", [128, 2, 128])
    scT = sb("scT", [128, 8, 128], BF16)
    hst = sb("hst", [128, 4, 64]); hstb = sb("hstb", [128, 4, 64], BF16)
    ybuf = sb("ybuf", [128, 512])
    mkT = sb("mkT", [128, 4, 256], BF16); mvb = sb("mvb", [128, 2, 512], BF16)
    mqT = sb("mqT", [128, 4, 128], BF16); PT = sb("PT", [128, 2, 512], BF16); rden = sb("rden", [128, 512])
    rdlo = sb("rdlo", [64, 512])
    hio = rdlo.alias("hio", rdlo[:])
    Isc = sb("Isc", [128, max(cfg.SMAX, 1024)]); junk = sb("junk", [128, cfg.SMAX], mybir.dt.uint8)
    kic = [sb("kic%d" % i, [64, 1024], BF16) for i in range(2)]
    rbuf = [sb("rbuf0", [128, 1024])] * 2
    diag = rbuf[0].alias("diag", rbuf[0][:].rearrange("p (h t) -> p h t", h=8))
    seg = Isc.alias("seg", Isc[:, 0:1024].rearrange("p (h t) -> p h t", h=8))
    stg = Isc.alias("stg", Isc[:, 0:1024])
    stgb = hT.alias("stgb", hT[:].rearrange("p a b -> p (a b)"))
    kTc = [sb("kTc%d" % i, [128, 512], BF16) for i in range(2)]
    vc = [sb("vc%d" % i, [128, 4, 256], BF16) for i in range(2)]
    qT = sb("qT", [128, 4, 128], BF16); qiT = sb("qiT", [64, 8, 128], BF16)
    mT = [sb("mT%d" % i, [128, 128], BF16) for i in range(2)]
    Eb = [sb("Eb0", [128, 8, 128], BF16)] * 2
    Pm = [sb("Pm0", [128, 8, 128], BF16)] * 2
    vbuf = sb("vbuf", [128, 2, 128], BF16); ropet = sb("ropet", [128, 16])
    awi = sb("awi", [128, 8]); swi = sb("swi", [128, 8])
    lo = sb("lo", [128, 1]); hi = sb("hi", [128, 1]); mid = sb("mid", [128, 1]); cnt = sb("cnt", [128, 1])
    pge = sb("pge", [128, 1], I32); plt = sb("plt", [128, 1], I32)
    pidx = sb("pidx", [128, cfg.NPG], I32); ptb = sb("ptb", [128, cfg.NPG], I32); iop = sb("iop", [128, 1], I32)
    pgk = sb("pgk", [128, 128]); pgv = sb("pgv", [128, 128]); pgi = sb("pgi", [128, 64])
    kout = sb("kout", [128, 128]); kiout = sb("kiout", [128, 64]); kbf = sb("kbf", [128, 128], BF16)
    kibf = sb("kibf", [128, 64], BF16); qn = tm2.alias("qn", tm2[:]); qbf = sb("qbf", [128, 512], BF16)
    kTs = sb("kTs", [128, 128], BF16); kiTs = sb("kiTs", [64, 128], BF16)
    stio = ybuf.alias("stio", ybuf[:])
    psA = ps("psA", [128, 512]); psB = ps("psB", [128, 512]); psC = ps("psC", [128, 512])
    psW = ps("psW", [128, 1024]); psO = [ps("psO0", [128, 512]), ps("psO1", [128, 512])]
    psT = ps("psT", [128, 1024], BF16)
    pr = [psA, psB, psC]
    rot = {"ps": 0, "w": 0, "kic": 0, "rb": 0, "kt": 0, "mt": 0}

    def nps():
        rot["ps"] = (rot["ps"] + 1) % 3
        return pr[rot["ps"]]

    def mm(o, oap, l, lap, r, rap, start=True, stop=True):
        P.op("pe", lambda e: e.matmul(oap, lhsT=lap, rhs=rap, start=start, stop=stop), reads=[l, r], writes=[o])

    def tr(o, oap, i, iap, idt):
        P.op("pe", lambda e: e.transpose(oap, iap, idt[0:iap.shape[0], 0:iap.shape[0]]), reads=[i, idt], writes=[o])

    def act(o, oap, i, iap, func, bias=None, scale=None, accum=None, extra=()):
        kw = {}
        if bias is not None: kw["bias"] = bias
        if scale is not None: kw["scale"] = scale
        wr = [o]
        if accum is not None:
            kw["accum_out"] = accum[1]; wr.append(accum[0])
        P.op("act", lambda e: e.activation(out=oap, in_=iap, func=func, **kw), reads=[i] + list(extra), writes=wr)

    def tt(o, oap, a, aap, b, bap, op, eng="dve"):
        P.op(eng, lambda e: e.tensor_tensor(out=oap, in0=aap, in1=bap, op=op), reads=[a, b], writes=[o])

    def ts(o, oap, a, aap, s1, op0, s2=None, op1=None, accum=None, extra=(), eng="dve"):
        kw = {}
        wr = [o]
        if op1 is not None: kw["op1"] = op1
        if accum is not None:
            kw["accum_out"] = accum[1]; wr.append(accum[0])
        P.op(eng, lambda e: e.tensor_scalar(out=oap, in0=aap, scalar1=s1, scalar2=s2, op0=op0, **kw),
             reads=[a] + list(extra), writes=wr)

    def stt(o, oap, a, aap, sc, b, bap, op0, op1, extra=()):
        P.op("dve", lambda e: e.scalar_tensor_tensor(out=oap, in0=aap, scalar=sc, in1=bap, op0=op0, op1=op1),
             reads=[a, b] + list(extra), writes=[o])

    def cp(o, oap, i, iap, eng="dve"):
        if eng == "act":
            P.op("act", lambda e: e.activation(out=oap, in_=iap, func=AF.Copy), reads=[i], writes=[o])
        else:
            P.op(eng, lambda e: e.tensor_copy(out=oap, in_=iap), reads=[i], writes=[o])

    def mset(o, oap, v, eng="pool"):
        P.op(eng, lambda e: e.memset(oap, v), writes=[o])

    def dma(o, oap, i, iap, q="sp", nonc=False):
        if nonc:
            return P.dma(lambda e: e.dma_start(out=oap, in_=iap, allow_slow_non_contiguous=True), reads=[i], writes=[o], q=q)
        return P.dma(lambda e: e.dma_start(out=oap, in_=iap), reads=[i], writes=[o], q=q)

    def recip(o, oap, i, iap):
        P.op("dve", lambda e: e.reciprocal(out=oap, in_=iap), reads=[i], writes=[o])

    def red(o, oap, i, iap, op):
        P.op("dve", lambda e: e.tensor_reduce(out=oap, in_=iap, axis=AX.X, op=op), reads=[i], writes=[o])

    def bc_last(ap, n):
        return ap.unsqueeze(2).to_broadcast([ap.shape[0], ap.shape[1], n])

    def bc_mid(ap, n):
        return ap.unsqueeze(1).to_broadcast([ap.shape[0], n, ap.shape[1]])

    mset(identf, identf[:], 1.0)
    P.op("pool", lambda e: e.affine_select(out=identf[:], in_=identf[:], pattern=[[-1, 128]], compare_op=ALU.is_equal,
                                            fill=0.0, base=0, channel_multiplier=1), reads=[identf], writes=[identf])
    cp(identb, identb[:], identf, identf[:])
    mset(onesf, onesf[:], 1.0); mset(onesb, onesb[:], 1.0)
    mset(trif, trif[:], 1.0)
    P.op("pool", lambda e: e.affine_select(out=trif[:], in_=trif[:], pattern=[[1, 128]], compare_op=ALU.is_ge,
                                            fill=0.0, base=0, channel_multiplier=-1), reads=[trif], writes=[trif])
    mset(cadd, cadd[:], 0.0)
    P.op("pool", lambda e: e.affine_select(out=cadd[:], in_=cadd[:], pattern=[[-1, 128]], compare_op=ALU.is_ge,
                                            fill=NEG, base=0, channel_multiplier=1), reads=[cadd], writes=[cadd])
    mset(sadd, sadd[:], NEG); mset(sadd, sadd[:, 0:1], 0.0)
    mset(elast, elast[:], 0.0); mset(elast, elast[127:128, :], 1.0) if False else None
    mset(elast, elast[:], 1.0)
    P.op("pool", lambda e: e.affine_select(out=elast[:], in_=elast[:], pattern=[[0, 128]], compare_op=ALU.is_equal,
                                            fill=0.0, base=-127, channel_multiplier=1), reads=[elast], writes=[elast])
    mset(epsT, epsT[:], EPS); mset(oneT, oneT[:], 1.0)
    mset(tokm1, tokm1[:], 1.0)
    mset(tokms, tokms[:], 1.0)
    P.op("pool", lambda e: e.affine_select(out=tokms[:], in_=tokms[:], pattern=[[0, 1]], compare_op=ALU.is_equal,
                                            fill=0.0, base=0, channel_multiplier=1), reads=[tokms], writes=[tokms])
    mset(vbuf, vbuf[:], 1.0)
    for g in range(2):
        mset(CTm[g], CTm[g][:], 0.0); mset(qTm[g], qTm[g][:], 0.0)
    P.op("pool", lambda e: e.iota(iop[:], pattern=[[0, 1]], base=0, channel_multiplier=1), writes=[iop])

    pc = [0]
    for nm in ("w_in", "w_mem_kv", "w_branch", "w_out", "w_ffn_up", "w_ffn_down"):
        src = I[nm]; dst = WB[nm]
        shp = src.t.shape
        R, C = shp[1], shp[2]
        for l in range(DEPTH):
            for r0 in range(0, R, 128):
                for c0 in range(0, C, 1024):
                    cw = min(1024, C - c0)
                    dma(stg, stg[:, 0:cw], src, src[l, r0:r0 + 128, c0:c0 + cw], q=("sp" if pc[0] % 2 == 0 else "act"))
                    cp(stgb, stgb[:, 0:cw], stg, stg[:, 0:cw], eng=("dve" if pc[0] % 2 == 0 else "pool"))
                    dma(dst, dst[l, r0:r0 + 128, c0:c0 + cw], stgb, stgb[:, 0:cw])
                    pc[0] += 1

    wrot = [0]

    def wload(nm, l, r0, G, c0, ncol, pp=128):
        s = wsl[wrot[0] % 3]; wrot[0] += 1
        w = WB[nm]
        dma(s, s[0:pp, 0:G, 0:ncol], w, w[l, r0:r0 + G * pp, c0:c0 + ncol].rearrange("(g p) c -> p g c", p=pp))
        return s

    def rstd_of(ss_b, ss_ap, n, out_b, out_ap):
        act(out_b, out_ap, ss_b, ss_ap, AF.Sqrt, bias=epsT[:, 0:1], scale=1.0 / n, extra=[epsT])
        recip(out_b, out_ap, out_b, out_ap)

    def rms_full(xb, xap, gb, ob, oap, n):
        act(tm1b_junk, tm1b_junk[:, 0:n], xb, xap, AF.Square, accum=(sm[0], sm[0][:, 0:1]))
        rstd_of(sm[0], sm[0][:, 0:1], n, sm[0], sm[0][:, 1:2])
        stt(ob, oap, xb, xap, sm[0][:, 1:2], gb, gb[:, 0:n], ALU.mult, ALU.mult, extra=[sm[0]])

    tm1b_junk = sb("sqjunk", [128, D], BF16)

    def rms_heads(xb, xap, H, hd, gb, ob, oap):
        n = H * hd
        tt(tm1b_junk, tm1b_junk[:, 0:n], xb, xap, xb, xap, ALU.mult)
        red(sm[1], sm[1][:, 0:H], tm1b_junk, tm1b_junk[:, 0:n].rearrange("p (h d) -> p h d", h=H), ALU.add)
        rstd_of(sm[1], sm[1][:, 0:H], hd, sm[1], sm[1][:, 8:8 + H])
        xv = xap.rearrange("p (h d) -> p h d", h=H); ov = oap.rearrange("p (h d) -> p h d", h=H)
        tt(ob, ov, xb, xv, sm[1], bc_last(sm[1][:, 8:8 + H], hd), ALU.mult)
        tt(ob, ov, ob, ov, gb, bc_mid(gb[:, 0:hd], H), ALU.mult)

    def rope(xb, xap, H, hd):
        xv = xap.rearrange("p (h d) -> p h d", h=H)
        x1 = xv[:, :, 0:8]; x2 = xv[:, :, 8:16]
        cs = bc_mid(ropet[:, 0:8], H); sn = bc_mid(ropet[:, 8:16], H)
        t1 = tm1[:, 0:H * 8].rearrange("p (h d) -> p h d", h=H); t2 = tm1[:, 64:64 + H * 8].rearrange("p (h d) -> p h d", h=H)
        t3 = tm1[:, 128:128 + H * 8].rearrange("p (h d) -> p h d", h=H); t4 = tm1[:, 192:192 + H * 8].rearrange("p (h d) -> p h d", h=H)
        tt(tm1, t1, xb, x1, ropet, cs, ALU.mult); tt(tm1, t2, xb, x2, ropet, sn, ALU.mult)
        tt(tm1, t3, xb, x2, ropet, cs, ALU.mult); tt(tm1, t4, xb, x1, ropet, sn, ALU.mult)
        tt(xb, x1, tm1, t1, tm1, t2, ALU.subtract); tt(xb, x2, tm1, t3, tm1, t4, ALU.add)

    def to_hT(src_b, src_ap, dstT, col0):
        for half in range(2):
            for j in range(4):
                kgi = half * 4 + j
                tr(psT, psT[:, j * 128:(j + 1) * 128], src_b, src_ap[:, kgi * 128:(kgi + 1) * 128], identb)
            cp(dstT, dstT[:, half * 4:half * 4 + 4, col0:col0 + 128],
               psT, psT[:, 0:512].rearrange("p (g t) -> p g t", g=4), eng="act")

    def tm2fm(dst_b, dst_fn, src_b, src_ap, G, W):
        for g in range(G):
            dma(dst_b, dst_fn(g), src_b, src_ap[:, g * 128:(g + 1) * 128].rearrange("w c -> c w"), nonc=True)

    def fm2tm(dst_b, dst_ap, src_b, src_fn, G, W):
        toks = []
        for g in range(G):
            toks.append(dma(dst_b, dst_ap[:, g * 128:(g + 1) * 128].rearrange("w c -> c w"), src_b, src_fn(g), nonc=True))
        return toks

    def bcast_row(dst_b, n, src_b, row_ap):
        dma(dst_b, dst_b[:, 0:n], src_b, row_ap.to_broadcast([128, n]))

    def load_params(l):
        bcast_row(gmix, D, I["norm_mix_g"], I["norm_mix_g"][l:l + 1, :])
        bcast_row(gffn, D, I["norm_ffn_g"], I["norm_ffn_g"][l:l + 1, :])
        bcast_row(gmem, D, I["mem_norm_g"], I["mem_norm_g"][l:l + 1, :])
        bcast_row(lnag, 512, I["ln_a_g"], I["ln_a_g"][l:l + 1, :]); bcast_row(lnab, 512, I["ln_a_b"], I["ln_a_b"][l:l + 1, :])
        bcast_row(ssmg, 512, I["ssm_norm_g"], I["ssm_norm_g"][l:l + 1, :])
        bcast_row(qg, 64, I["q_norm_g"], I["q_norm_g"][l:l + 1, :]); bcast_row(kg, 64, I["k_norm_g"], I["k_norm_g"][l:l + 1, :])
        bcast_row(mqg, 128, I["mq_norm_g"], I["mq_norm_g"][l:l + 1, :]); bcast_row(mkg, 128, I["mk_norm_g"], I["mk_norm_g"][l:l + 1, :])
        bcast_row(dtb, 8, I["dt_bias"], I["dt_bias"][l:l + 1, :]); bcast_row(dsk, 8, I["d_skip"], I["d_skip"][l:l + 1, :])
        bcast_row(aneg, 8, I["a_log"], I["a_log"][l:l + 1, :])
        act(aneg, aneg[:], aneg, aneg[:], AF.Exp)
        ts(aneg, aneg[:], aneg, aneg[:], -1.0, ALU.mult)
        tm2fm(cwa, lambda g: cwa[:, g, :], I["conv_a_w"], I["conv_a_w"][l], 4, 31)
        tm2fm(cws, lambda g: cws[:, g, :], I["ssm_conv_w"], I["ssm_conv_w"][l], 6, 4)
        tm2fm(cwf, lambda g: cwf[:, g, :], I["ffn_conv_w"], I["ffn_conv_w"][l], 44, 3)
        dma(cba, cba[:], I["conv_a_b"], I["conv_a_b"][l].rearrange("(g c) -> c g", c=128), nonc=True)
        dma(cbs, cbs[:], I["ssm_conv_b"], I["ssm_conv_b"][l].rearrange("(g c) -> c g", c=128), nonc=True)
        dma(cbf, cbf[:], I["ffn_conv_b"], I["ffn_conv_b"][l].rearrange("(g c) -> c g", c=128), nonc=True)

    def dwconv(ub, cb, G, W, wb, bb, Tn):
        for g in range(G):
            ts(cb, cb[:, g, 0:Tn], ub, ub[:, g, 0:Tn], wb[:, g, 0:1], ALU.mult, bb[:, g:g + 1], ALU.add, extra=[wb, bb])
            for j in range(1, W):
                stt(cb, cb[:, g, 0:Tn], ub, ub[:, g, j:j + Tn], wb[:, g, j:j + 1], cb, cb[:, g, 0:Tn], ALU.mult, ALU.add, extra=[wb])

    def mem_kv_prompt(l):
        for mt in range(2):
            dma(stio, stio[:, 0:512], I["memp"], I["memp"][mt * 128:(mt + 1) * 128, 0:512])
            dma(tm2, tm2[:], I["memp"], I["memp"][mt * 128:(mt + 1) * 128, 512:1024])
            act(tm1b_junk, tm1b_junk[:, 0:512], stio, stio[:, 0:512], AF.Square, accum=(sm[2], sm[2][:, 0:1]))
            act(tm1b_junk, tm1b_junk[:, 512:1024], tm2, tm2[:], AF.Square, accum=(sm[2], sm[2][:, 1:2]))
            tt(sm[2], sm[2][:, 2:3], sm[2], sm[2][:, 0:1], sm[2], sm[2][:, 1:2], ALU.add)
            rstd_of(sm[2], sm[2][:, 2:3], D, sm[2], sm[2][:, 3:4])
            stt(hb, hb[:, 0:512], stio, stio[:, 0:512], sm[2][:, 3:4], gmem, gmem[:, 0:512], ALU.mult, ALU.mult, extra=[sm[2]])
            stt(hb, hb[:, 512:1024], tm2, tm2[:], sm[2][:, 3:4], gmem, gmem[:, 512:1024], ALU.mult, ALU.mult, extra=[sm[2]])
            to_hT(hb, hb, hT, 0)
            for c in range(2):
                w = wload("w_mem_kv", l, 0, 8, c * 512, 512)
                p = nps()
                for k in range(8):
                    mm(p, p[:], hT, hT[:, k, 0:128], w, w[:, k, 0:512], start=(k == 0), stop=(k == 7))
                if c == 0:
                    cp(tm1, tm1[:], p, p[:], eng="act")
                    rms_heads(tm1, tm1[:], 4, 128, mkg, stio, stio[:, 0:512])
                    outtoks.append(dma(O["mk_p"], O["mk_p"][l, mt * 128:(mt + 1) * 128, :], stio, stio[:, 0:512]))
                    cp(tmb, tmb[:], stio, stio[:, 0:512])
                    for h in range(4):
                        tr(psT, psT[:, h * 128:(h + 1) * 128], tmb, tmb[:, h * 128:(h + 1) * 128], identb)
                    cp(mkT, mkT[:, :, mt * 128:(mt + 1) * 128], psT, psT[:, 0:512].rearrange("p (h t) -> p h t", h=4), eng="act")
                else:
                    cp(tm1, tm1[:], p, p[:], eng="act")
                    outtoks.append(dma(O["mv_p"], O["mv_p"][l, mt * 128:(mt + 1) * 128, :], tm1, tm1[:]))
                    cp(mvb, mvb[:, mt, :], tm1, tm1[:])

    def mem_kv_sample(l, j):
        for mt in range(2):
            dma(tm1, tm1[:], I["cmk"], I["cmk"][l, j, mt * 128:(mt + 1) * 128, :])
            dma(tm2, tm2[:], I["cmv"], I["cmv"][l, j, mt * 128:(mt + 1) * 128, :])
            cp(tmb, tmb[:], tm1, tm1[:])
            for h in range(4):
                tr(psT, psT[:, h * 128:(h + 1) * 128], tmb, tmb[:, h * 128:(h + 1) * 128], identb)
            cp(mkT, mkT[:, :, mt * 128:(mt + 1) * 128], psT, psT[:, 0:512].rearrange("p (h t) -> p h t", h=4), eng="act")
            cp(mvb, mvb[:, mt, :], tm2, tm2[:])

    class StopM(Exception):
        pass
    mstop = 99.0

    def chk(n):
        if mstop <= n:
            raise StopM()

    def macro(l, ctx):
        kind = ctx["kind"]; nsub = ctx["nsub"]; Tn = nsub * 128; tv = ctx["tv"]; pos0 = ctx["pos0"]
        samp = (kind == "s"); j = ctx.get("j", 0)
        tokm = tokms if samp else tokm1
        last_layer = (l == DEPTH - 1)
        for s in range(nsub):
            if samp:
                mset(x_mt, x_mt[:, s, :], 0.0, eng="dve")
                src = I["xs"] if l == 0 else xsres
                dma(x_mt, x_mt[0:1, s, :], src, src[j:j + 1, :])
            else:
                src = I["xp"] if l == 0 else xres
                dma(x_mt, x_mt[:, s, :], src, src[pos0 + s * 128: pos0 + (s + 1) * 128, :])
            rms_full(x_mt, x_mt[:, s, :], gmix, hb, hb[:], D)
            to_hT(hb, hb, hT, s * 128)
        def tm_seg(c_lo, c_hi, dsts, off0):
            c = c_lo
            while c < c_hi:
                n = min(512, c_hi - c)
                w = wload("w_in", l, 0, 8, c, n)
                for s in range(nsub):
                    p = nps()
                    for k in range(8):
                        mm(p, p[:, 0:n], hT, hT[:, k, s * 128:(s + 1) * 128], w, w[:, k, 0:n], start=(k == 0), stop=(k == 7))
                    cp(dsts[s], dsts[s][:, off0 + c - c_lo: off0 + c - c_lo + n], p, p[:, 0:n], eng="act")
                c += n
        chk(1)
        tm_seg(C_Q, C_XBC, projA, 0)
        tm_seg(C_DT, C_G, projB, 0)
        chk(2)
        if ctx["first"]:
            if samp:
                tm2fm(glu_u, lambda g: glu_u[:, g, 0:30], I["sconf"], I["sconf"][l, j], 4, 30)
                tm2fm(xbc_u, lambda g: xbc_u[:, g, 0:3], I["ssc"], I["ssc"][l, j], 6, 3)
                tm2fm(fcar, lambda g: fcar[:, g, :], I["sffn"], I["sffn"][l, j], 44, 2)
            else:
                mset(glu_u, glu_u[:, :, 0:30], 0.0, eng="dve"); mset(xbc_u, xbc_u[:, :, 0:3], 0.0, eng="dve")
                mset(fcar, fcar[:], 0.0, eng="dve")
        wv = wload("w_in", l, 0, 8, 0, 512); wg = wload("w_in", l, 0, 8, 512, 512)
        for c in range(4):
            pv = nps(); pg = nps()
            for k in range(8):
                mm(pv, pv[:, 0:Tn], wv, wv[:, k, c * 128:(c + 1) * 128], hT, hT[:, k, 0:Tn], start=(k == 0), stop=(k == 7))
            for k in range(8):
                mm(pg, pg[:, 0:Tn], wg, wg[:, k, c * 128:(c + 1) * 128], hT, hT[:, k, 0:Tn], start=(k == 0), stop=(k == 7))
            act(sgt, sgt[:, 0:Tn], pg, pg[:, 0:Tn], AF.Sigmoid)
            tt(glu_u, glu_u[:, c, 30:30 + Tn], pv, pv[:, 0:Tn], sgt, sgt[:, 0:Tn], ALU.mult)
        for (c0, ng) in ((0, 4), (4, 2)):
            w = wload("w_in", l, 0, 8, C_XBC + c0 * 128, ng * 128)
            for c in range(ng):
                p = nps()
                for k in range(8):
                    mm(p, p[:, 0:Tn], w, w[:, k, c * 128:(c + 1) * 128], hT, hT[:, k, 0:Tn], start=(k == 0), stop=(k == 7))
                cp(xbc_u, xbc_u[:, c0 + c, 3:3 + Tn], p, p[:, 0:Tn], eng="act")
        chk(3)
        if ctx["last"]:
            if samp:
                outtoks.extend(fm2tm(O["conf_s"], O["conf_s"][l, j], glu_u, lambda g: glu_u[:, g, tv:tv + 30], 4, 30))
                outtoks.extend(fm2tm(O["sc_s"], O["sc_s"][l, j], xbc_u, lambda g: xbc_u[:, g, tv:tv + 3], 6, 3))
            else:
                outtoks.extend(fm2tm(O["conf_p"], O["conf_p"][l], glu_u, lambda g: glu_u[:, g, tv:tv + 30], 4, 30))
                outtoks.extend(fm2tm(O["sc_p"], O["sc_p"][l], xbc_u, lambda g: xbc_u[:, g, tv:tv + 3], 6, 3))
        chk(4)
        dwconv(glu_u, glu_c, 4, 31, cwa, cba, Tn)
        dwconv(xbc_u, xbc_c, 6, 4, cws, cbs, Tn)
        act(xbc_c, xbc_c[:, :, 0:Tn], xbc_c, xbc_c[:, :, 0:Tn], AF.Silu)
        if not ctx["last"]:
            cp(glu_u, glu_u[:, :, 0:30], glu_u, glu_u[:, :, Tn:Tn + 30])
            cp(xbc_u, xbc_u[:, :, 0:3], xbc_u, xbc_u[:, :, Tn:Tn + 3])

        for s in range(nsub):
            cs = slice(s * 128, (s + 1) * 128)
            pA, pB = projA[s], projB[s]
            chk(5)
            p = nps()
            for c in range(4):
                tr(p, p[:, c * 128:(c + 1) * 128], glu_c, glu_c[:, c, cs], identf)
            P.op("dve", lambda e, p=p: e.bn_stats(out=sm[3][:, 0:6], in_=p[:, 0:512]), reads=[p], writes=[sm[3]])
            P.op("dve", lambda e: e.bn_aggr(out=sm[3][:, 6:8], in_=sm[3][:, 0:6]), reads=[sm[3]], writes=[sm[3]])
            act(sm[3], sm[3][:, 8:9], sm[3], sm[3][:, 7:8], AF.Sqrt, bias=epsT[:, 0:1], scale=1.0, extra=[epsT])
            recip(sm[3], sm[3][:, 8:9], sm[3], sm[3][:, 8:9])
            ts(tm1, tm1[:], p, p[:], sm[3][:, 6:7], ALU.subtract, sm[3][:, 8:9], ALU.mult, extra=[sm[3]])
            tt(tm1, tm1[:], tm1, tm1[:], lnag, lnag[:], ALU.mult)
            tt(tm1, tm1[:], tm1, tm1[:], lnab, lnab[:], ALU.add)
            act(tmb, tmb[:], tm1, tm1[:], AF.Silu)
            for c in range(4):
                tr(psT, psT[:, c * 128:(c + 1) * 128], tmb, tmb[:, c * 128:(c + 1) * 128], identb)
            cp(brT[0], brT[0][:, :, cs], psT, psT[:, 0:512].rearrange("p (g t) -> p g t", g=4), eng="act")

            chk(6)
            p = nps()
            for c in range(4):
                tr(p, p[:, c * 128:(c + 1) * 128], xbc_c, xbc_c[:, c, cs], identf)
            cp(xs_tm, xs_tm[:], p, p[:], eng="act")
            cp(BT, BT[:], xbc_c, xbc_c[:, 4, cs]); cp(CT, CT[:], xbc_c, xbc_c[:, 5, cs])
            for g in range(2):
                cp(CTm[g], CTm[g][g * 64:(g + 1) * 64, :], xbc_c, xbc_c[g * 64:(g + 1) * 64, 5, cs])
            tr(psT, psT[:, 0:128], BT, BT[:], identb)
            cp(B_tm, B_tm[:], psT, psT[:, 0:128], eng="act")
            d0 = sm[4]
            tt(d0, d0[:, 0:8], pB, pB[:, 0:8], dtb, dtb[:], ALU.add)
            ts(d0, d0[:, 8:16], d0, d0[:, 0:8], -1.0, ALU.mult)
            tt(d0, d0[:, 8:16], d0, d0[:, 8:16], d0, d0[:, 0:8], ALU.max)
            act(d0, d0[:, 8:16], d0, d0[:, 8:16], AF.Exp, scale=-1.0)
            act(d0, d0[:, 8:16], d0, d0[:, 8:16], AF.Ln, bias=oneT[:, 0:1], scale=1.0, extra=[oneT])
            stt(d0, d0[:, 0:8], d0, d0[:, 0:8], 0.0, d0, d0[:, 8:16], ALU.max, ALU.add)
            ts(d0, d0[:, 0:8], d0, d0[:, 0:8], tokm[:, 0:1], ALU.mult, extra=[tokm])
            tt(d0, d0[:, 8:16], d0, d0[:, 0:8], aneg, aneg[:], ALU.mult)
            a1 = sm[5]
            pa = nps()
            mm(pa, pa[:, 0:8], trif, trif[:], d0, d0[:, 8:16])
            cp(a1, a1[:, 0:8], pa, pa[:, 0:8])
            pa = nps()
            mm(pa, pa[:, 0:8], elast, elast[:], a1, a1[:, 0:8])
            cp(a1, a1[:, 8:16], pa, pa[:, 0:8])
            e1 = sm[6]
            act(e1, e1[:, 0:8], a1, a1[:, 0:8], AF.Exp)
            act(e1, e1[:, 8:16], a1, a1[:, 8:16], AF.Exp)
            tt(sm[7], sm[7][:, 0:8], a1, a1[:, 8:16], a1, a1[:, 0:8], ALU.subtract)
            act(sm[7], sm[7][:, 0:8], sm[7], sm[7][:, 0:8], AF.Exp)
            tt(sm[7], sm[7][:, 8:16], sm[7], sm[7][:, 0:8], d0, d0[:, 0:8], ALU.mult)
            xv = xs_tm[:].rearrange("p (h d) -> p h d", h=8)
            tt(xdt, xdt[:].rearrange("p (h d) -> p h d", h=8), xs_tm, xv, d0, bc_last(d0[:, 0:8], 64), ALU.mult)
            tt(xdtd, xdtd[:].rearrange("p (h d) -> p h d", h=8), xs_tm, xv, sm[7], bc_last(sm[7][:, 8:16], 64), ALU.mult)
            chk(6.1)
            pg_ = nps()
            for g in range(2):
                mm(pg_, pg_[:, g * 128:(g + 1) * 128], BT, BT[:], CTm[g], CTm[g][:])
            cp(GT, GT[:], pg_, pg_[:, 0:256].rearrange("p (g t) -> p g t", g=2), eng="act")
            tt(diag, diag[:], identf, bc_mid(identf[:], 8), a1, bc_last(a1[:, 0:8], 128), ALU.mult)
            for h in range(8):
                mm(psW, psW[:, h * 128:(h + 1) * 128], onesf, onesf[:], diag, diag[:, h, :])
            tt(seg, seg[:], psW, psW[:].rearrange("p (h t) -> p h t", h=8), a1, bc_last(a1[:, 0:8], 128), ALU.subtract)
            ts(seg, seg[:], seg, seg[:], 0.0, ALU.min)
            act(seg, seg[:], seg, seg[:], AF.Exp)
            tt(seg, seg[:].rearrange("p (g h) t -> p g h t", g=2), seg, seg[:].rearrange("p (g h) t -> p g h t", g=2),
               GT, GT[:].unsqueeze(2).to_broadcast([128, 2, 4, 128]), ALU.mult)
            tt(scT, scT[:], seg, seg[:], trif, bc_mid(trif[:], 8), ALU.mult)
            chk(6.2)
            py = nps()
            for h in range(8):
                mm(py, py[:, h * 64:(h + 1) * 64], scT, scT[:, h, :], xdt, xdt[:, h * 64:(h + 1) * 64])
            cp(ybuf, ybuf[:], py, py[:], eng="act")
            if ctx["first"] and s == 0:
                if samp:
                    for g in range(2):
                        dma(hio, hio[:].rearrange("p (hh g n) -> p hh g n", hh=4, g=2)[:, :, g, :], I["sssm"],
                            I["sssm"][l, j, g * 4:(g + 1) * 4].rearrange("hh p n -> p hh n"))
                    for hh in range(4):
                        pq = nps()
                        tr(pq, pq[:, 0:64], hio, hio[:, hh * 128:(hh + 1) * 128], identf)
                        cp(hst, hst[:, hh, :], pq, pq[:, 0:64])
                else:
                    mset(hst, hst[:], 0.0, eng="dve")
                cp(hstb, hstb[:], hst, hst[:])
            po = nps()
            for h in range(8):
                g = h // 4
                mm(po, po[:, h * 64:(h + 1) * 64], CTm[g], CTm[g][:], hstb, hstb[:, h % 4, :])
            tt(tm1, tm1[:].rearrange("p (h d) -> p h d", h=8), po, po[:].rearrange("p (h d) -> p h d", h=8),
               e1, bc_last(e1[:, 0:8], 64), ALU.mult)
            tt(ybuf, ybuf[:], ybuf, ybuf[:], tm1, tm1[:], ALU.add)
            tt(tm1, tm1[:].rearrange("p (h d) -> p h d", h=8), xs_tm, xv, dsk, bc_last(dsk[:], 64), ALU.mult)
            tt(ybuf, ybuf[:], ybuf, ybuf[:], tm1, tm1[:], ALU.add)
            chk(6.3)
            pst = nps()
            for h in range(8):
                mm(pst, pst[:, h * 64:(h + 1) * 64], B_tm, B_tm[:], xdtd, xdtd[:, h * 64:(h + 1) * 64])
            for g in range(2):
                r_ = slice(g * 64, (g + 1) * 64)
                tt(hst, hst[r_, :, :], hst, hst[r_, :, :], e1, bc_last(e1[r_, 8 + g * 4: 12 + g * 4], 64), ALU.mult)
                tt(hst, hst[r_, :, :], hst, hst[r_, :, :], pst,
                   pst[r_, g * 256:(g + 1) * 256].rearrange("p (h d) -> p h d", h=4), ALU.add)
            cp(hstb, hstb[:], hst, hst[:])
            if ctx["last"] and s == nsub - 1:
                for hh in range(4):
                    pq = nps()
                    tr(pq, pq[0:64, 0:128], hst, hst[:, hh, :], identf)
                    cp(hio, hio[:, hh * 128:(hh + 1) * 128], pq, pq[0:64, 0:128])
                od = O["ssm_s"][l, j] if samp else O["ssm_p"][l]
                for g in range(2):
                    outtoks.append(dma(O["ssm_s"] if samp else O["ssm_p"], od[g * 4:(g + 1) * 4].rearrange("hh p n -> p hh n"),
                                       hio, hio[:].rearrange("p (hh g n) -> p hh g n", hh=4, g=2)[:, :, g, :]))
            chk(6.4)
            act(tm1, tm1[:], pA, pA[:, C_Z - C_Q: C_Z - C_Q + 512], AF.Silu)
            tt(ybuf, ybuf[:], ybuf, ybuf[:], tm1, tm1[:], ALU.mult)
            rms_full(ybuf, ybuf[:], ssmg, tmb, tmb[:], 512)
            for c in range(4):
                tr(psT, psT[:, c * 128:(c + 1) * 128], tmb, tmb[:, c * 128:(c + 1) * 128], identb)
            cp(brT[1], brT[1][:, :, cs], psT, psT[:, 0:512].rearrange("p (g t) -> p g t", g=4), eng="act")

            chk(7)
            rms_heads(pB, pB[:, 8:520], 4, 128, mqg, tm1, tm1[:])
            cp(tmb, tmb[:], tm1, tm1[:])
            for h in range(4):
                tr(psT, psT[:, h * 128:(h + 1) * 128], tmb, tmb[:, h * 128:(h + 1) * 128], identb)
            cp(mqT, mqT[:], psT, psT[:, 0:512].rearrange("p (h t) -> p h t", h=4), eng="act")
            for mt in range(2):
                for h in range(4):
                    mm(psW, psW[:, mt * 512 + h * 128: mt * 512 + (h + 1) * 128], mkT, mkT[:, h, mt * 128:(mt + 1) * 128], mqT, mqT[:, h, :])
            act(PT, PT[:].rearrange("p a b -> p (a b)"), psW, psW[:], AF.Exp, scale=128 ** -0.5)
            pO = nps(); pD = nps()
            for h in range(4):
                for mt in range(2):
                    mm(pO, pO[:, h * 128:(h + 1) * 128], mvb, mvb[:, mt, h * 128:(h + 1) * 128], PT, PT[:, mt, h * 128:(h + 1) * 128],
                       start=(mt == 0), stop=(mt == 1))
            for mt in range(2):
                mm(pD, pD[:], onesb, onesb[:], PT, PT[:, mt, :], start=(mt == 0), stop=(mt == 1))
            recip(rden, rden[:], pD, pD[:])
            tt(brT[2], brT[2][:, :, cs], pO, pO[:].rearrange("p (h t) -> p h t", h=4), rden, rden[:].rearrange("p (h t) -> p h t", h=4), ALU.mult)

            chk(8)
            if samp:
                bcast_row(ropet, 16, I["ropes"], I["ropes"][0:1, :])
            else:
                dma(ropet, ropet[:], I["ropep"], I["ropep"][pos0 + s * 128: pos0 + (s + 1) * 128, :])
            rms_heads(pA, pA[:, 0:512], 8, 64, qg, qn, qn[:])
            rope(qn, qn[:], 8, 64)
            chk(8.05)
            for kv in range(2):
                cp(qbf, qbf[:].rearrange("p (g kv d) -> p g kv d", g=4, kv=2)[:, :, kv, :],
                   qn, qn[:].rearrange("p (kv g d) -> p kv g d", kv=2, g=4)[:, kv, :, :])
            for g in range(4):
                tr(psT, psT[:, g * 128:(g + 1) * 128], qbf, qbf[:, g * 128:(g + 1) * 128], identb)
            cp(qT, qT[:], psT, psT[:, 0:512].rearrange("p (g t) -> p g t", g=4), eng="act")
            chk(8.07)
            for kv in range(2):
                cp(qTm[kv], qTm[kv][kv * 64:(kv + 1) * 64, :, :], qT, qT[kv * 64:(kv + 1) * 64, :, :])
            chk(8.1)
            rms_heads(pA, pA[:, C_K - C_Q: C_K - C_Q + 128], 2, 64, kg, kout, kout[:])
            rope(kout, kout[:], 2, 64)
            if samp:
                outtoks.append(dma(O["k_s"], O["k_s"][l, j:j + 1, :], kout, kout[0:1, :]))
                outtoks.append(dma(O["v_s"], O["v_s"][l, j:j + 1, :], pA, pA[0:1, C_V - C_Q: C_V - C_Q + 128]))
            else:
                outtoks.append(dma(O["k_p"], O["k_p"][l, pos0 + s * 128: pos0 + (s + 1) * 128, :], kout, kout[:]))
                outtoks.append(dma(O["v_p"], O["v_p"][l, pos0 + s * 128: pos0 + (s + 1) * 128, :], pA, pA[:, C_V - C_Q: C_V - C_Q + 128]))
            cp(kbf, kbf[:], kout, kout[:])
            tr(psT, psT[:, 0:128], kbf, kbf[:], identb)
            cp(kTs, kTs[:], psT, psT[:, 0:128], eng="act")
            hp = (PAST if samp else pos0 + s * 128)
            dma(h_kT, h_kT[:, hp:hp + 128], kTs, kTs[:])
            cp(vbuf, vbuf[:, :, 0:64], pA, pA[:, C_V - C_Q: C_V - C_Q + 128].rearrange("p (kv d) -> p kv d", kv=2))
            dma(h_v, h_v[hp:hp + 128, :], vbuf, vbuf[:].rearrange("p a b -> p (a b)"))
            chk(8.2)
            cp(kiout, kiout[:], pA, pA[:, C_KI - C_Q: C_KI - C_Q + 64])
            rope(kiout, kiout[:], 1, 64)
            if samp:
                outtoks.append(dma(O["ki_s"], O["ki_s"][l, j:j + 1, :], kiout, kiout[0:1, :]))
            else:
                outtoks.append(dma(O["ki_p"], O["ki_p"][l, pos0 + s * 128: pos0 + (s + 1) * 128, :], kiout, kiout[:]))
            cp(kibf, kibf[:], kiout, kiout[:])
            tr(psT, psT[0:64, 0:128], kibf, kibf[:], identb)
            cp(kiTs, kiTs[:], psT, psT[0:64, 0:128], eng="act")
            dma(h_kiT, h_kiT[:, hp:hp + 128], kiTs, kiTs[:])
            chk(8.3)
            cp(qn, qn[:], pA, pA[:, C_QI - C_Q: C_QI - C_Q + 512])
            rope(qn, qn[:], 8, 64)
            cp(qbf, qbf[:], qn, qn[:])
            for half in range(2):
                for h4 in range(4):
                    h = half * 4 + h4
                    tr(psT, psT[0:64, h4 * 128:(h4 + 1) * 128], qbf, qbf[:, h * 64:(h + 1) * 64], identb)
                cp(qiT, qiT[:, half * 4:half * 4 + 4, :], psT, psT[0:64, 0:512].rearrange("p (h t) -> p h t", h=4), eng="act")
            wi_ap = pA[:, C_WI - C_Q: C_WI - C_Q + 8]
            ts(awi, awi[:], pA, wi_ap, -1.0, ALU.mult)
            tt(awi, awi[:], awi, awi[:], pA, wi_ap, ALU.max)
            act(swi, swi[:], pA, wi_ap, AF.Sign)
            ts(swi, swi[:], swi, swi[:], IDX_SCALE, ALU.mult)
            chk(9)
            nkt = hp // 128 + 1
            nkeys = nkt * 128
            for c0 in range(0, nkeys, 1024):
                n = min(1024, nkeys - c0)
                kb = kic[rot["kic"] % 2]; rot["kic"] += 1
                dma(kb, kb[:, 0:n], h_kiT, h_kiT[:, c0:c0 + n])
                for h in range(8):
                    for b0 in range(0, n, 512):
                        bn = min(512, n - b0)
                        mm(psW, psW[:, b0:b0 + bn], qiT, qiT[:, h, :], kb, kb[:, b0:b0 + bn])
                    rb = rbuf[rot["rb"] % 2]; rot["rb"] += 1
                    act(rb, rb[:, 0:n], psW, psW[:, 0:n], AF.Relu, scale=awi[:, h:h + 1], extra=[awi])
                    if h == 0:
                        ts(Isc, Isc[:, c0:c0 + n], rb, rb[:, 0:n], swi[:, 0:1], ALU.mult, extra=[swi])
                    else:
                        stt(Isc, Isc[:, c0:c0 + n], rb, rb[:, 0:n], swi[:, h:h + 1], Isc, Isc[:, c0:c0 + n], ALU.mult, ALU.add, extra=[swi])
            am = sadd if samp else cadd
            tt(Isc, Isc[:, nkeys - 128:nkeys], Isc, Isc[:, nkeys - 128:nkeys], am, am[:], ALU.add)
            chk(10)
            KSEL = cfg.KS if samp else cfg.KP
            if nkeys - 128 >= KSEL:
                red(lo, lo[:], Isc, Isc[:, 0:nkeys - 128], ALU.min)
                red(hi, hi[:], Isc, Isc[:, 0:nkeys], ALU.max)
                for it in range(NITER):
                    ts(mid, mid[:], lo, lo[:], hi[:, 0:1], ALU.add, 0.5, ALU.mult, extra=[hi])
                    ts(junk, junk[:, 0:nkeys], Isc, Isc[:, 0:nkeys], mid[:, 0:1], ALU.is_ge, 0.0, ALU.add,
                       accum=(cnt, cnt[:]), extra=[mid])
                    ts(pge, pge[:], cnt, cnt[:], float(KSEL), ALU.is_ge)
                    ts(plt, plt[:], cnt, cnt[:], float(KSEL), ALU.is_lt)
                    P.op("dve", lambda e: e.copy_predicated(out=lo[:], mask=pge[:], data=mid[:]), reads=[pge, mid], writes=[lo])
                    P.op("dve", lambda e: e.copy_predicated(out=hi[:], mask=plt[:], data=mid[:]), reads=[plt, mid], writes=[hi])
                ts(lo, lo[:], lo, lo[:], NEG / 2, ALU.max)
            else:
                mset(lo, lo[:], NEG / 2, eng="dve")
            ts(Isc, Isc[:, 0:nkeys], Isc, Isc[:, 0:nkeys], lo[:, 0:1], ALU.is_ge, extra=[lo])
            chk(11)
            for k0 in range(0, nkt, 4):
                nk = min(4, nkt - k0)
                _ki = -1
                kb = kTc[rot["kt"] % 2 if _ki < 0 else _ki]; vb = vc[rot["kt"] % 2 if _ki < 0 else _ki]; rot["kt"] += 1
                dma(kb, kb[:, 0:nk * 128], h_kT, h_kT[:, k0 * 128:(k0 + nk) * 128])
                dma(vb, vb[:, 0:nk, :], h_v, h_v[k0 * 128:(k0 + nk) * 128, :].rearrange("(t p) c -> p t c", p=128))
                for kk in range(nk):
                    kt = k0 + kk
                    i2 = rot["mt"] % 2; rot["mt"] += 1
                    pm_ = nps()
                    tr(pm_, pm_[:, 0:128], Isc, Isc[:, kt * 128:(kt + 1) * 128], identf)
                    cp(mT[i2], mT[i2][:], pm_, pm_[:, 0:128], eng="act")
                    for kv in range(2):
                        mm(psW, psW[:, kv * 512:(kv + 1) * 512], kb, kb[:, kk * 128:(kk + 1) * 128],
                           qTm[kv], qTm[kv][:].rearrange("p g t -> p (g t)"))
                    act(Eb[i2], Eb[i2][:].rearrange("p a b -> p (a b)"), psW, psW[:], AF.Exp, scale=0.125)
                    tt(Pm[i2], Pm[i2][:], Eb[i2], Eb[i2][:], mT[i2], bc_mid(mT[i2][:], 8), ALU.mult)
                    for kv in range(2):
                        mm(psO[kv], psO[kv][:], vb, vb[:, kk, kv * 128:(kv + 1) * 128],
                           Pm[i2], Pm[i2][:, kv * 4:(kv + 1) * 4, :].rearrange("p g t -> p (g t)"),
                           start=(kt == 0), stop=(kt == nkt - 1))
            for kv in range(2):
                recip(rden, rden[64:128, :], psO[kv], psO[kv][64:128, :])
                dma(rdlo, rdlo[:], rden, rden[64:128, :])
                tt(brTb, brTb[:, kv * 4:(kv + 1) * 4, cs], psO[kv], psO[kv][0:64, :].rearrange("p (g t) -> p g t", g=4),
                   rdlo, rdlo[:].rearrange("p (g t) -> p g t", g=4), ALU.mult)

        chk(12)
        if DBG and not samp and l == 0:
            for n, bsrc_ in ((0, brT[0]), (2, brT[1]), (3, brT[2])):
                outtoks.append(dma(O["dbg"], O["dbg"][n, :, pos0:pos0 + Tn].rearrange("(g p) t -> p g t", p=128), bsrc_, bsrc_[:, :, 0:Tn]))
            outtoks.append(dma(O["dbg"], O["dbg"][1, :, pos0:pos0 + Tn].rearrange("(h p) t -> p h t", p=64), brTb, brTb[:, :, 0:Tn]))
        for n in range(4):
            for c in range(2):
                wg_ = wload("w_in", l, 0, 8, C_G + n * 1024 + c * 512, 512)
                if n == 1:
                    wb_ = wload("w_branch", l, 512, 8, c * 512, 512, pp=64)
                else:
                    wb_ = wload("w_branch", l, n * 512, 4, c * 512, 512)
                bsrc = {0: brT[0], 2: brT[1], 3: brT[2]}.get(n)
                for s in range(nsub):
                    cs = slice(s * 128, (s + 1) * 128)
                    pg = nps(); pb = nps()
                    for k in range(8):
                        mm(pg, pg[:], hT, hT[:, k, cs], wg_, wg_[:, k, 0:512], start=(k == 0), stop=(k == 7))
                    if n == 1:
                        for k in range(8):
                            mm(pb, pb[:], brTb, brTb[:, k, cs], wb_, wb_[0:64, k, 0:512], start=(k == 0), stop=(k == 7))
                    else:
                        for k in range(4):
                            mm(pb, pb[:], bsrc, bsrc[:, k, cs], wb_, wb_[:, k, 0:512], start=(k == 0), stop=(k == 3))
                    act(tm2, tm2[:], pg, pg[:], AF.Sigmoid)
                    mx = mixed[s]
                    if n == 0:
                        tt(mx, mx[:, c * 512:(c + 1) * 512], tm2, tm2[:], pb, pb[:], ALU.mult)
                    else:
                        tt(tm2, tm2[:], tm2, tm2[:], pb, pb[:], ALU.mult)
                        tt(mx, mx[:, c * 512:(c + 1) * 512], mx, mx[:, c * 512:(c + 1) * 512], tm2, tm2[:], ALU.add)
        for s in range(nsub):
            cp(hb, hb[:], mixed[s], mixed[s][:])
            to_hT(hb, hb, hT, s * 128)
        for c in range(2):
            w = wload("w_out", l, 0, 8, c * 512, 512)
            for s in range(nsub):
                p = nps()
                for k in range(8):
                    mm(p, p[:], hT, hT[:, k, s * 128:(s + 1) * 128], w, w[:, k, 0:512], start=(k == 0), stop=(k == 7))
                tt(x_mt, x_mt[:, s, c * 512:(c + 1) * 512], x_mt, x_mt[:, s, c * 512:(c + 1) * 512], p, p[:], ALU.add)
        chk(13)
        if DBG and not samp and l == 0:
            for s in range(nsub):
                outtoks.append(dma(O["dbgx"], O["dbgx"][pos0 + s * 128: pos0 + (s + 1) * 128, :], x_mt, x_mt[:, s, :]))
        for s in range(nsub):
            rms_full(x_mt, x_mt[:, s, :], gffn, hb, hb[:], D)
            to_hT(hb, hb, hT, s * 128)
        for j0 in range(0, 22, 4):
            nj = min(4, 22 - j0)
            wgs = wload("w_ffn_up", l, 0, 8, j0 * 128, nj * 128)
            wus = wload("w_ffn_up", l, 0, 8, 2816 + j0 * 128, nj * 128)
            for jj in range(nj):
                jg = j0 + jj; ju = 22 + jg
                pg = nps(); pu = nps()
                for k in range(8):
                    mm(pg, pg[:, 0:Tn], wgs, wgs[:, k, jj * 128:(jj + 1) * 128], hT, hT[:, k, 0:Tn], start=(k == 0), stop=(k == 7))
                for k in range(8):
                    mm(pu, pu[:, 0:Tn], wus, wus[:, k, jj * 128:(jj + 1) * 128], hT, hT[:, k, 0:Tn], start=(k == 0), stop=(k == 7))
                for (st, pp_, jc, co) in ((fst_g, pg, jg, fcg), (fst_u, pu, ju, fcu)):
                    cp(st, st[:, 0:2], fcar, fcar[:, jc, :])
                    cp(st, st[:, 2:2 + Tn], pp_, pp_[:, 0:Tn], eng="act")
                    if ctx["last"]:
                        cp(fso, fso[:, jc, :], st, st[:, tv:tv + 2])
                    else:
                        cp(fcar, fcar[:, jc, :], st, st[:, Tn:Tn + 2])
                    ts(co, co[:, 0:Tn], st, st[:, 0:Tn], cwf[:, jc, 0:1], ALU.mult, cbf[:, jc:jc + 1], ALU.add, extra=[cwf, cbf])
                    stt(co, co[:, 0:Tn], st, st[:, 1:1 + Tn], cwf[:, jc, 1:2], co, co[:, 0:Tn], ALU.mult, ALU.add, extra=[cwf])
                    stt(co, co[:, 0:Tn], st, st[:, 2:2 + Tn], cwf[:, jc, 2:3], co, co[:, 0:Tn], ALU.mult, ALU.add, extra=[cwf])
                act(fcg, fcg[:, 0:Tn], fcg, fcg[:, 0:Tn], AF.Silu)
                tt(gT, gT[:, jg, 0:Tn], fcg, fcg[:, 0:Tn], fcu, fcu[:, 0:Tn], ALU.mult)
        if ctx["last"]:
            od = (O["ffn_s"], O["ffn_s"][l, j]) if samp else (O["ffn_p"], O["ffn_p"][l])
            outtoks.extend(fm2tm(od[0], od[1], fso, lambda g: fso[:, g, :], 44, 2))
        for c in range(2):
            ws_ = [wload("w_ffn_down", l, r0 * 128, min(8, 22 - r0), c * 512, 512) for r0 in (0, 8, 16)]
            for s in range(nsub):
                p = nps()
                for jg in range(22):
                    w = ws_[jg // 8]
                    mm(p, p[:], gT, gT[:, jg, s * 128:(s + 1) * 128], w, w[:, jg % 8, 0:512], start=(jg == 0), stop=(jg == 21))
                tt(x_mt, x_mt[:, s, c * 512:(c + 1) * 512], x_mt, x_mt[:, s, c * 512:(c + 1) * 512], p, p[:], ALU.add)
        for s in range(nsub):
            if samp:
                if last_layer:
                    outtoks.append(dma(O["y_s"], O["y_s"][j:j + 1, :], x_mt, x_mt[0:1, s, :]))
                else:
                    dma(xsres, xsres[j:j + 1, :], x_mt, x_mt[0:1, s, :])
            else:
                dst = O["y_p"] if last_layer else xres
                t_ = dma(dst, dst[pos0 + s * 128: pos0 + (s + 1) * 128, :], x_mt, x_mt[:, s, :])
                if last_layer:
                    outtoks.append(t_)

    def sample_history(l, j):
        dma(ptb, ptb[:], I["pt"], I["pt"][j:j + 1, :].to_broadcast([128, cfg.NPG]))
        ts(pidx, pidx[:], ptb, ptb[:], 128, ALU.mult, iop[:, 0:1], ALU.add, extra=[iop])
        if l > 0:
            ts(pidx, pidx[:], pidx, pidx[:], float(l * NPHYS * 128), ALU.add)
        for pg in range(cfg.NPG):
            for (srcn, dstb, w) in (("cki", pgi, 64), ("ck", pgk, 128), ("cv", pgv, 128)):
                srcb = I[srcn]
                P.dma(lambda e, srcb=srcb, dstb=dstb, pg=pg: e.indirect_dma_start(
                    out=dstb[:], out_offset=None, in_=srcb[:].rearrange("l r c -> (l r) c"),
                    in_offset=bass.IndirectOffsetOnAxis(ap=pidx[:, pg:pg + 1], axis=0)),
                    reads=[srcb, pidx], writes=[dstb], q="pool")
            cp(kibf, kibf[:], pgi, pgi[:])
            tr(psT, psT[0:64, 0:128], kibf, kibf[:], identb)
            cp(kiTs, kiTs[:], psT, psT[0:64, 0:128], eng="act")
            dma(h_kiT, h_kiT[:, pg * 128:(pg + 1) * 128], kiTs, kiTs[:])
            cp(kbf, kbf[:], pgk, pgk[:])
            tr(psT, psT[:, 128:256], kbf, kbf[:], identb)
            cp(kTs, kTs[:], psT, psT[:, 128:256], eng="act")
            dma(h_kT, h_kT[:, pg * 128:(pg + 1) * 128], kTs, kTs[:])
            cp(vbuf, vbuf[:, :, 0:64], pgv, pgv[:].rearrange("p (kv d) -> p kv d", kv=2))
            dma(h_v, h_v[pg * 128:(pg + 1) * 128, :], vbuf, vbuf[:].rearrange("p a b -> p (a b)"))

    nmac = SEQ // T
    stop = 99
    for l in range(DEPTH):
        if stop < 1: break
        load_params(l)
        if stop < 2: break
        mem_kv_prompt(l)
        if stop < 3: break
        for m in range(nmac):
            try:
                macro(l, dict(kind="p", pos0=m * T, nsub=NSUB, tv=T, first=(m == 0), last=(m == nmac - 1)))
            except StopM:
                pass
            if stop < 4: break
        if stop < 5: break
        for j in range(NS):
            mem_kv_sample(l, j)
            sample_history(l, j)
            if stop < 6: break
            macro(l, dict(kind="s", j=j, pos0=PAST, nsub=1, tv=1, first=True, last=True))
        if stop < 7: break
    P.finish(outtoks)
    P.emit()
    es.close()
    return nc


_W_NAMES = ["norm_mix_g", "w_in", "conv_a_w", "conv_a_b", "ln_a_g", "ln_a_b", "q_norm_g", "k_norm_g", "ssm_conv_w",
            "ssm_conv_b", "dt_bias", "a_log", "d_skip", "ssm_norm_g", "mem_norm_g", "w_mem_kv", "mq_norm_g",
            "mk_norm_g", "w_branch", "w_out", "norm_ffn_g", "w_ffn_up", "ffn_conv_w", "ffn_conv_b", "w_ffn_down"]


def _rope_tab(pos):
    inv = 500000.0 ** (-np.arange(8, dtype=np.float64) * (2.0 / 16))
    ang = (pos.astype(np.float32)[:, None] * inv.astype(np.float32)[None, :]).astype(np.float32)
    return np.concatenate([np.cos(ang), np.sin(ang)], axis=1).astype(np.float32)


def run(cfg, inputs, n_cores=8):
    f = lambda a: np.ascontiguousarray(np.asarray(a))
    SEQ, PAST, DEPTH, NS = cfg.SEQ, cfg.PAST, cfg.DEPTH, cfg.NS
    nc = build(cfg)
    B = inputs["x_prompt"].shape[0]
    shared = {n: f(inputs[n]) for n in _W_NAMES}
    shared["w_branch"] = shared["w_branch"].reshape(DEPTH, 2048, D)
    shared["ck"] = f(inputs["cache_k"]).reshape(DEPTH, -1, 128)
    shared["cv"] = f(inputs["cache_v"]).reshape(DEPTH, -1, 128)
    shared["cki"] = f(inputs["cache_kidx"]).reshape(DEPTH, -1, 64)
    shared["ropep"] = _rope_tab(np.arange(SEQ)); shared["ropes"] = _rope_tab(np.array([PAST]))
    in_maps = []
    for c in range(n_cores):
        b = c % B; sl = slice(c * NS, (c + 1) * NS)
        m = dict(shared)
        m["xp"] = f(inputs["x_prompt"][b]); m["memp"] = f(inputs["mem_prompt"][b])
        m["xs"] = f(inputs["x_sample"][sl, 0])
        m["cmk"] = f(inputs["cache_mem_k"][:, sl]).reshape(DEPTH, NS, 256, 512)
        m["cmv"] = f(inputs["cache_mem_v"][:, sl]).reshape(DEPTH, NS, 256, 512)
        m["sconf"] = f(inputs["state_conformer"][:, sl]); m["ssc"] = f(inputs["state_ssm_conv"][:, sl])
        m["sssm"] = f(inputs["state_ssm"][:, sl]); m["sffn"] = f(inputs["state_ffn_conv"][:, sl])
        m["pt"] = f(inputs["page_table"][sl]).astype(np.int32)
        in_maps.append(m)
    res = run_bass_kernel_spmd(nc, in_maps, core_ids=list(range(n_cores)))
    R = res.results
    st = lambda k, cores: np.stack([R[c][k] for c in cores])
    pc = list(range(B))
    ac = list(range(n_cores))
    cat = lambda k: np.concatenate([R[c][k] for c in ac], axis=1)
    y_p = st("y_p", pc)
    y_s = np.concatenate([R[c]["y_s"] for c in ac], axis=0)[:, None, :]
    k_p = st("k_p", pc).transpose(1, 0, 2, 3).reshape(DEPTH, B, SEQ, 2, 64)
    v_p = st("v_p", pc).transpose(1, 0, 2, 3).reshape(DEPTH, B, SEQ, 2, 64)
    ki_p = st("ki_p", pc).transpose(1, 0, 2, 3)
    mk_p = st("mk_p", pc).transpose(1, 0, 2, 3).reshape(DEPTH, B, 256, 4, 128)
    mv_p = st("mv_p", pc).transpose(1, 0, 2, 3).reshape(DEPTH, B, 256, 4, 128)
    conf_p = st("conf_p", pc).transpose(1, 0, 2, 3)
    sc_p = st("sc_p", pc).transpose(1, 0, 2, 3)
    ssm_p = st("ssm_p", pc).transpose(1, 0, 2, 3, 4)
    ffn_p = st("ffn_p", pc).transpose(1, 0, 2, 3)
    k_s = cat("k_s").reshape(DEPTH, -1, 1, 2, 64); v_s = cat("v_s").reshape(DEPTH, -1, 1, 2, 64)
    ki_s = cat("ki_s").reshape(DEPTH, -1, 1, 64)
    outs = (y_p, y_s, k_p, v_p, ki_p, mk_p, mv_p, conf_p, sc_p, ssm_p, ffn_p, k_s, v_s, ki_s,
            cat("conf_s"), cat("sc_s"), cat("ssm_s"), cat("ffn_s"))
    return tuple(np.ascontiguousarray(o.astype(np.float32)) for o in outs)


def kernel(**inputs):
    return run(CFG(), inputs)
```

```python
import numpy as np
from contextlib import ExitStack
import concourse.bass as bass
import concourse.mybir as mybir
from concourse.bass_utils import run_bass_kernel_spmd

F32 = mybir.dt.float32
BF16 = mybir.dt.bfloat16
I32 = mybir.dt.int32
U32 = mybir.dt.uint32
ALU = mybir.AluOpType
AF = mybir.ActivationFunctionType
AX = mybir.AxisListType

ENGS = ("pe", "act", "dve", "pool", "sp")
NDMA = 8
SAME_ENG_SYNC = True


class Buf:
    def __init__(self, name, t, roots=None):
        self.name = name
        self.t = t
        if roots is None:
            self.roots = [self]
            self._lastw = None
            self._readers = []
        else:
            self.roots = roots

    def __getitem__(self, idx):
        return self.t[idx]

    def alias(self, name, ap):
        return Buf(name, ap, roots=self.roots)


def multi(name, t, parts):
    roots = []
    for p in parts:
        for r in p.roots:
            if r not in roots:
                roots.append(r)
    return Buf(name, t, roots=roots)


class Prog:
    def __init__(self, nc, es):
        self.nc = nc
        self.es = es
        self.ops = {e: [] for e in ENGS}
        self.cnt = {e: 0 for e in ENGS}
        self.dma_i = {e: 0 for e in ENGS}
        self.seen = {e: {} for e in ENGS}
        self.sems = {}
        for e in ("pe", "act", "dve", "pool"):
            self.sems[("c", e)] = es.enter_context(nc.semaphore("s_" + e))
        for e in ("sp", "act", "pool"):
            for i in range(NDMA):
                self.sems[("d", e, i)] = es.enter_context(nc.semaphore("d_%s%d" % (e, i)))
        self.nbuf = 0

    def sb(self, name, shape, dt=F32):
        t = self.es.enter_context(self.nc.sbuf_tensor(name, list(shape), dt))
        return Buf(name, t)

    def ps(self, name, shape, dt=F32):
        t = self.es.enter_context(self.nc.psum_tensor(name, list(shape), dt))
        return Buf(name, t)

    def view(self, name, t):
        return Buf(name, t)

    def _need(self, eng, tok, waits):
        if tok is None:
            return
        key, val, teng = tok
        if key[0] == "c" and teng == eng and not (SAME_ENG_SYNC and eng != "pe"):
            return
        if self.seen[eng].get(key, 0) >= val:
            return
        self.seen[eng][key] = val
        waits.append((key, val))

    def _deps(self, eng, reads, writes):
        waits = []
        for b in reads:
            for r in b.roots:
                self._need(eng, r._lastw, waits)
        for b in writes:
            for r in b.roots:
                self._need(eng, r._lastw, waits)
                for rd in r._readers:
                    self._need(eng, rd, waits)
        return waits

    def _commit(self, tok, reads, writes):
        for b in reads:
            for r in b.roots:
                r._readers.append(tok)
        for b in writes:
            for r in b.roots:
                r._lastw = tok
                r._readers = []

    def op(self, eng, fn, reads=(), writes=()):
        waits = self._deps(eng, reads, writes)
        self.cnt[eng] += 1
        key = ("c", eng)
        tok = (key, self.cnt[eng], eng)
        self.ops[eng].append((waits, fn, (key, 1)))
        self._commit(tok, reads, writes)
        return tok

    def dma(self, fn, reads=(), writes=(), q="sp"):
        i = self.dma_i[q]
        self.dma_i[q] += 1
        slot = i % NDMA
        key = ("d", q, slot)
        val = 16 * (i // NDMA + 1)
        waits = self._deps(q, reads, writes)
        if i >= NDMA:
            self._need(q, (key, val - 16, "dma"), waits)
        tok = (key, val, "dma")
        self.ops[q].append((waits, fn, (key, 16)))
        self._commit(tok, reads, writes)
        return tok

    def finish(self, toks):
        waits = []
        for t in toks:
            self._need("sp", t, waits)
        self.ops["sp"].append((waits, None, None))

    def emit(self):
        nc = self.nc
        P = self

        def replay(name, e):
            for waits, fn, inc in P.ops[name]:
                for key, val in waits:
                    e.wait_ge(P.sems[key], val)
                if fn is None:
                    continue
                ins = fn(e)
                ins.then_inc(P.sems[inc[0]], inc[1])

        with nc.Block() as block:
            @block.sync
            def _(e):
                replay("sp", e)

            @block.tensor
            def _(e):
                replay("pe", e)

            @block.scalar
            def _(e):
                replay("act", e)

            @block.vector
            def _(e):
                replay("dve", e)

            @block.gpsimd
            def _(e):
                replay("pool", e)


D = 1024
NEG = -1.0e30
IDX_SCALE = (64 ** -0.5) * (8 ** -0.5)
EPS = 1e-6
IN_COLS = 8272
C_GLU, C_Q, C_K, C_V, C_QI, C_KI, C_WI, C_Z, C_XBC, C_DT, C_MQ, C_G = (
    0, 1024, 1536, 1664, 1792, 2304, 2368, 2376, 2888, 3656, 3664, 4176)
NITER = 22
SPLIT_MIN_NKT = 12


class CFG:
    def __init__(self, SEQ=8192, PAST=8192, NPHYS=2560, DEPTH=2, NS=4, T=128):
        self.SEQ, self.PAST, self.NPHYS, self.DEPTH, self.NS, self.T = SEQ, PAST, NPHYS, DEPTH, NS, T
        self.SMAX = max(SEQ, PAST + 128)
        self.KP = min(256, SEQ // 4)
        self.KS = min(256, (PAST + 1) // 4)
        self.NPG = PAST // 128


def build(cfg):
    SEQ, PAST, NPHYS, DEPTH, NS, T = cfg.SEQ, cfg.PAST, cfg.NPHYS, cfg.DEPTH, cfg.NS, cfg.T
    NSUB = T // 128
    nc = bass.Bass("TRN2", target_bir_lowering=False)
    es = ExitStack()
    P = Prog(nc, es)

    def din(name, shape, dt=F32):
        return Buf(name, nc.dram_tensor(name, list(shape), dt, kind="ExternalInput").ap())

    def dout(name, shape, dt=F32):
        return Buf(name, nc.dram_tensor(name, list(shape), dt, kind="ExternalOutput").ap())

    def dscr(name, shape, dt=BF16):
        return Buf(name, nc.dram_tensor(name, list(shape), dt, kind="Internal").ap())

    I = {}
    I["xp"] = din("xp", [SEQ, D]); I["xs"] = din("xs", [NS, D]); I["memp"] = din("memp", [256, D])
    I["ck"] = din("ck", [DEPTH, NPHYS * 128, 128]); I["cv"] = din("cv", [DEPTH, NPHYS * 128, 128])
    I["cki"] = din("cki", [DEPTH, NPHYS * 128, 64])
    I["cmk"] = din("cmk", [DEPTH, NS, 256, 512]); I["cmv"] = din("cmv", [DEPTH, NS, 256, 512])
    I["sconf"] = din("sconf", [DEPTH, NS, 30, 512]); I["ssc"] = din("ssc", [DEPTH, NS, 3, 768])
    I["sssm"] = din("sssm", [DEPTH, NS, 8, 64, 64]); I["sffn"] = din("sffn", [DEPTH, NS, 2, 5632])
    I["pt"] = din("pt", [NS, cfg.NPG], I32)
    I["ropep"] = din("ropep", [SEQ, 16]); I["ropes"] = din("ropes", [1, 16])
    for nm, shp in [("norm_mix_g", [DEPTH, D]), ("w_in", [DEPTH, D, IN_COLS]), ("conv_a_w", [DEPTH, 31, 512]),
                    ("conv_a_b", [DEPTH, 512]), ("ln_a_g", [DEPTH, 512]), ("ln_a_b", [DEPTH, 512]),
                    ("q_norm_g", [DEPTH, 64]), ("k_norm_g", [DEPTH, 64]), ("ssm_conv_w", [DEPTH, 4, 768]),
                    ("ssm_conv_b", [DEPTH, 768]), ("dt_bias", [DEPTH, 8]), ("a_log", [DEPTH, 8]),
                    ("d_skip", [DEPTH, 8]), ("ssm_norm_g", [DEPTH, 512]), ("mem_norm_g", [DEPTH, D]),
                    ("w_mem_kv", [DEPTH, D, 1024]), ("mq_norm_g", [DEPTH, 128]), ("mk_norm_g", [DEPTH, 128]),
                    ("w_branch", [DEPTH, 2048, D]), ("w_out", [DEPTH, D, D]), ("norm_ffn_g", [DEPTH, D]),
                    ("w_ffn_up", [DEPTH, D, 5632]), ("ffn_conv_w", [DEPTH, 3, 5632]),
                    ("ffn_conv_b", [DEPTH, 5632]), ("w_ffn_down", [DEPTH, 2816, D])]:
        I[nm] = din(nm, shp)
    O = {}
    O["y_p"] = dout("y_p", [SEQ, D]); O["y_s"] = dout("y_s", [NS, D])
    O["k_p"] = dout("k_p", [DEPTH, SEQ, 128]); O["v_p"] = dout("v_p", [DEPTH, SEQ, 128])
    O["ki_p"] = dout("ki_p", [DEPTH, SEQ, 64])
    O["mk_p"] = dout("mk_p", [DEPTH, 256, 512]); O["mv_p"] = dout("mv_p", [DEPTH, 256, 512])
    O["conf_p"] = dout("conf_p", [DEPTH, 30, 512]); O["sc_p"] = dout("sc_p", [DEPTH, 3, 768])
    O["ssm_p"] = dout("ssm_p", [DEPTH, 8, 64, 64]); O["ffn_p"] = dout("ffn_p", [DEPTH, 2, 5632])
    O["k_s"] = dout("k_s", [DEPTH, NS, 128]); O["v_s"] = dout("v_s", [DEPTH, NS, 128])
    O["ki_s"] = dout("ki_s", [DEPTH, NS, 64])
    O["conf_s"] = dout("conf_s", [DEPTH, NS, 30, 512]); O["sc_s"] = dout("sc_s", [DEPTH, NS, 3, 768])
    O["ssm_s"] = dout("ssm_s", [DEPTH, NS, 8, 64, 64]); O["ffn_s"] = dout("ffn_s", [DEPTH, NS, 2, 5632])
    outtoks = []
    DBG = False
    if DBG:
        O["dbg"] = dout("dbg", [4, 512, SEQ], BF16)
        O["dbgx"] = dout("dbgx", [SEQ, D])
    WB = {nm: dscr(nm + "_b", list(I[nm].t.shape)) for nm in
          ("w_in", "w_mem_kv", "w_branch", "w_out", "w_ffn_up", "w_ffn_down")}
    xres = dscr("xres", [SEQ, D], F32)
    xsres = dscr("xsres", [NS, D], F32)
    h_kiT = dscr("h_kiT", [64, cfg.SMAX]); h_kT = dscr("h_kT", [128, cfg.SMAX]); h_v = dscr("h_v", [cfg.SMAX, 256])

    sb, ps = P.sb, P.ps
    identf = sb("identf", [128, 128]); identb = sb("identb", [128, 128], BF16)
    onesf = sb("onesf", [128, 128]); onesb = sb("onesb", [128, 128], BF16)
    trif = sb("trif", [128, 128]); elast = sb("elast", [128, 128])
    cadd = sb("cadd", [128, 128]); sadd = sb("sadd", [128, 128])
    epsT = sb("epsT", [128, 1]); oneT = sb("oneT", [128, 1])
    tokm1 = sb("tokm1", [128, 1]); tokms = sb("tokms", [128, 1])
    gmix = sb("gmix", [128, D]); gffn = sb("gffn", [128, D])
    lnag = sb("lnag", [128, 512]); lnab = sb("lnab", [128, 512]); ssmg = sb("ssmg", [128, 512])
    qg = sb("qg", [128, 64]); kg = sb("kg", [128, 64]); mqg = sb("mqg", [128, 128]); mkg = sb("mkg", [128, 128])
    dtb = sb("dtb", [128, 8]); aneg = sb("aneg", [128, 8]); dsk = sb("dsk", [128, 8])
    cwa = sb("cwa", [128, 4, 31]); cba = sb("cba", [128, 4]); cws = sb("cws", [128, 6, 4]); cbs = sb("cbs", [128, 6])
    cwf = sb("cwf", [128, 44, 3]); cbf = sb("cbf", [128, 44])
    x_mt = sb("x_mt", [128, NSUB, D]); hT = sb("hT", [128, 8, T], BF16); hb = sb("hb", [128, D], BF16)
    wsl = [sb("wsl%d" % i, [128, 8, 512], BF16) for i in range(3)]
    projA = [sb("projA%d" % s, [128, 1864]) for s in range(NSUB)]
    projB = [sb("projB%d" % s, [128, 520]) for s in range(NSUB)]
    glu_u = sb("glu_u", [128, 4, 30 + T]); glu_c_t = sb("glu_c", [128, 4, T]).t; sgt = sb("sgt", [128, 512])
    glu_cg = [Buf("glu_c%d" % g, glu_c_t[:, g, :]) for g in range(4)]
    glu_c = multi("glu_c", glu_c_t, glu_cg)
    xbc_u = sb("xbc_u", [128, 6, 3 + T]); xbc_c_t = sb("xbc_c", [128, 6, T]).t
    xbc_cg = [Buf("xbc_c%d" % g, xbc_c_t[:, g, :]) for g in range(6)]
    xbc_c = multi("xbc_c", xbc_c_t, xbc_cg)
    fst_g = sb("fst_g", [128, 2 + T]); fst_u = sb("fst_u", [128, 2 + T]); fcar = sb("fcar", [128, 44, 2])
    fso = sb("fso", [128, 44, 2]); fcg = sb("fcg", [128, T]); fcu = sb("fcu", [128, T])
    gT = sb("gT", [128, 22, T], BF16)
    brT = [sb("brT%d" % n, [128, 4, T], BF16) for n in (0, 2, 3)]
    brTb = sb("brTb", [64, 8, T], BF16)
    mixed = [projA[s].alias("mixed%d" % s, projA[s][:, 0:D]) for s in range(NSUB)]
    gmem = projA[0].alias("gmem", projA[0][:, 0:D])
    tm1 = sb("tm1", [128, 512]); tm2 = sb("tm2", [128, 512]); tmb = sb("tmb", [128, 512], BF16)
    sm = [sb("sm%d" % i, [128, 16]) for i in range(8)]
    xs_tm = sb("xs_tm", [128, 512]); B_tm = sb("B_tm", [128, 128], BF16)
    BT = sb("BT", [128, 128], BF16); CT = sb("CT", [128, 128], BF16)
    CTm = [sb("CTm0", [128, 128], BF16), sb("CTm1", [128, 128], BF16)]
    qTm = [sb("qTm0", [128, 4, 128], BF16), sb("qTm1", [128, 4, 128], BF16)]
    xdt = sb("xdt", [128, 512], BF16); xdtd = sb("xdtd", [128, 512], BF16)
    GT = sb("GT", [128, 2, 128])
    scT = sb("scT", [128, 8, 128], BF16)
    hst = sb("hst", [128, 4, 64]); hstb = sb("hstb", [128, 4, 64], BF16)
    ybuf = sb("ybuf", [128, 512])
    mkT = sb("mkT", [128, 4, 256], BF16); mvb = sb("mvb", [128, 2, 512], BF16)
    mqT = sb("mqT", [128, 4, 128], BF16); PT = sb("PT", [128, 2, 512], BF16); rden = sb("rden", [128, 512])
    rdlo = sb("rdlo", [64, 512])
    hio = rdlo.alias("hio", rdlo[:])
    Isc = sb("Isc", [128, max(cfg.SMAX, 1024)])
    NKT_MAX = cfg.SMAX // 128
    N1MAX = max(12, int(round(NKT_MAX * 0.42))) * 128
    junkD = sb("junkD", [128, N1MAX], BF16); junkA = sb("junkA", [128, max(min(cfg.SMAX, 1024), cfg.SMAX - N1MAX + 128)], BF16)
    nmid = sb("nmid", [128, 1]); cnt2 = sb("cnt2", [128, 1])
    kic = [sb("kic%d" % i, [64, 1024], BF16) for i in range(2)]
    rbuf = [sb("rbuf0", [128, 1024]), sb("rbuf1", [128, 1024])]
    diag = rbuf[0].alias("diag", rbuf[0][:].rearrange("p (h t) -> p h t", h=8))
    seg = Isc.alias("seg", Isc[:, 0:1024].rearrange("p (h t) -> p h t", h=8))
    stg = [Isc.alias("stg", Isc[:, 0:1024]), rbuf[1].alias("stg1", rbuf[1][:])]
    stgb = [hT.alias("stgb", hT[:].rearrange("p a b -> p (a b)")[:, 0:1024]), hb.alias("stgb1", hb[:])]
    kTc = [sb("kTc%d" % i, [128, 512], BF16) for i in range(2)]
    vc = [sb("vc%d" % i, [128, 4, 256], BF16) for i in range(2)]
    qT = sb("qT", [128, 4, 128], BF16); qiT = sb("qiT", [64, 8, 128], BF16)
    mT4 = [sb("mT4_%d" % i, [128, 4, 128], BF16) for i in range(2)]
    Eb = [sb("Eb%d" % i, [128, 8, 128], BF16) for i in range(2)]
    Pm = [sb("Pm%d" % i, [128, 8, 128], BF16) for i in range(2)]
    vbuf = sb("vbuf", [128, 2, 128], BF16); ropet = sb("ropet", [128, 16])
    awi = sb("awi", [128, 8]); swi = sb("swi", [128, 8])
    lo = sb("lo", [128, 1]); hi = sb("hi", [128, 1]); mid = sb("mid", [128, 1]); cnt = sb("cnt", [128, 1])
    pge = sb("pge", [128, 1], I32); plt = sb("plt", [128, 1], I32)
    pidx = sb("pidx", [128, cfg.NPG], I32); ptb = sb("ptb", [128, cfg.NPG], I32); iop = sb("iop", [128, 1], I32)
    pgk = sb("pgk", [128, 128]); pgv = sb("pgv", [128, 128]); pgi = sb("pgi", [128, 64])
    kout = sb("kout", [128, 128]); kiout = sb("kiout", [128, 64]); kbf = sb("kbf", [128, 128], BF16)
    kibf = sb("kibf", [128, 64], BF16); qn = tm2.alias("qn", tm2[:]); qbf = sb("qbf", [128, 512], BF16)
    kTs = sb("kTs", [128, 128], BF16); kiTs = sb("kiTs", [64, 128], BF16)
    stio = ybuf.alias("stio", ybuf[:])
    psAB_t = ps("psAB", [128, 1024]).t
    psA = Buf("psA", psAB_t[:, 0:512]); psB = Buf("psB", psAB_t[:, 512:1024]); psC = ps("psC", [128, 512])
    psAB = multi("psAB2", psAB_t, [psA, psB])
    psW = ps("psW", [128, 1024]); psO = [ps("psO0", [128, 512]), ps("psO1", [128, 512])]
    psT = ps("psT", [128, 1024], BF16)
    pr = [psA, psB, psC]
    SS = [psW, psAB]
    rot = {"ps": 0, "w": 0, "kic": 0, "rb": 0, "kt": 0, "mt": 0, "mg": 0}

    def nps():
        rot["ps"] = (rot["ps"] + 1) % 3
        return pr[rot["ps"]]

    def mm(o, oap, l, lap, r, rap, start=True, stop=True):
        P.op("pe", lambda e: e.matmul(oap, lhsT=lap, rhs=rap, start=start, stop=stop), reads=[l, r], writes=[o])

    def tr(o, oap, i, iap, idt):
        P.op("pe", lambda e: e.transpose(oap, iap, idt[0:iap.shape[0], 0:iap.shape[0]]), reads=[i, idt], writes=[o])

    def act(o, oap, i, iap, func, bias=None, scale=None, accum=None, extra=()):
        kw = {}
        if bias is not None: kw["bias"] = bias
        if scale is not None: kw["scale"] = scale
        wr = [o]
        if accum is not None:
            kw["accum_out"] = accum[1]; wr.append(accum[0])
        P.op("act", lambda e: e.activation(out=oap, in_=iap, func=func, **kw), reads=[i] + list(extra), writes=wr)

    def tt(o, oap, a, aap, b, bap, op, eng="dve"):
        P.op(eng, lambda e: e.tensor_tensor(out=oap, in0=aap, in1=bap, op=op), reads=[a, b], writes=[o])

    def ts(o, oap, a, aap, s1, op0, s2=None, op1=None, accum=None, extra=(), eng="dve"):
        kw = {}
        wr = [o]
        if op1 is not None: kw["op1"] = op1
        if accum is not None:
            kw["accum_out"] = accum[1]; wr.append(accum[0])
        P.op(eng, lambda e: e.tensor_scalar(out=oap, in0=aap, scalar1=s1, scalar2=s2, op0=op0, **kw),
             reads=[a] + list(extra), writes=wr)

    def stt(o, oap, a, aap, sc, b, bap, op0, op1, extra=()):
        P.op("dve", lambda e: e.scalar_tensor_tensor(out=oap, in0=aap, scalar=sc, in1=bap, op0=op0, op1=op1),
             reads=[a, b] + list(extra), writes=[o])

    def cp(o, oap, i, iap, eng="dve"):
        if eng == "act":
            P.op("act", lambda e: e.activation(out=oap, in_=iap, func=AF.Copy), reads=[i], writes=[o])
        else:
            P.op(eng, lambda e: e.tensor_copy(out=oap, in_=iap), reads=[i], writes=[o])

    def mset(o, oap, v, eng="pool"):
        P.op(eng, lambda e: e.memset(oap, v), writes=[o])

    def dma(o, oap, i, iap, q="sp", nonc=False):
        if nonc:
            return P.dma(lambda e: e.dma_start(out=oap, in_=iap, allow_slow_non_contiguous=True), reads=[i], writes=[o], q=q)
        return P.dma(lambda e: e.dma_start(out=oap, in_=iap), reads=[i], writes=[o], q=q)

    def recip(o, oap, i, iap):
        P.op("dve", lambda e: e.reciprocal(out=oap, in_=iap), reads=[i], writes=[o])

    def red(o, oap, i, iap, op):
        P.op("dve", lambda e: e.tensor_reduce(out=oap, in_=iap, axis=AX.X, op=op), reads=[i], writes=[o])

    def bc_last(ap, n):
        return ap.unsqueeze(2).to_broadcast([ap.shape[0], ap.shape[1], n])

    def bc_mid(ap, n):
        return ap.unsqueeze(1).to_broadcast([ap.shape[0], n, ap.shape[1]])

    mset(identf, identf[:], 1.0)
    P.op("pool", lambda e: e.affine_select(out=identf[:], in_=identf[:], pattern=[[-1, 128]], compare_op=ALU.is_equal,
                                            fill=0.0, base=0, channel_multiplier=1), reads=[identf], writes=[identf])
    cp(identb, identb[:], identf, identf[:])
    mset(onesf, onesf[:], 1.0); mset(onesb, onesb[:], 1.0)
    mset(trif, trif[:], 1.0)
    P.op("pool", lambda e: e.affine_select(out=trif[:], in_=trif[:], pattern=[[1, 128]], compare_op=ALU.is_ge,
                                            fill=0.0, base=0, channel_multiplier=-1), reads=[trif], writes=[trif])
    mset(cadd, cadd[:], 0.0)
    P.op("pool", lambda e: e.affine_select(out=cadd[:], in_=cadd[:], pattern=[[-1, 128]], compare_op=ALU.is_ge,
                                            fill=NEG, base=0, channel_multiplier=1), reads=[cadd], writes=[cadd])
    mset(sadd, sadd[:], NEG); mset(sadd, sadd[:, 0:1], 0.0)
    mset(elast, elast[:], 0.0); mset(elast, elast[127:128, :], 1.0) if False else None
    mset(elast, elast[:], 1.0)
    P.op("pool", lambda e: e.affine_select(out=elast[:], in_=elast[:], pattern=[[0, 128]], compare_op=ALU.is_equal,
                                            fill=0.0, base=-127, channel_multiplier=1), reads=[elast], writes=[elast])
    mset(epsT, epsT[:], EPS); mset(oneT, oneT[:], 1.0)
    mset(tokm1, tokm1[:], 1.0)
    mset(tokms, tokms[:], 1.0)
    P.op("pool", lambda e: e.affine_select(out=tokms[:], in_=tokms[:], pattern=[[0, 1]], compare_op=ALU.is_equal,
                                            fill=0.0, base=0, channel_multiplier=1), reads=[tokms], writes=[tokms])
    mset(vbuf, vbuf[:], 1.0)
    for g in range(2):
        mset(CTm[g], CTm[g][:], 0.0); mset(qTm[g], qTm[g][:], 0.0)
    P.op("pool", lambda e: e.iota(iop[:], pattern=[[0, 1]], base=0, channel_multiplier=1), writes=[iop])

    pc = [0]
    for nm in ("w_in", "w_mem_kv", "w_branch", "w_out", "w_ffn_up", "w_ffn_down"):
        src = I[nm]; dst = WB[nm]
        shp = src.t.shape
        R, C = shp[1], shp[2]
        for l in range(DEPTH):
            for r0 in range(0, R, 128):
                for c0 in range(0, C, 1024):
                    cw = min(1024, C - c0)
                    pi = pc[0] % 2
                    qn_ = "sp" if pi == 0 else "act"
                    dma(stg[pi], stg[pi][:, 0:cw], src, src[l, r0:r0 + 128, c0:c0 + cw], q=qn_)
                    cp(stgb[pi], stgb[pi][:, 0:cw], stg[pi], stg[pi][:, 0:cw], eng=("dve" if pi == 0 else "act"))
                    dma(dst, dst[l, r0:r0 + 128, c0:c0 + cw], stgb[pi], stgb[pi][:, 0:cw], q=qn_)
                    pc[0] += 1

    wrot = [0]

    def wload(nm, l, r0, G, c0, ncol, pp=128):
        s = wsl[wrot[0] % 3]; wrot[0] += 1
        w = WB[nm]
        dma(s, s[0:pp, 0:G, 0:ncol], w, w[l, r0:r0 + G * pp, c0:c0 + ncol].rearrange("(g p) c -> p g c", p=pp))
        return s

    def rstd_of(ss_b, ss_ap, n, out_b, out_ap):
        act(out_b, out_ap, ss_b, ss_ap, AF.Sqrt, bias=epsT[:, 0:1], scale=1.0 / n, extra=[epsT])
        recip(out_b, out_ap, out_b, out_ap)

    def rms_full(xb, xap, gb, ob, oap, n):
        act(tm1b_junk, tm1b_junk[:, 0:n], xb, xap, AF.Square, accum=(sm[0], sm[0][:, 0:1]))
        rstd_of(sm[0], sm[0][:, 0:1], n, sm[0], sm[0][:, 1:2])
        stt(ob, oap, xb, xap, sm[0][:, 1:2], gb, gb[:, 0:n], ALU.mult, ALU.mult, extra=[sm[0]])

    tm1b_junk = sb("sqjunk", [128, D], BF16)

    def rms_heads(xb, xap, H, hd, gb, ob, oap):
        n = H * hd
        tt(tm1b_junk, tm1b_junk[:, 0:n], xb, xap, xb, xap, ALU.mult)
        red(sm[1], sm[1][:, 0:H], tm1b_junk, tm1b_junk[:, 0:n].rearrange("p (h d) -> p h d", h=H), ALU.add)
        rstd_of(sm[1], sm[1][:, 0:H], hd, sm[1], sm[1][:, 8:8 + H])
        xv = xap.rearrange("p (h d) -> p h d", h=H); ov = oap.rearrange("p (h d) -> p h d", h=H)
        tt(ob, ov, xb, xv, sm[1], bc_last(sm[1][:, 8:8 + H], hd), ALU.mult)
        tt(ob, ov, ob, ov, gb, bc_mid(gb[:, 0:hd], H), ALU.mult)

    def rope(xb, xap, H, hd):
        xv = xap.rearrange("p (h d) -> p h d", h=H)
        x1 = xv[:, :, 0:8]; x2 = xv[:, :, 8:16]
        cs = bc_mid(ropet[:, 0:8], H); sn = bc_mid(ropet[:, 8:16], H)
        t1 = tm1[:, 0:H * 8].rearrange("p (h d) -> p h d", h=H); t2 = tm1[:, 64:64 + H * 8].rearrange("p (h d) -> p h d", h=H)
        t3 = tm1[:, 128:128 + H * 8].rearrange("p (h d) -> p h d", h=H); t4 = tm1[:, 192:192 + H * 8].rearrange("p (h d) -> p h d", h=H)
        tt(tm1, t1, xb, x1, ropet, cs, ALU.mult); tt(tm1, t2, xb, x2, ropet, sn, ALU.mult)
        tt(tm1, t3, xb, x2, ropet, cs, ALU.mult); tt(tm1, t4, xb, x1, ropet, sn, ALU.mult)
        tt(xb, x1, tm1, t1, tm1, t2, ALU.subtract); tt(xb, x2, tm1, t3, tm1, t4, ALU.add)

    def to_hT(src_b, src_ap, dstT, col0):
        for half in range(2):
            for j in range(4):
                kgi = half * 4 + j
                tr(psT, psT[:, j * 128:(j + 1) * 128], src_b, src_ap[:, kgi * 128:(kgi + 1) * 128], identb)
            cp(dstT, dstT[:, half * 4:half * 4 + 4, col0:col0 + 128],
               psT, psT[:, 0:512].rearrange("p (g t) -> p g t", g=4), eng="act")

    def tm2fm(dst_b, dst_fn, src_b, src_ap, G, W):
        for g in range(G):
            dma(dst_b, dst_fn(g), src_b, src_ap[:, g * 128:(g + 1) * 128].rearrange("w c -> c w"), nonc=True)

    def fm2tm(dst_b, dst_ap, src_b, src_fn, G, W):
        toks = []
        for g in range(G):
            toks.append(dma(dst_b, dst_ap[:, g * 128:(g + 1) * 128].rearrange("w c -> c w"), src_b, src_fn(g), nonc=True))
        return toks

    def bcast_row(dst_b, n, src_b, row_ap):
        dma(dst_b, dst_b[:, 0:n], src_b, row_ap.to_broadcast([128, n]))

    def load_params(l):
        bcast_row(gmix, D, I["norm_mix_g"], I["norm_mix_g"][l:l + 1, :])
        bcast_row(gffn, D, I["norm_ffn_g"], I["norm_ffn_g"][l:l + 1, :])
        bcast_row(gmem, D, I["mem_norm_g"], I["mem_norm_g"][l:l + 1, :])
        bcast_row(lnag, 512, I["ln_a_g"], I["ln_a_g"][l:l + 1, :]); bcast_row(lnab, 512, I["ln_a_b"], I["ln_a_b"][l:l + 1, :])
        bcast_row(ssmg, 512, I["ssm_norm_g"], I["ssm_norm_g"][l:l + 1, :])
        bcast_row(qg, 64, I["q_norm_g"], I["q_norm_g"][l:l + 1, :]); bcast_row(kg, 64, I["k_norm_g"], I["k_norm_g"][l:l + 1, :])
        bcast_row(mqg, 128, I["mq_norm_g"], I["mq_norm_g"][l:l + 1, :]); bcast_row(mkg, 128, I["mk_norm_g"], I["mk_norm_g"][l:l + 1, :])
        bcast_row(dtb, 8, I["dt_bias"], I["dt_bias"][l:l + 1, :]); bcast_row(dsk, 8, I["d_skip"], I["d_skip"][l:l + 1, :])
        bcast_row(aneg, 8, I["a_log"], I["a_log"][l:l + 1, :])
        act(aneg, aneg[:], aneg, aneg[:], AF.Exp)
        ts(aneg, aneg[:], aneg, aneg[:], -1.0, ALU.mult)
        tm2fm(cwa, lambda g: cwa[:, g, :], I["conv_a_w"], I["conv_a_w"][l], 4, 31)
        tm2fm(cws, lambda g: cws[:, g, :], I["ssm_conv_w"], I["ssm_conv_w"][l], 6, 4)
        tm2fm(cwf, lambda g: cwf[:, g, :], I["ffn_conv_w"], I["ffn_conv_w"][l], 44, 3)
        dma(cba, cba[:], I["conv_a_b"], I["conv_a_b"][l].rearrange("(g c) -> c g", c=128), nonc=True)
        dma(cbs, cbs[:], I["ssm_conv_b"], I["ssm_conv_b"][l].rearrange("(g c) -> c g", c=128), nonc=True)
        dma(cbf, cbf[:], I["ffn_conv_b"], I["ffn_conv_b"][l].rearrange("(g c) -> c g", c=128), nonc=True)

    def dwconv(ub, cb, G, W, wb, bb, Tn):
        for g in range(G):
            ts(cb[g], cb[g][:, 0:Tn], ub, ub[:, g, 0:Tn], wb[:, g, 0:1], ALU.mult, bb[:, g:g + 1], ALU.add, extra=[wb, bb])
        for j in range(1, W):
            for g in range(G):
                stt(cb[g], cb[g][:, 0:Tn], ub, ub[:, g, j:j + Tn], wb[:, g, j:j + 1], cb[g], cb[g][:, 0:Tn], ALU.mult, ALU.add, extra=[wb])

    def mem_kv_prompt(l):
        for mt in range(2):
            dma(stio, stio[:, 0:512], I["memp"], I["memp"][mt * 128:(mt + 1) * 128, 0:512])
            dma(tm2, tm2[:], I["memp"], I["memp"][mt * 128:(mt + 1) * 128, 512:1024])
            act(tm1b_junk, tm1b_junk[:, 0:512], stio, stio[:, 0:512], AF.Square, accum=(sm[2], sm[2][:, 0:1]))
            act(tm1b_junk, tm1b_junk[:, 512:1024], tm2, tm2[:], AF.Square, accum=(sm[2], sm[2][:, 1:2]))
            tt(sm[2], sm[2][:, 2:3], sm[2], sm[2][:, 0:1], sm[2], sm[2][:, 1:2], ALU.add)
            rstd_of(sm[2], sm[2][:, 2:3], D, sm[2], sm[2][:, 3:4])
            stt(hb, hb[:, 0:512], stio, stio[:, 0:512], sm[2][:, 3:4], gmem, gmem[:, 0:512], ALU.mult, ALU.mult, extra=[sm[2]])
            stt(hb, hb[:, 512:1024], tm2, tm2[:], sm[2][:, 3:4], gmem, gmem[:, 512:1024], ALU.mult, ALU.mult, extra=[sm[2]])
            to_hT(hb, hb, hT, 0)
            for c in range(2):
                w = wload("w_mem_kv", l, 0, 8, c * 512, 512)
                p = nps()
                for k in range(8):
                    mm(p, p[:], hT, hT[:, k, 0:128], w, w[:, k, 0:512], start=(k == 0), stop=(k == 7))
                if c == 0:
                    cp(tm1, tm1[:], p, p[:], eng="act")
                    rms_heads(tm1, tm1[:], 4, 128, mkg, stio, stio[:, 0:512])
                    outtoks.append(dma(O["mk_p"], O["mk_p"][l, mt * 128:(mt + 1) * 128, :], stio, stio[:, 0:512]))
                    cp(tmb, tmb[:], stio, stio[:, 0:512])
                    for h in range(4):
                        tr(psT, psT[:, h * 128:(h + 1) * 128], tmb, tmb[:, h * 128:(h + 1) * 128], identb)
                    cp(mkT, mkT[:, :, mt * 128:(mt + 1) * 128], psT, psT[:, 0:512].rearrange("p (h t) -> p h t", h=4), eng="act")
                else:
                    cp(tm1, tm1[:], p, p[:], eng="act")
                    outtoks.append(dma(O["mv_p"], O["mv_p"][l, mt * 128:(mt + 1) * 128, :], tm1, tm1[:]))
                    cp(mvb, mvb[:, mt, :], tm1, tm1[:])

    def mem_kv_sample(l, j):
        for mt in range(2):
            dma(tm1, tm1[:], I["cmk"], I["cmk"][l, j, mt * 128:(mt + 1) * 128, :])
            dma(tm2, tm2[:], I["cmv"], I["cmv"][l, j, mt * 128:(mt + 1) * 128, :])
            cp(tmb, tmb[:], tm1, tm1[:])
            for h in range(4):
                tr(psT, psT[:, h * 128:(h + 1) * 128], tmb, tmb[:, h * 128:(h + 1) * 128], identb)
            cp(mkT, mkT[:, :, mt * 128:(mt + 1) * 128], psT, psT[:, 0:512].rearrange("p (h t) -> p h t", h=4), eng="act")
            cp(mvb, mvb[:, mt, :], tm2, tm2[:])

    class StopM(Exception):
        pass
    mstop = 99.0

    def chk(n):
        if mstop <= n:
            raise StopM()

    def macro(l, ctx):
        kind = ctx["kind"]; nsub = ctx["nsub"]; Tn = nsub * 128; tv = ctx["tv"]; pos0 = ctx["pos0"]
        samp = (kind == "s"); j = ctx.get("j", 0)
        tokm = tokms if samp else tokm1
        last_layer = (l == DEPTH - 1)
        for s in range(nsub):
            if samp:
                mset(x_mt, x_mt[:, s, :], 0.0, eng="dve")
                src = I["xs"] if l == 0 else xsres
                dma(x_mt, x_mt[0:1, s, :], src, src[j:j + 1, :])
            else:
                src = I["xp"] if l == 0 else xres
                dma(x_mt, x_mt[:, s, :], src, src[pos0 + s * 128: pos0 + (s + 1) * 128, :])
            rms_full(x_mt, x_mt[:, s, :], gmix, hb, hb[:], D)
            to_hT(hb, hb, hT, s * 128)
        def tm_seg(c_lo, c_hi, dsts, off0):
            c = c_lo
            while c < c_hi:
                n = min(512, c_hi - c)
                w = wload("w_in", l, 0, 8, c, n)
                for s in range(nsub):
                    p = nps()
                    for k in range(8):
                        mm(p, p[:, 0:n], hT, hT[:, k, s * 128:(s + 1) * 128], w, w[:, k, 0:n], start=(k == 0), stop=(k == 7))
                    cp(dsts[s], dsts[s][:, off0 + c - c_lo: off0 + c - c_lo + n], p, p[:, 0:n], eng="act")
                c += n
        chk(1)
        tm_seg(C_Q, C_XBC, projA, 0)
        tm_seg(C_DT, C_G, projB, 0)
        chk(2)
        if ctx["first"]:
            if samp:
                tm2fm(glu_u, lambda g: glu_u[:, g, 0:30], I["sconf"], I["sconf"][l, j], 4, 30)
                tm2fm(xbc_u, lambda g: xbc_u[:, g, 0:3], I["ssc"], I["ssc"][l, j], 6, 3)
                tm2fm(fcar, lambda g: fcar[:, g, :], I["sffn"], I["sffn"][l, j], 44, 2)
            else:
                mset(glu_u, glu_u[:, :, 0:30], 0.0, eng="dve"); mset(xbc_u, xbc_u[:, :, 0:3], 0.0, eng="dve")
                mset(fcar, fcar[:], 0.0, eng="dve")
        wv = wload("w_in", l, 0, 8, 0, 512); wg = wload("w_in", l, 0, 8, 512, 512)
        for c in range(4):
            pv = nps(); pg = nps()
            for k in range(8):
                mm(pv, pv[:, 0:Tn], wv, wv[:, k, c * 128:(c + 1) * 128], hT, hT[:, k, 0:Tn], start=(k == 0), stop=(k == 7))
            for k in range(8):
                mm(pg, pg[:, 0:Tn], wg, wg[:, k, c * 128:(c + 1) * 128], hT, hT[:, k, 0:Tn], start=(k == 0), stop=(k == 7))
            act(sgt, sgt[:, 0:Tn], pg, pg[:, 0:Tn], AF.Sigmoid)
            tt(glu_u, glu_u[:, c, 30:30 + Tn], pv, pv[:, 0:Tn], sgt, sgt[:, 0:Tn], ALU.mult)
        for (c0, ng) in ((0, 4), (4, 2)):
            w = wload("w_in", l, 0, 8, C_XBC + c0 * 128, ng * 128)
            for c in range(ng):
                p = nps()
                for k in range(8):
                    mm(p, p[:, 0:Tn], w, w[:, k, c * 128:(c + 1) * 128], hT, hT[:, k, 0:Tn], start=(k == 0), stop=(k == 7))
                cp(xbc_u, xbc_u[:, c0 + c, 3:3 + Tn], p, p[:, 0:Tn], eng="act")
        chk(3)
        if ctx["last"]:
            if samp:
                outtoks.extend(fm2tm(O["conf_s"], O["conf_s"][l, j], glu_u, lambda g: glu_u[:, g, tv:tv + 30], 4, 30))
                outtoks.extend(fm2tm(O["sc_s"], O["sc_s"][l, j], xbc_u, lambda g: xbc_u[:, g, tv:tv + 3], 6, 3))
            else:
                outtoks.extend(fm2tm(O["conf_p"], O["conf_p"][l], glu_u, lambda g: glu_u[:, g, tv:tv + 30], 4, 30))
                outtoks.extend(fm2tm(O["sc_p"], O["sc_p"][l], xbc_u, lambda g: xbc_u[:, g, tv:tv + 3], 6, 3))
        chk(4)
        dwconv(glu_u, glu_cg, 4, 31, cwa, cba, Tn)
        dwconv(xbc_u, xbc_cg, 6, 4, cws, cbs, Tn)
        act(xbc_c, xbc_c[:, :, 0:Tn], xbc_c, xbc_c[:, :, 0:Tn], AF.Silu)
        if not ctx["last"]:
            cp(glu_u, glu_u[:, :, 0:30], glu_u, glu_u[:, :, Tn:Tn + 30])
            cp(xbc_u, xbc_u[:, :, 0:3], xbc_u, xbc_u[:, :, Tn:Tn + 3])

        for s in range(nsub):
            cs = slice(s * 128, (s + 1) * 128)
            pA, pB = projA[s], projB[s]
            chk(5)
            p = nps()
            for c in range(4):
                tr(p, p[:, c * 128:(c + 1) * 128], glu_c, glu_c[:, c, cs], identf)
            P.op("dve", lambda e, p=p: e.bn_stats(out=sm[3][:, 0:6], in_=p[:, 0:512]), reads=[p], writes=[sm[3]])
            P.op("dve", lambda e: e.bn_aggr(out=sm[3][:, 6:8], in_=sm[3][:, 0:6]), reads=[sm[3]], writes=[sm[3]])
            act(sm[3], sm[3][:, 8:9], sm[3], sm[3][:, 7:8], AF.Sqrt, bias=epsT[:, 0:1], scale=1.0, extra=[epsT])
            recip(sm[3], sm[3][:, 8:9], sm[3], sm[3][:, 8:9])
            ts(tm1, tm1[:], p, p[:], sm[3][:, 6:7], ALU.subtract, sm[3][:, 8:9], ALU.mult, extra=[sm[3]])
            tt(tm1, tm1[:], tm1, tm1[:], lnag, lnag[:], ALU.mult)
            tt(tm1, tm1[:], tm1, tm1[:], lnab, lnab[:], ALU.add)
            act(tmb, tmb[:], tm1, tm1[:], AF.Silu)
            for c in range(4):
                tr(psT, psT[:, c * 128:(c + 1) * 128], tmb, tmb[:, c * 128:(c + 1) * 128], identb)
            cp(brT[0], brT[0][:, :, cs], psT, psT[:, 0:512].rearrange("p (g t) -> p g t", g=4), eng="act")

            chk(6)
            p = nps()
            for c in range(4):
                tr(p, p[:, c * 128:(c + 1) * 128], xbc_c, xbc_c[:, c, cs], identf)
            cp(xs_tm, xs_tm[:], p, p[:], eng="act")
            cp(BT, BT[:], xbc_c, xbc_c[:, 4, cs]); cp(CT, CT[:], xbc_c, xbc_c[:, 5, cs])
            for g in range(2):
                cp(CTm[g], CTm[g][g * 64:(g + 1) * 64, :], xbc_c, xbc_c[g * 64:(g + 1) * 64, 5, cs])
            tr(psT, psT[:, 0:128], BT, BT[:], identb)
            cp(B_tm, B_tm[:], psT, psT[:, 0:128], eng="act")
            d0 = sm[4]
            tt(d0, d0[:, 0:8], pB, pB[:, 0:8], dtb, dtb[:], ALU.add)
            ts(d0, d0[:, 8:16], d0, d0[:, 0:8], -1.0, ALU.mult)
            tt(d0, d0[:, 8:16], d0, d0[:, 8:16], d0, d0[:, 0:8], ALU.max)
            act(d0, d0[:, 8:16], d0, d0[:, 8:16], AF.Exp, scale=-1.0)
            act(d0, d0[:, 8:16], d0, d0[:, 8:16], AF.Ln, bias=oneT[:, 0:1], scale=1.0, extra=[oneT])
            stt(d0, d0[:, 0:8], d0, d0[:, 0:8], 0.0, d0, d0[:, 8:16], ALU.max, ALU.add)
            ts(d0, d0[:, 0:8], d0, d0[:, 0:8], tokm[:, 0:1], ALU.mult, extra=[tokm])
            tt(d0, d0[:, 8:16], d0, d0[:, 0:8], aneg, aneg[:], ALU.mult)
            a1 = sm[5]
            pa = nps()
            mm(pa, pa[:, 0:8], trif, trif[:], d0, d0[:, 8:16])
            cp(a1, a1[:, 0:8], pa, pa[:, 0:8])
            pa = nps()
            mm(pa, pa[:, 0:8], elast, elast[:], a1, a1[:, 0:8])
            cp(a1, a1[:, 8:16], pa, pa[:, 0:8])
            e1 = sm[6]
            act(e1, e1[:, 0:8], a1, a1[:, 0:8], AF.Exp)
            act(e1, e1[:, 8:16], a1, a1[:, 8:16], AF.Exp)
            tt(sm[7], sm[7][:, 0:8], a1, a1[:, 8:16], a1, a1[:, 0:8], ALU.subtract)
            act(sm[7], sm[7][:, 0:8], sm[7], sm[7][:, 0:8], AF.Exp)
            tt(sm[7], sm[7][:, 8:16], sm[7], sm[7][:, 0:8], d0, d0[:, 0:8], ALU.mult)
            xv = xs_tm[:].rearrange("p (h d) -> p h d", h=8)
            tt(xdt, xdt[:].rearrange("p (h d) -> p h d", h=8), xs_tm, xv, d0, bc_last(d0[:, 0:8], 64), ALU.mult)
            tt(xdtd, xdtd[:].rearrange("p (h d) -> p h d", h=8), xs_tm, xv, sm[7], bc_last(sm[7][:, 8:16], 64), ALU.mult)
            chk(6.1)
            pg_ = nps()
            for g in range(2):
                mm(pg_, pg_[:, g * 128:(g + 1) * 128], BT, BT[:], CTm[g], CTm[g][:])
            cp(GT, GT[:], pg_, pg_[:, 0:256].rearrange("p (g t) -> p g t", g=2), eng="act")
            tt(diag, diag[:], identf, bc_mid(identf[:], 8), a1, bc_last(a1[:, 0:8], 128), ALU.mult)
            for h in range(8):
                mm(psW, psW[:, h * 128:(h + 1) * 128], onesf, onesf[:], diag, diag[:, h, :])
            tt(seg, seg[:], psW, psW[:].rearrange("p (h t) -> p h t", h=8), a1, bc_last(a1[:, 0:8], 128), ALU.subtract)
            ts(seg, seg[:], seg, seg[:], 0.0, ALU.min)
            act(seg, seg[:], seg, seg[:], AF.Exp)
            tt(seg, seg[:].rearrange("p (g h) t -> p g h t", g=2), seg, seg[:].rearrange("p (g h) t -> p g h t", g=2),
               GT, GT[:].unsqueeze(2).to_broadcast([128, 2, 4, 128]), ALU.mult)
            tt(scT, scT[:], seg, seg[:], trif, bc_mid(trif[:], 8), ALU.mult)
            chk(6.2)
            py = nps()
            for h in range(8):
                mm(py, py[:, h * 64:(h + 1) * 64], scT, scT[:, h, :], xdt, xdt[:, h * 64:(h + 1) * 64])
            cp(ybuf, ybuf[:], py, py[:], eng="act")
            if ctx["first"] and s == 0:
                if samp:
                    for g in range(2):
                        dma(hio, hio[:].rearrange("p (hh g n) -> p hh g n", hh=4, g=2)[:, :, g, :], I["sssm"],
                            I["sssm"][l, j, g * 4:(g + 1) * 4].rearrange("hh p n -> p hh n"))
                    for hh in range(4):
                        pq = nps()
                        tr(pq, pq[:, 0:64], hio, hio[:, hh * 128:(hh + 1) * 128], identf)
                        cp(hst, hst[:, hh, :], pq, pq[:, 0:64])
                else:
                    mset(hst, hst[:], 0.0, eng="dve")
                cp(hstb, hstb[:], hst, hst[:])
            po = nps()
            for h in range(8):
                g = h // 4
                mm(po, po[:, h * 64:(h + 1) * 64], CTm[g], CTm[g][:], hstb, hstb[:, h % 4, :])
            tt(tm1, tm1[:].rearrange("p (h d) -> p h d", h=8), po, po[:].rearrange("p (h d) -> p h d", h=8),
               e1, bc_last(e1[:, 0:8], 64), ALU.mult)
            tt(ybuf, ybuf[:], ybuf, ybuf[:], tm1, tm1[:], ALU.add)
            tt(tm1, tm1[:].rearrange("p (h d) -> p h d", h=8), xs_tm, xv, dsk, bc_last(dsk[:], 64), ALU.mult)
            tt(ybuf, ybuf[:], ybuf, ybuf[:], tm1, tm1[:], ALU.add)
            chk(6.3)
            pst = nps()
            for h in range(8):
                mm(pst, pst[:, h * 64:(h + 1) * 64], B_tm, B_tm[:], xdtd, xdtd[:, h * 64:(h + 1) * 64])
            for g in range(2):
                r_ = slice(g * 64, (g + 1) * 64)
                tt(hst, hst[r_, :, :], hst, hst[r_, :, :], e1, bc_last(e1[r_, 8 + g * 4: 12 + g * 4], 64), ALU.mult)
                tt(hst, hst[r_, :, :], hst, hst[r_, :, :], pst,
                   pst[r_, g * 256:(g + 1) * 256].rearrange("p (h d) -> p h d", h=4), ALU.add)
            cp(hstb, hstb[:], hst, hst[:])
            if ctx["last"] and s == nsub - 1:
                for hh in range(4):
                    pq = nps()
                    tr(pq, pq[0:64, 0:128], hst, hst[:, hh, :], identf)
                    cp(hio, hio[:, hh * 128:(hh + 1) * 128], pq, pq[0:64, 0:128])
                od = O["ssm_s"][l, j] if samp else O["ssm_p"][l]
                for g in range(2):
                    outtoks.append(dma(O["ssm_s"] if samp else O["ssm_p"], od[g * 4:(g + 1) * 4].rearrange("hh p n -> p hh n"),
                                       hio, hio[:].rearrange("p (hh g n) -> p hh g n", hh=4, g=2)[:, :, g, :]))
            chk(6.4)
            act(tm1, tm1[:], pA, pA[:, C_Z - C_Q: C_Z - C_Q + 512], AF.Silu)
            tt(ybuf, ybuf[:], ybuf, ybuf[:], tm1, tm1[:], ALU.mult)
            rms_full(ybuf, ybuf[:], ssmg, tmb, tmb[:], 512)
            for c in range(4):
                tr(psT, psT[:, c * 128:(c + 1) * 128], tmb, tmb[:, c * 128:(c + 1) * 128], identb)
            cp(brT[1], brT[1][:, :, cs], psT, psT[:, 0:512].rearrange("p (g t) -> p g t", g=4), eng="act")

            chk(7)
            rms_heads(pB, pB[:, 8:520], 4, 128, mqg, tm1, tm1[:])
            cp(tmb, tmb[:], tm1, tm1[:])
            for h in range(4):
                tr(psT, psT[:, h * 128:(h + 1) * 128], tmb, tmb[:, h * 128:(h + 1) * 128], identb)
            cp(mqT, mqT[:], psT, psT[:, 0:512].rearrange("p (h t) -> p h t", h=4), eng="act")
            for mt in range(2):
                for h in range(4):
                    mm(psW, psW[:, mt * 512 + h * 128: mt * 512 + (h + 1) * 128], mkT, mkT[:, h, mt * 128:(mt + 1) * 128], mqT, mqT[:, h, :])
            act(PT, PT[:].rearrange("p a b -> p (a b)"), psW, psW[:], AF.Exp, scale=128 ** -0.5)
            pO = nps(); pD = nps()
            for h in range(4):
                for mt in range(2):
                    mm(pO, pO[:, h * 128:(h + 1) * 128], mvb, mvb[:, mt, h * 128:(h + 1) * 128], PT, PT[:, mt, h * 128:(h + 1) * 128],
                       start=(mt == 0), stop=(mt == 1))
            for mt in range(2):
                mm(pD, pD[:], onesb, onesb[:], PT, PT[:, mt, :], start=(mt == 0), stop=(mt == 1))
            recip(rden, rden[:], pD, pD[:])
            tt(brT[2], brT[2][:, :, cs], pO, pO[:].rearrange("p (h t) -> p h t", h=4), rden, rden[:].rearrange("p (h t) -> p h t", h=4), ALU.mult)

            chk(8)
            if samp:
                bcast_row(ropet, 16, I["ropes"], I["ropes"][0:1, :])
            else:
                dma(ropet, ropet[:], I["ropep"], I["ropep"][pos0 + s * 128: pos0 + (s + 1) * 128, :])
            rms_heads(pA, pA[:, 0:512], 8, 64, qg, qn, qn[:])
            rope(qn, qn[:], 8, 64)
            chk(8.05)
            for kv in range(2):
                cp(qbf, qbf[:].rearrange("p (g kv d) -> p g kv d", g=4, kv=2)[:, :, kv, :],
                   qn, qn[:].rearrange("p (kv g d) -> p kv g d", kv=2, g=4)[:, kv, :, :])
            for g in range(4):
                tr(psT, psT[:, g * 128:(g + 1) * 128], qbf, qbf[:, g * 128:(g + 1) * 128], identb)
            cp(qT, qT[:], psT, psT[:, 0:512].rearrange("p (g t) -> p g t", g=4), eng="act")
            chk(8.07)
            for kv in range(2):
                cp(qTm[kv], qTm[kv][kv * 64:(kv + 1) * 64, :, :], qT, qT[kv * 64:(kv + 1) * 64, :, :])
            chk(8.1)
            rms_heads(pA, pA[:, C_K - C_Q: C_K - C_Q + 128], 2, 64, kg, kout, kout[:])
            rope(kout, kout[:], 2, 64)
            if samp:
                outtoks.append(dma(O["k_s"], O["k_s"][l, j:j + 1, :], kout, kout[0:1, :]))
                outtoks.append(dma(O["v_s"], O["v_s"][l, j:j + 1, :], pA, pA[0:1, C_V - C_Q: C_V - C_Q + 128]))
            else:
                outtoks.append(dma(O["k_p"], O["k_p"][l, pos0 + s * 128: pos0 + (s + 1) * 128, :], kout, kout[:]))
                outtoks.append(dma(O["v_p"], O["v_p"][l, pos0 + s * 128: pos0 + (s + 1) * 128, :], pA, pA[:, C_V - C_Q: C_V - C_Q + 128]))
            cp(kbf, kbf[:], kout, kout[:])
            tr(psT, psT[:, 0:128], kbf, kbf[:], identb)
            cp(kTs, kTs[:], psT, psT[:, 0:128], eng="act")
            hp = (PAST if samp else pos0 + s * 128)
            dma(h_kT, h_kT[:, hp:hp + 128], kTs, kTs[:])
            cp(vbuf, vbuf[:, :, 0:64], pA, pA[:, C_V - C_Q: C_V - C_Q + 128].rearrange("p (kv d) -> p kv d", kv=2))
            dma(h_v, h_v[hp:hp + 128, :], vbuf, vbuf[:].rearrange("p a b -> p (a b)"))
            chk(8.2)
            cp(kiout, kiout[:], pA, pA[:, C_KI - C_Q: C_KI - C_Q + 64])
            rope(kiout, kiout[:], 1, 64)
            if samp:
                outtoks.append(dma(O["ki_s"], O["ki_s"][l, j:j + 1, :], kiout, kiout[0:1, :]))
            else:
                outtoks.append(dma(O["ki_p"], O["ki_p"][l, pos0 + s * 128: pos0 + (s + 1) * 128, :], kiout, kiout[:]))
            cp(kibf, kibf[:], kiout, kiout[:])
            tr(psT, psT[0:64, 0:128], kibf, kibf[:], identb)
            cp(kiTs, kiTs[:], psT, psT[0:64, 0:128], eng="act")
            dma(h_kiT, h_kiT[:, hp:hp + 128], kiTs, kiTs[:])
            chk(8.3)
            cp(qn, qn[:], pA, pA[:, C_QI - C_Q: C_QI - C_Q + 512])
            rope(qn, qn[:], 8, 64)
            cp(qbf, qbf[:], qn, qn[:])
            for half in range(2):
                for h4 in range(4):
                    h = half * 4 + h4
                    tr(psT, psT[0:64, h4 * 128:(h4 + 1) * 128], qbf, qbf[:, h * 64:(h + 1) * 64], identb)
                cp(qiT, qiT[:, half * 4:half * 4 + 4, :], psT, psT[0:64, 0:512].rearrange("p (h t) -> p h t", h=4), eng="act")
            wi_ap = pA[:, C_WI - C_Q: C_WI - C_Q + 8]
            ts(awi, awi[:], pA, wi_ap, -1.0, ALU.mult)
            tt(awi, awi[:], awi, awi[:], pA, wi_ap, ALU.max)
            act(swi, swi[:], pA, wi_ap, AF.Sign)
            ts(swi, swi[:], swi, swi[:], IDX_SCALE, ALU.mult)
            chk(9)
            nkt = hp // 128 + 1
            nkeys = nkt * 128
            for c0 in range(0, nkeys, 1024):
                n = min(1024, nkeys - c0)
                kb = kic[rot["kic"] % 2]; rot["kic"] += 1
                dma(kb, kb[:, 0:n], h_kiT, h_kiT[:, c0:c0 + n])
                for h in range(8):
                    S = SS[rot["rb"] % 2]
                    for b0 in range(0, n, 512):
                        bn = min(512, n - b0)
                        mm(S, S[:, b0:b0 + bn], qiT, qiT[:, h, :], kb, kb[:, b0:b0 + bn])
                    rb = rbuf[rot["rb"] % 2]; rot["rb"] += 1
                    act(rb, rb[:, 0:n], S, S[:, 0:n], AF.Relu, scale=awi[:, h:h + 1], extra=[awi])
                    if h == 0:
                        ts(Isc, Isc[:, c0:c0 + n], rb, rb[:, 0:n], swi[:, 0:1], ALU.mult, extra=[swi])
                    else:
                        stt(Isc, Isc[:, c0:c0 + n], rb, rb[:, 0:n], swi[:, h:h + 1], Isc, Isc[:, c0:c0 + n], ALU.mult, ALU.add, extra=[swi])
            am = sadd if samp else cadd
            tt(Isc, Isc[:, nkeys - 128:nkeys], Isc, Isc[:, nkeys - 128:nkeys], am, am[:], ALU.add)
            chk(10)
            KSEL = cfg.KS if samp else cfg.KP
            split = (nkt >= SPLIT_MIN_NKT)
            n1 = int(round(nkt * 0.42)) * 128 if split else nkeys
            n2 = nkeys - n1
            if nkeys - 128 >= KSEL:
                red(lo, lo[:], Isc, Isc[:, 0:nkeys - 128], ALU.min)
                red(hi, hi[:], Isc, Isc[:, 0:nkeys], ALU.max)
                thr = float(KSEL) - 0.5 * n2
                for it in range(NITER):
                    ts(mid, mid[:], lo, lo[:], hi[:, 0:1], ALU.add, 0.5, ALU.mult, extra=[hi])
                    if split:
                        ts(nmid, nmid[:], mid, mid[:], -1.0, ALU.mult)
                        act(junkA, junkA[:, 0:n2], Isc, Isc[:, n1:nkeys], AF.Sign, bias=nmid[:, 0:1], scale=1.0,
                            accum=(cnt2, cnt2[:]), extra=[nmid])
                    ts(junkD, junkD[:, 0:n1], Isc, Isc[:, 0:n1], mid[:, 0:1], ALU.is_ge, 0.0, ALU.add,
                       accum=(cnt, cnt[:]), extra=[mid])
                    if split:
                        stt(cnt, cnt[:], cnt2, cnt2[:], 0.5, cnt, cnt[:], ALU.mult, ALU.add)
                    ts(pge, pge[:], cnt, cnt[:], thr, ALU.is_ge)
                    ts(plt, plt[:], cnt, cnt[:], thr, ALU.is_lt)
                    P.op("dve", lambda e: e.copy_predicated(out=lo[:], mask=pge[:], data=mid[:]), reads=[pge, mid], writes=[lo])
                    P.op("dve", lambda e: e.copy_predicated(out=hi[:], mask=plt[:], data=mid[:]), reads=[plt, mid], writes=[hi])
                ts(lo, lo[:], lo, lo[:], NEG / 2, ALU.max)
            else:
                mset(lo, lo[:], NEG / 2, eng="dve")
            ts(junkD, junkD[:, 0:n1], Isc, Isc[:, 0:n1], lo[:, 0:1], ALU.is_ge, extra=[lo])
            if n2 > 0:
                ts(junkA, junkA[:, 0:n2], Isc, Isc[:, n1:nkeys], lo[:, 0:1], ALU.is_ge, extra=[lo])

            def mask_ap(kt):
                if kt * 128 < n1:
                    return junkD, junkD[:, kt * 128:(kt + 1) * 128]
                return junkA, junkA[:, kt * 128 - n1:(kt + 1) * 128 - n1]
            chk(11)
            for k0 in range(0, nkt, 4):
                nk = min(4, nkt - k0)
                _ki = -1
                kb = kTc[rot["kt"] % 2 if _ki < 0 else _ki]; vb = vc[rot["kt"] % 2 if _ki < 0 else _ki]; rot["kt"] += 1
                dma(kb, kb[:, 0:nk * 128], h_kT, h_kT[:, k0 * 128:(k0 + nk) * 128])
                dma(vb, vb[:, 0:nk, :], h_v, h_v[k0 * 128:(k0 + nk) * 128, :].rearrange("(t p) c -> p t c", p=128))
                gi = rot["mg"] % 2; rot["mg"] += 1
                for kk in range(nk):
                    mb_, map_ = mask_ap(k0 + kk)
                    tr(psT, psT[:, kk * 128:(kk + 1) * 128], mb_, map_, identb)
                cp(mT4[gi], mT4[gi][:, 0:nk, :], psT, psT[:, 0:nk * 128].rearrange("p (g t) -> p g t", g=nk), eng="act")
                for kk in range(nk):
                    kt = k0 + kk
                    i2 = rot["mt"] % 2; rot["mt"] += 1
                    S = SS[i2]
                    for kv in range(2):
                        mm(S, S[:, kv * 512:(kv + 1) * 512], kb, kb[:, kk * 128:(kk + 1) * 128],
                           qTm[kv], qTm[kv][:].rearrange("p g t -> p (g t)"))
                    act(Eb[i2], Eb[i2][:].rearrange("p a b -> p (a b)"), S, S[:], AF.Exp, scale=0.125)
                    tt(Pm[i2], Pm[i2][:], Eb[i2], Eb[i2][:], mT4[gi], bc_mid(mT4[gi][:, kk, :], 8), ALU.mult)
                    for kv in range(2):
                        mm(psO[kv], psO[kv][:], vb, vb[:, kk, kv * 128:(kv + 1) * 128],
                           Pm[i2], Pm[i2][:, kv * 4:(kv + 1) * 4, :].rearrange("p g t -> p (g t)"),
                           start=(kt == 0), stop=(kt == nkt - 1))
            for kv in range(2):
                recip(rden, rden[64:128, :], psO[kv], psO[kv][64:128, :])
                dma(rdlo, rdlo[:], rden, rden[64:128, :])
                tt(brTb, brTb[:, kv * 4:(kv + 1) * 4, cs], psO[kv], psO[kv][0:64, :].rearrange("p (g t) -> p g t", g=4),
                   rdlo, rdlo[:].rearrange("p (g t) -> p g t", g=4), ALU.mult)

        chk(12)
        if DBG and not samp and l == 0:
            for n, bsrc_ in ((0, brT[0]), (2, brT[1]), (3, brT[2])):
                outtoks.append(dma(O["dbg"], O["dbg"][n, :, pos0:pos0 + Tn].rearrange("(g p) t -> p g t", p=128), bsrc_, bsrc_[:, :, 0:Tn]))
            outtoks.append(dma(O["dbg"], O["dbg"][1, :, pos0:pos0 + Tn].rearrange("(h p) t -> p h t", p=64), brTb, brTb[:, :, 0:Tn]))
        for n in range(4):
            for c in range(2):
                wg_ = wload("w_in", l, 0, 8, C_G + n * 1024 + c * 512, 512)
                if n == 1:
                    wb_ = wload("w_branch", l, 512, 8, c * 512, 512, pp=64)
                else:
                    wb_ = wload("w_branch", l, n * 512, 4, c * 512, 512)
                bsrc = {0: brT[0], 2: brT[1], 3: brT[2]}.get(n)
                for s in range(nsub):
                    cs = slice(s * 128, (s + 1) * 128)
                    pg = nps(); pb = nps()
                    for k in range(8):
                        mm(pg, pg[:], hT, hT[:, k, cs], wg_, wg_[:, k, 0:512], start=(k == 0), stop=(k == 7))
                    if n == 1:
                        for k in range(8):
                            mm(pb, pb[:], brTb, brTb[:, k, cs], wb_, wb_[0:64, k, 0:512], start=(k == 0), stop=(k == 7))
                    else:
                        for k in range(4):
                            mm(pb, pb[:], bsrc, bsrc[:, k, cs], wb_, wb_[:, k, 0:512], start=(k == 0), stop=(k == 3))
                    act(tm2, tm2[:], pg, pg[:], AF.Sigmoid)
                    mx = mixed[s]
                    if n == 0:
                        tt(mx, mx[:, c * 512:(c + 1) * 512], tm2, tm2[:], pb, pb[:], ALU.mult)
                    else:
                        tt(tm2, tm2[:], tm2, tm2[:], pb, pb[:], ALU.mult)
                        tt(mx, mx[:, c * 512:(c + 1) * 512], mx, mx[:, c * 512:(c + 1) * 512], tm2, tm2[:], ALU.add)
        for s in range(nsub):
            cp(hb, hb[:], mixed[s], mixed[s][:])
            to_hT(hb, hb, hT, s * 128)
        for c in range(2):
            w = wload("w_out", l, 0, 8, c * 512, 512)
            for s in range(nsub):
                p = nps()
                for k in range(8):
                    mm(p, p[:], hT, hT[:, k, s * 128:(s + 1) * 128], w, w[:, k, 0:512], start=(k == 0), stop=(k == 7))
                tt(x_mt, x_mt[:, s, c * 512:(c + 1) * 512], x_mt, x_mt[:, s, c * 512:(c + 1) * 512], p, p[:], ALU.add)
        chk(13)
        if DBG and not samp and l == 0:
            for s in range(nsub):
                outtoks.append(dma(O["dbgx"], O["dbgx"][pos0 + s * 128: pos0 + (s + 1) * 128, :], x_mt, x_mt[:, s, :]))
        for s in range(nsub):
            rms_full(x_mt, x_mt[:, s, :], gffn, hb, hb[:], D)
            to_hT(hb, hb, hT, s * 128)
        for j0 in range(0, 22, 4):
            nj = min(4, 22 - j0)
            wgs = wload("w_ffn_up", l, 0, 8, j0 * 128, nj * 128)
            wus = wload("w_ffn_up", l, 0, 8, 2816 + j0 * 128, nj * 128)
            for jj in range(nj):
                jg = j0 + jj; ju = 22 + jg
                pg = nps(); pu = nps()
                for k in range(8):
                    mm(pg, pg[:, 0:Tn], wgs, wgs[:, k, jj * 128:(jj + 1) * 128], hT, hT[:, k, 0:Tn], start=(k == 0), stop=(k == 7))
                for k in range(8):
                    mm(pu, pu[:, 0:Tn], wus, wus[:, k, jj * 128:(jj + 1) * 128], hT, hT[:, k, 0:Tn], start=(k == 0), stop=(k == 7))
                for (st, pp_, jc, co) in ((fst_g, pg, jg, fcg), (fst_u, pu, ju, fcu)):
                    cp(st, st[:, 0:2], fcar, fcar[:, jc, :])
                    cp(st, st[:, 2:2 + Tn], pp_, pp_[:, 0:Tn], eng="act")
                    if ctx["last"]:
                        cp(fso, fso[:, jc, :], st, st[:, tv:tv + 2])
                    else:
                        cp(fcar, fcar[:, jc, :], st, st[:, Tn:Tn + 2])
                    ts(co, co[:, 0:Tn], st, st[:, 0:Tn], cwf[:, jc, 0:1], ALU.mult, cbf[:, jc:jc + 1], ALU.add, extra=[cwf, cbf])
                    stt(co, co[:, 0:Tn], st, st[:, 1:1 + Tn], cwf[:, jc, 1:2], co, co[:, 0:Tn], ALU.mult, ALU.add, extra=[cwf])
                    stt(co, co[:, 0:Tn], st, st[:, 2:2 + Tn], cwf[:, jc, 2:3], co, co[:, 0:Tn], ALU.mult, ALU.add, extra=[cwf])
                act(fcg, fcg[:, 0:Tn], fcg, fcg[:, 0:Tn], AF.Silu)
                tt(gT, gT[:, jg, 0:Tn], fcg, fcg[:, 0:Tn], fcu, fcu[:, 0:Tn], ALU.mult)
        if ctx["last"]:
            od = (O["ffn_s"], O["ffn_s"][l, j]) if samp else (O["ffn_p"], O["ffn_p"][l])
            outtoks.extend(fm2tm(od[0], od[1], fso, lambda g: fso[:, g, :], 44, 2))
        for c in range(2):
            ws_ = [wload("w_ffn_down", l, r0 * 128, min(8, 22 - r0), c * 512, 512) for r0 in (0, 8, 16)]
            for s in range(nsub):
                p = nps()
                for jg in range(22):
                    w = ws_[jg // 8]
                    mm(p, p[:], gT, gT[:, jg, s * 128:(s + 1) * 128], w, w[:, jg % 8, 0:512], start=(jg == 0), stop=(jg == 21))
                tt(x_mt, x_mt[:, s, c * 512:(c + 1) * 512], x_mt, x_mt[:, s, c * 512:(c + 1) * 512], p, p[:], ALU.add)
        for s in range(nsub):
            if samp:
                if last_layer:
                    outtoks.append(dma(O["y_s"], O["y_s"][j:j + 1, :], x_mt, x_mt[0:1, s, :]))
                else:
                    dma(xsres, xsres[j:j + 1, :], x_mt, x_mt[0:1, s, :])
            else:
                dst = O["y_p"] if last_layer else xres
                t_ = dma(dst, dst[pos0 + s * 128: pos0 + (s + 1) * 128, :], x_mt, x_mt[:, s, :])
                if last_layer:
                    outtoks.append(t_)

    def sample_history(l, j):
        dma(ptb, ptb[:], I["pt"], I["pt"][j:j + 1, :].to_broadcast([128, cfg.NPG]))
        ts(pidx, pidx[:], ptb, ptb[:], 128, ALU.mult, iop[:, 0:1], ALU.add, extra=[iop])
        if l > 0:
            ts(pidx, pidx[:], pidx, pidx[:], float(l * NPHYS * 128), ALU.add)
        for pg in range(cfg.NPG):
            for (srcn, dstb, w) in (("cki", pgi, 64), ("ck", pgk, 128), ("cv", pgv, 128)):
                srcb = I[srcn]
                P.dma(lambda e, srcb=srcb, dstb=dstb, pg=pg: e.indirect_dma_start(
                    out=dstb[:], out_offset=None, in_=srcb[:].rearrange("l r c -> (l r) c"),
                    in_offset=bass.IndirectOffsetOnAxis(ap=pidx[:, pg:pg + 1], axis=0)),
                    reads=[srcb, pidx], writes=[dstb], q="pool")
            cp(kibf, kibf[:], pgi, pgi[:])
            tr(psT, psT[0:64, 0:128], kibf, kibf[:], identb)
            cp(kiTs, kiTs[:], psT, psT[0:64, 0:128], eng="act")
            dma(h_kiT, h_kiT[:, pg * 128:(pg + 1) * 128], kiTs, kiTs[:])
            cp(kbf, kbf[:], pgk, pgk[:])
            tr(psT, psT[:, 128:256], kbf, kbf[:], identb)
            cp(kTs, kTs[:], psT, psT[:, 128:256], eng="act")
            dma(h_kT, h_kT[:, pg * 128:(pg + 1) * 128], kTs, kTs[:])
            cp(vbuf, vbuf[:, :, 0:64], pgv, pgv[:].rearrange("p (kv d) -> p kv d", kv=2))
            dma(h_v, h_v[pg * 128:(pg + 1) * 128, :], vbuf, vbuf[:].rearrange("p a b -> p (a b)"))

    nmac = SEQ // T
    stop = 99
    for l in range(DEPTH):
        if stop < 1: break
        load_params(l)
        if stop < 2: break
        mem_kv_prompt(l)
        if stop < 3: break
        for m in range(nmac):
            try:
                macro(l, dict(kind="p", pos0=m * T, nsub=NSUB, tv=T, first=(m == 0), last=(m == nmac - 1)))
            except StopM:
                pass
            if stop < 4: break
        if stop < 5: break
        for j in range(NS):
            mem_kv_sample(l, j)
            sample_history(l, j)
            if stop < 6: break
            macro(l, dict(kind="s", j=j, pos0=PAST, nsub=1, tv=1, first=True, last=True))
        if stop < 7: break
    P.finish(outtoks)
    P.emit()
    es.close()
    return nc


_W_NAMES = ["norm_mix_g", "w_in", "conv_a_w", "conv_a_b", "ln_a_g", "ln_a_b", "q_norm_g", "k_norm_g", "ssm_conv_w",
            "ssm_conv_b", "dt_bias", "a_log", "d_skip", "ssm_norm_g", "mem_norm_g", "w_mem_kv", "mq_norm_g",
            "mk_norm_g", "w_branch", "w_out", "norm_ffn_g", "w_ffn_up", "ffn_conv_w", "ffn_conv_b", "w_ffn_down"]


def _rope_tab(pos):
    inv = 500000.0 ** (-np.arange(8, dtype=np.float64) * (2.0 / 16))
    ang = (pos.astype(np.float32)[:, None] * inv.astype(np.float32)[None, :]).astype(np.float32)
    return np.concatenate([np.cos(ang), np.sin(ang)], axis=1).astype(np.float32)


def run(cfg, inputs, n_cores=8):
    f = lambda a: np.ascontiguousarray(np.asarray(a))
    SEQ, PAST, DEPTH, NS = cfg.SEQ, cfg.PAST, cfg.DEPTH, cfg.NS
    nc = build(cfg)
    B = inputs["x_prompt"].shape[0]
    shared = {n: f(inputs[n]) for n in _W_NAMES}
    shared["w_branch"] = shared["w_branch"].reshape(DEPTH, 2048, D)
    shared["ck"] = f(inputs["cache_k"]).reshape(DEPTH, -1, 128)
    shared["cv"] = f(inputs["cache_v"]).reshape(DEPTH, -1, 128)
    shared["cki"] = f(inputs["cache_kidx"]).reshape(DEPTH, -1, 64)
    shared["ropep"] = _rope_tab(np.arange(SEQ)); shared["ropes"] = _rope_tab(np.array([PAST]))
    in_maps = []
    for c in range(n_cores):
        b = c % B; sl = slice(c * NS, (c + 1) * NS)
        m = dict(shared)
        m["xp"] = f(inputs["x_prompt"][b]); m["memp"] = f(inputs["mem_prompt"][b])
        m["xs"] = f(inputs["x_sample"][sl, 0])
        m["cmk"] = f(inputs["cache_mem_k"][:, sl]).reshape(DEPTH, NS, 256, 512)
        m["cmv"] = f(inputs["cache_mem_v"][:, sl]).reshape(DEPTH, NS, 256, 512)
        m["sconf"] = f(inputs["state_conformer"][:, sl]); m["ssc"] = f(inputs["state_ssm_conv"][:, sl])
        m["sssm"] = f(inputs["state_ssm"][:, sl]); m["sffn"] = f(inputs["state_ffn_conv"][:, sl])
        m["pt"] = f(inputs["page_table"][sl]).astype(np.int32)
        in_maps.append(m)
    res = run_bass_kernel_spmd(nc, in_maps, core_ids=list(range(n_cores)))
    R = res.results
    st = lambda k, cores: np.stack([R[c][k] for c in cores])
    pc = list(range(B))
    ac = list(range(n_cores))
    cat = lambda k: np.concatenate([R[c][k] for c in ac], axis=1)
    y_p = st("y_p", pc)
    y_s = np.concatenate([R[c]["y_s"] for c in ac], axis=0)[:, None, :]
    k_p = st("k_p", pc).transpose(1, 0, 2, 3).reshape(DEPTH, B, SEQ, 2, 64)
    v_p = st("v_p", pc).transpose(1, 0, 2, 3).reshape(DEPTH, B, SEQ, 2, 64)
    ki_p = st("ki_p", pc).transpose(1, 0, 2, 3)
    mk_p = st("mk_p", pc).transpose(1, 0, 2, 3).reshape(DEPTH, B, 256, 4, 128)
    mv_p = st("mv_p", pc).transpose(1, 0, 2, 3).reshape(DEPTH, B, 256, 4, 128)
    conf_p = st("conf_p", pc).transpose(1, 0, 2, 3)
    sc_p = st("sc_p", pc).transpose(1, 0, 2, 3)
    ssm_p = st("ssm_p", pc).transpose(1, 0, 2, 3, 4)
    ffn_p = st("ffn_p", pc).transpose(1, 0, 2, 3)
    k_s = cat("k_s").reshape(DEPTH, -1, 1, 2, 64); v_s = cat("v_s").reshape(DEPTH, -1, 1, 2, 64)
    ki_s = cat("ki_s").reshape(DEPTH, -1, 1, 64)
    outs = (y_p, y_s, k_p, v_p, ki_p, mk_p, mv_p, conf_p, sc_p, ssm_p, ffn_p, k_s, v_s, ki_s,
            cat("conf_s"), cat("sc_s"), cat("ssm_s"), cat("ffn_s"))
    return tuple(np.ascontiguousarray(o.astype(np.float32)) for o in outs)


def kernel(**inputs):
    return run(CFG(), inputs)
```

```python
import numpy as np
from contextlib import ExitStack
import concourse.bass as bass
import concourse.mybir as mybir
from concourse.bass_utils import run_bass_kernel_spmd

F32 = mybir.dt.float32
BF16 = mybir.dt.bfloat16
I32 = mybir.dt.int32
U32 = mybir.dt.uint32
ALU = mybir.AluOpType
AF = mybir.ActivationFunctionType
AX = mybir.AxisListType

ENGS = ("pe", "act", "dve", "pool", "sp")
NDMA = 8
SAME_ENG_SYNC = True


class Buf:
    def __init__(self, name, t, roots=None):
        self.name = name
        self.t = t
        if roots is None:
            self.roots = [self]
            self._lastw = None
            self._readers = []
        else:
            self.roots = roots

    def __getitem__(self, idx):
        return self.t[idx]

    def alias(self, name, ap):
        return Buf(name, ap, roots=self.roots)


def multi(name, t, parts):
    roots = []
    for p in parts:
        for r in p.roots:
            if r not in roots:
                roots.append(r)
    return Buf(name, t, roots=roots)


class Prog:
    def __init__(self, nc, es):
        self.nc = nc
        self.es = es
        self.ops = {e: [] for e in ENGS}
        self.cnt = {e: 0 for e in ENGS}
        self.dma_i = {e: 0 for e in ENGS}
        self.seen = {e: {} for e in ENGS}
        self.sems = {}
        for e in ("pe", "act", "dve", "pool"):
            self.sems[("c", e)] = es.enter_context(nc.semaphore("s_" + e))
        for e in ("sp", "act", "pool"):
            for i in range(NDMA):
                self.sems[("d", e, i)] = es.enter_context(nc.semaphore("d_%s%d" % (e, i)))
        self.nbuf = 0

    def sb(self, name, shape, dt=F32):
        t = self.es.enter_context(self.nc.sbuf_tensor(name, list(shape), dt))
        return Buf(name, t)

    def ps(self, name, shape, dt=F32):
        t = self.es.enter_context(self.nc.psum_tensor(name, list(shape), dt))
        return Buf(name, t)

    def view(self, name, t):
        return Buf(name, t)

    def _need(self, eng, tok, waits):
        if tok is None:
            return
        key, val, teng = tok
        if key[0] == "c" and teng == eng and not (SAME_ENG_SYNC and eng != "pe"):
            return
        if self.seen[eng].get(key, 0) >= val:
            return
        self.seen[eng][key] = val
        waits.append((key, val))

    def _deps(self, eng, reads, writes):
        waits = []
        for b in reads:
            for r in b.roots:
                self._need(eng, r._lastw, waits)
        for b in writes:
            for r in b.roots:
                self._need(eng, r._lastw, waits)
                for rd in r._readers:
                    self._need(eng, rd, waits)
        return waits

    def _commit(self, tok, reads, writes):
        for b in reads:
            for r in b.roots:
                r._readers.append(tok)
        for b in writes:
            for r in b.roots:
                r._lastw = tok
                r._readers = []

    def op(self, eng, fn, reads=(), writes=(), signal=True):
        waits = self._deps(eng, reads, writes)
        key = ("c", eng)
        if signal:
            self.cnt[eng] += 1
            tok = (key, self.cnt[eng], eng)
            self.ops[eng].append((waits, fn, (key, 1)))
        else:
            tok = (key, self.cnt[eng] + 1, eng)
            self.ops[eng].append((waits, fn, None))
        self._commit(tok, reads, writes)
        return tok

    def dma(self, fn, reads=(), writes=(), q="sp"):
        i = self.dma_i[q]
        self.dma_i[q] += 1
        slot = i % NDMA
        key = ("d", q, slot)
        val = 16 * (i // NDMA + 1)
        waits = self._deps(q, reads, writes)
        if i >= NDMA:
            self._need(q, (key, val - 16, "dma"), waits)
        tok = (key, val, "dma")
        self.ops[q].append((waits, fn, (key, 16)))
        self._commit(tok, reads, writes)
        return tok

    def finish(self, toks):
        waits = []
        for t in toks:
            self._need("sp", t, waits)
        self.ops["sp"].append((waits, None, None))

    def emit(self):
        nc = self.nc
        P = self

        def replay(name, e):
            for waits, fn, inc in P.ops[name]:
                for key, val in waits:
                    e.wait_ge(P.sems[key], val)
                if fn is None:
                    continue
                ins = fn(e)
                if inc is not None:
                    ins.then_inc(P.sems[inc[0]], inc[1])

        with nc.Block() as block:
            @block.sync
            def _(e):
                replay("sp", e)

            @block.tensor
            def _(e):
                replay("pe", e)

            @block.scalar
            def _(e):
                replay("act", e)

            @block.vector
            def _(e):
                replay("dve", e)

            @block.gpsimd
            def _(e):
                replay("pool", e)


D = 1024
NEG = -1.0e30
IDX_SCALE = (64 ** -0.5) * (8 ** -0.5)
EPS = 1e-6
IN_COLS = 8272
C_GLU, C_Q, C_K, C_V, C_QI, C_KI, C_WI, C_Z, C_XBC, C_DT, C_MQ, C_G = (
    0, 1024, 1536, 1664, 1792, 2304, 2368, 2376, 2888, 3656, 3664, 4176)
NITER = 22
SPLIT_MIN_NKT = 12


class CFG:
    def __init__(self, SEQ=8192, PAST=8192, NPHYS=2560, DEPTH=2, NS=4, T=128):
        self.SEQ, self.PAST, self.NPHYS, self.DEPTH, self.NS, self.T = SEQ, PAST, NPHYS, DEPTH, NS, T
        self.SMAX = max(SEQ, PAST + 128)
        self.KP = min(256, SEQ // 4)
        self.KS = min(256, (PAST + 1) // 4)
        self.NPG = PAST // 128


def build(cfg):
    SEQ, PAST, NPHYS, DEPTH, NS, T = cfg.SEQ, cfg.PAST, cfg.NPHYS, cfg.DEPTH, cfg.NS, cfg.T
    NSUB = T // 128
    nc = bass.Bass("TRN2", target_bir_lowering=False)
    es = ExitStack()
    P = Prog(nc, es)

    def din(name, shape, dt=F32):
        return Buf(name, nc.dram_tensor(name, list(shape), dt, kind="ExternalInput").ap())

    def dout(name, shape, dt=F32):
        return Buf(name, nc.dram_tensor(name, list(shape), dt, kind="ExternalOutput").ap())

    def dscr(name, shape, dt=BF16):
        return Buf(name, nc.dram_tensor(name, list(shape), dt, kind="Internal").ap())

    I = {}
    I["xp"] = din("xp", [SEQ, D]); I["xs"] = din("xs", [NS, D]); I["memp"] = din("memp", [256, D])
    I["ck"] = din("ck", [DEPTH, NPHYS * 128, 128]); I["cv"] = din("cv", [DEPTH, NPHYS * 128, 128])
    I["cki"] = din("cki", [DEPTH, NPHYS * 128, 64])
    I["cmk"] = din("cmk", [DEPTH, NS, 256, 512]); I["cmv"] = din("cmv", [DEPTH, NS, 256, 512])
    I["sconf"] = din("sconf", [DEPTH, NS, 30, 512]); I["ssc"] = din("ssc", [DEPTH, NS, 3, 768])
    I["sssm"] = din("sssm", [DEPTH, NS, 8, 64, 64]); I["sffn"] = din("sffn", [DEPTH, NS, 2, 5632])
    I["pt"] = din("pt", [NS, cfg.NPG], I32)
    I["ropep"] = din("ropep", [SEQ, 16]); I["ropes"] = din("ropes", [1, 16])
    for nm, shp in [("norm_mix_g", [DEPTH, D]), ("w_in", [DEPTH, D, IN_COLS]), ("conv_a_w", [DEPTH, 31, 512]),
                    ("conv_a_b", [DEPTH, 512]), ("ln_a_g", [DEPTH, 512]), ("ln_a_b", [DEPTH, 512]),
                    ("q_norm_g", [DEPTH, 64]), ("k_norm_g", [DEPTH, 64]), ("ssm_conv_w", [DEPTH, 4, 768]),
                    ("ssm_conv_b", [DEPTH, 768]), ("dt_bias", [DEPTH, 8]), ("a_log", [DEPTH, 8]),
                    ("d_skip", [DEPTH, 8]), ("ssm_norm_g", [DEPTH, 512]), ("mem_norm_g", [DEPTH, D]),
                    ("w_mem_kv", [DEPTH, D, 1024]), ("mq_norm_g", [DEPTH, 128]), ("mk_norm_g", [DEPTH, 128]),
                    ("w_branch", [DEPTH, 2048, D]), ("w_out", [DEPTH, D, D]), ("norm_ffn_g", [DEPTH, D]),
                    ("w_ffn_up", [DEPTH, D, 5632]), ("ffn_conv_w", [DEPTH, 3, 5632]),
                    ("ffn_conv_b", [DEPTH, 5632]), ("w_ffn_down", [DEPTH, 2816, D])]:
        I[nm] = din(nm, shp)
    O = {}
    O["y_p"] = dout("y_p", [SEQ, D]); O["y_s"] = dout("y_s", [NS, D])
    O["k_p"] = dout("k_p", [DEPTH, SEQ, 128]); O["v_p"] = dout("v_p", [DEPTH, SEQ, 128])
    O["ki_p"] = dout("ki_p", [DEPTH, SEQ, 64])
    O["mk_p"] = dout("mk_p", [DEPTH, 256, 512]); O["mv_p"] = dout("mv_p", [DEPTH, 256, 512])
    O["conf_p"] = dout("conf_p", [DEPTH, 30, 512]); O["sc_p"] = dout("sc_p", [DEPTH, 3, 768])
    O["ssm_p"] = dout("ssm_p", [DEPTH, 8, 64, 64]); O["ffn_p"] = dout("ffn_p", [DEPTH, 2, 5632])
    O["k_s"] = dout("k_s", [DEPTH, NS, 128]); O["v_s"] = dout("v_s", [DEPTH, NS, 128])
    O["ki_s"] = dout("ki_s", [DEPTH, NS, 64])
    O["conf_s"] = dout("conf_s", [DEPTH, NS, 30, 512]); O["sc_s"] = dout("sc_s", [DEPTH, NS, 3, 768])
    O["ssm_s"] = dout("ssm_s", [DEPTH, NS, 8, 64, 64]); O["ffn_s"] = dout("ffn_s", [DEPTH, NS, 2, 5632])
    outtoks = []
    DBG = False
    if DBG:
        O["dbg"] = dout("dbg", [4, 512, SEQ], BF16)
        O["dbgx"] = dout("dbgx", [SEQ, D])
    WB = {nm: dscr(nm + "_b", list(I[nm].t.shape)) for nm in
          ("w_in", "w_mem_kv", "w_branch", "w_out", "w_ffn_up", "w_ffn_down")}
    xres = dscr("xres", [SEQ, D], F32)
    xsres = dscr("xsres", [NS, D], F32)
    h_kiT = dscr("h_kiT", [64, cfg.SMAX]); h_kT = dscr("h_kT", [128, cfg.SMAX]); h_v = dscr("h_v", [cfg.SMAX, 256])

    sb, ps = P.sb, P.ps
    identf = sb("identf", [128, 128]); identb = sb("identb", [128, 128], BF16)
    onesf = sb("onesf", [128, 128]); onesb = sb("onesb", [128, 128], BF16)
    trif = sb("trif", [128, 128]); elast = sb("elast", [128, 128])
    cadd = sb("cadd", [128, 128]); sadd = sb("sadd", [128, 128])
    epsT = sb("epsT", [128, 1]); oneT = sb("oneT", [128, 1])
    tokm1 = sb("tokm1", [128, 1]); tokms = sb("tokms", [128, 1])
    gmix = sb("gmix", [128, D]); gffn = sb("gffn", [128, D])
    lnag = sb("lnag", [128, 512]); lnab = sb("lnab", [128, 512]); ssmg = sb("ssmg", [128, 512])
    qg = sb("qg", [128, 64]); kg = sb("kg", [128, 64]); mqg = sb("mqg", [128, 128]); mkg = sb("mkg", [128, 128])
    dtb = sb("dtb", [128, 8]); aneg = sb("aneg", [128, 8]); dsk = sb("dsk", [128, 8])
    cwa = sb("cwa", [128, 4, 31]); cba = sb("cba", [128, 4]); cws = sb("cws", [128, 6, 4]); cbs = sb("cbs", [128, 6])
    cwf = sb("cwf", [128, 44, 3]); cbf = sb("cbf", [128, 44])
    x_mt = sb("x_mt", [128, NSUB, D]); hT = sb("hT", [128, 8, T], BF16); hb = sb("hb", [128, D], BF16)
    wsl = [sb("wsl%d" % i, [128, 8, 512], BF16) for i in range(3)]
    projA = [sb("projA%d" % s, [128, 1864]) for s in range(NSUB)]
    projB = [sb("projB%d" % s, [128, 520]) for s in range(NSUB)]
    glu_u = sb("glu_u", [128, 4, 30 + T]); glu_c_t = sb("glu_c", [128, 4, T]).t; sgt = sb("sgt", [128, 512])
    glu_cg = [Buf("glu_c%d" % g, glu_c_t[:, g, :]) for g in range(4)]
    glu_c = multi("glu_c", glu_c_t, glu_cg)
    xbc_u = sb("xbc_u", [128, 6, 3 + T]); xbc_c_t = sb("xbc_c", [128, 6, T]).t
    xbc_cg = [Buf("xbc_c%d" % g, xbc_c_t[:, g, :]) for g in range(6)]
    xbc_c = multi("xbc_c", xbc_c_t, xbc_cg)
    fst_g = sb("fst_g", [128, 2 + T]); fst_u = sb("fst_u", [128, 2 + T]); fcar = sb("fcar", [128, 44, 2])
    fso = sb("fso", [128, 44, 2]); fcg = sb("fcg", [128, T]); fcu = sb("fcu", [128, T])
    gT = sb("gT", [128, 22, T], BF16)
    brT = [sb("brT%d" % n, [128, 4, T], BF16) for n in (0, 2, 3)]
    brTb = sb("brTb", [64, 8, T], BF16)
    mixed = [projA[s].alias("mixed%d" % s, projA[s][:, 0:D]) for s in range(NSUB)]
    gmem = projA[0].alias("gmem", projA[0][:, 0:D])
    tm1 = sb("tm1", [128, 512]); tm2 = sb("tm2", [128, 512]); tmb = sb("tmb", [128, 512], BF16)
    sm = [sb("sm%d" % i, [128, 16]) for i in range(8)]
    xs_tm = sb("xs_tm", [128, 512]); B_tm = sb("B_tm", [128, 128], BF16)
    BT = sb("BT", [128, 128], BF16); CT = sb("CT", [128, 128], BF16)
    CTm = [sb("CTm0", [128, 128], BF16), sb("CTm1", [128, 128], BF16)]
    qTm = [sb("qTm0", [128, 4, 128], BF16), sb("qTm1", [128, 4, 128], BF16)]
    xdt = sb("xdt", [128, 512], BF16); xdtd = sb("xdtd", [128, 512], BF16)
    GT = sb("GT", [128, 2, 128])
    scT = sb("scT", [128, 8, 128], BF16)
    hst = sb("hst", [128, 4, 64]); hstb = sb("hstb", [128, 4, 64], BF16)
    ybuf = sb("ybuf", [128, 512])
    mkT = sb("mkT", [128, 4, 256], BF16); mvb = sb("mvb", [128, 2, 512], BF16)
    mqT = sb("mqT", [128, 4, 128], BF16); PT = sb("PT", [128, 2, 512], BF16); rden = sb("rden", [128, 512])
    rdlo = sb("rdlo", [64, 512])
    hio = rdlo.alias("hio", rdlo[:])
    Isc = sb("Isc", [128, max(cfg.SMAX, 1024)])
    NKT_MAX = cfg.SMAX // 128
    N1MAX = max(12, int(round(NKT_MAX * 0.42))) * 128
    junkD = sb("junkD", [128, N1MAX], BF16); junkA = sb("junkA", [128, max(min(cfg.SMAX, 1024), cfg.SMAX - N1MAX + 128)], BF16)
    nmid = sb("nmid", [128, 1]); cnt2 = sb("cnt2", [128, 1])
    kic = [sb("kic%d" % i, [64, 1024], BF16) for i in range(2)]
    rbuf = [sb("rbuf0", [128, 1024]), sb("rbuf1", [128, 1024])]
    diag = rbuf[0].alias("diag", rbuf[0][:].rearrange("p (h t) -> p h t", h=8))
    seg = Isc.alias("seg", Isc[:, 0:1024].rearrange("p (h t) -> p h t", h=8))
    stg = [Isc.alias("stg", Isc[:, 0:1024]), rbuf[1].alias("stg1", rbuf[1][:])]
    stgb = [hT.alias("stgb", hT[:].rearrange("p a b -> p (a b)")[:, 0:1024]), hb.alias("stgb1", hb[:])]
    kTc = [sb("kTc%d" % i, [128, 512], BF16) for i in range(2)]
    vc = [sb("vc%d" % i, [128, 4, 256], BF16) for i in range(2)]
    qT = sb("qT", [128, 4, 128], BF16); qiT = sb("qiT", [64, 8, 128], BF16)
    mT4 = [sb("mT4_%d" % i, [128, 4, 128], BF16) for i in range(2)]
    Eb = [sb("Eb%d" % i, [128, 8, 128], BF16) for i in range(2)]
    Pm = [sb("Pm%d" % i, [128, 8, 128], BF16) for i in range(2)]
    vbuf = sb("vbuf", [128, 2, 128], BF16); ropet = sb("ropet", [128, 16])
    awi = sb("awi", [128, 8]); swi = sb("swi", [128, 8])
    lo = sb("lo", [128, 1]); hi = sb("hi", [128, 1]); mid = sb("mid", [128, 1]); cnt = sb("cnt", [128, 1])
    pge = sb("pge", [128, 1], I32); plt = sb("plt", [128, 1], I32)
    pidx = sb("pidx", [128, cfg.NPG], I32); ptb = sb("ptb", [128, cfg.NPG], I32); iop = sb("iop", [128, 1], I32)
    pgk = sb("pgk", [128, 128]); pgv = sb("pgv", [128, 128]); pgi = sb("pgi", [128, 64])
    kout = sb("kout", [128, 128]); kiout = sb("kiout", [128, 64]); kbf = sb("kbf", [128, 128], BF16)
    kibf = sb("kibf", [128, 64], BF16); qn = tm2.alias("qn", tm2[:]); qbf = sb("qbf", [128, 512], BF16)
    kTs = sb("kTs", [128, 128], BF16); kiTs = sb("kiTs", [64, 128], BF16)
    stio = ybuf.alias("stio", ybuf[:])
    psAB_t = ps("psAB", [128, 1024]).t
    psA = Buf("psA", psAB_t[:, 0:512]); psB = Buf("psB", psAB_t[:, 512:1024]); psC = ps("psC", [128, 512])
    psAB = multi("psAB2", psAB_t, [psA, psB])
    psW = ps("psW", [128, 1024]); psO = [ps("psO0", [128, 512]), ps("psO1", [128, 512])]
    psT = ps("psT", [128, 1024], BF16)
    pr = [psA, psB, psC]
    SS = [psW, psAB]
    rot = {"ps": 0, "w": 0, "kic": 0, "rb": 0, "kt": 0, "mt": 0, "mg": 0}

    def nps():
        rot["ps"] = (rot["ps"] + 1) % 3
        return pr[rot["ps"]]

    def mm(o, oap, l, lap, r, rap, start=True, stop=True, signal=True):
        P.op("pe", lambda e: e.matmul(oap, lhsT=lap, rhs=rap, start=start, stop=stop), reads=[l, r], writes=[o],
             signal=signal)

    def tr(o, oap, i, iap, idt):
        P.op("pe", lambda e: e.transpose(oap, iap, idt[0:iap.shape[0], 0:iap.shape[0]]), reads=[i, idt], writes=[o])

    def act(o, oap, i, iap, func, bias=None, scale=None, accum=None, extra=()):
        kw = {}
        if bias is not None: kw["bias"] = bias
        if scale is not None: kw["scale"] = scale
        wr = [o]
        if accum is not None:
            kw["accum_out"] = accum[1]; wr.append(accum[0])
        P.op("act", lambda e: e.activation(out=oap, in_=iap, func=func, **kw), reads=[i] + list(extra), writes=wr)

    def tt(o, oap, a, aap, b, bap, op, eng="dve"):
        P.op(eng, lambda e: e.tensor_tensor(out=oap, in0=aap, in1=bap, op=op), reads=[a, b], writes=[o])

    def ts(o, oap, a, aap, s1, op0, s2=None, op1=None, accum=None, extra=(), eng="dve"):
        kw = {}
        wr = [o]
        if op1 is not None: kw["op1"] = op1
        if accum is not None:
            kw["accum_out"] = accum[1]; wr.append(accum[0])
        P.op(eng, lambda e: e.tensor_scalar(out=oap, in0=aap, scalar1=s1, scalar2=s2, op0=op0, **kw),
             reads=[a] + list(extra), writes=wr)

    def stt(o, oap, a, aap, sc, b, bap, op0, op1, extra=()):
        P.op("dve", lambda e: e.scalar_tensor_tensor(out=oap, in0=aap, scalar=sc, in1=bap, op0=op0, op1=op1),
             reads=[a, b] + list(extra), writes=[o])

    def cp(o, oap, i, iap, eng="dve"):
        if eng == "act":
            P.op("act", lambda e: e.activation(out=oap, in_=iap, func=AF.Copy), reads=[i], writes=[o])
        else:
            P.op(eng, lambda e: e.tensor_copy(out=oap, in_=iap), reads=[i], writes=[o])

    def mset(o, oap, v, eng="pool"):
        P.op(eng, lambda e: e.memset(oap, v), writes=[o])

    def dma(o, oap, i, iap, q="sp", nonc=False):
        if nonc:
            return P.dma(lambda e: e.dma_start(out=oap, in_=iap, allow_slow_non_contiguous=True), reads=[i], writes=[o], q=q)
        return P.dma(lambda e: e.dma_start(out=oap, in_=iap), reads=[i], writes=[o], q=q)

    def recip(o, oap, i, iap):
        P.op("dve", lambda e: e.reciprocal(out=oap, in_=iap), reads=[i], writes=[o])

    def red(o, oap, i, iap, op):
        P.op("dve", lambda e: e.tensor_reduce(out=oap, in_=iap, axis=AX.X, op=op), reads=[i], writes=[o])

    def bc_last(ap, n):
        return ap.unsqueeze(2).to_broadcast([ap.shape[0], ap.shape[1], n])

    def bc_mid(ap, n):
        return ap.unsqueeze(1).to_broadcast([ap.shape[0], n, ap.shape[1]])

    mset(identf, identf[:], 1.0)
    P.op("pool", lambda e: e.affine_select(out=identf[:], in_=identf[:], pattern=[[-1, 128]], compare_op=ALU.is_equal,
                                            fill=0.0, base=0, channel_multiplier=1), reads=[identf], writes=[identf])
    cp(identb, identb[:], identf, identf[:])
    mset(onesf, onesf[:], 1.0); mset(onesb, onesb[:], 1.0)
    mset(trif, trif[:], 1.0)
    P.op("pool", lambda e: e.affine_select(out=trif[:], in_=trif[:], pattern=[[1, 128]], compare_op=ALU.is_ge,
                                            fill=0.0, base=0, channel_multiplier=-1), reads=[trif], writes=[trif])
    mset(cadd, cadd[:], 0.0)
    P.op("pool", lambda e: e.affine_select(out=cadd[:], in_=cadd[:], pattern=[[-1, 128]], compare_op=ALU.is_ge,
                                            fill=NEG, base=0, channel_multiplier=1), reads=[cadd], writes=[cadd])
    mset(sadd, sadd[:], NEG); mset(sadd, sadd[:, 0:1], 0.0)
    mset(elast, elast[:], 0.0); mset(elast, elast[127:128, :], 1.0) if False else None
    mset(elast, elast[:], 1.0)
    P.op("pool", lambda e: e.affine_select(out=elast[:], in_=elast[:], pattern=[[0, 128]], compare_op=ALU.is_equal,
                                            fill=0.0, base=-127, channel_multiplier=1), reads=[elast], writes=[elast])
    mset(epsT, epsT[:], EPS); mset(oneT, oneT[:], 1.0)
    mset(tokm1, tokm1[:], 1.0)
    mset(tokms, tokms[:], 1.0)
    P.op("pool", lambda e: e.affine_select(out=tokms[:], in_=tokms[:], pattern=[[0, 1]], compare_op=ALU.is_equal,
                                            fill=0.0, base=0, channel_multiplier=1), reads=[tokms], writes=[tokms])
    mset(vbuf, vbuf[:], 1.0)
    for g in range(2):
        mset(CTm[g], CTm[g][:], 0.0); mset(qTm[g], qTm[g][:], 0.0)
    P.op("pool", lambda e: e.iota(iop[:], pattern=[[0, 1]], base=0, channel_multiplier=1), writes=[iop])

    pc = [0]
    for nm in ("w_in", "w_mem_kv", "w_branch", "w_out", "w_ffn_up", "w_ffn_down"):
        src = I[nm]; dst = WB[nm]
        shp = src.t.shape
        R, C = shp[1], shp[2]
        for l in range(DEPTH):
            for r0 in range(0, R, 128):
                for c0 in range(0, C, 1024):
                    cw = min(1024, C - c0)
                    pi = pc[0] % 2
                    qn_ = "sp" if pi == 0 else "act"
                    dma(stg[pi], stg[pi][:, 0:cw], src, src[l, r0:r0 + 128, c0:c0 + cw], q=qn_)
                    cp(stgb[pi], stgb[pi][:, 0:cw], stg[pi], stg[pi][:, 0:cw], eng=("dve" if pi == 0 else "act"))
                    dma(dst, dst[l, r0:r0 + 128, c0:c0 + cw], stgb[pi], stgb[pi][:, 0:cw], q=qn_)
                    pc[0] += 1

    wrot = [0]

    def wload(nm, l, r0, G, c0, ncol, pp=128):
        s = wsl[wrot[0] % 3]; wrot[0] += 1
        w = WB[nm]
        dma(s, s[0:pp, 0:G, 0:ncol], w, w[l, r0:r0 + G * pp, c0:c0 + ncol].rearrange("(g p) c -> p g c", p=pp))
        return s

    def rstd_of(ss_b, ss_ap, n, out_b, out_ap):
        act(out_b, out_ap, ss_b, ss_ap, AF.Sqrt, bias=epsT[:, 0:1], scale=1.0 / n, extra=[epsT])
        recip(out_b, out_ap, out_b, out_ap)

    def rms_full(xb, xap, gb, ob, oap, n):
        act(tm1b_junk, tm1b_junk[:, 0:n], xb, xap, AF.Square, accum=(sm[0], sm[0][:, 0:1]))
        rstd_of(sm[0], sm[0][:, 0:1], n, sm[0], sm[0][:, 1:2])
        stt(ob, oap, xb, xap, sm[0][:, 1:2], gb, gb[:, 0:n], ALU.mult, ALU.mult, extra=[sm[0]])

    tm1b_junk = sb("sqjunk", [128, D], BF16)

    def rms_heads(xb, xap, H, hd, gb, ob, oap):
        n = H * hd
        tt(tm1b_junk, tm1b_junk[:, 0:n], xb, xap, xb, xap, ALU.mult)
        red(sm[1], sm[1][:, 0:H], tm1b_junk, tm1b_junk[:, 0:n].rearrange("p (h d) -> p h d", h=H), ALU.add)
        rstd_of(sm[1], sm[1][:, 0:H], hd, sm[1], sm[1][:, 8:8 + H])
        xv = xap.rearrange("p (h d) -> p h d", h=H); ov = oap.rearrange("p (h d) -> p h d", h=H)
        tt(ob, ov, xb, xv, sm[1], bc_last(sm[1][:, 8:8 + H], hd), ALU.mult)
        tt(ob, ov, ob, ov, gb, bc_mid(gb[:, 0:hd], H), ALU.mult)

    def rope(xb, xap, H, hd):
        xv = xap.rearrange("p (h d) -> p h d", h=H)
        x1 = xv[:, :, 0:8]; x2 = xv[:, :, 8:16]
        cs = bc_mid(ropet[:, 0:8], H); sn = bc_mid(ropet[:, 8:16], H)
        t1 = tm1[:, 0:H * 8].rearrange("p (h d) -> p h d", h=H); t2 = tm1[:, 64:64 + H * 8].rearrange("p (h d) -> p h d", h=H)
        t3 = tm1[:, 128:128 + H * 8].rearrange("p (h d) -> p h d", h=H); t4 = tm1[:, 192:192 + H * 8].rearrange("p (h d) -> p h d", h=H)
        tt(tm1, t1, xb, x1, ropet, cs, ALU.mult); tt(tm1, t2, xb, x2, ropet, sn, ALU.mult)
        tt(tm1, t3, xb, x2, ropet, cs, ALU.mult); tt(tm1, t4, xb, x1, ropet, sn, ALU.mult)
        tt(xb, x1, tm1, t1, tm1, t2, ALU.subtract); tt(xb, x2, tm1, t3, tm1, t4, ALU.add)

    def to_hT(src_b, src_ap, dstT, col0):
        for half in range(2):
            for j in range(4):
                kgi = half * 4 + j
                tr(psT, psT[:, j * 128:(j + 1) * 128], src_b, src_ap[:, kgi * 128:(kgi + 1) * 128], identb)
            cp(dstT, dstT[:, half * 4:half * 4 + 4, col0:col0 + 128],
               psT, psT[:, 0:512].rearrange("p (g t) -> p g t", g=4), eng="act")

    def tm2fm(dst_b, dst_fn, src_b, src_ap, G, W):
        for g in range(G):
            dma(dst_b, dst_fn(g), src_b, src_ap[:, g * 128:(g + 1) * 128].rearrange("w c -> c w"), nonc=True)

    def fm2tm(dst_b, dst_ap, src_b, src_fn, G, W):
        toks = []
        for g in range(G):
            toks.append(dma(dst_b, dst_ap[:, g * 128:(g + 1) * 128].rearrange("w c -> c w"), src_b, src_fn(g), nonc=True))
        return toks

    def bcast_row(dst_b, n, src_b, row_ap):
        dma(dst_b, dst_b[:, 0:n], src_b, row_ap.to_broadcast([128, n]))

    def load_params(l):
        bcast_row(gmix, D, I["norm_mix_g"], I["norm_mix_g"][l:l + 1, :])
        bcast_row(gffn, D, I["norm_ffn_g"], I["norm_ffn_g"][l:l + 1, :])
        bcast_row(gmem, D, I["mem_norm_g"], I["mem_norm_g"][l:l + 1, :])
        bcast_row(lnag, 512, I["ln_a_g"], I["ln_a_g"][l:l + 1, :]); bcast_row(lnab, 512, I["ln_a_b"], I["ln_a_b"][l:l + 1, :])
        bcast_row(ssmg, 512, I["ssm_norm_g"], I["ssm_norm_g"][l:l + 1, :])
        bcast_row(qg, 64, I["q_norm_g"], I["q_norm_g"][l:l + 1, :]); bcast_row(kg, 64, I["k_norm_g"], I["k_norm_g"][l:l + 1, :])
        bcast_row(mqg, 128, I["mq_norm_g"], I["mq_norm_g"][l:l + 1, :]); bcast_row(mkg, 128, I["mk_norm_g"], I["mk_norm_g"][l:l + 1, :])
        bcast_row(dtb, 8, I["dt_bias"], I["dt_bias"][l:l + 1, :]); bcast_row(dsk, 8, I["d_skip"], I["d_skip"][l:l + 1, :])
        bcast_row(aneg, 8, I["a_log"], I["a_log"][l:l + 1, :])
        act(aneg, aneg[:], aneg, aneg[:], AF.Exp)
        ts(aneg, aneg[:], aneg, aneg[:], -1.0, ALU.mult)
        tm2fm(cwa, lambda g: cwa[:, g, :], I["conv_a_w"], I["conv_a_w"][l], 4, 31)
        tm2fm(cws, lambda g: cws[:, g, :], I["ssm_conv_w"], I["ssm_conv_w"][l], 6, 4)
        tm2fm(cwf, lambda g: cwf[:, g, :], I["ffn_conv_w"], I["ffn_conv_w"][l], 44, 3)
        dma(cba, cba[:], I["conv_a_b"], I["conv_a_b"][l].rearrange("(g c) -> c g", c=128), nonc=True)
        dma(cbs, cbs[:], I["ssm_conv_b"], I["ssm_conv_b"][l].rearrange("(g c) -> c g", c=128), nonc=True)
        dma(cbf, cbf[:], I["ffn_conv_b"], I["ffn_conv_b"][l].rearrange("(g c) -> c g", c=128), nonc=True)

    def dwconv(ub, cb, G, W, wb, bb, Tn):
        for g in range(G):
            ts(cb[g], cb[g][:, 0:Tn], ub, ub[:, g, 0:Tn], wb[:, g, 0:1], ALU.mult, bb[:, g:g + 1], ALU.add, extra=[wb, bb])
        for j in range(1, W):
            for g in range(G):
                stt(cb[g], cb[g][:, 0:Tn], ub, ub[:, g, j:j + Tn], wb[:, g, j:j + 1], cb[g], cb[g][:, 0:Tn], ALU.mult, ALU.add, extra=[wb])

    def mem_kv_prompt(l):
        for mt in range(2):
            dma(stio, stio[:, 0:512], I["memp"], I["memp"][mt * 128:(mt + 1) * 128, 0:512])
            dma(tm2, tm2[:], I["memp"], I["memp"][mt * 128:(mt + 1) * 128, 512:1024])
            act(tm1b_junk, tm1b_junk[:, 0:512], stio, stio[:, 0:512], AF.Square, accum=(sm[2], sm[2][:, 0:1]))
            act(tm1b_junk, tm1b_junk[:, 512:1024], tm2, tm2[:], AF.Square, accum=(sm[2], sm[2][:, 1:2]))
            tt(sm[2], sm[2][:, 2:3], sm[2], sm[2][:, 0:1], sm[2], sm[2][:, 1:2], ALU.add)
            rstd_of(sm[2], sm[2][:, 2:3], D, sm[2], sm[2][:, 3:4])
            stt(hb, hb[:, 0:512], stio, stio[:, 0:512], sm[2][:, 3:4], gmem, gmem[:, 0:512], ALU.mult, ALU.mult, extra=[sm[2]])
            stt(hb, hb[:, 512:1024], tm2, tm2[:], sm[2][:, 3:4], gmem, gmem[:, 512:1024], ALU.mult, ALU.mult, extra=[sm[2]])
            to_hT(hb, hb, hT, 0)
            for c in range(2):
                w = wload("w_mem_kv", l, 0, 8, c * 512, 512)
                p = nps()
                for k in range(8):
                    mm(p, p[:], hT, hT[:, k, 0:128], w, w[:, k, 0:512], start=(k == 0), stop=(k == 7), signal=(k == 7))
                if c == 0:
                    cp(tm1, tm1[:], p, p[:], eng="act")
                    rms_heads(tm1, tm1[:], 4, 128, mkg, stio, stio[:, 0:512])
                    outtoks.append(dma(O["mk_p"], O["mk_p"][l, mt * 128:(mt + 1) * 128, :], stio, stio[:, 0:512]))
                    cp(tmb, tmb[:], stio, stio[:, 0:512])
                    for h in range(4):
                        tr(psT, psT[:, h * 128:(h + 1) * 128], tmb, tmb[:, h * 128:(h + 1) * 128], identb)
                    cp(mkT, mkT[:, :, mt * 128:(mt + 1) * 128], psT, psT[:, 0:512].rearrange("p (h t) -> p h t", h=4), eng="act")
                else:
                    cp(tm1, tm1[:], p, p[:], eng="act")
                    outtoks.append(dma(O["mv_p"], O["mv_p"][l, mt * 128:(mt + 1) * 128, :], tm1, tm1[:]))
                    cp(mvb, mvb[:, mt, :], tm1, tm1[:])

    def mem_kv_sample(l, j):
        for mt in range(2):
            dma(tm1, tm1[:], I["cmk"], I["cmk"][l, j, mt * 128:(mt + 1) * 128, :])
            dma(tm2, tm2[:], I["cmv"], I["cmv"][l, j, mt * 128:(mt + 1) * 128, :])
            cp(tmb, tmb[:], tm1, tm1[:])
            for h in range(4):
                tr(psT, psT[:, h * 128:(h + 1) * 128], tmb, tmb[:, h * 128:(h + 1) * 128], identb)
            cp(mkT, mkT[:, :, mt * 128:(mt + 1) * 128], psT, psT[:, 0:512].rearrange("p (h t) -> p h t", h=4), eng="act")
            cp(mvb, mvb[:, mt, :], tm2, tm2[:])

    class StopM(Exception):
        pass
    mstop = 99.0

    def chk(n):
        if mstop <= n:
            raise StopM()

    def macro(l, ctx):
        kind = ctx["kind"]; nsub = ctx["nsub"]; Tn = nsub * 128; tv = ctx["tv"]; pos0 = ctx["pos0"]
        samp = (kind == "s"); j = ctx.get("j", 0)
        tokm = tokms if samp else tokm1
        last_layer = (l == DEPTH - 1)
        for s in range(nsub):
            if samp:
                mset(x_mt, x_mt[:, s, :], 0.0, eng="dve")
                src = I["xs"] if l == 0 else xsres
                dma(x_mt, x_mt[0:1, s, :], src, src[j:j + 1, :])
            else:
                src = I["xp"] if l == 0 else xres
                dma(x_mt, x_mt[:, s, :], src, src[pos0 + s * 128: pos0 + (s + 1) * 128, :])
            rms_full(x_mt, x_mt[:, s, :], gmix, hb, hb[:], D)
            to_hT(hb, hb, hT, s * 128)
        def tm_seg(c_lo, c_hi, dsts, off0):
            c = c_lo
            while c < c_hi:
                n = min(512, c_hi - c)
                w = wload("w_in", l, 0, 8, c, n)
                for s in range(nsub):
                    p = nps()
                    for k in range(8):
                        mm(p, p[:, 0:n], hT, hT[:, k, s * 128:(s + 1) * 128], w, w[:, k, 0:n], start=(k == 0), stop=(k == 7), signal=(k == 7))
                    cp(dsts[s], dsts[s][:, off0 + c - c_lo: off0 + c - c_lo + n], p, p[:, 0:n], eng="act")
                c += n
        chk(1)
        tm_seg(C_Q, C_XBC, projA, 0)
        tm_seg(C_DT, C_G, projB, 0)
        chk(2)
        if ctx["first"]:
            if samp:
                tm2fm(glu_u, lambda g: glu_u[:, g, 0:30], I["sconf"], I["sconf"][l, j], 4, 30)
                tm2fm(xbc_u, lambda g: xbc_u[:, g, 0:3], I["ssc"], I["ssc"][l, j], 6, 3)
                tm2fm(fcar, lambda g: fcar[:, g, :], I["sffn"], I["sffn"][l, j], 44, 2)
            else:
                mset(glu_u, glu_u[:, :, 0:30], 0.0, eng="dve"); mset(xbc_u, xbc_u[:, :, 0:3], 0.0, eng="dve")
                mset(fcar, fcar[:], 0.0, eng="dve")
        wv = wload("w_in", l, 0, 8, 0, 512); wg = wload("w_in", l, 0, 8, 512, 512)
        for c in range(4):
            pv = nps(); pg = nps()
            for k in range(8):
                mm(pv, pv[:, 0:Tn], wv, wv[:, k, c * 128:(c + 1) * 128], hT, hT[:, k, 0:Tn], start=(k == 0), stop=(k == 7), signal=(k == 7))
            for k in range(8):
                mm(pg, pg[:, 0:Tn], wg, wg[:, k, c * 128:(c + 1) * 128], hT, hT[:, k, 0:Tn], start=(k == 0), stop=(k == 7), signal=(k == 7))
            act(sgt, sgt[:, 0:Tn], pg, pg[:, 0:Tn], AF.Sigmoid)
            tt(glu_u, glu_u[:, c, 30:30 + Tn], pv, pv[:, 0:Tn], sgt, sgt[:, 0:Tn], ALU.mult)
        for (c0, ng) in ((0, 4), (4, 2)):
            w = wload("w_in", l, 0, 8, C_XBC + c0 * 128, ng * 128)
            for c in range(ng):
                p = nps()
                for k in range(8):
                    mm(p, p[:, 0:Tn], w, w[:, k, c * 128:(c + 1) * 128], hT, hT[:, k, 0:Tn], start=(k == 0), stop=(k == 7), signal=(k == 7))
                cp(xbc_u, xbc_u[:, c0 + c, 3:3 + Tn], p, p[:, 0:Tn], eng="act")
        chk(3)
        if ctx["last"]:
            if samp:
                outtoks.extend(fm2tm(O["conf_s"], O["conf_s"][l, j], glu_u, lambda g: glu_u[:, g, tv:tv + 30], 4, 30))
                outtoks.extend(fm2tm(O["sc_s"], O["sc_s"][l, j], xbc_u, lambda g: xbc_u[:, g, tv:tv + 3], 6, 3))
            else:
                outtoks.extend(fm2tm(O["conf_p"], O["conf_p"][l], glu_u, lambda g: glu_u[:, g, tv:tv + 30], 4, 30))
                outtoks.extend(fm2tm(O["sc_p"], O["sc_p"][l], xbc_u, lambda g: xbc_u[:, g, tv:tv + 3], 6, 3))
        chk(4)
        dwconv(glu_u, glu_cg, 4, 31, cwa, cba, Tn)
        dwconv(xbc_u, xbc_cg, 6, 4, cws, cbs, Tn)
        act(xbc_c, xbc_c[:, :, 0:Tn], xbc_c, xbc_c[:, :, 0:Tn], AF.Silu)
        if not ctx["last"]:
            cp(glu_u, glu_u[:, :, 0:30], glu_u, glu_u[:, :, Tn:Tn + 30])
            cp(xbc_u, xbc_u[:, :, 0:3], xbc_u, xbc_u[:, :, Tn:Tn + 3])

        for s in range(nsub):
            cs = slice(s * 128, (s + 1) * 128)
            pA, pB = projA[s], projB[s]
            chk(5)
            p = nps()
            for c in range(4):
                tr(p, p[:, c * 128:(c + 1) * 128], glu_c, glu_c[:, c, cs], identf)
            P.op("dve", lambda e, p=p: e.bn_stats(out=sm[3][:, 0:6], in_=p[:, 0:512]), reads=[p], writes=[sm[3]])
            P.op("dve", lambda e: e.bn_aggr(out=sm[3][:, 6:8], in_=sm[3][:, 0:6]), reads=[sm[3]], writes=[sm[3]])
            act(sm[3], sm[3][:, 8:9], sm[3], sm[3][:, 7:8], AF.Sqrt, bias=epsT[:, 0:1], scale=1.0, extra=[epsT])
            recip(sm[3], sm[3][:, 8:9], sm[3], sm[3][:, 8:9])
            ts(tm1, tm1[:], p, p[:], sm[3][:, 6:7], ALU.subtract, sm[3][:, 8:9], ALU.mult, extra=[sm[3]])
            tt(tm1, tm1[:], tm1, tm1[:], lnag, lnag[:], ALU.mult)
            tt(tm1, tm1[:], tm1, tm1[:], lnab, lnab[:], ALU.add)
            act(tmb, tmb[:], tm1, tm1[:], AF.Silu)
            for c in range(4):
                tr(psT, psT[:, c * 128:(c + 1) * 128], tmb, tmb[:, c * 128:(c + 1) * 128], identb)
            cp(brT[0], brT[0][:, :, cs], psT, psT[:, 0:512].rearrange("p (g t) -> p g t", g=4), eng="act")

            chk(6)
            p = nps()
            for c in range(4):
                tr(p, p[:, c * 128:(c + 1) * 128], xbc_c, xbc_c[:, c, cs], identf)
            cp(xs_tm, xs_tm[:], p, p[:], eng="act")
            cp(BT, BT[:], xbc_c, xbc_c[:, 4, cs]); cp(CT, CT[:], xbc_c, xbc_c[:, 5, cs])
            for g in range(2):
                cp(CTm[g], CTm[g][g * 64:(g + 1) * 64, :], xbc_c, xbc_c[g * 64:(g + 1) * 64, 5, cs])
            tr(psT, psT[:, 0:128], BT, BT[:], identb)
            cp(B_tm, B_tm[:], psT, psT[:, 0:128], eng="act")
            d0 = sm[4]
            tt(d0, d0[:, 0:8], pB, pB[:, 0:8], dtb, dtb[:], ALU.add)
            ts(d0, d0[:, 8:16], d0, d0[:, 0:8], -1.0, ALU.mult)
            tt(d0, d0[:, 8:16], d0, d0[:, 8:16], d0, d0[:, 0:8], ALU.max)
            act(d0, d0[:, 8:16], d0, d0[:, 8:16], AF.Exp, scale=-1.0)
            act(d0, d0[:, 8:16], d0, d0[:, 8:16], AF.Ln, bias=oneT[:, 0:1], scale=1.0, extra=[oneT])
            stt(d0, d0[:, 0:8], d0, d0[:, 0:8], 0.0, d0, d0[:, 8:16], ALU.max, ALU.add)
            ts(d0, d0[:, 0:8], d0, d0[:, 0:8], tokm[:, 0:1], ALU.mult, extra=[tokm])
            tt(d0, d0[:, 8:16], d0, d0[:, 0:8], aneg, aneg[:], ALU.mult)
            a1 = sm[5]
            pa = nps()
            mm(pa, pa[:, 0:8], trif, trif[:], d0, d0[:, 8:16])
            cp(a1, a1[:, 0:8], pa, pa[:, 0:8])
            pa = nps()
            mm(pa, pa[:, 0:8], elast, elast[:], a1, a1[:, 0:8])
            cp(a1, a1[:, 8:16], pa, pa[:, 0:8])
            e1 = sm[6]
            act(e1, e1[:, 0:8], a1, a1[:, 0:8], AF.Exp)
            act(e1, e1[:, 8:16], a1, a1[:, 8:16], AF.Exp)
            tt(sm[7], sm[7][:, 0:8], a1, a1[:, 8:16], a1, a1[:, 0:8], ALU.subtract)
            act(sm[7], sm[7][:, 0:8], sm[7], sm[7][:, 0:8], AF.Exp)
            tt(sm[7], sm[7][:, 8:16], sm[7], sm[7][:, 0:8], d0, d0[:, 0:8], ALU.mult)
            xv = xs_tm[:].rearrange("p (h d) -> p h d", h=8)
            tt(xdt, xdt[:].rearrange("p (h d) -> p h d", h=8), xs_tm, xv, d0, bc_last(d0[:, 0:8], 64), ALU.mult)
            tt(xdtd, xdtd[:].rearrange("p (h d) -> p h d", h=8), xs_tm, xv, sm[7], bc_last(sm[7][:, 8:16], 64), ALU.mult)
            chk(6.1)
            pg_ = nps()
            for g in range(2):
                mm(pg_, pg_[:, g * 128:(g + 1) * 128], BT, BT[:], CTm[g], CTm[g][:])
            cp(GT, GT[:], pg_, pg_[:, 0:256].rearrange("p (g t) -> p g t", g=2), eng="act")
            tt(diag, diag[:], identf, bc_mid(identf[:], 8), a1, bc_last(a1[:, 0:8], 128), ALU.mult)
            for h in range(8):
                mm(psW, psW[:, h * 128:(h + 1) * 128], onesf, onesf[:], diag, diag[:, h, :])
            tt(seg, seg[:], psW, psW[:].rearrange("p (h t) -> p h t", h=8), a1, bc_last(a1[:, 0:8], 128), ALU.subtract)
            ts(seg, seg[:], seg, seg[:], 0.0, ALU.min)
            act(seg, seg[:], seg, seg[:], AF.Exp)
            tt(seg, seg[:].rearrange("p (g h) t -> p g h t", g=2), seg, seg[:].rearrange("p (g h) t -> p g h t", g=2),
               GT, GT[:].unsqueeze(2).to_broadcast([128, 2, 4, 128]), ALU.mult)
            tt(scT, scT[:], seg, seg[:], trif, bc_mid(trif[:], 8), ALU.mult)
            chk(6.2)
            py = nps()
            for h in range(8):
                mm(py, py[:, h * 64:(h + 1) * 64], scT, scT[:, h, :], xdt, xdt[:, h * 64:(h + 1) * 64])
            cp(ybuf, ybuf[:], py, py[:], eng="act")
            if ctx["first"] and s == 0:
                if samp:
                    for g in range(2):
                        dma(hio, hio[:].rearrange("p (hh g n) -> p hh g n", hh=4, g=2)[:, :, g, :], I["sssm"],
                            I["sssm"][l, j, g * 4:(g + 1) * 4].rearrange("hh p n -> p hh n"))
                    for hh in range(4):
                        pq = nps()
                        tr(pq, pq[:, 0:64], hio, hio[:, hh * 128:(hh + 1) * 128], identf)
                        cp(hst, hst[:, hh, :], pq, pq[:, 0:64])
                else:
                    mset(hst, hst[:], 0.0, eng="dve")
                cp(hstb, hstb[:], hst, hst[:])
            po = nps()
            for h in range(8):
                g = h // 4
                mm(po, po[:, h * 64:(h + 1) * 64], CTm[g], CTm[g][:], hstb, hstb[:, h % 4, :])
            tt(tm1, tm1[:].rearrange("p (h d) -> p h d", h=8), po, po[:].rearrange("p (h d) -> p h d", h=8),
               e1, bc_last(e1[:, 0:8], 64), ALU.mult)
            tt(ybuf, ybuf[:], ybuf, ybuf[:], tm1, tm1[:], ALU.add)
            tt(tm1, tm1[:].rearrange("p (h d) -> p h d", h=8), xs_tm, xv, dsk, bc_last(dsk[:], 64), ALU.mult)
            tt(ybuf, ybuf[:], ybuf, ybuf[:], tm1, tm1[:], ALU.add)
            chk(6.3)
            pst = nps()
            for h in range(8):
                mm(pst, pst[:, h * 64:(h + 1) * 64], B_tm, B_tm[:], xdtd, xdtd[:, h * 64:(h + 1) * 64])
            for g in range(2):
                r_ = slice(g * 64, (g + 1) * 64)
                tt(hst, hst[r_, :, :], hst, hst[r_, :, :], e1, bc_last(e1[r_, 8 + g * 4: 12 + g * 4], 64), ALU.mult)
                tt(hst, hst[r_, :, :], hst, hst[r_, :, :], pst,
                   pst[r_, g * 256:(g + 1) * 256].rearrange("p (h d) -> p h d", h=4), ALU.add)
            cp(hstb, hstb[:], hst, hst[:])
            if ctx["last"] and s == nsub - 1:
                for hh in range(4):
                    pq = nps()
                    tr(pq, pq[0:64, 0:128], hst, hst[:, hh, :], identf)
                    cp(hio, hio[:, hh * 128:(hh + 1) * 128], pq, pq[0:64, 0:128])
                od = O["ssm_s"][l, j] if samp else O["ssm_p"][l]
                for g in range(2):
                    outtoks.append(dma(O["ssm_s"] if samp else O["ssm_p"], od[g * 4:(g + 1) * 4].rearrange("hh p n -> p hh n"),
                                       hio, hio[:].rearrange("p (hh g n) -> p hh g n", hh=4, g=2)[:, :, g, :]))
            chk(6.4)
            act(tm1, tm1[:], pA, pA[:, C_Z - C_Q: C_Z - C_Q + 512], AF.Silu)
            tt(ybuf, ybuf[:], ybuf, ybuf[:], tm1, tm1[:], ALU.mult)
            rms_full(ybuf, ybuf[:], ssmg, tmb, tmb[:], 512)
            for c in range(4):
                tr(psT, psT[:, c * 128:(c + 1) * 128], tmb, tmb[:, c * 128:(c + 1) * 128], identb)
            cp(brT[1], brT[1][:, :, cs], psT, psT[:, 0:512].rearrange("p (g t) -> p g t", g=4), eng="act")

            chk(7)
            rms_heads(pB, pB[:, 8:520], 4, 128, mqg, tm1, tm1[:])
            cp(tmb, tmb[:], tm1, tm1[:])
            for h in range(4):
                tr(psT, psT[:, h * 128:(h + 1) * 128], tmb, tmb[:, h * 128:(h + 1) * 128], identb)
            cp(mqT, mqT[:], psT, psT[:, 0:512].rearrange("p (h t) -> p h t", h=4), eng="act")
            for mt in range(2):
                for h in range(4):
                    mm(psW, psW[:, mt * 512 + h * 128: mt * 512 + (h + 1) * 128], mkT, mkT[:, h, mt * 128:(mt + 1) * 128], mqT, mqT[:, h, :])
            act(PT, PT[:].rearrange("p a b -> p (a b)"), psW, psW[:], AF.Exp, scale=128 ** -0.5)
            pO = nps(); pD = nps()
            for h in range(4):
                for mt in range(2):
                    mm(pO, pO[:, h * 128:(h + 1) * 128], mvb, mvb[:, mt, h * 128:(h + 1) * 128], PT, PT[:, mt, h * 128:(h + 1) * 128],
                       start=(mt == 0), stop=(mt == 1))
            for mt in range(2):
                mm(pD, pD[:], onesb, onesb[:], PT, PT[:, mt, :], start=(mt == 0), stop=(mt == 1))
            recip(rden, rden[:], pD, pD[:])
            tt(brT[2], brT[2][:, :, cs], pO, pO[:].rearrange("p (h t) -> p h t", h=4), rden, rden[:].rearrange("p (h t) -> p h t", h=4), ALU.mult)

            chk(8)
            if samp:
                bcast_row(ropet, 16, I["ropes"], I["ropes"][0:1, :])
            else:
                dma(ropet, ropet[:], I["ropep"], I["ropep"][pos0 + s * 128: pos0 + (s + 1) * 128, :])
            rms_heads(pA, pA[:, 0:512], 8, 64, qg, qn, qn[:])
            rope(qn, qn[:], 8, 64)
            chk(8.05)
            for kv in range(2):
                cp(qbf, qbf[:].rearrange("p (g kv d) -> p g kv d", g=4, kv=2)[:, :, kv, :],
                   qn, qn[:].rearrange("p (kv g d) -> p kv g d", kv=2, g=4)[:, kv, :, :])
            for g in range(4):
                tr(psT, psT[:, g * 128:(g + 1) * 128], qbf, qbf[:, g * 128:(g + 1) * 128], identb)
            cp(qT, qT[:], psT, psT[:, 0:512].rearrange("p (g t) -> p g t", g=4), eng="act")
            chk(8.07)
            for kv in range(2):
                cp(qTm[kv], qTm[kv][kv * 64:(kv + 1) * 64, :, :], qT, qT[kv * 64:(kv + 1) * 64, :, :])
            chk(8.1)
            rms_heads(pA, pA[:, C_K - C_Q: C_K - C_Q + 128], 2, 64, kg, kout, kout[:])
            rope(kout, kout[:], 2, 64)
            if samp:
                outtoks.append(dma(O["k_s"], O["k_s"][l, j:j + 1, :], kout, kout[0:1, :]))
                outtoks.append(dma(O["v_s"], O["v_s"][l, j:j + 1, :], pA, pA[0:1, C_V - C_Q: C_V - C_Q + 128]))
            else:
                outtoks.append(dma(O["k_p"], O["k_p"][l, pos0 + s * 128: pos0 + (s + 1) * 128, :], kout, kout[:]))
                outtoks.append(dma(O["v_p"], O["v_p"][l, pos0 + s * 128: pos0 + (s + 1) * 128, :], pA, pA[:, C_V - C_Q: C_V - C_Q + 128]))
            cp(kbf, kbf[:], kout, kout[:])
            tr(psT, psT[:, 0:128], kbf, kbf[:], identb)
            cp(kTs, kTs[:], psT, psT[:, 0:128], eng="act")
            hp = (PAST if samp else pos0 + s * 128)
            dma(h_kT, h_kT[:, hp:hp + 128], kTs, kTs[:])
            cp(vbuf, vbuf[:, :, 0:64], pA, pA[:, C_V - C_Q: C_V - C_Q + 128].rearrange("p (kv d) -> p kv d", kv=2))
            dma(h_v, h_v[hp:hp + 128, :], vbuf, vbuf[:].rearrange("p a b -> p (a b)"))
            chk(8.2)
            cp(kiout, kiout[:], pA, pA[:, C_KI - C_Q: C_KI - C_Q + 64])
            rope(kiout, kiout[:], 1, 64)
            if samp:
                outtoks.append(dma(O["ki_s"], O["ki_s"][l, j:j + 1, :], kiout, kiout[0:1, :]))
            else:
                outtoks.append(dma(O["ki_p"], O["ki_p"][l, pos0 + s * 128: pos0 + (s + 1) * 128, :], kiout, kiout[:]))
            cp(kibf, kibf[:], kiout, kiout[:])
            tr(psT, psT[0:64, 0:128], kibf, kibf[:], identb)
            cp(kiTs, kiTs[:], psT, psT[0:64, 0:128], eng="act")
            dma(h_kiT, h_kiT[:, hp:hp + 128], kiTs, kiTs[:])
            chk(8.3)
            cp(qn, qn[:], pA, pA[:, C_QI - C_Q: C_QI - C_Q + 512])
            rope(qn, qn[:], 8, 64)
            cp(qbf, qbf[:], qn, qn[:])
            for half in range(2):
                for h4 in range(4):
                    h = half * 4 + h4
                    tr(psT, psT[0:64, h4 * 128:(h4 + 1) * 128], qbf, qbf[:, h * 64:(h + 1) * 64], identb)
                cp(qiT, qiT[:, half * 4:half * 4 + 4, :], psT, psT[0:64, 0:512].rearrange("p (h t) -> p h t", h=4), eng="act")
            wi_ap = pA[:, C_WI - C_Q: C_WI - C_Q + 8]
            ts(awi, awi[:], pA, wi_ap, -1.0, ALU.mult)
            tt(awi, awi[:], awi, awi[:], pA, wi_ap, ALU.max)
            act(swi, swi[:], pA, wi_ap, AF.Sign)
            ts(swi, swi[:], swi, swi[:], IDX_SCALE, ALU.mult)
            chk(9)
            nkt = hp // 128 + 1
            nkeys = nkt * 128
            for c0 in range(0, nkeys, 1024):
                n = min(1024, nkeys - c0)
                kb = kic[rot["kic"] % 2]; rot["kic"] += 1
                dma(kb, kb[:, 0:n], h_kiT, h_kiT[:, c0:c0 + n])
                for h in range(8):
                    S = SS[rot["rb"] % 2]
                    for b0 in range(0, n, 512):
                        bn = min(512, n - b0)
                        mm(S, S[:, b0:b0 + bn], qiT, qiT[:, h, :], kb, kb[:, b0:b0 + bn])
                    rb = rbuf[rot["rb"] % 2]; rot["rb"] += 1
                    act(rb, rb[:, 0:n], S, S[:, 0:n], AF.Relu, scale=awi[:, h:h + 1], extra=[awi])
                    if h == 0:
                        ts(Isc, Isc[:, c0:c0 + n], rb, rb[:, 0:n], swi[:, 0:1], ALU.mult, extra=[swi])
                    else:
                        stt(Isc, Isc[:, c0:c0 + n], rb, rb[:, 0:n], swi[:, h:h + 1], Isc, Isc[:, c0:c0 + n], ALU.mult, ALU.add, extra=[swi])
            am = sadd if samp else cadd
            tt(Isc, Isc[:, nkeys - 128:nkeys], Isc, Isc[:, nkeys - 128:nkeys], am, am[:], ALU.add)
            chk(10)
            KSEL = cfg.KS if samp else cfg.KP
            split = (nkt >= SPLIT_MIN_NKT)
            n1 = int(round(nkt * 0.42)) * 128 if split else nkeys
            n2 = nkeys - n1
            if nkeys - 128 >= KSEL:
                red(lo, lo[:], Isc, Isc[:, 0:nkeys - 128], ALU.min)
                red(hi, hi[:], Isc, Isc[:, 0:nkeys], ALU.max)
                thr = float(KSEL) - 0.5 * n2
                for it in range(NITER):
                    ts(mid, mid[:], lo, lo[:], hi[:, 0:1], ALU.add, 0.5, ALU.mult, extra=[hi])
                    if split:
                        ts(nmid, nmid[:], mid, mid[:], -1.0, ALU.mult)
                        act(junkA, junkA[:, 0:n2], Isc, Isc[:, n1:nkeys], AF.Sign, bias=nmid[:, 0:1], scale=1.0,
                            accum=(cnt2, cnt2[:]), extra=[nmid])
                    ts(junkD, junkD[:, 0:n1], Isc, Isc[:, 0:n1], mid[:, 0:1], ALU.is_ge, 0.0, ALU.add,
                       accum=(cnt, cnt[:]), extra=[mid])
                    if split:
                        stt(cnt, cnt[:], cnt2, cnt2[:], 0.5, cnt, cnt[:], ALU.mult, ALU.add)
                    ts(pge, pge[:], cnt, cnt[:], thr, ALU.is_ge)
                    ts(plt, plt[:], cnt, cnt[:], thr, ALU.is_lt)
                    P.op("dve", lambda e: e.copy_predicated(out=lo[:], mask=pge[:], data=mid[:]), reads=[pge, mid], writes=[lo])
                    P.op("dve", lambda e: e.copy_predicated(out=hi[:], mask=plt[:], data=mid[:]), reads=[plt, mid], writes=[hi])
                ts(lo, lo[:], lo, lo[:], NEG / 2, ALU.max)
            else:
                mset(lo, lo[:], NEG / 2, eng="dve")
            ts(junkD, junkD[:, 0:n1], Isc, Isc[:, 0:n1], lo[:, 0:1], ALU.is_ge, extra=[lo])
            if n2 > 0:
                ts(junkA, junkA[:, 0:n2], Isc, Isc[:, n1:nkeys], lo[:, 0:1], ALU.is_ge, extra=[lo])

            def mask_ap(kt):
                if kt * 128 < n1:
                    return junkD, junkD[:, kt * 128:(kt + 1) * 128]
                return junkA, junkA[:, kt * 128 - n1:(kt + 1) * 128 - n1]
            chk(11)
            grp = {}

            def setup_group(g):
                k0 = g * 4
                nk = min(4, nkt - k0)
                kb = kTc[rot["kt"] % 2]; vb = vc[rot["kt"] % 2]; rot["kt"] += 1
                dma(kb, kb[:, 0:nk * 128], h_kT, h_kT[:, k0 * 128:(k0 + nk) * 128])
                dma(vb, vb[:, 0:nk, :], h_v, h_v[k0 * 128:(k0 + nk) * 128, :].rearrange("(t p) c -> p t c", p=128))
                gi = rot["mg"] % 2; rot["mg"] += 1
                for kk in range(nk):
                    mb_, map_ = mask_ap(k0 + kk)
                    tr(psT, psT[:, kk * 128:(kk + 1) * 128], mb_, map_, identb)
                cp(mT4[gi], mT4[gi][:, 0:nk, :], psT, psT[:, 0:nk * 128].rearrange("p (g t) -> p g t", g=nk), eng="act")
                grp[g] = (kb, vb, gi)

            def qk(kt):
                g, kk = divmod(kt, 4)
                if g not in grp:
                    setup_group(g)
                kb = grp[g][0]
                S = SS[kt % 2]
                for kv in range(2):
                    mm(S, S[:, kv * 512:(kv + 1) * 512], kb, kb[:, kk * 128:(kk + 1) * 128],
                       qTm[kv], qTm[kv][:].rearrange("p g t -> p (g t)"))

            qk(0)
            for kt in range(nkt):
                g, kk = divmod(kt, 4)
                if kk == 0 and (g + 1) * 4 < nkt:
                    setup_group(g + 1)
                if kt + 1 < nkt:
                    qk(kt + 1)
                kb, vb, gi = grp[g]
                i2 = kt % 2
                S = SS[i2]
                act(Eb[i2], Eb[i2][:].rearrange("p a b -> p (a b)"), S, S[:], AF.Exp, scale=0.125)
                tt(Pm[i2], Pm[i2][:], Eb[i2], Eb[i2][:], mT4[gi], bc_mid(mT4[gi][:, kk, :], 8), ALU.mult)
                for kv in range(2):
                    mm(psO[kv], psO[kv][:], vb, vb[:, kk, kv * 128:(kv + 1) * 128],
                       Pm[i2], Pm[i2][:, kv * 4:(kv + 1) * 4, :].rearrange("p g t -> p (g t)"),
                       start=(kt == 0), stop=(kt == nkt - 1))
            for kv in range(2):
                recip(rden, rden[64:128, :], psO[kv], psO[kv][64:128, :])
                dma(rdlo, rdlo[:], rden, rden[64:128, :])
                tt(brTb, brTb[:, kv * 4:(kv + 1) * 4, cs], psO[kv], psO[kv][0:64, :].rearrange("p (g t) -> p g t", g=4),
                   rdlo, rdlo[:].rearrange("p (g t) -> p g t", g=4), ALU.mult)

        chk(12)
        if DBG and not samp and l == 0:
            for n, bsrc_ in ((0, brT[0]), (2, brT[1]), (3, brT[2])):
                outtoks.append(dma(O["dbg"], O["dbg"][n, :, pos0:pos0 + Tn].rearrange("(g p) t -> p g t", p=128), bsrc_, bsrc_[:, :, 0:Tn]))
            outtoks.append(dma(O["dbg"], O["dbg"][1, :, pos0:pos0 + Tn].rearrange("(h p) t -> p h t", p=64), brTb, brTb[:, :, 0:Tn]))
        for n in range(4):
            for c in range(2):
                wg_ = wload("w_in", l, 0, 8, C_G + n * 1024 + c * 512, 512)
                if n == 1:
                    wb_ = wload("w_branch", l, 512, 8, c * 512, 512, pp=64)
                else:
                    wb_ = wload("w_branch", l, n * 512, 4, c * 512, 512)
                bsrc = {0: brT[0], 2: brT[1], 3: brT[2]}.get(n)
                for s in range(nsub):
                    cs = slice(s * 128, (s + 1) * 128)
                    pg = nps(); pb = nps()
                    for k in range(8):
                        mm(pg, pg[:], hT, hT[:, k, cs], wg_, wg_[:, k, 0:512], start=(k == 0), stop=(k == 7), signal=(k == 7))
                    if n == 1:
                        for k in range(8):
                            mm(pb, pb[:], brTb, brTb[:, k, cs], wb_, wb_[0:64, k, 0:512], start=(k == 0), stop=(k == 7), signal=(k == 7))
                    else:
                        for k in range(4):
                            mm(pb, pb[:], bsrc, bsrc[:, k, cs], wb_, wb_[:, k, 0:512], start=(k == 0), stop=(k == 3), signal=(k == 3))
                    act(tm2, tm2[:], pg, pg[:], AF.Sigmoid)
                    mx = mixed[s]
                    if n == 0:
                        tt(mx, mx[:, c * 512:(c + 1) * 512], tm2, tm2[:], pb, pb[:], ALU.mult)
                    else:
                        tt(tm2, tm2[:], tm2, tm2[:], pb, pb[:], ALU.mult)
                        tt(mx, mx[:, c * 512:(c + 1) * 512], mx, mx[:, c * 512:(c + 1) * 512], tm2, tm2[:], ALU.add)
        for s in range(nsub):
            cp(hb, hb[:], mixed[s], mixed[s][:])
            to_hT(hb, hb, hT, s * 128)
        for c in range(2):
            w = wload("w_out", l, 0, 8, c * 512, 512)
            for s in range(nsub):
                p = nps()
                for k in range(8):
                    mm(p, p[:], hT, hT[:, k, s * 128:(s + 1) * 128], w, w[:, k, 0:512], start=(k == 0), stop=(k == 7), signal=(k == 7))
                tt(x_mt, x_mt[:, s, c * 512:(c + 1) * 512], x_mt, x_mt[:, s, c * 512:(c + 1) * 512], p, p[:], ALU.add)
        chk(13)
        if DBG and not samp and l == 0:
            for s in range(nsub):
                outtoks.append(dma(O["dbgx"], O["dbgx"][pos0 + s * 128: pos0 + (s + 1) * 128, :], x_mt, x_mt[:, s, :]))
        for s in range(nsub):
            rms_full(x_mt, x_mt[:, s, :], gffn, hb, hb[:], D)
            to_hT(hb, hb, hT, s * 128)
        for j0 in range(0, 22, 4):
            nj = min(4, 22 - j0)
            wgs = wload("w_ffn_up", l, 0, 8, j0 * 128, nj * 128)
            wus = wload("w_ffn_up", l, 0, 8, 2816 + j0 * 128, nj * 128)
            for jj in range(nj):
                jg = j0 + jj; ju = 22 + jg
                pg = nps(); pu = nps()
                for k in range(8):
                    mm(pg, pg[:, 0:Tn], wgs, wgs[:, k, jj * 128:(jj + 1) * 128], hT, hT[:, k, 0:Tn], start=(k == 0), stop=(k == 7), signal=(k == 7))
                for k in range(8):
                    mm(pu, pu[:, 0:Tn], wus, wus[:, k, jj * 128:(jj + 1) * 128], hT, hT[:, k, 0:Tn], start=(k == 0), stop=(k == 7), signal=(k == 7))
                for (st, pp_, jc, co) in ((fst_g, pg, jg, fcg), (fst_u, pu, ju, fcu)):
                    cp(st, st[:, 0:2], fcar, fcar[:, jc, :])
                    cp(st, st[:, 2:2 + Tn], pp_, pp_[:, 0:Tn], eng="act")
                    if ctx["last"]:
                        cp(fso, fso[:, jc, :], st, st[:, tv:tv + 2])
                    else:
                        cp(fcar, fcar[:, jc, :], st, st[:, Tn:Tn + 2])
                    ts(co, co[:, 0:Tn], st, st[:, 0:Tn], cwf[:, jc, 0:1], ALU.mult, cbf[:, jc:jc + 1], ALU.add, extra=[cwf, cbf])
                    stt(co, co[:, 0:Tn], st, st[:, 1:1 + Tn], cwf[:, jc, 1:2], co, co[:, 0:Tn], ALU.mult, ALU.add, extra=[cwf])
                    stt(co, co[:, 0:Tn], st, st[:, 2:2 + Tn], cwf[:, jc, 2:3], co, co[:, 0:Tn], ALU.mult, ALU.add, extra=[cwf])
                act(fcg, fcg[:, 0:Tn], fcg, fcg[:, 0:Tn], AF.Silu)
                tt(gT, gT[:, jg, 0:Tn], fcg, fcg[:, 0:Tn], fcu, fcu[:, 0:Tn], ALU.mult)
        if ctx["last"]:
            od = (O["ffn_s"], O["ffn_s"][l, j]) if samp else (O["ffn_p"], O["ffn_p"][l])
            outtoks.extend(fm2tm(od[0], od[1], fso, lambda g: fso[:, g, :], 44, 2))
        for c in range(2):
            ws_ = [wload("w_ffn_down", l, r0 * 128, min(8, 22 - r0), c * 512, 512) for r0 in (0, 8, 16)]
            for s in range(nsub):
                p = nps()
                for jg in range(22):
                    w = ws_[jg // 8]
                    mm(p, p[:], gT, gT[:, jg, s * 128:(s + 1) * 128], w, w[:, jg % 8, 0:512], start=(jg == 0), stop=(jg == 21), signal=(jg == 21))
                tt(x_mt, x_mt[:, s, c * 512:(c + 1) * 512], x_mt, x_mt[:, s, c * 512:(c + 1) * 512], p, p[:], ALU.add)
        for s in range(nsub):
            if samp:
                if last_layer:
                    outtoks.append(dma(O["y_s"], O["y_s"][j:j + 1, :], x_mt, x_mt[0:1, s, :]))
                else:
                    dma(xsres, xsres[j:j + 1, :], x_mt, x_mt[0:1, s, :])
            else:
                dst = O["y_p"] if last_layer else xres
                t_ = dma(dst, dst[pos0 + s * 128: pos0 + (s + 1) * 128, :], x_mt, x_mt[:, s, :])
                if last_layer:
                    outtoks.append(t_)

    def sample_history(l, j):
        dma(ptb, ptb[:], I["pt"], I["pt"][j:j + 1, :].to_broadcast([128, cfg.NPG]))
        ts(pidx, pidx[:], ptb, ptb[:], 128, ALU.mult, iop[:, 0:1], ALU.add, extra=[iop])
        if l > 0:
            ts(pidx, pidx[:], pidx, pidx[:], float(l * NPHYS * 128), ALU.add)
        for pg in range(cfg.NPG):
            for (srcn, dstb, w) in (("cki", pgi, 64), ("ck", pgk, 128), ("cv", pgv, 128)):
                srcb = I[srcn]
                P.dma(lambda e, srcb=srcb, dstb=dstb, pg=pg: e.indirect_dma_start(
                    out=dstb[:], out_offset=None, in_=srcb[:].rearrange("l r c -> (l r) c"),
                    in_offset=bass.IndirectOffsetOnAxis(ap=pidx[:, pg:pg + 1], axis=0)),
                    reads=[srcb, pidx], writes=[dstb], q="pool")
            cp(kibf, kibf[:], pgi, pgi[:])
            tr(psT, psT[0:64, 0:128], kibf, kibf[:], identb)
            cp(kiTs, kiTs[:], psT, psT[0:64, 0:128], eng="act")
            dma(h_kiT, h_kiT[:, pg * 128:(pg + 1) * 128], kiTs, kiTs[:])
            cp(kbf, kbf[:], pgk, pgk[:])
            tr(psT, psT[:, 128:256], kbf, kbf[:], identb)
            cp(kTs, kTs[:], psT, psT[:, 128:256], eng="act")
            dma(h_kT, h_kT[:, pg * 128:(pg + 1) * 128], kTs, kTs[:])
            cp(vbuf, vbuf[:, :, 0:64], pgv, pgv[:].rearrange("p (kv d) -> p kv d", kv=2))
            dma(h_v, h_v[pg * 128:(pg + 1) * 128, :], vbuf, vbuf[:].rearrange("p a b -> p (a b)"))

    nmac = SEQ // T
    stop = 99
    for l in range(DEPTH):
        if stop < 1: break
        load_params(l)
        if stop < 2: break
        mem_kv_prompt(l)
        if stop < 3: break
        for m in range(nmac):
            try:
                macro(l, dict(kind="p", pos0=m * T, nsub=NSUB, tv=T, first=(m == 0), last=(m == nmac - 1)))
            except StopM:
                pass
            if stop < 4: break
        if stop < 5: break
        for j in range(NS):
            mem_kv_sample(l, j)
            sample_history(l, j)
            if stop < 6: break
            macro(l, dict(kind="s", j=j, pos0=PAST, nsub=1, tv=1, first=True, last=True))
        if stop < 7: break
    P.finish(outtoks)
    P.emit()
    es.close()
    return nc


_W_NAMES = ["norm_mix_g", "w_in", "conv_a_w", "conv_a_b", "ln_a_g", "ln_a_b", "q_norm_g", "k_norm_g", "ssm_conv_w",
            "ssm_conv_b", "dt_bias", "a_log", "d_skip", "ssm_norm_g", "mem_norm_g", "w_mem_kv", "mq_norm_g",
            "mk_norm_g", "w_branch", "w_out", "norm_ffn_g", "w_ffn_up", "ffn_conv_w", "ffn_conv_b", "w_ffn_down"]


def _rope_tab(pos):
    inv = 500000.0 ** (-np.arange(8, dtype=np.float64) * (2.0 / 16))
    ang = (pos.astype(np.float32)[:, None] * inv.astype(np.float32)[None, :]).astype(np.float32)
    return np.concatenate([np.cos(ang), np.sin(ang)], axis=1).astype(np.float32)


def run(cfg, inputs, n_cores=8):
    f = lambda a: np.ascontiguousarray(np.asarray(a))
    SEQ, PAST, DEPTH, NS = cfg.SEQ, cfg.PAST, cfg.DEPTH, cfg.NS
    nc = build(cfg)
    B = inputs["x_prompt"].shape[0]
    shared = {n: f(inputs[n]) for n in _W_NAMES}
    shared["w_branch"] = shared["w_branch"].reshape(DEPTH, 2048, D)
    shared["ck"] = f(inputs["cache_k"]).reshape(DEPTH, -1, 128)
    shared["cv"] = f(inputs["cache_v"]).reshape(DEPTH, -1, 128)
    shared["cki"] = f(inputs["cache_kidx"]).reshape(DEPTH, -1, 64)
    shared["ropep"] = _rope_tab(np.arange(SEQ)); shared["ropes"] = _rope_tab(np.array([PAST]))
    in_maps = []
    for c in range(n_cores):
        b = c % B; sl = slice(c * NS, (c + 1) * NS)
        m = dict(shared)
        m["xp"] = f(inputs["x_prompt"][b]); m["memp"] = f(inputs["mem_prompt"][b])
        m["xs"] = f(inputs["x_sample"][sl, 0])
        m["cmk"] = f(inputs["cache_mem_k"][:, sl]).reshape(DEPTH, NS, 256, 512)
        m["cmv"] = f(inputs["cache_mem_v"][:, sl]).reshape(DEPTH, NS, 256, 512)
        m["sconf"] = f(inputs["state_conformer"][:, sl]); m["ssc"] = f(inputs["state_ssm_conv"][:, sl])
        m["sssm"] = f(inputs["state_ssm"][:, sl]); m["sffn"] = f(inputs["state_ffn_conv"][:, sl])
        m["pt"] = f(inputs["page_table"][sl]).astype(np.int32)
        in_maps.append(m)
    res = run_bass_kernel_spmd(nc, in_maps, core_ids=list(range(n_cores)))
    R = res.results
    st = lambda k, cores: np.stack([R[c][k] for c in cores])
    pc = list(range(B))
    ac = list(range(n_cores))
    cat = lambda k: np.concatenate([R[c][k] for c in ac], axis=1)
    y_p = st("y_p", pc)
    y_s = np.concatenate([R[c]["y_s"] for c in ac], axis=0)[:, None, :]
    k_p = st("k_p", pc).transpose(1, 0, 2, 3).reshape(DEPTH, B, SEQ, 2, 64)
    v_p = st("v_p", pc).transpose(1, 0, 2, 3).reshape(DEPTH, B, SEQ, 2, 64)
    ki_p = st("ki_p", pc).transpose(1, 0, 2, 3)
    mk_p = st("mk_p", pc).transpose(1, 0, 2, 3).reshape(DEPTH, B, 256, 4, 128)
    mv_p = st("mv_p", pc).transpose(1, 0, 2, 3).reshape(DEPTH, B, 256, 4, 128)
    conf_p = st("conf_p", pc).transpose(1, 0, 2, 3)
    sc_p = st("sc_p", pc).transpose(1, 0, 2, 3)
    ssm_p = st("ssm_p", pc).transpose(1, 0, 2, 3, 4)
    ffn_p = st("ffn_p", pc).transpose(1, 0, 2, 3)
    k_s = cat("k_s").reshape(DEPTH, -1, 1, 2, 64); v_s = cat("v_s").reshape(DEPTH, -1, 1, 2, 64)
    ki_s = cat("ki_s").reshape(DEPTH, -1, 1, 64)
    outs = (y_p, y_s, k_p, v_p, ki_p, mk_p, mv_p, conf_p, sc_p, ssm_p, ffn_p, k_s, v_s, ki_s,
            cat("conf_s"), cat("sc_s"), cat("ssm_s"), cat("ffn_s"))
    return tuple(np.ascontiguousarray(o.astype(np.float32)) for o in outs)


def kernel(**inputs):
    return run(CFG(), inputs)
```

```python
import numpy as np
from contextlib import ExitStack
import concourse.bass as bass
import concourse.mybir as mybir
from concourse.bass_utils import run_bass_kernel_spmd

F32 = mybir.dt.float32
BF16 = mybir.dt.bfloat16
I32 = mybir.dt.int32
U32 = mybir.dt.uint32
ALU = mybir.AluOpType
AF = mybir.ActivationFunctionType
AX = mybir.AxisListType

ENGS = ("pe", "act", "dve", "pool", "sp")
NDMA = 8
SAME_ENG_SYNC = True


class Buf:
    def __init__(self, name, t, roots=None):
        self.name = name
        self.t = t
        if roots is None:
            self.roots = [self]
            self._lastw = None
            self._readers = []
        else:
            self.roots = roots

    def __getitem__(self, idx):
        return self.t[idx]

    def alias(self, name, ap):
        return Buf(name, ap, roots=self.roots)


def multi(name, t, parts):
    roots = []
    for p in parts:
        for r in p.roots:
            if r not in roots:
                roots.append(r)
    return Buf(name, t, roots=roots)


class Prog:
    def __init__(self, nc, es):
        self.nc = nc
        self.es = es
        self.ops = {e: [] for e in ENGS}
        self.cnt = {e: 0 for e in ENGS}
        self.dma_i = {e: 0 for e in ENGS}
        self.seen = {e: {} for e in ENGS}
        self.sems = {}
        for e in ("pe", "act", "dve", "pool"):
            self.sems[("c", e)] = es.enter_context(nc.semaphore("s_" + e))
        for e in ("sp", "act", "pool"):
            for i in range(NDMA):
                self.sems[("d", e, i)] = es.enter_context(nc.semaphore("d_%s%d" % (e, i)))
        self.nbuf = 0

    def sb(self, name, shape, dt=F32):
        t = self.es.enter_context(self.nc.sbuf_tensor(name, list(shape), dt))
        return Buf(name, t)

    def ps(self, name, shape, dt=F32):
        t = self.es.enter_context(self.nc.psum_tensor(name, list(shape), dt))
        return Buf(name, t)

    def view(self, name, t):
        return Buf(name, t)

    def _need(self, eng, tok, waits):
        if tok is None:
            return
        key, val, teng = tok
        if key[0] == "c" and teng == eng and not (SAME_ENG_SYNC and eng != "pe"):
            return
        if self.seen[eng].get(key, 0) >= val:
            return
        self.seen[eng][key] = val
        waits.append((key, val))

    def _deps(self, eng, reads, writes):
        waits = []
        for b in reads:
            for r in b.roots:
                self._need(eng, r._lastw, waits)
        for b in writes:
            for r in b.roots:
                self._need(eng, r._lastw, waits)
                for rd in r._readers:
                    self._need(eng, rd, waits)
        return waits

    def _commit(self, tok, reads, writes):
        for b in reads:
            for r in b.roots:
                r._readers.append(tok)
        for b in writes:
            for r in b.roots:
                r._lastw = tok
                r._readers = []

    def op(self, eng, fn, reads=(), writes=(), signal=True):
        waits = self._deps(eng, reads, writes)
        key = ("c", eng)
        if signal:
            self.cnt[eng] += 1
            tok = (key, self.cnt[eng], eng)
            self.ops[eng].append((waits, fn, (key, 1)))
        else:
            tok = (key, self.cnt[eng] + 1, eng)
            self.ops[eng].append((waits, fn, None))
        self._commit(tok, reads, writes)
        return tok

    def dma(self, fn, reads=(), writes=(), q="sp"):
        i = self.dma_i[q]
        self.dma_i[q] += 1
        slot = i % NDMA
        key = ("d", q, slot)
        val = 16 * (i // NDMA + 1)
        waits = self._deps(q, reads, writes)
        if i >= NDMA:
            self._need(q, (key, val - 16, "dma"), waits)
        tok = (key, val, "dma")
        self.ops[q].append((waits, fn, (key, 16)))
        self._commit(tok, reads, writes)
        return tok

    def finish(self, toks):
        waits = []
        for t in toks:
            self._need("sp", t, waits)
        self.ops["sp"].append((waits, None, None))

    def emit(self):
        nc = self.nc
        P = self

        def replay(name, e):
            for waits, fn, inc in P.ops[name]:
                for key, val in waits:
                    e.wait_ge(P.sems[key], val)
                if fn is None:
                    continue
                ins = fn(e)
                if inc is not None:
                    ins.then_inc(P.sems[inc[0]], inc[1])

        with nc.Block() as block:
            @block.sync
            def _(e):
                replay("sp", e)

            @block.tensor
            def _(e):
                replay("pe", e)

            @block.scalar
            def _(e):
                replay("act", e)

            @block.vector
            def _(e):
                replay("dve", e)

            @block.gpsimd
            def _(e):
                replay("pool", e)


D = 1024
NEG = -1.0e30
IDX_SCALE = (64 ** -0.5) * (8 ** -0.5)
EPS = 1e-6
IN_COLS = 8272
C_GLU, C_Q, C_K, C_V, C_QI, C_KI, C_WI, C_Z, C_XBC, C_DT, C_MQ, C_G = (
    0, 1024, 1536, 1664, 1792, 2304, 2368, 2376, 2888, 3656, 3664, 4176)
NITER = 22
SPLIT_MIN_NKT = 12


class CFG:
    def __init__(self, SEQ=8192, PAST=8192, NPHYS=2560, DEPTH=2, NS=4, T=128):
        self.SEQ, self.PAST, self.NPHYS, self.DEPTH, self.NS, self.T = SEQ, PAST, NPHYS, DEPTH, NS, T
        self.SMAX = max(SEQ, PAST + 128)
        self.KP = min(256, SEQ // 4)
        self.KS = min(256, (PAST + 1) // 4)
        self.NPG = PAST // 128


def build(cfg):
    SEQ, PAST, NPHYS, DEPTH, NS, T = cfg.SEQ, cfg.PAST, cfg.NPHYS, cfg.DEPTH, cfg.NS, cfg.T
    NSUB = T // 128
    nc = bass.Bass("TRN2", target_bir_lowering=False)
    es = ExitStack()
    P = Prog(nc, es)

    def din(name, shape, dt=F32):
        return Buf(name, nc.dram_tensor(name, list(shape), dt, kind="ExternalInput").ap())

    def dout(name, shape, dt=F32):
        return Buf(name, nc.dram_tensor(name, list(shape), dt, kind="ExternalOutput").ap())

    def dscr(name, shape, dt=BF16):
        return Buf(name, nc.dram_tensor(name, list(shape), dt, kind="Internal").ap())

    I = {}
    I["xp"] = din("xp", [SEQ, D]); I["xs"] = din("xs", [NS, D]); I["memp"] = din("memp", [256, D])
    I["ck"] = din("ck", [DEPTH, NPHYS * 128, 128]); I["cv"] = din("cv", [DEPTH, NPHYS * 128, 128])
    I["cki"] = din("cki", [DEPTH, NPHYS * 128, 64])
    I["cmk"] = din("cmk", [DEPTH, NS, 256, 512]); I["cmv"] = din("cmv", [DEPTH, NS, 256, 512])
    I["sconf"] = din("sconf", [DEPTH, NS, 30, 512]); I["ssc"] = din("ssc", [DEPTH, NS, 3, 768])
    I["sssm"] = din("sssm", [DEPTH, NS, 8, 64, 64]); I["sffn"] = din("sffn", [DEPTH, NS, 2, 5632])
    I["pt"] = din("pt", [NS, cfg.NPG], I32)
    I["ropep"] = din("ropep", [SEQ, 16]); I["ropes"] = din("ropes", [1, 16])
    for nm, shp in [("norm_mix_g", [DEPTH, D]), ("w_in", [DEPTH, D, IN_COLS]), ("conv_a_w", [DEPTH, 31, 512]),
                    ("conv_a_b", [DEPTH, 512]), ("ln_a_g", [DEPTH, 512]), ("ln_a_b", [DEPTH, 512]),
                    ("q_norm_g", [DEPTH, 64]), ("k_norm_g", [DEPTH, 64]), ("ssm_conv_w", [DEPTH, 4, 768]),
                    ("ssm_conv_b", [DEPTH, 768]), ("dt_bias", [DEPTH, 8]), ("a_log", [DEPTH, 8]),
                    ("d_skip", [DEPTH, 8]), ("ssm_norm_g", [DEPTH, 512]), ("mem_norm_g", [DEPTH, D]),
                    ("w_mem_kv", [DEPTH, D, 1024]), ("mq_norm_g", [DEPTH, 128]), ("mk_norm_g", [DEPTH, 128]),
                    ("w_branch", [DEPTH, 2048, D]), ("w_out", [DEPTH, D, D]), ("norm_ffn_g", [DEPTH, D]),
                    ("w_ffn_up", [DEPTH, D, 5632]), ("ffn_conv_w", [DEPTH, 3, 5632]),
                    ("ffn_conv_b", [DEPTH, 5632]), ("w_ffn_down", [DEPTH, 2816, D])]:
        I[nm] = din(nm, shp)
    O = {}
    O["y_p"] = dout("y_p", [SEQ, D]); O["y_s"] = dout("y_s", [NS, D])
    O["k_p"] = dout("k_p", [DEPTH, SEQ, 128]); O["v_p"] = dout("v_p", [DEPTH, SEQ, 128])
    O["ki_p"] = dout("ki_p", [DEPTH, SEQ, 64])
    O["mk_p"] = dout("mk_p", [DEPTH, 256, 512]); O["mv_p"] = dout("mv_p", [DEPTH, 256, 512])
    O["conf_p"] = dout("conf_p", [DEPTH, 30, 512]); O["sc_p"] = dout("sc_p", [DEPTH, 3, 768])
    O["ssm_p"] = dout("ssm_p", [DEPTH, 8, 64, 64]); O["ffn_p"] = dout("ffn_p", [DEPTH, 2, 5632])
    O["k_s"] = dout("k_s", [DEPTH, NS, 128]); O["v_s"] = dout("v_s", [DEPTH, NS, 128])
    O["ki_s"] = dout("ki_s", [DEPTH, NS, 64])
    O["conf_s"] = dout("conf_s", [DEPTH, NS, 30, 512]); O["sc_s"] = dout("sc_s", [DEPTH, NS, 3, 768])
    O["ssm_s"] = dout("ssm_s", [DEPTH, NS, 8, 64, 64]); O["ffn_s"] = dout("ffn_s", [DEPTH, NS, 2, 5632])
    outtoks = []
    DBG = False
    if DBG:
        O["dbg"] = dout("dbg", [4, 512, SEQ], BF16)
        O["dbgx"] = dout("dbgx", [SEQ, D])
    WB = {nm: dscr(nm + "_b", list(I[nm].t.shape)) for nm in
          ("w_in", "w_mem_kv", "w_branch", "w_out", "w_ffn_up", "w_ffn_down")}
    xres = dscr("xres", [SEQ, D], F32)
    xsres = dscr("xsres", [NS, D], F32)
    h_kiT = dscr("h_kiT", [64, cfg.SMAX]); h_kT = dscr("h_kT", [128, cfg.SMAX]); h_v = dscr("h_v", [cfg.SMAX, 256])

    sb, ps = P.sb, P.ps
    identf = sb("identf", [128, 128]); identb = sb("identb", [128, 128], BF16)
    onesf = sb("onesf", [128, 128]); onesb = sb("onesb", [128, 128], BF16)
    trif = sb("trif", [128, 128]); elast = sb("elast", [128, 128])
    cadd = sb("cadd", [128, 128]); sadd = sb("sadd", [128, 128])
    epsT = sb("epsT", [128, 1]); oneT = sb("oneT", [128, 1])
    tokm1 = sb("tokm1", [128, 1]); tokms = sb("tokms", [128, 1])
    gmix = sb("gmix", [128, D]); gffn = sb("gffn", [128, D])
    lnag = sb("lnag", [128, 512]); lnab = sb("lnab", [128, 512]); ssmg = sb("ssmg", [128, 512])
    qg = sb("qg", [128, 64]); kg = sb("kg", [128, 64]); mqg = sb("mqg", [128, 128]); mkg = sb("mkg", [128, 128])
    dtb = sb("dtb", [128, 8]); aneg = sb("aneg", [128, 8]); dsk = sb("dsk", [128, 8])
    cwa = sb("cwa", [128, 4, 31]); cba = sb("cba", [128, 4]); cws = sb("cws", [128, 6, 4]); cbs = sb("cbs", [128, 6])
    cwf = sb("cwf", [128, 44, 3]); cbf = sb("cbf", [128, 44])
    x_mt = sb("x_mt", [128, NSUB, D]); hT = sb("hT", [128, 8, T], BF16); hb = sb("hb", [128, D], BF16)
    wsl = [sb("wsl%d" % i, [128, 8, 512], BF16) for i in range(3)]
    projA = [sb("projA%d" % s, [128, 1864]) for s in range(NSUB)]
    projB = [sb("projB%d" % s, [128, 520]) for s in range(NSUB)]
    glu_u = sb("glu_u", [128, 4, 30 + T]); glu_c_t = sb("glu_c", [128, 4, T]).t; sgt = sb("sgt", [128, 512])
    glu_cg = [Buf("glu_c%d" % g, glu_c_t[:, g, :]) for g in range(4)]
    glu_c = multi("glu_c", glu_c_t, glu_cg)
    xbc_u = sb("xbc_u", [128, 6, 3 + T]); xbc_c_t = sb("xbc_c", [128, 6, T]).t
    xbc_cg = [Buf("xbc_c%d" % g, xbc_c_t[:, g, :]) for g in range(6)]
    xbc_c = multi("xbc_c", xbc_c_t, xbc_cg)
    fst_g = sb("fst_g", [128, 2 + T]); fst_u = sb("fst_u", [128, 2 + T]); fcar = sb("fcar", [128, 44, 2])
    fso = sb("fso", [128, 44, 2]); fcg = sb("fcg", [128, T]); fcu = sb("fcu", [128, T])
    gT = sb("gT", [128, 22, T], BF16)
    brT = [sb("brT%d" % n, [128, 4, T], BF16) for n in (0, 2, 3)]
    brTb = sb("brTb", [64, 8, T], BF16)
    mixed = [projA[s].alias("mixed%d" % s, projA[s][:, 0:D]) for s in range(NSUB)]
    gmem = projA[0].alias("gmem", projA[0][:, 0:D])
    tm1 = sb("tm1", [128, 512]); tm2 = sb("tm2", [128, 512]); tmb = sb("tmb", [128, 512], BF16)
    sm = [sb("sm%d" % i, [128, 16]) for i in range(8)]
    xs_tm = sb("xs_tm", [128, 512]); B_tm = sb("B_tm", [128, 128], BF16)
    BT = sb("BT", [128, 128], BF16); CT = sb("CT", [128, 128], BF16)
    CTm = [sb("CTm0", [128, 128], BF16), sb("CTm1", [128, 128], BF16)]
    qTm = [sb("qTm0", [128, 4, 128], BF16), sb("qTm1", [128, 4, 128], BF16)]
    xdt = sb("xdt", [128, 512], BF16); xdtd = sb("xdtd", [128, 512], BF16)
    GT = sb("GT", [128, 2, 128])
    scT = sb("scT", [128, 8, 128], BF16)
    hst = sb("hst", [128, 4, 64]); hstb = sb("hstb", [128, 4, 64], BF16)
    ybuf = sb("ybuf", [128, 512])
    mkT = sb("mkT", [128, 4, 256], BF16); mvb = sb("mvb", [128, 2, 512], BF16)
    mqT = sb("mqT", [128, 4, 128], BF16); PT = sb("PT", [128, 2, 512], BF16); rden = sb("rden", [128, 512])
    rdlo = sb("rdlo", [64, 512])
    hio = rdlo.alias("hio", rdlo[:])
    Isc = sb("Isc", [128, max(cfg.SMAX, 1024)])
    NKT_MAX = cfg.SMAX // 128
    N1MAX = max(12, int(round(NKT_MAX * 0.42))) * 128
    junkD = sb("junkD", [128, N1MAX], BF16); junkA = sb("junkA", [128, max(min(cfg.SMAX, 1024), cfg.SMAX - N1MAX + 128)], BF16)
    nmid = sb("nmid", [128, 1]); cnt2 = sb("cnt2", [128, 1])
    kic = [sb("kic%d" % i, [64, 1024], BF16) for i in range(2)]
    rbuf = [sb("rbuf0", [128, 1024]), sb("rbuf1", [128, 1024])]
    diag = rbuf[0].alias("diag", rbuf[0][:].rearrange("p (h t) -> p h t", h=8))
    seg = Isc.alias("seg", Isc[:, 0:1024].rearrange("p (h t) -> p h t", h=8))
    stg = [Isc.alias("stg", Isc[:, 0:1024]), rbuf[1].alias("stg1", rbuf[1][:])]
    stgb = [hT.alias("stgb", hT[:].rearrange("p a b -> p (a b)")[:, 0:1024]), hb.alias("stgb1", hb[:])]
    kTc = [sb("kTc%d" % i, [128, 512], BF16) for i in range(2)]
    vc = [sb("vc%d" % i, [128, 4, 256], BF16) for i in range(2)]
    qT = sb("qT", [128, 4, 128], BF16); qiT = sb("qiT", [64, 8, 128], BF16)
    mT4 = [sb("mT4_%d" % i, [128, 4, 128], BF16) for i in range(2)]
    Eb = [sb("Eb%d" % i, [128, 8, 128], BF16) for i in range(2)]
    Pm = [sb("Pm%d" % i, [128, 8, 128], BF16) for i in range(2)]
    vbuf = sb("vbuf", [128, 2, 128], BF16); ropet = sb("ropet", [128, 16])
    awi = sb("awi", [128, 8]); swi = sb("swi", [128, 8])
    lo = sb("lo", [128, 1]); hi = sb("hi", [128, 1]); mid = sb("mid", [128, 1]); cnt = sb("cnt", [128, 1])
    pge = sb("pge", [128, 1], I32); plt = sb("plt", [128, 1], I32)
    pidx = sb("pidx", [128, cfg.NPG], I32); ptb = sb("ptb", [128, cfg.NPG], I32); iop = sb("iop", [128, 1], I32)
    pgk = sb("pgk", [128, 128]); pgv = sb("pgv", [128, 128]); pgi = sb("pgi", [128, 64])
    kout = sb("kout", [128, 128]); kiout = sb("kiout", [128, 64]); kbf = sb("kbf", [128, 128], BF16)
    kibf = sb("kibf", [128, 64], BF16); qn = tm2.alias("qn", tm2[:]); qbf = sb("qbf", [128, 512], BF16)
    kTs = sb("kTs", [128, 128], BF16); kiTs = sb("kiTs", [64, 128], BF16)
    stio = ybuf.alias("stio", ybuf[:])
    psAB_t = ps("psAB", [128, 1024]).t
    psA = Buf("psA", psAB_t[:, 0:512]); psB = Buf("psB", psAB_t[:, 512:1024]); psC = ps("psC", [128, 512])
    psAB = multi("psAB2", psAB_t, [psA, psB])
    psW = ps("psW", [128, 1024]); psO = [ps("psO0", [128, 512]), ps("psO1", [128, 512])]
    psT = ps("psT", [128, 1024], BF16)
    pr = [psA, psB, psC]
    SS = [psW, psAB]
    rot = {"ps": 0, "w": 0, "kic": 0, "rb": 0, "kt": 0, "mt": 0, "mg": 0}

    def nps():
        rot["ps"] = (rot["ps"] + 1) % 3
        return pr[rot["ps"]]

    def mm(o, oap, l, lap, r, rap, start=True, stop=True, signal=True):
        P.op("pe", lambda e: e.matmul(oap, lhsT=lap, rhs=rap, start=start, stop=stop), reads=[l, r], writes=[o],
             signal=signal)

    def tr(o, oap, i, iap, idt):
        P.op("pe", lambda e: e.transpose(oap, iap, idt[0:iap.shape[0], 0:iap.shape[0]]), reads=[i, idt], writes=[o])

    def act(o, oap, i, iap, func, bias=None, scale=None, accum=None, extra=()):
        kw = {}
        if bias is not None: kw["bias"] = bias
        if scale is not None: kw["scale"] = scale
        wr = [o]
        if accum is not None:
            kw["accum_out"] = accum[1]; wr.append(accum[0])
        P.op("act", lambda e: e.activation(out=oap, in_=iap, func=func, **kw), reads=[i] + list(extra), writes=wr)

    def tt(o, oap, a, aap, b, bap, op, eng="dve"):
        P.op(eng, lambda e: e.tensor_tensor(out=oap, in0=aap, in1=bap, op=op), reads=[a, b], writes=[o])

    def ts(o, oap, a, aap, s1, op0, s2=None, op1=None, accum=None, extra=(), eng="dve"):
        kw = {}
        wr = [o]
        if op1 is not None: kw["op1"] = op1
        if accum is not None:
            kw["accum_out"] = accum[1]; wr.append(accum[0])
        P.op(eng, lambda e: e.tensor_scalar(out=oap, in0=aap, scalar1=s1, scalar2=s2, op0=op0, **kw),
             reads=[a] + list(extra), writes=wr)

    def stt(o, oap, a, aap, sc, b, bap, op0, op1, extra=()):
        P.op("dve", lambda e: e.scalar_tensor_tensor(out=oap, in0=aap, scalar=sc, in1=bap, op0=op0, op1=op1),
             reads=[a, b] + list(extra), writes=[o])

    def cp(o, oap, i, iap, eng="dve"):
        if eng == "act":
            P.op("act", lambda e: e.activation(out=oap, in_=iap, func=AF.Copy), reads=[i], writes=[o])
        else:
            P.op(eng, lambda e: e.tensor_copy(out=oap, in_=iap), reads=[i], writes=[o])

    def mset(o, oap, v, eng="pool"):
        P.op(eng, lambda e: e.memset(oap, v), writes=[o])

    def dma(o, oap, i, iap, q="sp", nonc=False):
        if nonc:
            return P.dma(lambda e: e.dma_start(out=oap, in_=iap, allow_slow_non_contiguous=True), reads=[i], writes=[o], q=q)
        return P.dma(lambda e: e.dma_start(out=oap, in_=iap), reads=[i], writes=[o], q=q)

    def recip(o, oap, i, iap):
        P.op("dve", lambda e: e.reciprocal(out=oap, in_=iap), reads=[i], writes=[o])

    def red(o, oap, i, iap, op):
        P.op("dve", lambda e: e.tensor_reduce(out=oap, in_=iap, axis=AX.X, op=op), reads=[i], writes=[o])

    def bc_last(ap, n):
        return ap.unsqueeze(2).to_broadcast([ap.shape[0], ap.shape[1], n])

    def bc_mid(ap, n):
        return ap.unsqueeze(1).to_broadcast([ap.shape[0], n, ap.shape[1]])

    mset(identf, identf[:], 1.0)
    P.op("pool", lambda e: e.affine_select(out=identf[:], in_=identf[:], pattern=[[-1, 128]], compare_op=ALU.is_equal,
                                            fill=0.0, base=0, channel_multiplier=1), reads=[identf], writes=[identf])
    cp(identb, identb[:], identf, identf[:])
    mset(onesf, onesf[:], 1.0); mset(onesb, onesb[:], 1.0)
    mset(trif, trif[:], 1.0)
    P.op("pool", lambda e: e.affine_select(out=trif[:], in_=trif[:], pattern=[[1, 128]], compare_op=ALU.is_ge,
                                            fill=0.0, base=0, channel_multiplier=-1), reads=[trif], writes=[trif])
    mset(cadd, cadd[:], 0.0)
    P.op("pool", lambda e: e.affine_select(out=cadd[:], in_=cadd[:], pattern=[[-1, 128]], compare_op=ALU.is_ge,
                                            fill=NEG, base=0, channel_multiplier=1), reads=[cadd], writes=[cadd])
    mset(sadd, sadd[:], NEG); mset(sadd, sadd[:, 0:1], 0.0)
    mset(elast, elast[:], 0.0); mset(elast, elast[127:128, :], 1.0) if False else None
    mset(elast, elast[:], 1.0)
    P.op("pool", lambda e: e.affine_select(out=elast[:], in_=elast[:], pattern=[[0, 128]], compare_op=ALU.is_equal,
                                            fill=0.0, base=-127, channel_multiplier=1), reads=[elast], writes=[elast])
    mset(epsT, epsT[:], EPS); mset(oneT, oneT[:], 1.0)
    mset(tokm1, tokm1[:], 1.0)
    mset(tokms, tokms[:], 1.0)
    P.op("pool", lambda e: e.affine_select(out=tokms[:], in_=tokms[:], pattern=[[0, 1]], compare_op=ALU.is_equal,
                                            fill=0.0, base=0, channel_multiplier=1), reads=[tokms], writes=[tokms])
    mset(vbuf, vbuf[:], 1.0)
    for g in range(2):
        mset(CTm[g], CTm[g][:], 0.0); mset(qTm[g], qTm[g][:], 0.0)
    P.op("pool", lambda e: e.iota(iop[:], pattern=[[0, 1]], base=0, channel_multiplier=1), writes=[iop])

    pc = [0]
    for nm in ("w_in", "w_mem_kv", "w_branch", "w_out", "w_ffn_up", "w_ffn_down"):
        src = I[nm]; dst = WB[nm]
        shp = src.t.shape
        R, C = shp[1], shp[2]
        for l in range(DEPTH):
            for r0 in range(0, R, 128):
                for c0 in range(0, C, 1024):
                    cw = min(1024, C - c0)
                    pi = pc[0] % 2
                    qn_ = "sp" if pi == 0 else "act"
                    dma(stg[pi], stg[pi][:, 0:cw], src, src[l, r0:r0 + 128, c0:c0 + cw], q=qn_)
                    cp(stgb[pi], stgb[pi][:, 0:cw], stg[pi], stg[pi][:, 0:cw], eng=("dve" if pi == 0 else "act"))
                    dma(dst, dst[l, r0:r0 + 128, c0:c0 + cw], stgb[pi], stgb[pi][:, 0:cw], q=qn_)
                    pc[0] += 1

    wrot = [0]

    def wload(nm, l, r0, G, c0, ncol, pp=128):
        s = wsl[wrot[0] % 3]; wrot[0] += 1
        w = WB[nm]
        dma(s, s[0:pp, 0:G, 0:ncol], w, w[l, r0:r0 + G * pp, c0:c0 + ncol].rearrange("(g p) c -> p g c", p=pp))
        return s

    def rstd_of(ss_b, ss_ap, n, out_b, out_ap):
        act(out_b, out_ap, ss_b, ss_ap, AF.Sqrt, bias=epsT[:, 0:1], scale=1.0 / n, extra=[epsT])
        recip(out_b, out_ap, out_b, out_ap)

    def rms_full(xb, xap, gb, ob, oap, n):
        act(tm1b_junk, tm1b_junk[:, 0:n], xb, xap, AF.Square, accum=(sm[0], sm[0][:, 0:1]))
        rstd_of(sm[0], sm[0][:, 0:1], n, sm[0], sm[0][:, 1:2])
        stt(ob, oap, xb, xap, sm[0][:, 1:2], gb, gb[:, 0:n], ALU.mult, ALU.mult, extra=[sm[0]])

    tm1b_junk = sb("sqjunk", [128, D], BF16)

    def rms_heads(xb, xap, H, hd, gb, ob, oap):
        n = H * hd
        tt(tm1b_junk, tm1b_junk[:, 0:n], xb, xap, xb, xap, ALU.mult)
        red(sm[1], sm[1][:, 0:H], tm1b_junk, tm1b_junk[:, 0:n].rearrange("p (h d) -> p h d", h=H), ALU.add)
        rstd_of(sm[1], sm[1][:, 0:H], hd, sm[1], sm[1][:, 8:8 + H])
        xv = xap.rearrange("p (h d) -> p h d", h=H); ov = oap.rearrange("p (h d) -> p h d", h=H)
        tt(ob, ov, xb, xv, sm[1], bc_last(sm[1][:, 8:8 + H], hd), ALU.mult)
        tt(ob, ov, ob, ov, gb, bc_mid(gb[:, 0:hd], H), ALU.mult)

    def rope(xb, xap, H, hd):
        xv = xap.rearrange("p (h d) -> p h d", h=H)
        x1 = xv[:, :, 0:8]; x2 = xv[:, :, 8:16]
        cs = bc_mid(ropet[:, 0:8], H); sn = bc_mid(ropet[:, 8:16], H)
        t1 = tm1[:, 0:H * 8].rearrange("p (h d) -> p h d", h=H); t2 = tm1[:, 64:64 + H * 8].rearrange("p (h d) -> p h d", h=H)
        t3 = tm1[:, 128:128 + H * 8].rearrange("p (h d) -> p h d", h=H); t4 = tm1[:, 192:192 + H * 8].rearrange("p (h d) -> p h d", h=H)
        tt(tm1, t1, xb, x1, ropet, cs, ALU.mult); tt(tm1, t2, xb, x2, ropet, sn, ALU.mult)
        tt(tm1, t3, xb, x2, ropet, cs, ALU.mult); tt(tm1, t4, xb, x1, ropet, sn, ALU.mult)
        tt(xb, x1, tm1, t1, tm1, t2, ALU.subtract); tt(xb, x2, tm1, t3, tm1, t4, ALU.add)

    def to_hT(src_b, src_ap, dstT, col0):
        for half in range(2):
            for j in range(4):
                kgi = half * 4 + j
                tr(psT, psT[:, j * 128:(j + 1) * 128], src_b, src_ap[:, kgi * 128:(kgi + 1) * 128], identb)
            cp(dstT, dstT[:, half * 4:half * 4 + 4, col0:col0 + 128],
               psT, psT[:, 0:512].rearrange("p (g t) -> p g t", g=4), eng="act")

    def tm2fm(dst_b, dst_fn, src_b, src_ap, G, W):
        for g in range(G):
            dma(dst_b, dst_fn(g), src_b, src_ap[:, g * 128:(g + 1) * 128].rearrange("w c -> c w"), nonc=True)

    def fm2tm(dst_b, dst_ap, src_b, src_fn, G, W):
        toks = []
        for g in range(G):
            toks.append(dma(dst_b, dst_ap[:, g * 128:(g + 1) * 128].rearrange("w c -> c w"), src_b, src_fn(g), nonc=True))
        return toks

    def bcast_row(dst_b, n, src_b, row_ap):
        dma(dst_b, dst_b[:, 0:n], src_b, row_ap.to_broadcast([128, n]))

    def load_params(l):
        bcast_row(gmix, D, I["norm_mix_g"], I["norm_mix_g"][l:l + 1, :])
        bcast_row(gffn, D, I["norm_ffn_g"], I["norm_ffn_g"][l:l + 1, :])
        bcast_row(gmem, D, I["mem_norm_g"], I["mem_norm_g"][l:l + 1, :])
        bcast_row(lnag, 512, I["ln_a_g"], I["ln_a_g"][l:l + 1, :]); bcast_row(lnab, 512, I["ln_a_b"], I["ln_a_b"][l:l + 1, :])
        bcast_row(ssmg, 512, I["ssm_norm_g"], I["ssm_norm_g"][l:l + 1, :])
        bcast_row(qg, 64, I["q_norm_g"], I["q_norm_g"][l:l + 1, :]); bcast_row(kg, 64, I["k_norm_g"], I["k_norm_g"][l:l + 1, :])
        bcast_row(mqg, 128, I["mq_norm_g"], I["mq_norm_g"][l:l + 1, :]); bcast_row(mkg, 128, I["mk_norm_g"], I["mk_norm_g"][l:l + 1, :])
        bcast_row(dtb, 8, I["dt_bias"], I["dt_bias"][l:l + 1, :]); bcast_row(dsk, 8, I["d_skip"], I["d_skip"][l:l + 1, :])
        bcast_row(aneg, 8, I["a_log"], I["a_log"][l:l + 1, :])
        act(aneg, aneg[:], aneg, aneg[:], AF.Exp)
        ts(aneg, aneg[:], aneg, aneg[:], -1.0, ALU.mult)
        tm2fm(cwa, lambda g: cwa[:, g, :], I["conv_a_w"], I["conv_a_w"][l], 4, 31)
        tm2fm(cws, lambda g: cws[:, g, :], I["ssm_conv_w"], I["ssm_conv_w"][l], 6, 4)
        tm2fm(cwf, lambda g: cwf[:, g, :], I["ffn_conv_w"], I["ffn_conv_w"][l], 44, 3)
        dma(cba, cba[:], I["conv_a_b"], I["conv_a_b"][l].rearrange("(g c) -> c g", c=128), nonc=True)
        dma(cbs, cbs[:], I["ssm_conv_b"], I["ssm_conv_b"][l].rearrange("(g c) -> c g", c=128), nonc=True)
        dma(cbf, cbf[:], I["ffn_conv_b"], I["ffn_conv_b"][l].rearrange("(g c) -> c g", c=128), nonc=True)

    def dwconv(ub, cb, G, W, wb, bb, Tn):
        for g in range(G):
            ts(cb[g], cb[g][:, 0:Tn], ub, ub[:, g, 0:Tn], wb[:, g, 0:1], ALU.mult, bb[:, g:g + 1], ALU.add, extra=[wb, bb])
        for j in range(1, W):
            for g in range(G):
                stt(cb[g], cb[g][:, 0:Tn], ub, ub[:, g, j:j + Tn], wb[:, g, j:j + 1], cb[g], cb[g][:, 0:Tn], ALU.mult, ALU.add, extra=[wb])

    def mem_kv_prompt(l):
        for mt in range(2):
            dma(stio, stio[:, 0:512], I["memp"], I["memp"][mt * 128:(mt + 1) * 128, 0:512])
            dma(tm2, tm2[:], I["memp"], I["memp"][mt * 128:(mt + 1) * 128, 512:1024])
            act(tm1b_junk, tm1b_junk[:, 0:512], stio, stio[:, 0:512], AF.Square, accum=(sm[2], sm[2][:, 0:1]))
            act(tm1b_junk, tm1b_junk[:, 512:1024], tm2, tm2[:], AF.Square, accum=(sm[2], sm[2][:, 1:2]))
            tt(sm[2], sm[2][:, 2:3], sm[2], sm[2][:, 0:1], sm[2], sm[2][:, 1:2], ALU.add)
            rstd_of(sm[2], sm[2][:, 2:3], D, sm[2], sm[2][:, 3:4])
            stt(hb, hb[:, 0:512], stio, stio[:, 0:512], sm[2][:, 3:4], gmem, gmem[:, 0:512], ALU.mult, ALU.mult, extra=[sm[2]])
            stt(hb, hb[:, 512:1024], tm2, tm2[:], sm[2][:, 3:4], gmem, gmem[:, 512:1024], ALU.mult, ALU.mult, extra=[sm[2]])
            to_hT(hb, hb, hT, 0)
            for c in range(2):
                w = wload("w_mem_kv", l, 0, 8, c * 512, 512)
                p = nps()
                for k in range(8):
                    mm(p, p[:], hT, hT[:, k, 0:128], w, w[:, k, 0:512], start=(k == 0), stop=(k == 7), signal=(k == 7))
                if c == 0:
                    cp(tm1, tm1[:], p, p[:], eng="act")
                    rms_heads(tm1, tm1[:], 4, 128, mkg, stio, stio[:, 0:512])
                    outtoks.append(dma(O["mk_p"], O["mk_p"][l, mt * 128:(mt + 1) * 128, :], stio, stio[:, 0:512]))
                    cp(tmb, tmb[:], stio, stio[:, 0:512])
                    for h in range(4):
                        tr(psT, psT[:, h * 128:(h + 1) * 128], tmb, tmb[:, h * 128:(h + 1) * 128], identb)
                    cp(mkT, mkT[:, :, mt * 128:(mt + 1) * 128], psT, psT[:, 0:512].rearrange("p (h t) -> p h t", h=4), eng="act")
                else:
                    cp(tm1, tm1[:], p, p[:], eng="act")
                    outtoks.append(dma(O["mv_p"], O["mv_p"][l, mt * 128:(mt + 1) * 128, :], tm1, tm1[:]))
                    cp(mvb, mvb[:, mt, :], tm1, tm1[:])

    def mem_kv_sample(l, j):
        for mt in range(2):
            dma(tm1, tm1[:], I["cmk"], I["cmk"][l, j, mt * 128:(mt + 1) * 128, :])
            dma(tm2, tm2[:], I["cmv"], I["cmv"][l, j, mt * 128:(mt + 1) * 128, :])
            cp(tmb, tmb[:], tm1, tm1[:])
            for h in range(4):
                tr(psT, psT[:, h * 128:(h + 1) * 128], tmb, tmb[:, h * 128:(h + 1) * 128], identb)
            cp(mkT, mkT[:, :, mt * 128:(mt + 1) * 128], psT, psT[:, 0:512].rearrange("p (h t) -> p h t", h=4), eng="act")
            cp(mvb, mvb[:, mt, :], tm2, tm2[:])

    class StopM(Exception):
        pass
    mstop = 99.0

    def chk(n):
        if mstop <= n:
            raise StopM()

    def macro(l, ctx):
        kind = ctx["kind"]; nsub = ctx["nsub"]; Tn = nsub * 128; tv = ctx["tv"]; pos0 = ctx["pos0"]
        samp = (kind == "s"); j = ctx.get("j", 0)
        tokm = tokms if samp else tokm1
        last_layer = (l == DEPTH - 1)
        for s in range(nsub):
            if samp:
                mset(x_mt, x_mt[:, s, :], 0.0, eng="dve")
                src = I["xs"] if l == 0 else xsres
                dma(x_mt, x_mt[0:1, s, :], src, src[j:j + 1, :])
            else:
                src = I["xp"] if l == 0 else xres
                dma(x_mt, x_mt[:, s, :], src, src[pos0 + s * 128: pos0 + (s + 1) * 128, :])
            rms_full(x_mt, x_mt[:, s, :], gmix, hb, hb[:], D)
            to_hT(hb, hb, hT, s * 128)
        def tm_seg(c_lo, c_hi, dsts, off0):
            c = c_lo
            while c < c_hi:
                n = min(512, c_hi - c)
                w = wload("w_in", l, 0, 8, c, n)
                for s in range(nsub):
                    p = nps()
                    for k in range(8):
                        mm(p, p[:, 0:n], hT, hT[:, k, s * 128:(s + 1) * 128], w, w[:, k, 0:n], start=(k == 0), stop=(k == 7), signal=(k == 7))
                    cp(dsts[s], dsts[s][:, off0 + c - c_lo: off0 + c - c_lo + n], p, p[:, 0:n], eng="act")
                c += n
        chk(1)
        tm_seg(C_Q, C_XBC, projA, 0)
        tm_seg(C_DT, C_G, projB, 0)
        chk(2)
        if ctx["first"]:
            if samp:
                tm2fm(glu_u, lambda g: glu_u[:, g, 0:30], I["sconf"], I["sconf"][l, j], 4, 30)
                tm2fm(xbc_u, lambda g: xbc_u[:, g, 0:3], I["ssc"], I["ssc"][l, j], 6, 3)
                tm2fm(fcar, lambda g: fcar[:, g, :], I["sffn"], I["sffn"][l, j], 44, 2)
            else:
                mset(glu_u, glu_u[:, :, 0:30], 0.0, eng="dve"); mset(xbc_u, xbc_u[:, :, 0:3], 0.0, eng="dve")
                mset(fcar, fcar[:], 0.0, eng="dve")
        wv = wload("w_in", l, 0, 8, 0, 512); wg = wload("w_in", l, 0, 8, 512, 512)
        for c in range(4):
            pv = nps(); pg = nps()
            for k in range(8):
                mm(pv, pv[:, 0:Tn], wv, wv[:, k, c * 128:(c + 1) * 128], hT, hT[:, k, 0:Tn], start=(k == 0), stop=(k == 7), signal=(k == 7))
            for k in range(8):
                mm(pg, pg[:, 0:Tn], wg, wg[:, k, c * 128:(c + 1) * 128], hT, hT[:, k, 0:Tn], start=(k == 0), stop=(k == 7), signal=(k == 7))
            act(sgt, sgt[:, 0:Tn], pg, pg[:, 0:Tn], AF.Sigmoid)
            tt(glu_u, glu_u[:, c, 30:30 + Tn], pv, pv[:, 0:Tn], sgt, sgt[:, 0:Tn], ALU.mult)
        for (c0, ng) in ((0, 4), (4, 2)):
            w = wload("w_in", l, 0, 8, C_XBC + c0 * 128, ng * 128)
            for c in range(ng):
                p = nps()
                for k in range(8):
                    mm(p, p[:, 0:Tn], w, w[:, k, c * 128:(c + 1) * 128], hT, hT[:, k, 0:Tn], start=(k == 0), stop=(k == 7), signal=(k == 7))
                cp(xbc_u, xbc_u[:, c0 + c, 3:3 + Tn], p, p[:, 0:Tn], eng="act")
        chk(3)
        if ctx["last"]:
            if samp:
                outtoks.extend(fm2tm(O["conf_s"], O["conf_s"][l, j], glu_u, lambda g: glu_u[:, g, tv:tv + 30], 4, 30))
                outtoks.extend(fm2tm(O["sc_s"], O["sc_s"][l, j], xbc_u, lambda g: xbc_u[:, g, tv:tv + 3], 6, 3))
            else:
                outtoks.extend(fm2tm(O["conf_p"], O["conf_p"][l], glu_u, lambda g: glu_u[:, g, tv:tv + 30], 4, 30))
                outtoks.extend(fm2tm(O["sc_p"], O["sc_p"][l], xbc_u, lambda g: xbc_u[:, g, tv:tv + 3], 6, 3))
        chk(4)
        dwconv(glu_u, glu_cg, 4, 31, cwa, cba, Tn)
        dwconv(xbc_u, xbc_cg, 6, 4, cws, cbs, Tn)
        act(xbc_c, xbc_c[:, :, 0:Tn], xbc_c, xbc_c[:, :, 0:Tn], AF.Silu)
        if not ctx["last"]:
            cp(glu_u, glu_u[:, :, 0:30], glu_u, glu_u[:, :, Tn:Tn + 30])
            cp(xbc_u, xbc_u[:, :, 0:3], xbc_u, xbc_u[:, :, Tn:Tn + 3])

        def merge_chunk(n, c, s, early):
            cs = slice(s * 128, (s + 1) * 128)
            wg_ = wload("w_in", l, 0, 8, C_G + n * 1024 + c * 512, 512)
            if n == 1:
                wb_ = wload("w_branch", l, 512, 8, c * 512, 512, pp=64)
            else:
                wb_ = wload("w_branch", l, n * 512, 4, c * 512, 512)
            bsrc = {0: brT[0], 2: brT[1], 3: brT[2]}.get(n)
            pg = nps(); pb = nps()
            for k in range(8):
                mm(pg, pg[:], hT, hT[:, k, cs], wg_, wg_[:, k, 0:512], start=(k == 0), stop=(k == 7), signal=(k == 7))
            if n == 1:
                for k in range(8):
                    mm(pb, pb[:], brTb, brTb[:, k, cs], wb_, wb_[0:64, k, 0:512], start=(k == 0), stop=(k == 7), signal=(k == 7))
            else:
                for k in range(4):
                    mm(pb, pb[:], bsrc, bsrc[:, k, cs], wb_, wb_[:, k, 0:512], start=(k == 0), stop=(k == 3), signal=(k == 3))
            act(tm2, tm2[:], pg, pg[:], AF.Sigmoid)
            mx = mixed[s]
            mxs = mx[:, c * 512:(c + 1) * 512]
            if early:
                cp(tm1, tm1[:], pb, pb[:], eng="act")
                if n == 0:
                    tt(mx, mxs, tm2, tm2[:], tm1, tm1[:], ALU.mult, eng="pool")
                else:
                    tt(tm2, tm2[:], tm2, tm2[:], tm1, tm1[:], ALU.mult, eng="pool")
                    tt(mx, mxs, mx, mxs, tm2, tm2[:], ALU.add, eng="pool")
            elif n == 0:
                tt(mx, mxs, tm2, tm2[:], pb, pb[:], ALU.mult)
            else:
                tt(tm2, tm2[:], tm2, tm2[:], pb, pb[:], ALU.mult)
                tt(mx, mxs, mx, mxs, tm2, tm2[:], ALU.add)

        for s in range(nsub):
            cs = slice(s * 128, (s + 1) * 128)
            pA, pB = projA[s], projB[s]
            chk(5)
            p = nps()
            for c in range(4):
                tr(p, p[:, c * 128:(c + 1) * 128], glu_c, glu_c[:, c, cs], identf)
            P.op("dve", lambda e, p=p: e.bn_stats(out=sm[3][:, 0:6], in_=p[:, 0:512]), reads=[p], writes=[sm[3]])
            P.op("dve", lambda e: e.bn_aggr(out=sm[3][:, 6:8], in_=sm[3][:, 0:6]), reads=[sm[3]], writes=[sm[3]])
            act(sm[3], sm[3][:, 8:9], sm[3], sm[3][:, 7:8], AF.Sqrt, bias=epsT[:, 0:1], scale=1.0, extra=[epsT])
            recip(sm[3], sm[3][:, 8:9], sm[3], sm[3][:, 8:9])
            ts(tm1, tm1[:], p, p[:], sm[3][:, 6:7], ALU.subtract, sm[3][:, 8:9], ALU.mult, extra=[sm[3]])
            tt(tm1, tm1[:], tm1, tm1[:], lnag, lnag[:], ALU.mult)
            tt(tm1, tm1[:], tm1, tm1[:], lnab, lnab[:], ALU.add)
            act(tmb, tmb[:], tm1, tm1[:], AF.Silu)
            for c in range(4):
                tr(psT, psT[:, c * 128:(c + 1) * 128], tmb, tmb[:, c * 128:(c + 1) * 128], identb)
            cp(brT[0], brT[0][:, :, cs], psT, psT[:, 0:512].rearrange("p (g t) -> p g t", g=4), eng="act")

            chk(6)
            p = nps()
            for c in range(4):
                tr(p, p[:, c * 128:(c + 1) * 128], xbc_c, xbc_c[:, c, cs], identf)
            cp(xs_tm, xs_tm[:], p, p[:], eng="act")
            cp(BT, BT[:], xbc_c, xbc_c[:, 4, cs]); cp(CT, CT[:], xbc_c, xbc_c[:, 5, cs])
            for g in range(2):
                cp(CTm[g], CTm[g][g * 64:(g + 1) * 64, :], xbc_c, xbc_c[g * 64:(g + 1) * 64, 5, cs])
            tr(psT, psT[:, 0:128], BT, BT[:], identb)
            cp(B_tm, B_tm[:], psT, psT[:, 0:128], eng="act")
            d0 = sm[4]
            tt(d0, d0[:, 0:8], pB, pB[:, 0:8], dtb, dtb[:], ALU.add)
            ts(d0, d0[:, 8:16], d0, d0[:, 0:8], -1.0, ALU.mult)
            tt(d0, d0[:, 8:16], d0, d0[:, 8:16], d0, d0[:, 0:8], ALU.max)
            act(d0, d0[:, 8:16], d0, d0[:, 8:16], AF.Exp, scale=-1.0)
            act(d0, d0[:, 8:16], d0, d0[:, 8:16], AF.Ln, bias=oneT[:, 0:1], scale=1.0, extra=[oneT])
            stt(d0, d0[:, 0:8], d0, d0[:, 0:8], 0.0, d0, d0[:, 8:16], ALU.max, ALU.add)
            ts(d0, d0[:, 0:8], d0, d0[:, 0:8], tokm[:, 0:1], ALU.mult, extra=[tokm])
            tt(d0, d0[:, 8:16], d0, d0[:, 0:8], aneg, aneg[:], ALU.mult)
            a1 = sm[5]
            pa = nps()
            mm(pa, pa[:, 0:8], trif, trif[:], d0, d0[:, 8:16])
            cp(a1, a1[:, 0:8], pa, pa[:, 0:8])
            pa = nps()
            mm(pa, pa[:, 0:8], elast, elast[:], a1, a1[:, 0:8])
            cp(a1, a1[:, 8:16], pa, pa[:, 0:8])
            e1 = sm[6]
            act(e1, e1[:, 0:8], a1, a1[:, 0:8], AF.Exp)
            act(e1, e1[:, 8:16], a1, a1[:, 8:16], AF.Exp)
            tt(sm[7], sm[7][:, 0:8], a1, a1[:, 8:16], a1, a1[:, 0:8], ALU.subtract)
            act(sm[7], sm[7][:, 0:8], sm[7], sm[7][:, 0:8], AF.Exp)
            tt(sm[7], sm[7][:, 8:16], sm[7], sm[7][:, 0:8], d0, d0[:, 0:8], ALU.mult)
            xv = xs_tm[:].rearrange("p (h d) -> p h d", h=8)
            tt(xdt, xdt[:].rearrange("p (h d) -> p h d", h=8), xs_tm, xv, d0, bc_last(d0[:, 0:8], 64), ALU.mult)
            tt(xdtd, xdtd[:].rearrange("p (h d) -> p h d", h=8), xs_tm, xv, sm[7], bc_last(sm[7][:, 8:16], 64), ALU.mult)
            chk(6.1)
            pg_ = nps()
            for g in range(2):
                mm(pg_, pg_[:, g * 128:(g + 1) * 128], BT, BT[:], CTm[g], CTm[g][:])
            cp(GT, GT[:], pg_, pg_[:, 0:256].rearrange("p (g t) -> p g t", g=2), eng="act")
            tt(diag, diag[:], identf, bc_mid(identf[:], 8), a1, bc_last(a1[:, 0:8], 128), ALU.mult)
            for h in range(8):
                mm(psW, psW[:, h * 128:(h + 1) * 128], onesf, onesf[:], diag, diag[:, h, :])
            tt(seg, seg[:], psW, psW[:].rearrange("p (h t) -> p h t", h=8), a1, bc_last(a1[:, 0:8], 128), ALU.subtract)
            ts(seg, seg[:], seg, seg[:], 0.0, ALU.min)
            act(seg, seg[:], seg, seg[:], AF.Exp)
            tt(seg, seg[:].rearrange("p (g h) t -> p g h t", g=2), seg, seg[:].rearrange("p (g h) t -> p g h t", g=2),
               GT, GT[:].unsqueeze(2).to_broadcast([128, 2, 4, 128]), ALU.mult)
            tt(scT, scT[:], seg, seg[:], trif, bc_mid(trif[:], 8), ALU.mult)
            chk(6.2)
            py = nps()
            for h in range(8):
                mm(py, py[:, h * 64:(h + 1) * 64], scT, scT[:, h, :], xdt, xdt[:, h * 64:(h + 1) * 64])
            cp(ybuf, ybuf[:], py, py[:], eng="act")
            if ctx["first"] and s == 0:
                if samp:
                    for g in range(2):
                        dma(hio, hio[:].rearrange("p (hh g n) -> p hh g n", hh=4, g=2)[:, :, g, :], I["sssm"],
                            I["sssm"][l, j, g * 4:(g + 1) * 4].rearrange("hh p n -> p hh n"))
                    for hh in range(4):
                        pq = nps()
                        tr(pq, pq[:, 0:64], hio, hio[:, hh * 128:(hh + 1) * 128], identf)
                        cp(hst, hst[:, hh, :], pq, pq[:, 0:64])
                else:
                    mset(hst, hst[:], 0.0, eng="dve")
                cp(hstb, hstb[:], hst, hst[:])
            po = nps()
            for h in range(8):
                g = h // 4
                mm(po, po[:, h * 64:(h + 1) * 64], CTm[g], CTm[g][:], hstb, hstb[:, h % 4, :])
            tt(tm1, tm1[:].rearrange("p (h d) -> p h d", h=8), po, po[:].rearrange("p (h d) -> p h d", h=8),
               e1, bc_last(e1[:, 0:8], 64), ALU.mult)
            tt(ybuf, ybuf[:], ybuf, ybuf[:], tm1, tm1[:], ALU.add)
            tt(tm1, tm1[:].rearrange("p (h d) -> p h d", h=8), xs_tm, xv, dsk, bc_last(dsk[:], 64), ALU.mult)
            tt(ybuf, ybuf[:], ybuf, ybuf[:], tm1, tm1[:], ALU.add)
            chk(6.3)
            pst = nps()
            for h in range(8):
                mm(pst, pst[:, h * 64:(h + 1) * 64], B_tm, B_tm[:], xdtd, xdtd[:, h * 64:(h + 1) * 64])
            for g in range(2):
                r_ = slice(g * 64, (g + 1) * 64)
                tt(hst, hst[r_, :, :], hst, hst[r_, :, :], e1, bc_last(e1[r_, 8 + g * 4: 12 + g * 4], 64), ALU.mult)
                tt(hst, hst[r_, :, :], hst, hst[r_, :, :], pst,
                   pst[r_, g * 256:(g + 1) * 256].rearrange("p (h d) -> p h d", h=4), ALU.add)
            cp(hstb, hstb[:], hst, hst[:])
            if ctx["last"] and s == nsub - 1:
                for hh in range(4):
                    pq = nps()
                    tr(pq, pq[0:64, 0:128], hst, hst[:, hh, :], identf)
                    cp(hio, hio[:, hh * 128:(hh + 1) * 128], pq, pq[0:64, 0:128])
                od = O["ssm_s"][l, j] if samp else O["ssm_p"][l]
                for g in range(2):
                    outtoks.append(dma(O["ssm_s"] if samp else O["ssm_p"], od[g * 4:(g + 1) * 4].rearrange("hh p n -> p hh n"),
                                       hio, hio[:].rearrange("p (hh g n) -> p hh g n", hh=4, g=2)[:, :, g, :]))
            chk(6.4)
            act(tm1, tm1[:], pA, pA[:, C_Z - C_Q: C_Z - C_Q + 512], AF.Silu)
            tt(ybuf, ybuf[:], ybuf, ybuf[:], tm1, tm1[:], ALU.mult)
            rms_full(ybuf, ybuf[:], ssmg, tmb, tmb[:], 512)
            for c in range(4):
                tr(psT, psT[:, c * 128:(c + 1) * 128], tmb, tmb[:, c * 128:(c + 1) * 128], identb)
            cp(brT[1], brT[1][:, :, cs], psT, psT[:, 0:512].rearrange("p (g t) -> p g t", g=4), eng="act")

            chk(7)
            rms_heads(pB, pB[:, 8:520], 4, 128, mqg, tm1, tm1[:])
            cp(tmb, tmb[:], tm1, tm1[:])
            for h in range(4):
                tr(psT, psT[:, h * 128:(h + 1) * 128], tmb, tmb[:, h * 128:(h + 1) * 128], identb)
            cp(mqT, mqT[:], psT, psT[:, 0:512].rearrange("p (h t) -> p h t", h=4), eng="act")
            for mt in range(2):
                for h in range(4):
                    mm(psW, psW[:, mt * 512 + h * 128: mt * 512 + (h + 1) * 128], mkT, mkT[:, h, mt * 128:(mt + 1) * 128], mqT, mqT[:, h, :])
            act(PT, PT[:].rearrange("p a b -> p (a b)"), psW, psW[:], AF.Exp, scale=128 ** -0.5)
            pO = nps(); pD = nps()
            for h in range(4):
                for mt in range(2):
                    mm(pO, pO[:, h * 128:(h + 1) * 128], mvb, mvb[:, mt, h * 128:(h + 1) * 128], PT, PT[:, mt, h * 128:(h + 1) * 128],
                       start=(mt == 0), stop=(mt == 1))
            for mt in range(2):
                mm(pD, pD[:], onesb, onesb[:], PT, PT[:, mt, :], start=(mt == 0), stop=(mt == 1))
            recip(rden, rden[:], pD, pD[:])
            tt(brT[2], brT[2][:, :, cs], pO, pO[:].rearrange("p (h t) -> p h t", h=4), rden, rden[:].rearrange("p (h t) -> p h t", h=4), ALU.mult)

            chk(8)
            if samp:
                bcast_row(ropet, 16, I["ropes"], I["ropes"][0:1, :])
            else:
                dma(ropet, ropet[:], I["ropep"], I["ropep"][pos0 + s * 128: pos0 + (s + 1) * 128, :])
            rms_heads(pA, pA[:, 0:512], 8, 64, qg, qn, qn[:])
            rope(qn, qn[:], 8, 64)
            chk(8.05)
            for kv in range(2):
                cp(qbf, qbf[:].rearrange("p (g kv d) -> p g kv d", g=4, kv=2)[:, :, kv, :],
                   qn, qn[:].rearrange("p (kv g d) -> p kv g d", kv=2, g=4)[:, kv, :, :])
            for g in range(4):
                tr(psT, psT[:, g * 128:(g + 1) * 128], qbf, qbf[:, g * 128:(g + 1) * 128], identb)
            cp(qT, qT[:], psT, psT[:, 0:512].rearrange("p (g t) -> p g t", g=4), eng="act")
            chk(8.07)
            for kv in range(2):
                cp(qTm[kv], qTm[kv][kv * 64:(kv + 1) * 64, :, :], qT, qT[kv * 64:(kv + 1) * 64, :, :])
            chk(8.1)
            rms_heads(pA, pA[:, C_K - C_Q: C_K - C_Q + 128], 2, 64, kg, kout, kout[:])
            rope(kout, kout[:], 2, 64)
            if samp:
                outtoks.append(dma(O["k_s"], O["k_s"][l, j:j + 1, :], kout, kout[0:1, :]))
                outtoks.append(dma(O["v_s"], O["v_s"][l, j:j + 1, :], pA, pA[0:1, C_V - C_Q: C_V - C_Q + 128]))
            else:
                outtoks.append(dma(O["k_p"], O["k_p"][l, pos0 + s * 128: pos0 + (s + 1) * 128, :], kout, kout[:]))
                outtoks.append(dma(O["v_p"], O["v_p"][l, pos0 + s * 128: pos0 + (s + 1) * 128, :], pA, pA[:, C_V - C_Q: C_V - C_Q + 128]))
            cp(kbf, kbf[:], kout, kout[:])
            tr(psT, psT[:, 0:128], kbf, kbf[:], identb)
            cp(kTs, kTs[:], psT, psT[:, 0:128], eng="act")
            hp = (PAST if samp else pos0 + s * 128)
            dma(h_kT, h_kT[:, hp:hp + 128], kTs, kTs[:])
            cp(vbuf, vbuf[:, :, 0:64], pA, pA[:, C_V - C_Q: C_V - C_Q + 128].rearrange("p (kv d) -> p kv d", kv=2))
            dma(h_v, h_v[hp:hp + 128, :], vbuf, vbuf[:].rearrange("p a b -> p (a b)"))
            chk(8.2)
            cp(kiout, kiout[:], pA, pA[:, C_KI - C_Q: C_KI - C_Q + 64])
            rope(kiout, kiout[:], 1, 64)
            if samp:
                outtoks.append(dma(O["ki_s"], O["ki_s"][l, j:j + 1, :], kiout, kiout[0:1, :]))
            else:
                outtoks.append(dma(O["ki_p"], O["ki_p"][l, pos0 + s * 128: pos0 + (s + 1) * 128, :], kiout, kiout[:]))
            cp(kibf, kibf[:], kiout, kiout[:])
            tr(psT, psT[0:64, 0:128], kibf, kibf[:], identb)
            cp(kiTs, kiTs[:], psT, psT[0:64, 0:128], eng="act")
            dma(h_kiT, h_kiT[:, hp:hp + 128], kiTs, kiTs[:])
            chk(8.3)
            cp(qn, qn[:], pA, pA[:, C_QI - C_Q: C_QI - C_Q + 512])
            rope(qn, qn[:], 8, 64)
            cp(qbf, qbf[:], qn, qn[:])
            for half in range(2):
                for h4 in range(4):
                    h = half * 4 + h4
                    tr(psT, psT[0:64, h4 * 128:(h4 + 1) * 128], qbf, qbf[:, h * 64:(h + 1) * 64], identb)
                cp(qiT, qiT[:, half * 4:half * 4 + 4, :], psT, psT[0:64, 0:512].rearrange("p (h t) -> p h t", h=4), eng="act")
            wi_ap = pA[:, C_WI - C_Q: C_WI - C_Q + 8]
            ts(awi, awi[:], pA, wi_ap, -1.0, ALU.mult)
            tt(awi, awi[:], awi, awi[:], pA, wi_ap, ALU.max)
            act(swi, swi[:], pA, wi_ap, AF.Sign)
            ts(swi, swi[:], swi, swi[:], IDX_SCALE, ALU.mult)
            chk(9)
            nkt = hp // 128 + 1
            nkeys = nkt * 128
            for c0 in range(0, nkeys, 1024):
                n = min(1024, nkeys - c0)
                kb = kic[rot["kic"] % 2]; rot["kic"] += 1
                dma(kb, kb[:, 0:n], h_kiT, h_kiT[:, c0:c0 + n])
                for h in range(8):
                    S = SS[rot["rb"] % 2]
                    for b0 in range(0, n, 512):
                        bn = min(512, n - b0)
                        mm(S, S[:, b0:b0 + bn], qiT, qiT[:, h, :], kb, kb[:, b0:b0 + bn])
                    rb = rbuf[rot["rb"] % 2]; rot["rb"] += 1
                    act(rb, rb[:, 0:n], S, S[:, 0:n], AF.Relu, scale=awi[:, h:h + 1], extra=[awi])
                    if h == 0:
                        ts(Isc, Isc[:, c0:c0 + n], rb, rb[:, 0:n], swi[:, 0:1], ALU.mult, extra=[swi])
                    else:
                        stt(Isc, Isc[:, c0:c0 + n], rb, rb[:, 0:n], swi[:, h:h + 1], Isc, Isc[:, c0:c0 + n], ALU.mult, ALU.add, extra=[swi])
            am = sadd if samp else cadd
            tt(Isc, Isc[:, nkeys - 128:nkeys], Isc, Isc[:, nkeys - 128:nkeys], am, am[:], ALU.add)
            chk(10)
            KSEL = cfg.KS if samp else cfg.KP
            split = (nkt >= SPLIT_MIN_NKT)
            n1 = int(round(nkt * 0.42)) * 128 if split else nkeys
            n2 = nkeys - n1
            early = [(n_, c_) for n_ in (0, 2, 3) for c_ in (0, 1)]
            if nkeys - 128 >= KSEL:
                red(lo, lo[:], Isc, Isc[:, 0:nkeys - 128], ALU.min)
                red(hi, hi[:], Isc, Isc[:, 0:nkeys], ALU.max)
                thr = float(KSEL) - 0.5 * n2
                for it in range(NITER):
                    ts(mid, mid[:], lo, lo[:], hi[:, 0:1], ALU.add, 0.5, ALU.mult, extra=[hi])
                    if split:
                        ts(nmid, nmid[:], mid, mid[:], -1.0, ALU.mult)
                        act(junkA, junkA[:, 0:n2], Isc, Isc[:, n1:nkeys], AF.Sign, bias=nmid[:, 0:1], scale=1.0,
                            accum=(cnt2, cnt2[:]), extra=[nmid])
                    ts(junkD, junkD[:, 0:n1], Isc, Isc[:, 0:n1], mid[:, 0:1], ALU.is_ge, 0.0, ALU.add,
                       accum=(cnt, cnt[:]), extra=[mid])
                    if split:
                        stt(cnt, cnt[:], cnt2, cnt2[:], 0.5, cnt, cnt[:], ALU.mult, ALU.add)
                    ts(pge, pge[:], cnt, cnt[:], thr, ALU.is_ge)
                    ts(plt, plt[:], cnt, cnt[:], thr, ALU.is_lt)
                    P.op("dve", lambda e: e.copy_predicated(out=lo[:], mask=pge[:], data=mid[:]), reads=[pge, mid], writes=[lo])
                    P.op("dve", lambda e: e.copy_predicated(out=hi[:], mask=plt[:], data=mid[:]), reads=[plt, mid], writes=[hi])
                    if it % 3 == 2 and early:
                        merge_chunk(early[0][0], early[0][1], s, True)
                        early.pop(0)
                ts(lo, lo[:], lo, lo[:], NEG / 2, ALU.max)
            else:
                mset(lo, lo[:], NEG / 2, eng="dve")
            while early:
                merge_chunk(early[0][0], early[0][1], s, True)
                early.pop(0)
            ts(junkD, junkD[:, 0:n1], Isc, Isc[:, 0:n1], lo[:, 0:1], ALU.is_ge, extra=[lo])
            if n2 > 0:
                ts(junkA, junkA[:, 0:n2], Isc, Isc[:, n1:nkeys], lo[:, 0:1], ALU.is_ge, extra=[lo])

            def mask_ap(kt):
                if kt * 128 < n1:
                    return junkD, junkD[:, kt * 128:(kt + 1) * 128]
                return junkA, junkA[:, kt * 128 - n1:(kt + 1) * 128 - n1]
            chk(11)
            grp = {}

            def setup_group(g):
                k0 = g * 4
                nk = min(4, nkt - k0)
                kb = kTc[rot["kt"] % 2]; vb = vc[rot["kt"] % 2]; rot["kt"] += 1
                dma(kb, kb[:, 0:nk * 128], h_kT, h_kT[:, k0 * 128:(k0 + nk) * 128])
                dma(vb, vb[:, 0:nk, :], h_v, h_v[k0 * 128:(k0 + nk) * 128, :].rearrange("(t p) c -> p t c", p=128))
                gi = rot["mg"] % 2; rot["mg"] += 1
                for kk in range(nk):
                    mb_, map_ = mask_ap(k0 + kk)
                    tr(psT, psT[:, kk * 128:(kk + 1) * 128], mb_, map_, identb)
                cp(mT4[gi], mT4[gi][:, 0:nk, :], psT, psT[:, 0:nk * 128].rearrange("p (g t) -> p g t", g=nk), eng="act")
                grp[g] = (kb, vb, gi)

            def qk(kt):
                g, kk = divmod(kt, 4)
                if g not in grp:
                    setup_group(g)
                kb = grp[g][0]
                S = SS[kt % 2]
                for kv in range(2):
                    mm(S, S[:, kv * 512:(kv + 1) * 512], kb, kb[:, kk * 128:(kk + 1) * 128],
                       qTm[kv], qTm[kv][:].rearrange("p g t -> p (g t)"))

            qk(0)
            for kt in range(nkt):
                g, kk = divmod(kt, 4)
                if kk == 0 and (g + 1) * 4 < nkt:
                    setup_group(g + 1)
                if kt + 1 < nkt:
                    qk(kt + 1)
                kb, vb, gi = grp[g]
                i2 = kt % 2
                S = SS[i2]
                act(Eb[i2], Eb[i2][:].rearrange("p a b -> p (a b)"), S, S[:], AF.Exp, scale=0.125)
                tt(Pm[i2], Pm[i2][:], Eb[i2], Eb[i2][:], mT4[gi], bc_mid(mT4[gi][:, kk, :], 8), ALU.mult)
                for kv in range(2):
                    mm(psO[kv], psO[kv][:], vb, vb[:, kk, kv * 128:(kv + 1) * 128],
                       Pm[i2], Pm[i2][:, kv * 4:(kv + 1) * 4, :].rearrange("p g t -> p (g t)"),
                       start=(kt == 0), stop=(kt == nkt - 1))
            for kv in range(2):
                recip(rden, rden[64:128, :], psO[kv], psO[kv][64:128, :])
                dma(rdlo, rdlo[:], rden, rden[64:128, :])
                tt(brTb, brTb[:, kv * 4:(kv + 1) * 4, cs], psO[kv], psO[kv][0:64, :].rearrange("p (g t) -> p g t", g=4),
                   rdlo, rdlo[:].rearrange("p (g t) -> p g t", g=4), ALU.mult)

        chk(12)
        if DBG and not samp and l == 0:
            for n, bsrc_ in ((0, brT[0]), (2, brT[1]), (3, brT[2])):
                outtoks.append(dma(O["dbg"], O["dbg"][n, :, pos0:pos0 + Tn].rearrange("(g p) t -> p g t", p=128), bsrc_, bsrc_[:, :, 0:Tn]))
            outtoks.append(dma(O["dbg"], O["dbg"][1, :, pos0:pos0 + Tn].rearrange("(h p) t -> p h t", p=64), brTb, brTb[:, :, 0:Tn]))
        for c in range(2):
            for s in range(nsub):
                merge_chunk(1, c, s, False)
        for s in range(nsub):
            cp(hb, hb[:], mixed[s], mixed[s][:])
            to_hT(hb, hb, hT, s * 128)
        for c in range(2):
            w = wload("w_out", l, 0, 8, c * 512, 512)
            for s in range(nsub):
                p = nps()
                for k in range(8):
                    mm(p, p[:], hT, hT[:, k, s * 128:(s + 1) * 128], w, w[:, k, 0:512], start=(k == 0), stop=(k == 7), signal=(k == 7))
                tt(x_mt, x_mt[:, s, c * 512:(c + 1) * 512], x_mt, x_mt[:, s, c * 512:(c + 1) * 512], p, p[:], ALU.add)
        chk(13)
        if DBG and not samp and l == 0:
            for s in range(nsub):
                outtoks.append(dma(O["dbgx"], O["dbgx"][pos0 + s * 128: pos0 + (s + 1) * 128, :], x_mt, x_mt[:, s, :]))
        for s in range(nsub):
            rms_full(x_mt, x_mt[:, s, :], gffn, hb, hb[:], D)
            to_hT(hb, hb, hT, s * 128)
        for j0 in range(0, 22, 4):
            nj = min(4, 22 - j0)
            wgs = wload("w_ffn_up", l, 0, 8, j0 * 128, nj * 128)
            wus = wload("w_ffn_up", l, 0, 8, 2816 + j0 * 128, nj * 128)
            for jj in range(nj):
                jg = j0 + jj; ju = 22 + jg
                pg = nps(); pu = nps()
                for k in range(8):
                    mm(pg, pg[:, 0:Tn], wgs, wgs[:, k, jj * 128:(jj + 1) * 128], hT, hT[:, k, 0:Tn], start=(k == 0), stop=(k == 7), signal=(k == 7))
                for k in range(8):
                    mm(pu, pu[:, 0:Tn], wus, wus[:, k, jj * 128:(jj + 1) * 128], hT, hT[:, k, 0:Tn], start=(k == 0), stop=(k == 7), signal=(k == 7))
                for (st, pp_, jc, co) in ((fst_g, pg, jg, fcg), (fst_u, pu, ju, fcu)):
                    cp(st, st[:, 0:2], fcar, fcar[:, jc, :])
                    cp(st, st[:, 2:2 + Tn], pp_, pp_[:, 0:Tn], eng="act")
                    if ctx["last"]:
                        cp(fso, fso[:, jc, :], st, st[:, tv:tv + 2])
                    else:
                        cp(fcar, fcar[:, jc, :], st, st[:, Tn:Tn + 2])
                    ts(co, co[:, 0:Tn], st, st[:, 0:Tn], cwf[:, jc, 0:1], ALU.mult, cbf[:, jc:jc + 1], ALU.add, extra=[cwf, cbf])
                    stt(co, co[:, 0:Tn], st, st[:, 1:1 + Tn], cwf[:, jc, 1:2], co, co[:, 0:Tn], ALU.mult, ALU.add, extra=[cwf])
                    stt(co, co[:, 0:Tn], st, st[:, 2:2 + Tn], cwf[:, jc, 2:3], co, co[:, 0:Tn], ALU.mult, ALU.add, extra=[cwf])
                act(fcg, fcg[:, 0:Tn], fcg, fcg[:, 0:Tn], AF.Silu)
                tt(gT, gT[:, jg, 0:Tn], fcg, fcg[:, 0:Tn], fcu, fcu[:, 0:Tn], ALU.mult)
        if ctx["last"]:
            od = (O["ffn_s"], O["ffn_s"][l, j]) if samp else (O["ffn_p"], O["ffn_p"][l])
            outtoks.extend(fm2tm(od[0], od[1], fso, lambda g: fso[:, g, :], 44, 2))
        for c in range(2):
            ws_ = [wload("w_ffn_down", l, r0 * 128, min(8, 22 - r0), c * 512, 512) for r0 in (0, 8, 16)]
            for s in range(nsub):
                p = nps()
                for jg in range(22):
                    w = ws_[jg // 8]
                    mm(p, p[:], gT, gT[:, jg, s * 128:(s + 1) * 128], w, w[:, jg % 8, 0:512], start=(jg == 0), stop=(jg == 21), signal=(jg == 21))
                tt(x_mt, x_mt[:, s, c * 512:(c + 1) * 512], x_mt, x_mt[:, s, c * 512:(c + 1) * 512], p, p[:], ALU.add)
        for s in range(nsub):
            if samp:
                if last_layer:
                    outtoks.append(dma(O["y_s"], O["y_s"][j:j + 1, :], x_mt, x_mt[0:1, s, :]))
                else:
                    dma(xsres, xsres[j:j + 1, :], x_mt, x_mt[0:1, s, :])
            else:
                dst = O["y_p"] if last_layer else xres
                t_ = dma(dst, dst[pos0 + s * 128: pos0 + (s + 1) * 128, :], x_mt, x_mt[:, s, :])
                if last_layer:
                    outtoks.append(t_)

    def sample_history(l, j):
        dma(ptb, ptb[:], I["pt"], I["pt"][j:j + 1, :].to_broadcast([128, cfg.NPG]))
        ts(pidx, pidx[:], ptb, ptb[:], 128, ALU.mult, iop[:, 0:1], ALU.add, extra=[iop])
        if l > 0:
            ts(pidx, pidx[:], pidx, pidx[:], float(l * NPHYS * 128), ALU.add)
        for pg in range(cfg.NPG):
            for (srcn, dstb, w) in (("cki", pgi, 64), ("ck", pgk, 128), ("cv", pgv, 128)):
                srcb = I[srcn]
                P.dma(lambda e, srcb=srcb, dstb=dstb, pg=pg: e.indirect_dma_start(
                    out=dstb[:], out_offset=None, in_=srcb[:].rearrange("l r c -> (l r) c"),
                    in_offset=bass.IndirectOffsetOnAxis(ap=pidx[:, pg:pg + 1], axis=0)),
                    reads=[srcb, pidx], writes=[dstb], q="pool")
            cp(kibf, kibf[:], pgi, pgi[:])
            tr(psT, psT[0:64, 0:128], kibf, kibf[:], identb)
            cp(kiTs, kiTs[:], psT, psT[0:64, 0:128], eng="act")
            dma(h_kiT, h_kiT[:, pg * 128:(pg + 1) * 128], kiTs, kiTs[:])
            cp(kbf, kbf[:], pgk, pgk[:])
            tr(psT, psT[:, 128:256], kbf, kbf[:], identb)
            cp(kTs, kTs[:], psT, psT[:, 128:256], eng="act")
            dma(h_kT, h_kT[:, pg * 128:(pg + 1) * 128], kTs, kTs[:])
            cp(vbuf, vbuf[:, :, 0:64], pgv, pgv[:].rearrange("p (kv d) -> p kv d", kv=2))
            dma(h_v, h_v[pg * 128:(pg + 1) * 128, :], vbuf, vbuf[:].rearrange("p a b -> p (a b)"))

    nmac = SEQ // T
    stop = 99
    for l in range(DEPTH):
        if stop < 1: break
        load_params(l)
        if stop < 2: break
        mem_kv_prompt(l)
        if stop < 3: break
        for m in range(nmac):
            try:
                macro(l, dict(kind="p", pos0=m * T, nsub=NSUB, tv=T, first=(m == 0), last=(m == nmac - 1)))
            except StopM:
                pass
            if stop < 4: break
        if stop < 5: break
        for j in range(NS):
            mem_kv_sample(l, j)
            sample_history(l, j)
            if stop < 6: break
            macro(l, dict(kind="s", j=j, pos0=PAST, nsub=1, tv=1, first=True, last=True))
        if stop < 7: break
    P.finish(outtoks)
    P.emit()
    es.close()
    return nc


_W_NAMES = ["norm_mix_g", "w_in", "conv_a_w", "conv_a_b", "ln_a_g", "ln_a_b", "q_norm_g", "k_norm_g", "ssm_conv_w",
            "ssm_conv_b", "dt_bias", "a_log", "d_skip", "ssm_norm_g", "mem_norm_g", "w_mem_kv", "mq_norm_g",
            "mk_norm_g", "w_branch", "w_out", "norm_ffn_g", "w_ffn_up", "ffn_conv_w", "ffn_conv_b", "w_ffn_down"]


def _rope_tab(pos):
    inv = 500000.0 ** (-np.arange(8, dtype=np.float64) * (2.0 / 16))
    ang = (pos.astype(np.float32)[:, None] * inv.astype(np.float32)[None, :]).astype(np.float32)
    return np.concatenate([np.cos(ang), np.sin(ang)], axis=1).astype(np.float32)


def run(cfg, inputs, n_cores=8):
    f = lambda a: np.ascontiguousarray(np.asarray(a))
    SEQ, PAST, DEPTH, NS = cfg.SEQ, cfg.PAST, cfg.DEPTH, cfg.NS
    nc = build(cfg)
    B = inputs["x_prompt"].shape[0]
    shared = {n: f(inputs[n]) for n in _W_NAMES}
    shared["w_branch"] = shared["w_branch"].reshape(DEPTH, 2048, D)
    shared["ck"] = f(inputs["cache_k"]).reshape(DEPTH, -1, 128)
    shared["cv"] = f(inputs["cache_v"]).reshape(DEPTH, -1, 128)
    shared["cki"] = f(inputs["cache_kidx"]).reshape(DEPTH, -1, 64)
    shared["ropep"] = _rope_tab(np.arange(SEQ)); shared["ropes"] = _rope_tab(np.array([PAST]))
    in_maps = []
    for c in range(n_cores):
        b = c % B; sl = slice(c * NS, (c + 1) * NS)
        m = dict(shared)
        m["xp"] = f(inputs["x_prompt"][b]); m["memp"] = f(inputs["mem_prompt"][b])
        m["xs"] = f(inputs["x_sample"][sl, 0])
        m["cmk"] = f(inputs["cache_mem_k"][:, sl]).reshape(DEPTH, NS, 256, 512)
        m["cmv"] = f(inputs["cache_mem_v"][:, sl]).reshape(DEPTH, NS, 256, 512)
        m["sconf"] = f(inputs["state_conformer"][:, sl]); m["ssc"] = f(inputs["state_ssm_conv"][:, sl])
        m["sssm"] = f(inputs["state_ssm"][:, sl]); m["sffn"] = f(inputs["state_ffn_conv"][:, sl])
        m["pt"] = f(inputs["page_table"][sl]).astype(np.int32)
        in_maps.append(m)
    res = run_bass_kernel_spmd(nc, in_maps, core_ids=list(range(n_cores)))
    R = res.results
    st = lambda k, cores: np.stack([R[c][k] for c in cores])
    pc = list(range(B))
    ac = list(range(n_cores))
    cat = lambda k: np.concatenate([R[c][k] for c in ac], axis=1)
    y_p = st("y_p", pc)
    y_s = np.concatenate([R[c]["y_s"] for c in ac], axis=0)[:, None, :]
    k_p = st("k_p", pc).transpose(1, 0, 2, 3).reshape(DEPTH, B, SEQ, 2, 64)
    v_p = st("v_p", pc).transpose(1, 0, 2, 3).reshape(DEPTH, B, SEQ, 2, 64)
    ki_p = st("ki_p", pc).transpose(1, 0, 2, 3)
    mk_p = st("mk_p", pc).transpose(1, 0, 2, 3).reshape(DEPTH, B, 256, 4, 128)
    mv_p = st("mv_p", pc).transpose(1, 0, 2, 3).reshape(DEPTH, B, 256, 4, 128)
    conf_p = st("conf_p", pc).transpose(1, 0, 2, 3)
    sc_p = st("sc_p", pc).transpose(1, 0, 2, 3)
    ssm_p = st("ssm_p", pc).transpose(1, 0, 2, 3, 4)
    ffn_p = st("ffn_p", pc).transpose(1, 0, 2, 3)
    k_s = cat("k_s").reshape(DEPTH, -1, 1, 2, 64); v_s = cat("v_s").reshape(DEPTH, -1, 1, 2, 64)
    ki_s = cat("ki_s").reshape(DEPTH, -1, 1, 64)
    outs = (y_p, y_s, k_p, v_p, ki_p, mk_p, mv_p, conf_p, sc_p, ssm_p, ffn_p, k_s, v_s, ki_s,
            cat("conf_s"), cat("sc_s"), cat("ssm_s"), cat("ffn_s"))
    return tuple(np.ascontiguousarray(o.astype(np.float32)) for o in outs)


def kernel(**inputs):
    return run(CFG(), inputs)
```

```python
import numpy as np
from contextlib import ExitStack
import concourse.bass as bass
import concourse.mybir as mybir
from concourse.bass_utils import run_bass_kernel_spmd

F32 = mybir.dt.float32
BF16 = mybir.dt.bfloat16
I32 = mybir.dt.int32
U32 = mybir.dt.uint32
ALU = mybir.AluOpType
AF = mybir.ActivationFunctionType
AX = mybir.AxisListType

ENGS = ("pe", "act", "dve", "pool", "sp")
NDMA = 8
SAME_ENG_SYNC = True


class Buf:
    def __init__(self, name, t, roots=None):
        self.name = name
        self.t = t
        if roots is None:
            self.roots = [self]
            self._lastw = None
            self._readers = []
        else:
            self.roots = roots

    def __getitem__(self, idx):
        return self.t[idx]

    def alias(self, name, ap):
        return Buf(name, ap, roots=self.roots)


def multi(name, t, parts):
    roots = []
    for p in parts:
        for r in p.roots:
            if r not in roots:
                roots.append(r)
    return Buf(name, t, roots=roots)


class Prog:
    def __init__(self, nc, es):
        self.nc = nc
        self.es = es
        self.ops = {e: [] for e in ENGS}
        self.cnt = {e: 0 for e in ENGS}
        self.dma_i = {e: 0 for e in ENGS}
        self.seen = {e: {} for e in ENGS}
        self.sems = {}
        for e in ("pe", "act", "dve", "pool"):
            self.sems[("c", e)] = es.enter_context(nc.semaphore("s_" + e))
        for e in ("sp", "act", "pool"):
            for i in range(NDMA):
                self.sems[("d", e, i)] = es.enter_context(nc.semaphore("d_%s%d" % (e, i)))
        self.nbuf = 0

    def sb(self, name, shape, dt=F32):
        t = self.es.enter_context(self.nc.sbuf_tensor(name, list(shape), dt))
        return Buf(name, t)

    def ps(self, name, shape, dt=F32):
        t = self.es.enter_context(self.nc.psum_tensor(name, list(shape), dt))
        return Buf(name, t)

    def view(self, name, t):
        return Buf(name, t)

    def _need(self, eng, tok, waits):
        if tok is None:
            return
        key, val, teng = tok
        if key[0] == "c" and teng == eng and not (SAME_ENG_SYNC and eng != "pe"):
            return
        if self.seen[eng].get(key, 0) >= val:
            return
        self.seen[eng][key] = val
        waits.append((key, val))

    def _deps(self, eng, reads, writes):
        waits = []
        for b in reads:
            for r in b.roots:
                self._need(eng, r._lastw, waits)
        for b in writes:
            for r in b.roots:
                self._need(eng, r._lastw, waits)
                for rd in r._readers:
                    self._need(eng, rd, waits)
        return waits

    def _commit(self, tok, reads, writes):
        for b in reads:
            for r in b.roots:
                r._readers.append(tok)
        for b in writes:
            for r in b.roots:
                r._lastw = tok
                r._readers = []

    def op(self, eng, fn, reads=(), writes=(), signal=True):
        waits = self._deps(eng, reads, writes)
        key = ("c", eng)
        if signal:
            self.cnt[eng] += 1
            tok = (key, self.cnt[eng], eng)
            self.ops[eng].append((waits, fn, (key, 1)))
        else:
            tok = (key, self.cnt[eng] + 1, eng)
            self.ops[eng].append((waits, fn, None))
        self._commit(tok, reads, writes)
        return tok

    def dma(self, fn, reads=(), writes=(), q="sp"):
        i = self.dma_i[q]
        self.dma_i[q] += 1
        slot = i % NDMA
        key = ("d", q, slot)
        val = 16 * (i // NDMA + 1)
        waits = self._deps(q, reads, writes)
        if i >= NDMA:
            self._need(q, (key, val - 16, "dma"), waits)
        tok = (key, val, "dma")
        self.ops[q].append((waits, fn, (key, 16)))
        self._commit(tok, reads, writes)
        return tok

    def finish(self, toks):
        waits = []
        for t in toks:
            self._need("sp", t, waits)
        self.ops["sp"].append((waits, None, None))

    def emit(self):
        nc = self.nc
        P = self

        def replay(name, e):
            for waits, fn, inc in P.ops[name]:
                for key, val in waits:
                    e.wait_ge(P.sems[key], val)
                if fn is None:
                    continue
                ins = fn(e)
                if inc is not None:
                    ins.then_inc(P.sems[inc[0]], inc[1])

        with nc.Block() as block:
            @block.sync
            def _(e):
                replay("sp", e)

            @block.tensor
            def _(e):
                replay("pe", e)

            @block.scalar
            def _(e):
                replay("act", e)

            @block.vector
            def _(e):
                replay("dve", e)

            @block.gpsimd
            def _(e):
                replay("pool", e)


D = 1024
NEG = -1.0e30
IDX_SCALE = (64 ** -0.5) * (8 ** -0.5)
EPS = 1e-6
IN_COLS = 8272
C_GLU, C_Q, C_K, C_V, C_QI, C_KI, C_WI, C_Z, C_XBC, C_DT, C_MQ, C_G = (
    0, 1024, 1536, 1664, 1792, 2304, 2368, 2376, 2888, 3656, 3664, 4176)
NITER = 22
SPLIT_MIN_NKT = 12


class CFG:
    def __init__(self, SEQ=8192, PAST=8192, NPHYS=2560, DEPTH=2, NS=4, T=128):
        self.SEQ, self.PAST, self.NPHYS, self.DEPTH, self.NS, self.T = SEQ, PAST, NPHYS, DEPTH, NS, T
        self.SMAX = max(SEQ, PAST + 128)
        self.KP = min(256, SEQ // 4)
        self.KS = min(256, (PAST + 1) // 4)
        self.NPG = PAST // 128


def build(cfg):
    SEQ, PAST, NPHYS, DEPTH, NS, T = cfg.SEQ, cfg.PAST, cfg.NPHYS, cfg.DEPTH, cfg.NS, cfg.T
    NSUB = T // 128
    nc = bass.Bass("TRN2", target_bir_lowering=False)
    es = ExitStack()
    P = Prog(nc, es)

    def din(name, shape, dt=F32):
        return Buf(name, nc.dram_tensor(name, list(shape), dt, kind="ExternalInput").ap())

    def dout(name, shape, dt=F32):
        return Buf(name, nc.dram_tensor(name, list(shape), dt, kind="ExternalOutput").ap())

    def dscr(name, shape, dt=BF16):
        return Buf(name, nc.dram_tensor(name, list(shape), dt, kind="Internal").ap())

    I = {}
    I["xp"] = din("xp", [SEQ, D]); I["xs"] = din("xs", [NS, D]); I["memp"] = din("memp", [256, D])
    I["ck"] = din("ck", [DEPTH, NPHYS * 128, 128]); I["cv"] = din("cv", [DEPTH, NPHYS * 128, 128])
    I["cki"] = din("cki", [DEPTH, NPHYS * 128, 64])
    I["cmk"] = din("cmk", [DEPTH, NS, 256, 512]); I["cmv"] = din("cmv", [DEPTH, NS, 256, 512])
    I["sconf"] = din("sconf", [DEPTH, NS, 30, 512]); I["ssc"] = din("ssc", [DEPTH, NS, 3, 768])
    I["sssm"] = din("sssm", [DEPTH, NS, 8, 64, 64]); I["sffn"] = din("sffn", [DEPTH, NS, 2, 5632])
    I["pt"] = din("pt", [NS, cfg.NPG], I32)
    I["ropep"] = din("ropep", [SEQ, 16]); I["ropes"] = din("ropes", [1, 16])
    for nm, shp in [("norm_mix_g", [DEPTH, D]), ("w_in", [DEPTH, D, IN_COLS]), ("conv_a_w", [DEPTH, 31, 512]),
                    ("conv_a_b", [DEPTH, 512]), ("ln_a_g", [DEPTH, 512]), ("ln_a_b", [DEPTH, 512]),
                    ("q_norm_g", [DEPTH, 64]), ("k_norm_g", [DEPTH, 64]), ("ssm_conv_w", [DEPTH, 4, 768]),
                    ("ssm_conv_b", [DEPTH, 768]), ("dt_bias", [DEPTH, 8]), ("a_log", [DEPTH, 8]),
                    ("d_skip", [DEPTH, 8]), ("ssm_norm_g", [DEPTH, 512]), ("mem_norm_g", [DEPTH, D]),
                    ("w_mem_kv", [DEPTH, D, 1024]), ("mq_norm_g", [DEPTH, 128]), ("mk_norm_g", [DEPTH, 128]),
                    ("w_branch", [DEPTH, 2048, D]), ("w_out", [DEPTH, D, D]), ("norm_ffn_g", [DEPTH, D]),
                    ("w_ffn_up", [DEPTH, D, 5632]), ("ffn_conv_w", [DEPTH, 3, 5632]),
                    ("ffn_conv_b", [DEPTH, 5632]), ("w_ffn_down", [DEPTH, 2816, D])]:
        I[nm] = din(nm, shp)
    O = {}
    O["y_p"] = dout("y_p", [SEQ, D]); O["y_s"] = dout("y_s", [NS, D])
    O["k_p"] = dout("k_p", [DEPTH, SEQ, 128]); O["v_p"] = dout("v_p", [DEPTH, SEQ, 128])
    O["ki_p"] = dout("ki_p", [DEPTH, SEQ, 64])
    O["mk_p"] = dout("mk_p", [DEPTH, 256, 512]); O["mv_p"] = dout("mv_p", [DEPTH, 256, 512])
    O["conf_p"] = dout("conf_p", [DEPTH, 30, 512]); O["sc_p"] = dout("sc_p", [DEPTH, 3, 768])
    O["ssm_p"] = dout("ssm_p", [DEPTH, 8, 64, 64]); O["ffn_p"] = dout("ffn_p", [DEPTH, 2, 5632])
    O["k_s"] = dout("k_s", [DEPTH, NS, 128]); O["v_s"] = dout("v_s", [DEPTH, NS, 128])
    O["ki_s"] = dout("ki_s", [DEPTH, NS, 64])
    O["conf_s"] = dout("conf_s", [DEPTH, NS, 30, 512]); O["sc_s"] = dout("sc_s", [DEPTH, NS, 3, 768])
    O["ssm_s"] = dout("ssm_s", [DEPTH, NS, 8, 64, 64]); O["ffn_s"] = dout("ffn_s", [DEPTH, NS, 2, 5632])
    outtoks = []
    DBG = False
    if DBG:
        O["dbg"] = dout("dbg", [4, 512, SEQ], BF16)
        O["dbgx"] = dout("dbgx", [SEQ, D])
    WB = {nm: dscr(nm + "_b", list(I[nm].t.shape)) for nm in
          ("w_in", "w_mem_kv", "w_branch", "w_out", "w_ffn_up", "w_ffn_down")}
    xres = dscr("xres", [SEQ, D], F32)
    xsres = dscr("xsres", [NS, D], F32)
    h_kiT = dscr("h_kiT", [64, cfg.SMAX]); h_kT = dscr("h_kT", [128, cfg.SMAX]); h_v = dscr("h_v", [cfg.SMAX, 256])

    sb, ps = P.sb, P.ps
    identf = sb("identf", [128, 128]); identb = sb("identb", [128, 128], BF16)
    onesf = sb("onesf", [128, 128]); onesb = sb("onesb", [128, 128], BF16)
    trif = sb("trif", [128, 128]); elast = sb("elast", [128, 128])
    cadd = sb("cadd", [128, 128]); sadd = sb("sadd", [128, 128])
    epsT = sb("epsT", [128, 1]); oneT = sb("oneT", [128, 1])
    tokm1 = sb("tokm1", [128, 1]); tokms = sb("tokms", [128, 1])
    gmix = sb("gmix", [128, D]); gffn = sb("gffn", [128, D])
    lnag = sb("lnag", [128, 512]); lnab = sb("lnab", [128, 512]); ssmg = sb("ssmg", [128, 512])
    qg = sb("qg", [128, 64]); kg = sb("kg", [128, 64]); mqg = sb("mqg", [128, 128]); mkg = sb("mkg", [128, 128])
    dtb = sb("dtb", [128, 8]); aneg = sb("aneg", [128, 8]); dsk = sb("dsk", [128, 8])
    cwa = sb("cwa", [128, 4, 31]); cba = sb("cba", [128, 4]); cws = sb("cws", [128, 6, 4]); cbs = sb("cbs", [128, 6])
    cwf = sb("cwf", [128, 44, 3]); cbf = sb("cbf", [128, 44])
    x_mt = sb("x_mt", [128, NSUB, D]); hT = sb("hT", [128, 8, T], BF16); hb = sb("hb", [128, D], BF16)
    wsl = [sb("wsl%d" % i, [128, 8, 512], BF16) for i in range(3)]
    projA = [sb("projA%d" % s, [128, 1864]) for s in range(NSUB)]
    projB = [sb("projB%d" % s, [128, 520]) for s in range(NSUB)]
    glu_u = sb("glu_u", [128, 4, 30 + T]); glu_c_t = sb("glu_c", [128, 4, T]).t; sgt = sb("sgt", [128, 512])
    glu_cg = [Buf("glu_c%d" % g, glu_c_t[:, g, :]) for g in range(4)]
    glu_c = multi("glu_c", glu_c_t, glu_cg)
    xbc_u = sb("xbc_u", [128, 6, 3 + T]); xbc_c_t = sb("xbc_c", [128, 6, T]).t
    xbc_cg = [Buf("xbc_c%d" % g, xbc_c_t[:, g, :]) for g in range(6)]
    xbc_c = multi("xbc_c", xbc_c_t, xbc_cg)
    fst_g = sb("fst_g", [128, 2 + T]); fst_u = sb("fst_u", [128, 2 + T]); fcar = sb("fcar", [128, 44, 2])
    fso = sb("fso", [128, 44, 2]); fcg = sb("fcg", [128, T]); fcu = sb("fcu", [128, T])
    gT = sb("gT", [128, 22, T], BF16)
    brT = [sb("brT%d" % n, [128, 4, T], BF16) for n in (0, 2, 3)]
    brTb = sb("brTb", [64, 8, T], BF16)
    mixed = [projA[s].alias("mixed%d" % s, projA[s][:, 0:D]) for s in range(NSUB)]
    gmem = projA[0].alias("gmem", projA[0][:, 0:D])
    tm1 = sb("tm1", [128, 512]); tm2 = sb("tm2", [128, 512]); tmb = sb("tmb", [128, 512], BF16)
    sm = [sb("sm%d" % i, [128, 16]) for i in range(8)]
    xs_tm = sb("xs_tm", [128, 512]); B_tm = sb("B_tm", [128, 128], BF16)
    BT = sb("BT", [128, 128], BF16); CT = sb("CT", [128, 128], BF16)
    CTm = [sb("CTm0", [128, 128], BF16), sb("CTm1", [128, 128], BF16)]
    qTm = [sb("qTm0", [128, 4, 128], BF16), sb("qTm1", [128, 4, 128], BF16)]
    xdt = sb("xdt", [128, 512], BF16); xdtd = sb("xdtd", [128, 512], BF16)
    GT = sb("GT", [128, 2, 128])
    scT = sb("scT", [128, 8, 128], BF16)
    hst = sb("hst", [128, 4, 64]); hstb = sb("hstb", [128, 4, 64], BF16)
    ybuf = sb("ybuf", [128, 512])
    mkT = sb("mkT", [128, 4, 256], BF16); mvb = sb("mvb", [128, 2, 512], BF16)
    mqT = sb("mqT", [128, 4, 128], BF16); PT = sb("PT", [128, 2, 512], BF16); rden = sb("rden", [128, 512])
    rdlo = sb("rdlo", [64, 512])
    hio = rdlo.alias("hio", rdlo[:])
    Isc = sb("Isc", [128, max(cfg.SMAX, 1024)])
    NKT_MAX = cfg.SMAX // 128
    N1MAX = max(12, int(round(NKT_MAX * 0.42))) * 128
    junkD = sb("junkD", [128, N1MAX], BF16); junkA = sb("junkA", [128, max(min(cfg.SMAX, 1024), cfg.SMAX - N1MAX + 128)], BF16)
    nmid = sb("nmid", [128, 1]); cnt2 = sb("cnt2", [128, 1])
    kic = [sb("kic%d" % i, [64, 1024], BF16) for i in range(2)]
    rbuf = [sb("rbuf0", [128, 1024]), sb("rbuf1", [128, 1024])]
    diag = rbuf[0].alias("diag", rbuf[0][:].rearrange("p (h t) -> p h t", h=8))
    seg = Isc.alias("seg", Isc[:, 0:1024].rearrange("p (h t) -> p h t", h=8))
    stg = [Isc.alias("stg", Isc[:, 0:1024]), rbuf[1].alias("stg1", rbuf[1][:])]
    stgb = [hT.alias("stgb", hT[:].rearrange("p a b -> p (a b)")[:, 0:1024]), hb.alias("stgb1", hb[:])]
    kTc = [sb("kTc%d" % i, [128, 512], BF16) for i in range(2)]
    vc = [sb("vc%d" % i, [128, 4, 256], BF16) for i in range(2)]
    qT = sb("qT", [128, 4, 128], BF16); qiT = sb("qiT", [64, 8, 128], BF16)
    mT4 = [sb("mT4_%d" % i, [128, 4, 128], BF16) for i in range(2)]
    Eb = [sb("Eb%d" % i, [128, 8, 128], BF16) for i in range(2)]
    Pm = [sb("Pm%d" % i, [128, 8, 128], BF16) for i in range(2)]
    vbuf = sb("vbuf", [128, 2, 128], BF16); ropet = sb("ropet", [128, 16])
    awi = sb("awi", [128, 8]); swi = sb("swi", [128, 8])
    lo = sb("lo", [128, 1]); hi = sb("hi", [128, 1]); mid = sb("mid", [128, 1]); cnt = sb("cnt", [128, 1])
    pge = sb("pge", [128, 1], I32); plt = sb("plt", [128, 1], I32)
    pidx = sb("pidx", [128, cfg.NPG], I32); ptb = sb("ptb", [128, cfg.NPG], I32); iop = sb("iop", [128, 1], I32)
    pgk = sb("pgk", [128, 128]); pgv = sb("pgv", [128, 128]); pgi = sb("pgi", [128, 64])
    kout = sb("kout", [128, 128]); kiout = sb("kiout", [128, 64]); kbf = sb("kbf", [128, 128], BF16)
    kibf = sb("kibf", [128, 64], BF16); qn = tm2.alias("qn", tm2[:]); qbf = sb("qbf", [128, 512], BF16)
    kTs = sb("kTs", [128, 128], BF16); kiTs = sb("kiTs", [64, 128], BF16)
    stio = ybuf.alias("stio", ybuf[:])
    psAB_t = ps("psAB", [128, 1024]).t
    psA = Buf("psA", psAB_t[:, 0:512]); psB = Buf("psB", psAB_t[:, 512:1024]); psC = ps("psC", [128, 512])
    psAB = multi("psAB2", psAB_t, [psA, psB])
    psW = ps("psW", [128, 1024]); psO = [ps("psO0", [128, 512]), ps("psO1", [128, 512])]
    psT = ps("psT", [128, 1024], BF16)
    pr = [psA, psB, psC]
    SS = [psW, psAB]
    rot = {"ps": 0, "w": 0, "kic": 0, "rb": 0, "kt": 0, "mt": 0, "mg": 0}

    def nps():
        rot["ps"] = (rot["ps"] + 1) % 3
        return pr[rot["ps"]]

    def mm(o, oap, l, lap, r, rap, start=True, stop=True, signal=True):
        P.op("pe", lambda e: e.matmul(oap, lhsT=lap, rhs=rap, start=start, stop=stop), reads=[l, r], writes=[o],
             signal=signal)

    def tr(o, oap, i, iap, idt):
        P.op("pe", lambda e: e.transpose(oap, iap, idt[0:iap.shape[0], 0:iap.shape[0]]), reads=[i, idt], writes=[o])

    def act(o, oap, i, iap, func, bias=None, scale=None, accum=None, extra=()):
        kw = {}
        if bias is not None: kw["bias"] = bias
        if scale is not None: kw["scale"] = scale
        wr = [o]
        if accum is not None:
            kw["accum_out"] = accum[1]; wr.append(accum[0])
        P.op("act", lambda e: e.activation(out=oap, in_=iap, func=func, **kw), reads=[i] + list(extra), writes=wr)

    def tt(o, oap, a, aap, b, bap, op, eng="dve"):
        P.op(eng, lambda e: e.tensor_tensor(out=oap, in0=aap, in1=bap, op=op), reads=[a, b], writes=[o])

    def ts(o, oap, a, aap, s1, op0, s2=None, op1=None, accum=None, extra=(), eng="dve"):
        kw = {}
        wr = [o]
        if op1 is not None: kw["op1"] = op1
        if accum is not None:
            kw["accum_out"] = accum[1]; wr.append(accum[0])
        P.op(eng, lambda e: e.tensor_scalar(out=oap, in0=aap, scalar1=s1, scalar2=s2, op0=op0, **kw),
             reads=[a] + list(extra), writes=wr)

    def stt(o, oap, a, aap, sc, b, bap, op0, op1, extra=()):
        P.op("dve", lambda e: e.scalar_tensor_tensor(out=oap, in0=aap, scalar=sc, in1=bap, op0=op0, op1=op1),
             reads=[a, b] + list(extra), writes=[o])

    def cp(o, oap, i, iap, eng="dve"):
        if eng == "act":
            P.op("act", lambda e: e.activation(out=oap, in_=iap, func=AF.Copy), reads=[i], writes=[o])
        else:
            P.op(eng, lambda e: e.tensor_copy(out=oap, in_=iap), reads=[i], writes=[o])

    def mset(o, oap, v, eng="pool"):
        P.op(eng, lambda e: e.memset(oap, v), writes=[o])

    def dma(o, oap, i, iap, q="sp", nonc=False):
        if nonc:
            return P.dma(lambda e: e.dma_start(out=oap, in_=iap, allow_slow_non_contiguous=True), reads=[i], writes=[o], q=q)
        return P.dma(lambda e: e.dma_start(out=oap, in_=iap), reads=[i], writes=[o], q=q)

    def recip(o, oap, i, iap):
        P.op("dve", lambda e: e.reciprocal(out=oap, in_=iap), reads=[i], writes=[o])

    def red(o, oap, i, iap, op):
        P.op("dve", lambda e: e.tensor_reduce(out=oap, in_=iap, axis=AX.X, op=op), reads=[i], writes=[o])

    def bc_last(ap, n):
        return ap.unsqueeze(2).to_broadcast([ap.shape[0], ap.shape[1], n])

    def bc_mid(ap, n):
        return ap.unsqueeze(1).to_broadcast([ap.shape[0], n, ap.shape[1]])

    mset(identf, identf[:], 1.0)
    P.op("pool", lambda e: e.affine_select(out=identf[:], in_=identf[:], pattern=[[-1, 128]], compare_op=ALU.is_equal,
                                            fill=0.0, base=0, channel_multiplier=1), reads=[identf], writes=[identf])
    cp(identb, identb[:], identf, identf[:])
    mset(onesf, onesf[:], 1.0); mset(onesb, onesb[:], 1.0)
    mset(trif, trif[:], 1.0)
    P.op("pool", lambda e: e.affine_select(out=trif[:], in_=trif[:], pattern=[[1, 128]], compare_op=ALU.is_ge,
                                            fill=0.0, base=0, channel_multiplier=-1), reads=[trif], writes=[trif])
    mset(cadd, cadd[:], 0.0)
    P.op("pool", lambda e: e.affine_select(out=cadd[:], in_=cadd[:], pattern=[[-1, 128]], compare_op=ALU.is_ge,
                                            fill=NEG, base=0, channel_multiplier=1), reads=[cadd], writes=[cadd])
    mset(sadd, sadd[:], NEG); mset(sadd, sadd[:, 0:1], 0.0)
    mset(elast, elast[:], 0.0); mset(elast, elast[127:128, :], 1.0) if False else None
    mset(elast, elast[:], 1.0)
    P.op("pool", lambda e: e.affine_select(out=elast[:], in_=elast[:], pattern=[[0, 128]], compare_op=ALU.is_equal,
                                            fill=0.0, base=-127, channel_multiplier=1), reads=[elast], writes=[elast])
    mset(epsT, epsT[:], EPS); mset(oneT, oneT[:], 1.0)
    mset(tokm1, tokm1[:], 1.0)
    mset(tokms, tokms[:], 1.0)
    P.op("pool", lambda e: e.affine_select(out=tokms[:], in_=tokms[:], pattern=[[0, 1]], compare_op=ALU.is_equal,
                                            fill=0.0, base=0, channel_multiplier=1), reads=[tokms], writes=[tokms])
    mset(vbuf, vbuf[:], 1.0)
    for g in range(2):
        mset(CTm[g], CTm[g][:], 0.0); mset(qTm[g], qTm[g][:], 0.0)
    P.op("pool", lambda e: e.iota(iop[:], pattern=[[0, 1]], base=0, channel_multiplier=1), writes=[iop])

    pc = [0]
    for nm in ("w_in", "w_mem_kv", "w_branch", "w_out", "w_ffn_up", "w_ffn_down"):
        src = I[nm]; dst = WB[nm]
        shp = src.t.shape
        R, C = shp[1], shp[2]
        for l in range(DEPTH):
            for r0 in range(0, R, 128):
                for c0 in range(0, C, 1024):
                    cw = min(1024, C - c0)
                    pi = pc[0] % 2
                    qn_ = "sp" if pi == 0 else "act"
                    dma(stg[pi], stg[pi][:, 0:cw], src, src[l, r0:r0 + 128, c0:c0 + cw], q=qn_)
                    cp(stgb[pi], stgb[pi][:, 0:cw], stg[pi], stg[pi][:, 0:cw], eng=("dve" if pi == 0 else "act"))
                    dma(dst, dst[l, r0:r0 + 128, c0:c0 + cw], stgb[pi], stgb[pi][:, 0:cw], q=qn_)
                    pc[0] += 1

    wrot = [0]

    def wload(nm, l, r0, G, c0, ncol, pp=128):
        s = wsl[wrot[0] % 3]; wrot[0] += 1
        w = WB[nm]
        dma(s, s[0:pp, 0:G, 0:ncol], w, w[l, r0:r0 + G * pp, c0:c0 + ncol].rearrange("(g p) c -> p g c", p=pp))
        return s

    def rstd_of(ss_b, ss_ap, n, out_b, out_ap):
        act(out_b, out_ap, ss_b, ss_ap, AF.Sqrt, bias=epsT[:, 0:1], scale=1.0 / n, extra=[epsT])
        recip(out_b, out_ap, out_b, out_ap)

    def rms_full(xb, xap, gb, ob, oap, n):
        act(tm1b_junk, tm1b_junk[:, 0:n], xb, xap, AF.Square, accum=(sm[0], sm[0][:, 0:1]))
        rstd_of(sm[0], sm[0][:, 0:1], n, sm[0], sm[0][:, 1:2])
        stt(ob, oap, xb, xap, sm[0][:, 1:2], gb, gb[:, 0:n], ALU.mult, ALU.mult, extra=[sm[0]])

    tm1b_junk = sb("sqjunk", [128, D], BF16)

    def rms_heads(xb, xap, H, hd, gb, ob, oap):
        n = H * hd
        tt(tm1b_junk, tm1b_junk[:, 0:n], xb, xap, xb, xap, ALU.mult)
        red(sm[1], sm[1][:, 0:H], tm1b_junk, tm1b_junk[:, 0:n].rearrange("p (h d) -> p h d", h=H), ALU.add)
        rstd_of(sm[1], sm[1][:, 0:H], hd, sm[1], sm[1][:, 8:8 + H])
        xv = xap.rearrange("p (h d) -> p h d", h=H); ov = oap.rearrange("p (h d) -> p h d", h=H)
        tt(ob, ov, xb, xv, sm[1], bc_last(sm[1][:, 8:8 + H], hd), ALU.mult)
        tt(ob, ov, ob, ov, gb, bc_mid(gb[:, 0:hd], H), ALU.mult)

    def rope(xb, xap, H, hd):
        xv = xap.rearrange("p (h d) -> p h d", h=H)
        x1 = xv[:, :, 0:8]; x2 = xv[:, :, 8:16]
        cs = bc_mid(ropet[:, 0:8], H); sn = bc_mid(ropet[:, 8:16], H)
        t1 = tm1[:, 0:H * 8].rearrange("p (h d) -> p h d", h=H); t2 = tm1[:, 64:64 + H * 8].rearrange("p (h d) -> p h d", h=H)
        t3 = tm1[:, 128:128 + H * 8].rearrange("p (h d) -> p h d", h=H); t4 = tm1[:, 192:192 + H * 8].rearrange("p (h d) -> p h d", h=H)
        tt(tm1, t1, xb, x1, ropet, cs, ALU.mult); tt(tm1, t2, xb, x2, ropet, sn, ALU.mult)
        tt(tm1, t3, xb, x2, ropet, cs, ALU.mult); tt(tm1, t4, xb, x1, ropet, sn, ALU.mult)
        tt(xb, x1, tm1, t1, tm1, t2, ALU.subtract); tt(xb, x2, tm1, t3, tm1, t4, ALU.add)

    def to_hT(src_b, src_ap, dstT, col0):
        for half in range(2):
            for j in range(4):
                kgi = half * 4 + j
                tr(psT, psT[:, j * 128:(j + 1) * 128], src_b, src_ap[:, kgi * 128:(kgi + 1) * 128], identb)
            cp(dstT, dstT[:, half * 4:half * 4 + 4, col0:col0 + 128],
               psT, psT[:, 0:512].rearrange("p (g t) -> p g t", g=4), eng="act")

    def tm2fm(dst_b, dst_fn, src_b, src_ap, G, W):
        for g in range(G):
            dma(dst_b, dst_fn(g), src_b, src_ap[:, g * 128:(g + 1) * 128].rearrange("w c -> c w"), nonc=True)

    def fm2tm(dst_b, dst_ap, src_b, src_fn, G, W):
        toks = []
        for g in range(G):
            toks.append(dma(dst_b, dst_ap[:, g * 128:(g + 1) * 128].rearrange("w c -> c w"), src_b, src_fn(g), nonc=True))
        return toks

    def bcast_row(dst_b, n, src_b, row_ap):
        dma(dst_b, dst_b[:, 0:n], src_b, row_ap.to_broadcast([128, n]))

    def load_params(l):
        bcast_row(gmix, D, I["norm_mix_g"], I["norm_mix_g"][l:l + 1, :])
        bcast_row(gffn, D, I["norm_ffn_g"], I["norm_ffn_g"][l:l + 1, :])
        bcast_row(gmem, D, I["mem_norm_g"], I["mem_norm_g"][l:l + 1, :])
        bcast_row(lnag, 512, I["ln_a_g"], I["ln_a_g"][l:l + 1, :]); bcast_row(lnab, 512, I["ln_a_b"], I["ln_a_b"][l:l + 1, :])
        bcast_row(ssmg, 512, I["ssm_norm_g"], I["ssm_norm_g"][l:l + 1, :])
        bcast_row(qg, 64, I["q_norm_g"], I["q_norm_g"][l:l + 1, :]); bcast_row(kg, 64, I["k_norm_g"], I["k_norm_g"][l:l + 1, :])
        bcast_row(mqg, 128, I["mq_norm_g"], I["mq_norm_g"][l:l + 1, :]); bcast_row(mkg, 128, I["mk_norm_g"], I["mk_norm_g"][l:l + 1, :])
        bcast_row(dtb, 8, I["dt_bias"], I["dt_bias"][l:l + 1, :]); bcast_row(dsk, 8, I["d_skip"], I["d_skip"][l:l + 1, :])
        bcast_row(aneg, 8, I["a_log"], I["a_log"][l:l + 1, :])
        act(aneg, aneg[:], aneg, aneg[:], AF.Exp)
        ts(aneg, aneg[:], aneg, aneg[:], -1.0, ALU.mult)
        tm2fm(cwa, lambda g: cwa[:, g, :], I["conv_a_w"], I["conv_a_w"][l], 4, 31)
        tm2fm(cws, lambda g: cws[:, g, :], I["ssm_conv_w"], I["ssm_conv_w"][l], 6, 4)
        tm2fm(cwf, lambda g: cwf[:, g, :], I["ffn_conv_w"], I["ffn_conv_w"][l], 44, 3)
        dma(cba, cba[:], I["conv_a_b"], I["conv_a_b"][l].rearrange("(g c) -> c g", c=128), nonc=True)
        dma(cbs, cbs[:], I["ssm_conv_b"], I["ssm_conv_b"][l].rearrange("(g c) -> c g", c=128), nonc=True)
        dma(cbf, cbf[:], I["ffn_conv_b"], I["ffn_conv_b"][l].rearrange("(g c) -> c g", c=128), nonc=True)

    def dwconv(ub, cb, G, W, wb, bb, Tn):
        for g in range(G):
            ts(cb[g], cb[g][:, 0:Tn], ub, ub[:, g, 0:Tn], wb[:, g, 0:1], ALU.mult, bb[:, g:g + 1], ALU.add, extra=[wb, bb])
        for j in range(1, W):
            for g in range(G):
                stt(cb[g], cb[g][:, 0:Tn], ub, ub[:, g, j:j + Tn], wb[:, g, j:j + 1], cb[g], cb[g][:, 0:Tn], ALU.mult, ALU.add, extra=[wb])

    def mem_kv_prompt(l):
        for mt in range(2):
            dma(stio, stio[:, 0:512], I["memp"], I["memp"][mt * 128:(mt + 1) * 128, 0:512])
            dma(tm2, tm2[:], I["memp"], I["memp"][mt * 128:(mt + 1) * 128, 512:1024])
            act(tm1b_junk, tm1b_junk[:, 0:512], stio, stio[:, 0:512], AF.Square, accum=(sm[2], sm[2][:, 0:1]))
            act(tm1b_junk, tm1b_junk[:, 512:1024], tm2, tm2[:], AF.Square, accum=(sm[2], sm[2][:, 1:2]))
            tt(sm[2], sm[2][:, 2:3], sm[2], sm[2][:, 0:1], sm[2], sm[2][:, 1:2], ALU.add)
            rstd_of(sm[2], sm[2][:, 2:3], D, sm[2], sm[2][:, 3:4])
            stt(hb, hb[:, 0:512], stio, stio[:, 0:512], sm[2][:, 3:4], gmem, gmem[:, 0:512], ALU.mult, ALU.mult, extra=[sm[2]])
            stt(hb, hb[:, 512:1024], tm2, tm2[:], sm[2][:, 3:4], gmem, gmem[:, 512:1024], ALU.mult, ALU.mult, extra=[sm[2]])
            to_hT(hb, hb, hT, 0)
            for c in range(2):
                w = wload("w_mem_kv", l, 0, 8, c * 512, 512)
                p = nps()
                for k in range(8):
                    mm(p, p[:], hT, hT[:, k, 0:128], w, w[:, k, 0:512], start=(k == 0), stop=(k == 7), signal=(k == 7))
                if c == 0:
                    cp(tm1, tm1[:], p, p[:], eng="act")
                    rms_heads(tm1, tm1[:], 4, 128, mkg, stio, stio[:, 0:512])
                    outtoks.append(dma(O["mk_p"], O["mk_p"][l, mt * 128:(mt + 1) * 128, :], stio, stio[:, 0:512]))
                    cp(tmb, tmb[:], stio, stio[:, 0:512])
                    for h in range(4):
                        tr(psT, psT[:, h * 128:(h + 1) * 128], tmb, tmb[:, h * 128:(h + 1) * 128], identb)
                    cp(mkT, mkT[:, :, mt * 128:(mt + 1) * 128], psT, psT[:, 0:512].rearrange("p (h t) -> p h t", h=4), eng="act")
                else:
                    cp(tm1, tm1[:], p, p[:], eng="act")
                    outtoks.append(dma(O["mv_p"], O["mv_p"][l, mt * 128:(mt + 1) * 128, :], tm1, tm1[:]))
                    cp(mvb, mvb[:, mt, :], tm1, tm1[:])

    def mem_kv_sample(l, j):
        for mt in range(2):
            dma(tm1, tm1[:], I["cmk"], I["cmk"][l, j, mt * 128:(mt + 1) * 128, :])
            dma(tm2, tm2[:], I["cmv"], I["cmv"][l, j, mt * 128:(mt + 1) * 128, :])
            cp(tmb, tmb[:], tm1, tm1[:])
            for h in range(4):
                tr(psT, psT[:, h * 128:(h + 1) * 128], tmb, tmb[:, h * 128:(h + 1) * 128], identb)
            cp(mkT, mkT[:, :, mt * 128:(mt + 1) * 128], psT, psT[:, 0:512].rearrange("p (h t) -> p h t", h=4), eng="act")
            cp(mvb, mvb[:, mt, :], tm2, tm2[:])

    class StopM(Exception):
        pass
    mstop = 99.0

    def chk(n):
        if mstop <= n:
            raise StopM()

    def macro(l, ctx):
        kind = ctx["kind"]; nsub = ctx["nsub"]; Tn = nsub * 128; tv = ctx["tv"]; pos0 = ctx["pos0"]
        samp = (kind == "s"); j = ctx.get("j", 0)
        tokm = tokms if samp else tokm1
        last_layer = (l == DEPTH - 1)
        for s in range(nsub):
            if samp:
                mset(x_mt, x_mt[:, s, :], 0.0, eng="dve")
                src = I["xs"] if l == 0 else xsres
                dma(x_mt, x_mt[0:1, s, :], src, src[j:j + 1, :])
            else:
                src = I["xp"] if l == 0 else xres
                dma(x_mt, x_mt[:, s, :], src, src[pos0 + s * 128: pos0 + (s + 1) * 128, :])
            rms_full(x_mt, x_mt[:, s, :], gmix, hb, hb[:], D)
            to_hT(hb, hb, hT, s * 128)
        def tm_seg(c_lo, c_hi, dsts, off0):
            c = c_lo
            while c < c_hi:
                n = min(512, c_hi - c)
                w = wload("w_in", l, 0, 8, c, n)
                for s in range(nsub):
                    p = nps()
                    for k in range(8):
                        mm(p, p[:, 0:n], hT, hT[:, k, s * 128:(s + 1) * 128], w, w[:, k, 0:n], start=(k == 0), stop=(k == 7), signal=(k == 7))
                    cp(dsts[s], dsts[s][:, off0 + c - c_lo: off0 + c - c_lo + n], p, p[:, 0:n], eng="act")
                c += n
        chk(1)
        tm_seg(C_Q, C_XBC, projA, 0)
        tm_seg(C_DT, C_G, projB, 0)
        chk(2)
        if ctx["first"]:
            if samp:
                tm2fm(glu_u, lambda g: glu_u[:, g, 0:30], I["sconf"], I["sconf"][l, j], 4, 30)
                tm2fm(xbc_u, lambda g: xbc_u[:, g, 0:3], I["ssc"], I["ssc"][l, j], 6, 3)
                tm2fm(fcar, lambda g: fcar[:, g, :], I["sffn"], I["sffn"][l, j], 44, 2)
            else:
                mset(glu_u, glu_u[:, :, 0:30], 0.0, eng="dve"); mset(xbc_u, xbc_u[:, :, 0:3], 0.0, eng="dve")
                mset(fcar, fcar[:], 0.0, eng="dve")
        wv = wload("w_in", l, 0, 8, 0, 512); wg = wload("w_in", l, 0, 8, 512, 512)
        for c in range(4):
            pv = nps(); pg = nps()
            for k in range(8):
                mm(pv, pv[:, 0:Tn], wv, wv[:, k, c * 128:(c + 1) * 128], hT, hT[:, k, 0:Tn], start=(k == 0), stop=(k == 7), signal=(k == 7))
            for k in range(8):
                mm(pg, pg[:, 0:Tn], wg, wg[:, k, c * 128:(c + 1) * 128], hT, hT[:, k, 0:Tn], start=(k == 0), stop=(k == 7), signal=(k == 7))
            act(sgt, sgt[:, 0:Tn], pg, pg[:, 0:Tn], AF.Sigmoid)
            tt(glu_u, glu_u[:, c, 30:30 + Tn], pv, pv[:, 0:Tn], sgt, sgt[:, 0:Tn], ALU.mult)
        for (c0, ng) in ((0, 4), (4, 2)):
            w = wload("w_in", l, 0, 8, C_XBC + c0 * 128, ng * 128)
            for c in range(ng):
                p = nps()
                for k in range(8):
                    mm(p, p[:, 0:Tn], w, w[:, k, c * 128:(c + 1) * 128], hT, hT[:, k, 0:Tn], start=(k == 0), stop=(k == 7), signal=(k == 7))
                cp(xbc_u, xbc_u[:, c0 + c, 3:3 + Tn], p, p[:, 0:Tn], eng="act")
        chk(3)
        if ctx["last"]:
            if samp:
                outtoks.extend(fm2tm(O["conf_s"], O["conf_s"][l, j], glu_u, lambda g: glu_u[:, g, tv:tv + 30], 4, 30))
                outtoks.extend(fm2tm(O["sc_s"], O["sc_s"][l, j], xbc_u, lambda g: xbc_u[:, g, tv:tv + 3], 6, 3))
            else:
                outtoks.extend(fm2tm(O["conf_p"], O["conf_p"][l], glu_u, lambda g: glu_u[:, g, tv:tv + 30], 4, 30))
                outtoks.extend(fm2tm(O["sc_p"], O["sc_p"][l], xbc_u, lambda g: xbc_u[:, g, tv:tv + 3], 6, 3))
        chk(4)
        dwconv(glu_u, glu_cg, 4, 31, cwa, cba, Tn)
        dwconv(xbc_u, xbc_cg, 6, 4, cws, cbs, Tn)
        act(xbc_c, xbc_c[:, :, 0:Tn], xbc_c, xbc_c[:, :, 0:Tn], AF.Silu)
        if not ctx["last"]:
            cp(glu_u, glu_u[:, :, 0:30], glu_u, glu_u[:, :, Tn:Tn + 30])
            cp(xbc_u, xbc_u[:, :, 0:3], xbc_u, xbc_u[:, :, Tn:Tn + 3])

        g1 = rbuf[1].alias("g1", rbuf[1][:])

        def gate1_chunk(c, s):
            cs = slice(s * 128, (s + 1) * 128)
            wg_ = wload("w_in", l, 0, 8, C_G + 1024 + c * 512, 512)
            pg = nps()
            for k in range(8):
                mm(pg, pg[:], hT, hT[:, k, cs], wg_, wg_[:, k, 0:512], start=(k == 0), stop=(k == 7), signal=(k == 7))
            act(g1, g1[:, c * 512:(c + 1) * 512], pg, pg[:], AF.Sigmoid)

        def merge1_late(c, s):
            cs = slice(s * 128, (s + 1) * 128)
            wb_ = wload("w_branch", l, 512, 8, c * 512, 512, pp=64)
            pb = nps()
            for k in range(8):
                mm(pb, pb[:], brTb, brTb[:, k, cs], wb_, wb_[0:64, k, 0:512], start=(k == 0), stop=(k == 7), signal=(k == 7))
            mx = mixed[s]
            mxs = mx[:, c * 512:(c + 1) * 512]
            tt(tm2, tm2[:], g1, g1[:, c * 512:(c + 1) * 512], pb, pb[:], ALU.mult)
            tt(mx, mxs, mx, mxs, tm2, tm2[:], ALU.add)

        def merge_chunk(n, c, s, early):
            cs = slice(s * 128, (s + 1) * 128)
            wg_ = wload("w_in", l, 0, 8, C_G + n * 1024 + c * 512, 512)
            if n == 1:
                wb_ = wload("w_branch", l, 512, 8, c * 512, 512, pp=64)
            else:
                wb_ = wload("w_branch", l, n * 512, 4, c * 512, 512)
            bsrc = {0: brT[0], 2: brT[1], 3: brT[2]}.get(n)
            pg = nps(); pb = nps()
            for k in range(8):
                mm(pg, pg[:], hT, hT[:, k, cs], wg_, wg_[:, k, 0:512], start=(k == 0), stop=(k == 7), signal=(k == 7))
            if n == 1:
                for k in range(8):
                    mm(pb, pb[:], brTb, brTb[:, k, cs], wb_, wb_[0:64, k, 0:512], start=(k == 0), stop=(k == 7), signal=(k == 7))
            else:
                for k in range(4):
                    mm(pb, pb[:], bsrc, bsrc[:, k, cs], wb_, wb_[:, k, 0:512], start=(k == 0), stop=(k == 3), signal=(k == 3))
            act(tm2, tm2[:], pg, pg[:], AF.Sigmoid)
            mx = mixed[s]
            mxs = mx[:, c * 512:(c + 1) * 512]
            if early:
                cp(tm1, tm1[:], pb, pb[:], eng="act")
                if n == 0:
                    tt(mx, mxs, tm2, tm2[:], tm1, tm1[:], ALU.mult, eng="pool")
                else:
                    tt(tm2, tm2[:], tm2, tm2[:], tm1, tm1[:], ALU.mult, eng="pool")
                    tt(mx, mxs, mx, mxs, tm2, tm2[:], ALU.add, eng="pool")
            elif n == 0:
                tt(mx, mxs, tm2, tm2[:], pb, pb[:], ALU.mult)
            else:
                tt(tm2, tm2[:], tm2, tm2[:], pb, pb[:], ALU.mult)
                tt(mx, mxs, mx, mxs, tm2, tm2[:], ALU.add)

        for s in range(nsub):
            cs = slice(s * 128, (s + 1) * 128)
            pA, pB = projA[s], projB[s]
            chk(5)
            p = nps()
            for c in range(4):
                tr(p, p[:, c * 128:(c + 1) * 128], glu_c, glu_c[:, c, cs], identf)
            P.op("dve", lambda e, p=p: e.bn_stats(out=sm[3][:, 0:6], in_=p[:, 0:512]), reads=[p], writes=[sm[3]])
            P.op("dve", lambda e: e.bn_aggr(out=sm[3][:, 6:8], in_=sm[3][:, 0:6]), reads=[sm[3]], writes=[sm[3]])
            act(sm[3], sm[3][:, 8:9], sm[3], sm[3][:, 7:8], AF.Sqrt, bias=epsT[:, 0:1], scale=1.0, extra=[epsT])
            recip(sm[3], sm[3][:, 8:9], sm[3], sm[3][:, 8:9])
            ts(tm1, tm1[:], p, p[:], sm[3][:, 6:7], ALU.subtract, sm[3][:, 8:9], ALU.mult, extra=[sm[3]])
            tt(tm1, tm1[:], tm1, tm1[:], lnag, lnag[:], ALU.mult)
            tt(tm1, tm1[:], tm1, tm1[:], lnab, lnab[:], ALU.add)
            act(tmb, tmb[:], tm1, tm1[:], AF.Silu)
            for c in range(4):
                tr(psT, psT[:, c * 128:(c + 1) * 128], tmb, tmb[:, c * 128:(c + 1) * 128], identb)
            cp(brT[0], brT[0][:, :, cs], psT, psT[:, 0:512].rearrange("p (g t) -> p g t", g=4), eng="act")

            chk(6)
            p = nps()
            for c in range(4):
                tr(p, p[:, c * 128:(c + 1) * 128], xbc_c, xbc_c[:, c, cs], identf)
            cp(xs_tm, xs_tm[:], p, p[:], eng="act")
            cp(BT, BT[:], xbc_c, xbc_c[:, 4, cs]); cp(CT, CT[:], xbc_c, xbc_c[:, 5, cs])
            for g in range(2):
                cp(CTm[g], CTm[g][g * 64:(g + 1) * 64, :], xbc_c, xbc_c[g * 64:(g + 1) * 64, 5, cs])
            tr(psT, psT[:, 0:128], BT, BT[:], identb)
            cp(B_tm, B_tm[:], psT, psT[:, 0:128], eng="act")
            d0 = sm[4]
            tt(d0, d0[:, 0:8], pB, pB[:, 0:8], dtb, dtb[:], ALU.add)
            ts(d0, d0[:, 8:16], d0, d0[:, 0:8], -1.0, ALU.mult)
            tt(d0, d0[:, 8:16], d0, d0[:, 8:16], d0, d0[:, 0:8], ALU.max)
            act(d0, d0[:, 8:16], d0, d0[:, 8:16], AF.Exp, scale=-1.0)
            act(d0, d0[:, 8:16], d0, d0[:, 8:16], AF.Ln, bias=oneT[:, 0:1], scale=1.0, extra=[oneT])
            stt(d0, d0[:, 0:8], d0, d0[:, 0:8], 0.0, d0, d0[:, 8:16], ALU.max, ALU.add)
            ts(d0, d0[:, 0:8], d0, d0[:, 0:8], tokm[:, 0:1], ALU.mult, extra=[tokm])
            tt(d0, d0[:, 8:16], d0, d0[:, 0:8], aneg, aneg[:], ALU.mult)
            a1 = sm[5]
            pa = nps()
            mm(pa, pa[:, 0:8], trif, trif[:], d0, d0[:, 8:16])
            cp(a1, a1[:, 0:8], pa, pa[:, 0:8])
            pa = nps()
            mm(pa, pa[:, 0:8], elast, elast[:], a1, a1[:, 0:8])
            cp(a1, a1[:, 8:16], pa, pa[:, 0:8])
            e1 = sm[6]
            act(e1, e1[:, 0:8], a1, a1[:, 0:8], AF.Exp)
            act(e1, e1[:, 8:16], a1, a1[:, 8:16], AF.Exp)
            tt(sm[7], sm[7][:, 0:8], a1, a1[:, 8:16], a1, a1[:, 0:8], ALU.subtract)
            act(sm[7], sm[7][:, 0:8], sm[7], sm[7][:, 0:8], AF.Exp)
            tt(sm[7], sm[7][:, 8:16], sm[7], sm[7][:, 0:8], d0, d0[:, 0:8], ALU.mult)
            xv = xs_tm[:].rearrange("p (h d) -> p h d", h=8)
            tt(xdt, xdt[:].rearrange("p (h d) -> p h d", h=8), xs_tm, xv, d0, bc_last(d0[:, 0:8], 64), ALU.mult)
            tt(xdtd, xdtd[:].rearrange("p (h d) -> p h d", h=8), xs_tm, xv, sm[7], bc_last(sm[7][:, 8:16], 64), ALU.mult)
            chk(6.1)
            pg_ = nps()
            for g in range(2):
                mm(pg_, pg_[:, g * 128:(g + 1) * 128], BT, BT[:], CTm[g], CTm[g][:])
            cp(GT, GT[:], pg_, pg_[:, 0:256].rearrange("p (g t) -> p g t", g=2), eng="act")
            tt(diag, diag[:], identf, bc_mid(identf[:], 8), a1, bc_last(a1[:, 0:8], 128), ALU.mult)
            for h in range(8):
                mm(psW, psW[:, h * 128:(h + 1) * 128], onesf, onesf[:], diag, diag[:, h, :])
            tt(seg, seg[:], psW, psW[:].rearrange("p (h t) -> p h t", h=8), a1, bc_last(a1[:, 0:8], 128), ALU.subtract)
            ts(seg, seg[:], seg, seg[:], 0.0, ALU.min)
            act(seg, seg[:], seg, seg[:], AF.Exp)
            tt(seg, seg[:].rearrange("p (g h) t -> p g h t", g=2), seg, seg[:].rearrange("p (g h) t -> p g h t", g=2),
               GT, GT[:].unsqueeze(2).to_broadcast([128, 2, 4, 128]), ALU.mult)
            tt(scT, scT[:], seg, seg[:], trif, bc_mid(trif[:], 8), ALU.mult)
            chk(6.2)
            py = nps()
            for h in range(8):
                mm(py, py[:, h * 64:(h + 1) * 64], scT, scT[:, h, :], xdt, xdt[:, h * 64:(h + 1) * 64])
            cp(ybuf, ybuf[:], py, py[:], eng="act")
            if ctx["first"] and s == 0:
                if samp:
                    for g in range(2):
                        dma(hio, hio[:].rearrange("p (hh g n) -> p hh g n", hh=4, g=2)[:, :, g, :], I["sssm"],
                            I["sssm"][l, j, g * 4:(g + 1) * 4].rearrange("hh p n -> p hh n"))
                    for hh in range(4):
                        pq = nps()
                        tr(pq, pq[:, 0:64], hio, hio[:, hh * 128:(hh + 1) * 128], identf)
                        cp(hst, hst[:, hh, :], pq, pq[:, 0:64])
                else:
                    mset(hst, hst[:], 0.0, eng="dve")
                cp(hstb, hstb[:], hst, hst[:])
            po = nps()
            for h in range(8):
                g = h // 4
                mm(po, po[:, h * 64:(h + 1) * 64], CTm[g], CTm[g][:], hstb, hstb[:, h % 4, :])
            tt(tm1, tm1[:].rearrange("p (h d) -> p h d", h=8), po, po[:].rearrange("p (h d) -> p h d", h=8),
               e1, bc_last(e1[:, 0:8], 64), ALU.mult)
            tt(ybuf, ybuf[:], ybuf, ybuf[:], tm1, tm1[:], ALU.add)
            tt(tm1, tm1[:].rearrange("p (h d) -> p h d", h=8), xs_tm, xv, dsk, bc_last(dsk[:], 64), ALU.mult)
            tt(ybuf, ybuf[:], ybuf, ybuf[:], tm1, tm1[:], ALU.add)
            chk(6.3)
            pst = nps()
            for h in range(8):
                mm(pst, pst[:, h * 64:(h + 1) * 64], B_tm, B_tm[:], xdtd, xdtd[:, h * 64:(h + 1) * 64])
            for g in range(2):
                r_ = slice(g * 64, (g + 1) * 64)
                tt(hst, hst[r_, :, :], hst, hst[r_, :, :], e1, bc_last(e1[r_, 8 + g * 4: 12 + g * 4], 64), ALU.mult)
                tt(hst, hst[r_, :, :], hst, hst[r_, :, :], pst,
                   pst[r_, g * 256:(g + 1) * 256].rearrange("p (h d) -> p h d", h=4), ALU.add)
            cp(hstb, hstb[:], hst, hst[:])
            if ctx["last"] and s == nsub - 1:
                for hh in range(4):
                    pq = nps()
                    tr(pq, pq[0:64, 0:128], hst, hst[:, hh, :], identf)
                    cp(hio, hio[:, hh * 128:(hh + 1) * 128], pq, pq[0:64, 0:128])
                od = O["ssm_s"][l, j] if samp else O["ssm_p"][l]
                for g in range(2):
                    outtoks.append(dma(O["ssm_s"] if samp else O["ssm_p"], od[g * 4:(g + 1) * 4].rearrange("hh p n -> p hh n"),
                                       hio, hio[:].rearrange("p (hh g n) -> p hh g n", hh=4, g=2)[:, :, g, :]))
            chk(6.4)
            act(tm1, tm1[:], pA, pA[:, C_Z - C_Q: C_Z - C_Q + 512], AF.Silu)
            tt(ybuf, ybuf[:], ybuf, ybuf[:], tm1, tm1[:], ALU.mult)
            rms_full(ybuf, ybuf[:], ssmg, tmb, tmb[:], 512)
            for c in range(4):
                tr(psT, psT[:, c * 128:(c + 1) * 128], tmb, tmb[:, c * 128:(c + 1) * 128], identb)
            cp(brT[1], brT[1][:, :, cs], psT, psT[:, 0:512].rearrange("p (g t) -> p g t", g=4), eng="act")

            chk(7)
            rms_heads(pB, pB[:, 8:520], 4, 128, mqg, tm1, tm1[:])
            cp(tmb, tmb[:], tm1, tm1[:])
            for h in range(4):
                tr(psT, psT[:, h * 128:(h + 1) * 128], tmb, tmb[:, h * 128:(h + 1) * 128], identb)
            cp(mqT, mqT[:], psT, psT[:, 0:512].rearrange("p (h t) -> p h t", h=4), eng="act")
            for mt in range(2):
                for h in range(4):
                    mm(psW, psW[:, mt * 512 + h * 128: mt * 512 + (h + 1) * 128], mkT, mkT[:, h, mt * 128:(mt + 1) * 128], mqT, mqT[:, h, :])
            act(PT, PT[:].rearrange("p a b -> p (a b)"), psW, psW[:], AF.Exp, scale=128 ** -0.5)
            pO = nps(); pD = nps()
            for h in range(4):
                for mt in range(2):
                    mm(pO, pO[:, h * 128:(h + 1) * 128], mvb, mvb[:, mt, h * 128:(h + 1) * 128], PT, PT[:, mt, h * 128:(h + 1) * 128],
                       start=(mt == 0), stop=(mt == 1))
            for mt in range(2):
                mm(pD, pD[:], onesb, onesb[:], PT, PT[:, mt, :], start=(mt == 0), stop=(mt == 1))
            recip(rden, rden[:], pD, pD[:])
            tt(brT[2], brT[2][:, :, cs], pO, pO[:].rearrange("p (h t) -> p h t", h=4), rden, rden[:].rearrange("p (h t) -> p h t", h=4), ALU.mult)

            chk(8)
            if samp:
                bcast_row(ropet, 16, I["ropes"], I["ropes"][0:1, :])
            else:
                dma(ropet, ropet[:], I["ropep"], I["ropep"][pos0 + s * 128: pos0 + (s + 1) * 128, :])
            rms_heads(pA, pA[:, 0:512], 8, 64, qg, qn, qn[:])
            rope(qn, qn[:], 8, 64)
            chk(8.05)
            for kv in range(2):
                cp(qbf, qbf[:].rearrange("p (g kv d) -> p g kv d", g=4, kv=2)[:, :, kv, :],
                   qn, qn[:].rearrange("p (kv g d) -> p kv g d", kv=2, g=4)[:, kv, :, :])
            for g in range(4):
                tr(psT, psT[:, g * 128:(g + 1) * 128], qbf, qbf[:, g * 128:(g + 1) * 128], identb)
            cp(qT, qT[:], psT, psT[:, 0:512].rearrange("p (g t) -> p g t", g=4), eng="act")
            chk(8.07)
            for kv in range(2):
                cp(qTm[kv], qTm[kv][kv * 64:(kv + 1) * 64, :, :], qT, qT[kv * 64:(kv + 1) * 64, :, :])
            chk(8.1)
            rms_heads(pA, pA[:, C_K - C_Q: C_K - C_Q + 128], 2, 64, kg, kout, kout[:])
            rope(kout, kout[:], 2, 64)
            if samp:
                outtoks.append(dma(O["k_s"], O["k_s"][l, j:j + 1, :], kout, kout[0:1, :]))
                outtoks.append(dma(O["v_s"], O["v_s"][l, j:j + 1, :], pA, pA[0:1, C_V - C_Q: C_V - C_Q + 128]))
            else:
                outtoks.append(dma(O["k_p"], O["k_p"][l, pos0 + s * 128: pos0 + (s + 1) * 128, :], kout, kout[:]))
                outtoks.append(dma(O["v_p"], O["v_p"][l, pos0 + s * 128: pos0 + (s + 1) * 128, :], pA, pA[:, C_V - C_Q: C_V - C_Q + 128]))
            cp(kbf, kbf[:], kout, kout[:])
            tr(psT, psT[:, 0:128], kbf, kbf[:], identb)
            cp(kTs, kTs[:], psT, psT[:, 0:128], eng="act")
            hp = (PAST if samp else pos0 + s * 128)
            dma(h_kT, h_kT[:, hp:hp + 128], kTs, kTs[:])
            cp(vbuf, vbuf[:, :, 0:64], pA, pA[:, C_V - C_Q: C_V - C_Q + 128].rearrange("p (kv d) -> p kv d", kv=2))
            dma(h_v, h_v[hp:hp + 128, :], vbuf, vbuf[:].rearrange("p a b -> p (a b)"))
            chk(8.2)
            cp(kiout, kiout[:], pA, pA[:, C_KI - C_Q: C_KI - C_Q + 64])
            rope(kiout, kiout[:], 1, 64)
            if samp:
                outtoks.append(dma(O["ki_s"], O["ki_s"][l, j:j + 1, :], kiout, kiout[0:1, :]))
            else:
                outtoks.append(dma(O["ki_p"], O["ki_p"][l, pos0 + s * 128: pos0 + (s + 1) * 128, :], kiout, kiout[:]))
            cp(kibf, kibf[:], kiout, kiout[:])
            tr(psT, psT[0:64, 0:128], kibf, kibf[:], identb)
            cp(kiTs, kiTs[:], psT, psT[0:64, 0:128], eng="act")
            dma(h_kiT, h_kiT[:, hp:hp + 128], kiTs, kiTs[:])
            chk(8.3)
            cp(qn, qn[:], pA, pA[:, C_QI - C_Q: C_QI - C_Q + 512])
            rope(qn, qn[:], 8, 64)
            cp(qbf, qbf[:], qn, qn[:])
            for half in range(2):
                for h4 in range(4):
                    h = half * 4 + h4
                    tr(psT, psT[0:64, h4 * 128:(h4 + 1) * 128], qbf, qbf[:, h * 64:(h + 1) * 64], identb)
                cp(qiT, qiT[:, half * 4:half * 4 + 4, :], psT, psT[0:64, 0:512].rearrange("p (h t) -> p h t", h=4), eng="act")
            wi_ap = pA[:, C_WI - C_Q: C_WI - C_Q + 8]
            ts(awi, awi[:], pA, wi_ap, -1.0, ALU.mult)
            tt(awi, awi[:], awi, awi[:], pA, wi_ap, ALU.max)
            act(swi, swi[:], pA, wi_ap, AF.Sign)
            ts(swi, swi[:], swi, swi[:], IDX_SCALE, ALU.mult)
            chk(9)
            nkt = hp // 128 + 1
            nkeys = nkt * 128
            for c0 in range(0, nkeys, 1024):
                n = min(1024, nkeys - c0)
                kb = kic[rot["kic"] % 2]; rot["kic"] += 1
                dma(kb, kb[:, 0:n], h_kiT, h_kiT[:, c0:c0 + n])
                for h in range(8):
                    S = SS[rot["rb"] % 2]
                    for b0 in range(0, n, 512):
                        bn = min(512, n - b0)
                        mm(S, S[:, b0:b0 + bn], qiT, qiT[:, h, :], kb, kb[:, b0:b0 + bn])
                    rb = rbuf[rot["rb"] % 2]; rot["rb"] += 1
                    act(rb, rb[:, 0:n], S, S[:, 0:n], AF.Relu, scale=awi[:, h:h + 1], extra=[awi])
                    if h == 0:
                        ts(Isc, Isc[:, c0:c0 + n], rb, rb[:, 0:n], swi[:, 0:1], ALU.mult, extra=[swi])
                    else:
                        stt(Isc, Isc[:, c0:c0 + n], rb, rb[:, 0:n], swi[:, h:h + 1], Isc, Isc[:, c0:c0 + n], ALU.mult, ALU.add, extra=[swi])
            am = sadd if samp else cadd
            tt(Isc, Isc[:, nkeys - 128:nkeys], Isc, Isc[:, nkeys - 128:nkeys], am, am[:], ALU.add)
            chk(10)
            KSEL = cfg.KS if samp else cfg.KP
            split = (nkt >= SPLIT_MIN_NKT)
            n1 = int(round(nkt * 0.42)) * 128 if split else nkeys
            n2 = nkeys - n1
            early = [(n_, c_) for n_ in (0, 2, 3, "g") for c_ in (0, 1)]

            def emit_early():
                n_, c_ = early.pop(0)
                if n_ == "g":
                    gate1_chunk(c_, s)
                else:
                    merge_chunk(n_, c_, s, True)
            if nkeys - 128 >= KSEL:
                red(lo, lo[:], Isc, Isc[:, 0:nkeys - 128], ALU.min)
                red(hi, hi[:], Isc, Isc[:, 0:nkeys], ALU.max)
                thr = float(KSEL) - 0.5 * n2
                for it in range(NITER):
                    ts(mid, mid[:], lo, lo[:], hi[:, 0:1], ALU.add, 0.5, ALU.mult, extra=[hi])
                    if split:
                        act(junkA, junkA[:, 0:n2], Isc, Isc[:, n1:nkeys], AF.Sign, bias=mid[:, 0:1], scale=-1.0,
                            accum=(cnt2, cnt2[:]), extra=[mid])
                    ts(junkD, junkD[:, 0:n1], Isc, Isc[:, 0:n1], mid[:, 0:1], ALU.is_ge, 0.0, ALU.add,
                       accum=(cnt, cnt[:]), extra=[mid])
                    if split:
                        stt(cnt, cnt[:], cnt2, cnt2[:], -0.5, cnt, cnt[:], ALU.mult, ALU.add)
                    ts(pge, pge[:], cnt, cnt[:], thr, ALU.is_ge)
                    ts(plt, plt[:], cnt, cnt[:], thr, ALU.is_lt)
                    P.op("dve", lambda e: e.copy_predicated(out=lo[:], mask=pge[:], data=mid[:]), reads=[pge, mid], writes=[lo])
                    P.op("dve", lambda e: e.copy_predicated(out=hi[:], mask=plt[:], data=mid[:]), reads=[plt, mid], writes=[hi])
                    if it in (2, 5, 8, 11, 14, 17, 19, 21) and early:
                        emit_early()
                ts(lo, lo[:], lo, lo[:], NEG / 2, ALU.max)
            else:
                mset(lo, lo[:], NEG / 2, eng="dve")
            while early:
                emit_early()
            ts(junkD, junkD[:, 0:n1], Isc, Isc[:, 0:n1], lo[:, 0:1], ALU.is_ge, extra=[lo])
            if n2 > 0:
                ts(junkA, junkA[:, 0:n2], Isc, Isc[:, n1:nkeys], lo[:, 0:1], ALU.is_ge, extra=[lo])

            def mask_ap(kt):
                if kt * 128 < n1:
                    return junkD, junkD[:, kt * 128:(kt + 1) * 128]
                return junkA, junkA[:, kt * 128 - n1:(kt + 1) * 128 - n1]
            chk(11)
            grp = {}

            def setup_group(g):
                k0 = g * 4
                nk = min(4, nkt - k0)
                kb = kTc[rot["kt"] % 2]; vb = vc[rot["kt"] % 2]; rot["kt"] += 1
                dma(kb, kb[:, 0:nk * 128], h_kT, h_kT[:, k0 * 128:(k0 + nk) * 128])
                dma(vb, vb[:, 0:nk, :], h_v, h_v[k0 * 128:(k0 + nk) * 128, :].rearrange("(t p) c -> p t c", p=128))
                gi = rot["mg"] % 2; rot["mg"] += 1
                for kk in range(nk):
                    mb_, map_ = mask_ap(k0 + kk)
                    tr(psT, psT[:, kk * 128:(kk + 1) * 128], mb_, map_, identb)
                cp(mT4[gi], mT4[gi][:, 0:nk, :], psT, psT[:, 0:nk * 128].rearrange("p (g t) -> p g t", g=nk), eng="act")
                grp[g] = (kb, vb, gi)

            def qk(kt):
                g, kk = divmod(kt, 4)
                if g not in grp:
                    setup_group(g)
                kb = grp[g][0]
                S = SS[kt % 2]
                for kv in range(2):
                    mm(S, S[:, kv * 512:(kv + 1) * 512], kb, kb[:, kk * 128:(kk + 1) * 128],
                       qTm[kv], qTm[kv][:].rearrange("p g t -> p (g t)"))

            qk(0)
            for kt in range(nkt):
                g, kk = divmod(kt, 4)
                if kk == 0 and (g + 1) * 4 < nkt:
                    setup_group(g + 1)
                if kt + 1 < nkt:
                    qk(kt + 1)
                kb, vb, gi = grp[g]
                i2 = kt % 2
                S = SS[i2]
                act(Eb[i2], Eb[i2][:].rearrange("p a b -> p (a b)"), S, S[:], AF.Exp, scale=0.125)
                tt(Pm[i2], Pm[i2][:], Eb[i2], Eb[i2][:], mT4[gi], bc_mid(mT4[gi][:, kk, :], 8), ALU.mult)
                for kv in range(2):
                    mm(psO[kv], psO[kv][:], vb, vb[:, kk, kv * 128:(kv + 1) * 128],
                       Pm[i2], Pm[i2][:, kv * 4:(kv + 1) * 4, :].rearrange("p g t -> p (g t)"),
                       start=(kt == 0), stop=(kt == nkt - 1))
            for kv in range(2):
                recip(rden, rden[64:128, :], psO[kv], psO[kv][64:128, :])
                dma(rdlo, rdlo[:], rden, rden[64:128, :])
                tt(brTb, brTb[:, kv * 4:(kv + 1) * 4, cs], psO[kv], psO[kv][0:64, :].rearrange("p (g t) -> p g t", g=4),
                   rdlo, rdlo[:].rearrange("p (g t) -> p g t", g=4), ALU.mult)

        chk(12)
        if DBG and not samp and l == 0:
            for n, bsrc_ in ((0, brT[0]), (2, brT[1]), (3, brT[2])):
                outtoks.append(dma(O["dbg"], O["dbg"][n, :, pos0:pos0 + Tn].rearrange("(g p) t -> p g t", p=128), bsrc_, bsrc_[:, :, 0:Tn]))
            outtoks.append(dma(O["dbg"], O["dbg"][1, :, pos0:pos0 + Tn].rearrange("(h p) t -> p h t", p=64), brTb, brTb[:, :, 0:Tn]))
        for c in range(2):
            for s in range(nsub):
                merge1_late(c, s)
        for s in range(nsub):
            cp(hb, hb[:], mixed[s], mixed[s][:])
            to_hT(hb, hb, hT, s * 128)
        for c in range(2):
            w = wload("w_out", l, 0, 8, c * 512, 512)
            for s in range(nsub):
                p = nps()
                for k in range(8):
                    mm(p, p[:], hT, hT[:, k, s * 128:(s + 1) * 128], w, w[:, k, 0:512], start=(k == 0), stop=(k == 7), signal=(k == 7))
                tt(x_mt, x_mt[:, s, c * 512:(c + 1) * 512], x_mt, x_mt[:, s, c * 512:(c + 1) * 512], p, p[:], ALU.add)
        chk(13)
        if DBG and not samp and l == 0:
            for s in range(nsub):
                outtoks.append(dma(O["dbgx"], O["dbgx"][pos0 + s * 128: pos0 + (s + 1) * 128, :], x_mt, x_mt[:, s, :]))
        for s in range(nsub):
            rms_full(x_mt, x_mt[:, s, :], gffn, hb, hb[:], D)
            to_hT(hb, hb, hT, s * 128)
        for j0 in range(0, 22, 4):
            nj = min(4, 22 - j0)
            wgs = wload("w_ffn_up", l, 0, 8, j0 * 128, nj * 128)
            wus = wload("w_ffn_up", l, 0, 8, 2816 + j0 * 128, nj * 128)
            for jj in range(nj):
                jg = j0 + jj; ju = 22 + jg
                pg = nps(); pu = nps()
                for k in range(8):
                    mm(pg, pg[:, 0:Tn], wgs, wgs[:, k, jj * 128:(jj + 1) * 128], hT, hT[:, k, 0:Tn], start=(k == 0), stop=(k == 7), signal=(k == 7))
                for k in range(8):
                    mm(pu, pu[:, 0:Tn], wus, wus[:, k, jj * 128:(jj + 1) * 128], hT, hT[:, k, 0:Tn], start=(k == 0), stop=(k == 7), signal=(k == 7))
                for (st, pp_, jc, co) in ((fst_g, pg, jg, fcg), (fst_u, pu, ju, fcu)):
                    cp(st, st[:, 0:2], fcar, fcar[:, jc, :])
                    cp(st, st[:, 2:2 + Tn], pp_, pp_[:, 0:Tn], eng="act")
                    if ctx["last"]:
                        cp(fso, fso[:, jc, :], st, st[:, tv:tv + 2])
                    else:
                        cp(fcar, fcar[:, jc, :], st, st[:, Tn:Tn + 2])
                    ts(co, co[:, 0:Tn], st, st[:, 0:Tn], cwf[:, jc, 0:1], ALU.mult, cbf[:, jc:jc + 1], ALU.add, extra=[cwf, cbf])
                    stt(co, co[:, 0:Tn], st, st[:, 1:1 + Tn], cwf[:, jc, 1:2], co, co[:, 0:Tn], ALU.mult, ALU.add, extra=[cwf])
                    stt(co, co[:, 0:Tn], st, st[:, 2:2 + Tn], cwf[:, jc, 2:3], co, co[:, 0:Tn], ALU.mult, ALU.add, extra=[cwf])
                act(fcg, fcg[:, 0:Tn], fcg, fcg[:, 0:Tn], AF.Silu)
                tt(gT, gT[:, jg, 0:Tn], fcg, fcg[:, 0:Tn], fcu, fcu[:, 0:Tn], ALU.mult)
        if ctx["last"]:
            od = (O["ffn_s"], O["ffn_s"][l, j]) if samp else (O["ffn_p"], O["ffn_p"][l])
            outtoks.extend(fm2tm(od[0], od[1], fso, lambda g: fso[:, g, :], 44, 2))
        for c in range(2):
            ws_ = [wload("w_ffn_down", l, r0 * 128, min(8, 22 - r0), c * 512, 512) for r0 in (0, 8, 16)]
            for s in range(nsub):
                p = nps()
                for jg in range(22):
                    w = ws_[jg // 8]
                    mm(p, p[:], gT, gT[:, jg, s * 128:(s + 1) * 128], w, w[:, jg % 8, 0:512], start=(jg == 0), stop=(jg == 21), signal=(jg == 21))
                tt(x_mt, x_mt[:, s, c * 512:(c + 1) * 512], x_mt, x_mt[:, s, c * 512:(c + 1) * 512], p, p[:], ALU.add)
        for s in range(nsub):
            if samp:
                if last_layer:
                    outtoks.append(dma(O["y_s"], O["y_s"][j:j + 1, :], x_mt, x_mt[0:1, s, :]))
                else:
                    dma(xsres, xsres[j:j + 1, :], x_mt, x_mt[0:1, s, :])
            else:
                dst = O["y_p"] if last_layer else xres
                t_ = dma(dst, dst[pos0 + s * 128: pos0 + (s + 1) * 128, :], x_mt, x_mt[:, s, :])
                if last_layer:
                    outtoks.append(t_)

    def sample_history(l, j):
        dma(ptb, ptb[:], I["pt"], I["pt"][j:j + 1, :].to_broadcast([128, cfg.NPG]))
        ts(pidx, pidx[:], ptb, ptb[:], 128, ALU.mult, iop[:, 0:1], ALU.add, extra=[iop])
        if l > 0:
            ts(pidx, pidx[:], pidx, pidx[:], float(l * NPHYS * 128), ALU.add)
        for pg in range(cfg.NPG):
            for (srcn, dstb, w) in (("cki", pgi, 64), ("ck", pgk, 128), ("cv", pgv, 128)):
                srcb = I[srcn]
                P.dma(lambda e, srcb=srcb, dstb=dstb, pg=pg: e.indirect_dma_start(
                    out=dstb[:], out_offset=None, in_=srcb[:].rearrange("l r c -> (l r) c"),
                    in_offset=bass.IndirectOffsetOnAxis(ap=pidx[:, pg:pg + 1], axis=0)),
                    reads=[srcb, pidx], writes=[dstb], q="pool")
            cp(kibf, kibf[:], pgi, pgi[:])
            tr(psT, psT[0:64, 0:128], kibf, kibf[:], identb)
            cp(kiTs, kiTs[:], psT, psT[0:64, 0:128], eng="act")
            dma(h_kiT, h_kiT[:, pg * 128:(pg + 1) * 128], kiTs, kiTs[:])
            cp(kbf, kbf[:], pgk, pgk[:])
            tr(psT, psT[:, 128:256], kbf, kbf[:], identb)
            cp(kTs, kTs[:], psT, psT[:, 128:256], eng="act")
            dma(h_kT, h_kT[:, pg * 128:(pg + 1) * 128], kTs, kTs[:])
            cp(vbuf, vbuf[:, :, 0:64], pgv, pgv[:].rearrange("p (kv d) -> p kv d", kv=2))
            dma(h_v, h_v[pg * 128:(pg + 1) * 128, :], vbuf, vbuf[:].rearrange("p a b -> p (a b)"))

    nmac = SEQ // T
    stop = 99
    for l in range(DEPTH):
        if stop < 1: break
        load_params(l)
        if stop < 2: break
        mem_kv_prompt(l)
        if stop < 3: break
        for m in range(nmac):
            try:
                macro(l, dict(kind="p", pos0=m * T, nsub=NSUB, tv=T, first=(m == 0), last=(m == nmac - 1)))
            except StopM:
                pass
            if stop < 4: break
        if stop < 5: break
        for j in range(NS):
            mem_kv_sample(l, j)
            sample_history(l, j)
            if stop < 6: break
            macro(l, dict(kind="s", j=j, pos0=PAST, nsub=1, tv=1, first=True, last=True))
        if stop < 7: break
    P.finish(outtoks)
    P.emit()
    es.close()
    return nc


_W_NAMES = ["norm_mix_g", "w_in", "conv_a_w", "conv_a_b", "ln_a_g", "ln_a_b", "q_norm_g", "k_norm_g", "ssm_conv_w",
            "ssm_conv_b", "dt_bias", "a_log", "d_skip", "ssm_norm_g", "mem_norm_g", "w_mem_kv", "mq_norm_g",
            "mk_norm_g", "w_branch", "w_out", "norm_ffn_g", "w_ffn_up", "ffn_conv_w", "ffn_conv_b", "w_ffn_down"]


def _rope_tab(pos):
    inv = 500000.0 ** (-np.arange(8, dtype=np.float64) * (2.0 / 16))
    ang = (pos.astype(np.float32)[:, None] * inv.astype(np.float32)[None, :]).astype(np.float32)
    return np.concatenate([np.cos(ang), np.sin(ang)], axis=1).astype(np.float32)


def run(cfg, inputs, n_cores=8):
    f = lambda a: np.ascontiguousarray(np.asarray(a))
    SEQ, PAST, DEPTH, NS = cfg.SEQ, cfg.PAST, cfg.DEPTH, cfg.NS
    nc = build(cfg)
    B = inputs["x_prompt"].shape[0]
    shared = {n: f(inputs[n]) for n in _W_NAMES}
    shared["w_branch"] = shared["w_branch"].reshape(DEPTH, 2048, D)
    shared["ck"] = f(inputs["cache_k"]).reshape(DEPTH, -1, 128)
    shared["cv"] = f(inputs["cache_v"]).reshape(DEPTH, -1, 128)
    shared["cki"] = f(inputs["cache_kidx"]).reshape(DEPTH, -1, 64)
    shared["ropep"] = _rope_tab(np.arange(SEQ)); shared["ropes"] = _rope_tab(np.array([PAST]))
    in_maps = []
    for c in range(n_cores):
        b = c % B; sl = slice(c * NS, (c + 1) * NS)
        m = dict(shared)
        m["xp"] = f(inputs["x_prompt"][b]); m["memp"] = f(inputs["mem_prompt"][b])
        m["xs"] = f(inputs["x_sample"][sl, 0])
        m["cmk"] = f(inputs["cache_mem_k"][:, sl]).reshape(DEPTH, NS, 256, 512)
        m["cmv"] = f(inputs["cache_mem_v"][:, sl]).reshape(DEPTH, NS, 256, 512)
        m["sconf"] = f(inputs["state_conformer"][:, sl]); m["ssc"] = f(inputs["state_ssm_conv"][:, sl])
        m["sssm"] = f(inputs["state_ssm"][:, sl]); m["sffn"] = f(inputs["state_ffn_conv"][:, sl])
        m["pt"] = f(inputs["page_table"][sl]).astype(np.int32)
        in_maps.append(m)
    res = run_bass_kernel_spmd(nc, in_maps, core_ids=list(range(n_cores)))
    R = res.results
    st = lambda k, cores: np.stack([R[c][k] for c in cores])
    pc = list(range(B))
    ac = list(range(n_cores))
    cat = lambda k: np.concatenate([R[c][k] for c in ac], axis=1)
    y_p = st("y_p", pc)
    y_s = np.concatenate([R[c]["y_s"] for c in ac], axis=0)[:, None, :]
    k_p = st("k_p", pc).transpose(1, 0, 2, 3).reshape(DEPTH, B, SEQ, 2, 64)
    v_p = st("v_p", pc).transpose(1, 0, 2, 3).reshape(DEPTH, B, SEQ, 2, 64)
    ki_p = st("ki_p", pc).transpose(1, 0, 2, 3)
    mk_p = st("mk_p", pc).transpose(1, 0, 2, 3).reshape(DEPTH, B, 256, 4, 128)
    mv_p = st("mv_p", pc).transpose(1, 0, 2, 3).reshape(DEPTH, B, 256, 4, 128)
    conf_p = st("conf_p", pc).transpose(1, 0, 2, 3)
    sc_p = st("sc_p", pc).transpose(1, 0, 2, 3)
    ssm_p = st("ssm_p", pc).transpose(1, 0, 2, 3, 4)
    ffn_p = st("ffn_p", pc).transpose(1, 0, 2, 3)
    k_s = cat("k_s").reshape(DEPTH, -1, 1, 2, 64); v_s = cat("v_s").reshape(DEPTH, -1, 1, 2, 64)
    ki_s = cat("ki_s").reshape(DEPTH, -1, 1, 64)
    outs = (y_p, y_s, k_p, v_p, ki_p, mk_p, mv_p, conf_p, sc_p, ssm_p, ffn_p, k_s, v_s, ki_s,
            cat("conf_s"), cat("sc_s"), cat("ssm_s"), cat("ffn_s"))
    return tuple(np.ascontiguousarray(o.astype(np.float32)) for o in outs)


def kernel(**inputs):
    return run(CFG(), inputs)
```
